# Optimizing a Trainium2 kernel written in Bass

```python
import math
import numpy as np
import jax
import jax.numpy as jnp
from jax import lax

D_MODEL = 1024
BATCH = 8
SEQ = 2048
DEPTH = 4

M_HEADS = 4
M_DK = D_MODEL // M_HEADS
M_DV = D_MODEL // M_HEADS
M_WIDTH = M_HEADS * M_DV
M_CHUNK = 64
DN_DK = 128
DN_DV = 128
DN_HEADS = D_MODEL // DN_DV
DN_WIDTH = DN_HEADS * DN_DV
DN_QKV = 2 * DN_HEADS * DN_DK + DN_WIDTH
DN_CHUNK = 64
CONV_K = 3
SC_WIDTH = D_MODEL
N_BRANCH = 3
BRANCH_W = D_MODEL
D_FF = 4 * D_MODEL
EPS = 1e-6

PROJ_SIZES = (
    M_HEADS * M_DK,
    M_HEADS * M_DK,
    M_WIDTH,
    M_WIDTH,
    4 * M_HEADS,
    DN_QKV,
    DN_WIDTH,
    4 * DN_HEADS,
    3 * SC_WIDTH,
    N_BRANCH * D_MODEL,
)
N_PROJ = sum(PROJ_SIZES)

kernel_name = "hybrid_mlstm_deltanet_shortconv_encoder"


def _split_points():
    return np.cumsum(np.array(PROJ_SIZES))[:-1].tolist()


def _rmsnorm(x, g):
    xf = x.astype(jnp.float32)
    y = xf * lax.rsqrt(jnp.mean(jnp.square(xf), axis=-1, keepdims=True) + EPS)
    return (y * g.astype(jnp.float32)).astype(x.dtype)


def _l2norm(t):
    return t * lax.rsqrt(jnp.sum(t * t, axis=-1, keepdims=True) + EPS)


def _dwconv_centred(x, w):
    k = w.shape[0]
    return lax.conv_general_dilated(
        x, w[:, None, :].astype(x.dtype), window_strides=(1,),
        padding=[(k // 2, k // 2)], dimension_numbers=("NWC", "WIO", "NWC"),
        feature_group_count=x.shape[-1])


def _to_chunks(t, chunk):
    b, s, h = t.shape[:3]
    t = t.reshape((b, s // chunk, chunk, h) + t.shape[3:])
    return jnp.moveaxis(t, (1, 3), (0, 2))


def _from_chunks(t):
    nc, b, h, l = t.shape[:4]
    t = jnp.moveaxis(t, (0, 2), (1, 3))
    return t.reshape((b, nc * l, h) + t.shape[4:])


def _mlstm_dir(q, k, v, i_pre, f_pre):
    b, s, h, dk = k.shape
    dv = v.shape[-1]
    L = M_CHUNK
    qc, kc, vc = _to_chunks(q, L), _to_chunks(k, L), _to_chunks(v, L)
    ic = _to_chunks(i_pre, L)
    lfc = _to_chunks(jax.nn.log_sigmoid(f_pre), L)
    tril = jnp.tril(jnp.ones((L, L), dtype=bool))

    def step(carry, inp):
        c_st, n_st, m_st = carry
        qq, kk, vv, ig, lf = inp
        bcum = jnp.cumsum(lf, axis=-1)
        log_d = jnp.where(tril, bcum[..., :, None] - bcum[..., None, :] + ig[..., None, :], -jnp.inf)
        m_t = jnp.maximum(bcum + m_st[..., None], jnp.max(log_d, axis=-1))
        d = jnp.exp(log_d - m_t[..., None])
        inter = jnp.exp(bcum + m_st[..., None] - m_t)
        sc = jnp.einsum("bhld,bhsd->bhls", qq, kk) * d
        num = inter[..., None] * jnp.einsum("bhld,bhde->bhle", qq, c_st) + jnp.einsum("bhls,bhse->bhle", sc, vv)
        den = inter * jnp.einsum("bhld,bhd->bhl", qq, n_st) + jnp.sum(sc, axis=-1)
        h_out = num / jnp.maximum(jnp.abs(den), jnp.exp(-m_t))[..., None]
        a = bcum[..., -1:] - bcum + ig
        m_new = jnp.maximum(bcum[..., -1] + m_st, jnp.max(a, axis=-1))
        wgt = jnp.exp(a - m_new[..., None])
        dec = jnp.exp(bcum[..., -1] + m_st - m_new)
        kw = kk * wgt[..., None]
        c_new = dec[..., None, None] * c_st + jnp.einsum("bhld,bhle->bhde", kw, vv)
        n_new = dec[..., None] * n_st + jnp.sum(kw, axis=2)
        return (c_new, n_new, m_new), h_out

    init = (jnp.zeros((b, h, dk, dv), jnp.float32), jnp.zeros((b, h, dk), jnp.float32),
            jnp.zeros((b, h), jnp.float32))
    _, hs = lax.scan(step, init, (qc, kc, vc, ic, lfc))
    return _from_chunks(hs)


def _gated_delta_dir(q, k, v, beta, g):
    b, s, h, dk = k.shape
    dv = v.shape[-1]
    L = DN_CHUNK
    qc, kc, vc = _to_chunks(q, L), _to_chunks(k, L), _to_chunks(v, L)
    bc = _to_chunks(beta, L)
    gcum = jnp.cumsum(_to_chunks(g, L), axis=-1)
    tril = jnp.tril(jnp.ones((L, L), dtype=bool))
    strict = jnp.tril(jnp.ones((L, L), dtype=bool), -1)
    decay = jnp.exp(jnp.where(tril, gcum[..., :, None] - gcum[..., None, :], -jnp.inf))
    kb = kc * bc[..., None]
    a_mat = jnp.where(strict, jnp.einsum("nbhld,nbhsd->nbhls", kb, kc) * decay, 0.0)
    system = a_mat + jnp.eye(L, dtype=a_mat.dtype)
    rhs = jnp.concatenate([kb * jnp.exp(gcum)[..., None], vc * bc[..., None]], axis=-1)
    sol = lax.linalg.triangular_solve(system, rhs, left_side=True, lower=True, unit_diagonal=True)
    w_c, u_c = sol[..., :dk], sol[..., dk:]
    attn = jnp.einsum("nbhld,nbhsd->nbhls", qc, kc) * decay
    q_dec = qc * jnp.exp(gcum)[..., None]
    k_dec = kc * jnp.exp(gcum[..., -1:] - gcum)[..., None]
    g_last = jnp.exp(gcum[..., -1])

    def step(st, inp):
        ww, uu, at, qd, kd, gl = inp
        v_new = uu - jnp.einsum("bhld,bhde->bhle", ww, st)
        o = jnp.einsum("bhld,bhde->bhle", qd, st) + jnp.einsum("bhls,bhse->bhle", at, v_new)
        st = gl[..., None, None] * st + jnp.einsum("bhld,bhle->bhde", kd, v_new)
        return st, o

    s0 = jnp.zeros((b, h, dk, dv), jnp.float32)
    _, o = lax.scan(step, s0, (w_c, u_c, attn, q_dec, k_dec, g_last))
    return _from_chunks(o)


def _flip(t):
    return jnp.flip(t, axis=1)


def _layer(x, norm_mix_g, w_in, m_gate_b, m_norm_g, dn_conv_w, dn_a_log, dn_dt_bias,
           dn_norm_g, sc_conv_w, w_branch, w_out, norm_mlp_g, w_up, w_down):
    dt = x.dtype
    f32 = jnp.float32
    b, s, _ = x.shape
    hn = _rmsnorm(x, norm_mix_g)
    proj = jnp.einsum("bsd,dn->bsn", hn, w_in)
    (m_q, m_k, m_v, m_o, m_g, dn_qkv, dn_z, dn_g, sc_bcx, merge_pre) = jnp.split(proj, _split_points(), axis=-1)

    mq = m_q.reshape(b, s, M_HEADS, M_DK).astype(f32) * (M_DK ** -0.5)
    mk = m_k.reshape(b, s, M_HEADS, M_DK).astype(f32)
    mv = m_v.reshape(b, s, M_HEADS, M_DV).astype(f32)
    mg = m_g.reshape(b, s, 4, M_HEADS).astype(f32) + m_gate_b.astype(f32)
    h_fwd = _mlstm_dir(mq, mk, mv, mg[:, :, 0], mg[:, :, 1])
    h_bwd = _mlstm_dir(_flip(mq), _flip(mk), _flip(mv), _flip(mg[:, :, 2]), _flip(mg[:, :, 3]))
    hm = h_fwd + _flip(h_bwd)
    y_m = jax.nn.sigmoid(m_o.astype(f32)) * _rmsnorm(hm, m_norm_g.reshape(M_HEADS, M_DV)).reshape(b, s, M_WIDTH)

    qkv = jax.nn.silu(_dwconv_centred(dn_qkv, dn_conv_w))
    dq, dk_, dv_ = jnp.split(qkv, [DN_HEADS * DN_DK, 2 * DN_HEADS * DN_DK], axis=-1)
    dq = _l2norm(dq.reshape(b, s, DN_HEADS, DN_DK).astype(f32)) * (DN_DK ** -0.5)
    dk_ = _l2norm(dk_.reshape(b, s, DN_HEADS, DN_DK).astype(f32))
    dv_ = dv_.reshape(b, s, DN_HEADS, DN_DV).astype(f32)
    dg = dn_g.reshape(b, s, 4, DN_HEADS).astype(f32)
    beta = jax.nn.sigmoid(dg[:, :, 0::2])
    gdec = -jnp.exp(dn_a_log.astype(f32)) * jax.nn.softplus(dg[:, :, 1::2] + dn_dt_bias.astype(f32))
    o_fwd = _gated_delta_dir(dq, dk_, dv_, beta[:, :, 0], gdec[:, :, 0])
    o_bwd = _gated_delta_dir(_flip(dq), _flip(dk_), _flip(dv_), _flip(beta[:, :, 1]), _flip(gdec[:, :, 1]))
    od = o_fwd + _flip(o_bwd)
    y_d = (_rmsnorm(od, dn_norm_g) * jax.nn.silu(dn_z.reshape(b, s, DN_HEADS, DN_DV).astype(f32))).reshape(b, s, DN_WIDTH)

    sc_b, sc_c, sc_x = jnp.split(sc_bcx, 3, axis=-1)
    y_c = sc_b * _dwconv_centred(sc_c * sc_x, sc_conv_w)

    ys = jnp.stack([y_m.astype(dt), y_d.astype(dt), y_c], axis=2)
    branch = jnp.einsum("bsnw,nwd->bsnd", ys, w_branch)
    gates = jax.nn.sigmoid(merge_pre.reshape(b, s, N_BRANCH, D_MODEL))
    mixed = jnp.einsum("bsnd,bsnd->bsd", gates, branch)
    x = x + jnp.einsum("bsd,de->bse", mixed, w_out)

    h2 = _rmsnorm(x, norm_mlp_g)
    up = jnp.square(jax.nn.relu(jnp.einsum("bsd,df->bsf", h2, w_up)))
    x = x + jnp.einsum("bsf,fd->bsd", up, w_down)
    return x


def setup_inputs(seed: int = 0) -> dict:
    key = jax.random.key(seed)
    ks = jax.random.split(key, 16)
    f32 = jnp.float32

    def nrm(k, shape, scale):
        return jax.random.normal(k, shape, f32) * scale

    x = nrm(ks[0], (BATCH, SEQ, D_MODEL), 1.0)
    norm_mix_g = 1.0 + nrm(ks[1], (DEPTH, D_MODEL), 0.02)
    w_in = nrm(ks[2], (DEPTH, D_MODEL, N_PROJ), D_MODEL ** -0.5)
    f_base = jnp.zeros((4, M_HEADS), f32).at[1::2].set(jnp.linspace(3.0, 6.0, M_HEADS, dtype=f32))
    m_gate_b = f_base[None] + nrm(ks[3], (DEPTH, 4, M_HEADS), 0.1)
    m_norm_g = 1.0 + nrm(ks[4], (DEPTH, M_WIDTH), 0.02)
    dn_conv_w = nrm(ks[5], (DEPTH, CONV_K, DN_QKV), CONV_K ** -0.5)
    dn_a_log = jnp.log(jax.random.uniform(ks[6], (DEPTH, 2, DN_HEADS), f32, 1.0, 16.0))
    dt0 = jnp.exp(jax.random.uniform(ks[7], (DEPTH, 2, DN_HEADS), f32, math.log(1e-3), math.log(1e-1)))
    dn_dt_bias = dt0 + jnp.log(-jnp.expm1(-dt0))
    dn_norm_g = 1.0 + nrm(ks[8], (DEPTH, DN_DV), 0.02)
    sc_conv_w = nrm(ks[9], (DEPTH, CONV_K, SC_WIDTH), CONV_K ** -0.5)
    w_branch = nrm(ks[10], (DEPTH, N_BRANCH, BRANCH_W, D_MODEL), BRANCH_W ** -0.5)
    w_out = nrm(ks[11], (DEPTH, D_MODEL, D_MODEL), D_MODEL ** -0.5)
    norm_mlp_g = 1.0 + nrm(ks[12], (DEPTH, D_MODEL), 0.02)
    w_up = nrm(ks[13], (DEPTH, D_MODEL, D_FF), D_MODEL ** -0.5)
    w_down = nrm(ks[14], (DEPTH, D_FF, D_MODEL), D_FF ** -0.5)
    norm_final_g = 1.0 + nrm(ks[15], (D_MODEL,), 0.02)
    return {"x": x, "norm_mix_g": norm_mix_g, "w_in": w_in, "m_gate_b": m_gate_b,
            "m_norm_g": m_norm_g, "dn_conv_w": dn_conv_w, "dn_a_log": dn_a_log,
            "dn_dt_bias": dn_dt_bias, "dn_norm_g": dn_norm_g, "sc_conv_w": sc_conv_w,
            "w_branch": w_branch, "w_out": w_out, "norm_mlp_g": norm_mlp_g,
            "w_up": w_up, "w_down": w_down, "norm_final_g": norm_final_g}


def reference(x, norm_mix_g, w_in, m_gate_b, m_norm_g, dn_conv_w, dn_a_log, dn_dt_bias,
              dn_norm_g, sc_conv_w, w_branch, w_out, norm_mlp_g, w_up, w_down, norm_final_g):
    for l in range(DEPTH):
        x = _layer(x, norm_mix_g[l], w_in[l], m_gate_b[l], m_norm_g[l], dn_conv_w[l],
                   dn_a_log[l], dn_dt_bias[l], dn_norm_g[l], sc_conv_w[l], w_branch[l],
                   w_out[l], norm_mlp_g[l], w_up[l], w_down[l])
    return _rmsnorm(x, norm_final_g)
```

```python
import numpy as np
import concourse.bass as bass
import concourse.mybir as mybir
from concourse.bass_utils import run_bass_kernel_spmd
from contextlib import ExitStack

F32 = mybir.dt.float32
BF16 = mybir.dt.bfloat16
ALU = mybir.AluOpType
AF = mybir.ActivationFunctionType
AX = mybir.AxisListType

S = 2048
D = 1024
NT = 16
DEPTH = 4
NPROJ = 14384
DFF = 4096
EPS = 1e-6
NCOL = 112
NROW = 1200
O_MQ, O_MK, O_MV, O_MO, O_MG = 0, 1024, 2048, 3072, 4096
O_DQ, O_DK, O_DV, O_DZ, O_DG = 4112, 5136, 6160, 7184, 8208
O_SB, O_SC, O_SX, O_MRG = 8240, 9264, 10288, 11312
NEG = -30000.0


class Tok:
    __slots__ = ("sem", "val", "clk")

    def __init__(self, sem, val, clk):
        self.sem, self.val, self.clk = sem, val, clk


class Buf:
    __slots__ = ("t", "w", "r", "name", "excl")

    def __init__(self, t, name, excl=False):
        self.t, self.name = t, name
        self.w = None
        self.r = []
        self.excl = excl


class Prog:
    NDMA = 12

    def __init__(self, nc, es, same_engine_sync=True):
        self.nc, self.es = nc, es
        self.same = same_engine_sync
        self.eng = {"pe": nc.tensor, "act": nc.scalar, "dve": nc.vector, "pool": nc.gpsimd, "sp": nc.sync}
        self.sem = {k: es.enter_context(nc.semaphore("s_" + k)) for k in self.eng}
        self.cnt = {k: 0 for k in self.eng}
        self.clk = {k: {} for k in self.eng}
        self.pend = {k: [] for k in self.eng}
        self.dsem = [es.enter_context(nc.semaphore("d%d" % i)) for i in range(self.NDMA)]
        self.dcnt = [0] * self.NDMA
        self.dnext = 0
        self.nwait = 0
        self.nops = 0

    def sb(self, name, shape, dt):
        t = self.es.enter_context(self.nc.sbuf_tensor(name, list(shape), dt))
        return Buf(t, name)

    def ps(self, name, shape, dt):
        t = self.es.enter_context(self.nc.psum_tensor(name, list(shape), dt))
        return Buf(t, name, excl=True)

    def dram(self, name, shape, dt):
        t = self.nc.dram_tensor(name, list(shape), dt, kind="Internal").ap()
        return Buf(t, name)

    def _need(self, reads, writes):
        toks = []
        for b in reads:
            if b.w is not None:
                toks.append(b.w)
        for b in writes:
            if b.w is not None:
                toks.append(b.w)
            toks.extend(b.r)
        return toks

    def _wait(self, e, toks):
        clk = self.clk[e]
        eng = self.eng[e]
        own = self.sem[e]
        best = {}
        for t in toks:
            if t.sem is own and (e == "pe" or not self.same):
                continue
            k = id(t.sem)
            if clk.get(k, 0) >= t.val:
                continue
            if k not in best or best[k].val < t.val:
                best[k] = t
        for k, t in best.items():
            if clk.get(k, 0) >= t.val:
                continue
            eng.wait_ge(t.sem, t.val)
            self.nwait += 1
            clk[k] = t.val
            for kk, vv in t.clk.items():
                if clk.get(kk, 0) < vv:
                    clk[kk] = vv

    def op(self, e, fn, reads, writes, inc=True):
        if any(b.excl for b in reads):
            writes = list(writes) + [b for b in reads if b.excl]
            reads = [b for b in reads if not b.excl]
        self._wait(e, self._need(reads, writes))
        ins = fn(self.eng[e])
        self.nops += 1
        if not inc:
            self.pend[e].append((reads, writes))
            return ins
        self.cnt[e] += 1
        ins.then_inc(self.sem[e], 1)
        tok = Tok(self.sem[e], self.cnt[e], dict(self.clk[e]))
        tok.clk[id(self.sem[e])] = self.cnt[e]
        for (rs, ws) in self.pend[e] + [(reads, writes)]:
            for b in rs:
                b.r.append(tok)
            for b in ws:
                b.w = tok
                b.r = []
        self.pend[e] = []
        return ins

    def dma(self, out, in_, reads, writes, q="sp"):
        toks = self._need(reads, writes)
        i = self.dnext
        self.dnext = (self.dnext + 1) % self.NDMA
        s = self.dsem[i]
        if self.dcnt[i] > 0:
            toks.append(Tok(s, self.dcnt[i], {}))
        self._wait(q, toks)
        ins = self.eng[q].dma_start(out=out, in_=in_)
        self.dcnt[i] += 16
        ins.then_inc(s, 16)
        tok = Tok(s, self.dcnt[i], dict(self.clk[q]))
        for b in reads:
            b.r.append(tok)
        for b in writes:
            b.w = tok
            b.r = []
        self.nops += 1
        return ins

    def barrier(self):
        toks = []
        for i in range(self.NDMA):
            if self.dcnt[i] > 0:
                toks.append(Tok(self.dsem[i], self.dcnt[i], {}))
        for k in self.eng:
            if self.cnt[k] > 0:
                toks.append(Tok(self.sem[k], self.cnt[k], {}))
        for e in self.eng:
            self._wait(e, [t for t in toks if t.sem is not self.sem[e]])

    def finish(self):
        toks = []
        for i in range(self.NDMA):
            if self.dcnt[i] > 0:
                toks.append(Tok(self.dsem[i], self.dcnt[i], {}))
        for k in self.eng:
            if self.cnt[k] > 0 and k != "sp":
                toks.append(Tok(self.sem[k], self.cnt[k], {}))
        self._wait("sp", toks)


STAG_CFG = [2]
NS_CFG = [5]


def build(nlayers=DEPTH, dbg=(), stop=None):
    nc = bass.Bass("TRN2", target_bir_lowering=False)

    def din(name, shape):
        return nc.dram_tensor(name, list(shape), F32, kind="ExternalInput").ap()

    x_d = din("x", [S, D])
    w_in_d = din("w_in", [DEPTH, D, NPROJ])
    w_br_d = din("w_branch", [DEPTH, 3, D, D])
    w_out_d = din("w_out", [DEPTH, D, D])
    w_up_d = din("w_up", [DEPTH, D, DFF])
    w_dn_d = din("w_down", [DEPTH, DFF, D])
    colp_d = din("colp", [DEPTH, 128, NCOL])
    rowp_d = din("rowp", [DEPTH, NROW])
    gfin_d = din("gfin", [D])
    out_d = nc.dram_tensor("out", [S, D], F32, kind="ExternalOutput").ap()
    dbg_d = {}
    for name, shape, dt in (("d_ys", [3, D, S], BF16), ("d_xm", [S, D], F32)):
        if name in dbg:
            dbg_d[name] = nc.dram_tensor(name, shape, dt, kind="ExternalOutput").ap()

    with ExitStack() as es:
        P = Prog(nc, es)
        hT = P.sb("hT", [128, 8, S], BF16)
        hTv = [Buf(hT.t, "hT%d" % i) for i in range(NT)]
        NWB = 4
        wb = [P.sb("wb%d" % i, [128, 8, 512], BF16) for i in range(NWB)]
        wbi = [0]

        wlim = [NWB]

        def nextw():
            b = wb[wbi[0] % wlim[0]]
            wbi[0] += 1
            return b

        identb = P.sb("identb", [128, 128], BF16)
        identf = P.sb("identf", [128, 128], F32)
        onesb = P.sb("onesb", [128, 128], BF16)
        onesf = P.sb("onesf", [128, 128], F32)
        tincl = P.sb("tincl", [128, 128], F32)
        tinclT = P.sb("tinclT", [128, 128], F32)
        mlo = P.sb("mlo", [128, 128], F32)
        mup = P.sb("mup", [128, 128], F32)
        slf = P.sb("slf", [128, 128], F32)
        suf = P.sb("suf", [128, 128], F32)
        colps = [P.sb("colp_sb%d" % i, [128, NCOL], F32) for i in range(2)]
        rowp = P.sb("rowp_sb", [128, NROW], F32)
        wsm = P.sb("wsm", [128, 8, 48], BF16)
        wsf = P.sb("wsf", [128, 8, 48], F32)
        nsq_l = [P.sb("nsq%d" % i, [128, D], BF16) for i in range(2)]
        nss_l = [P.sb("nss%d" % i, [128, 2], F32) for i in range(2)]
        nxn_l = [P.sb("nxn%d" % i, [128, D], BF16) for i in range(2)]
        nrm_i = [0]
        pb = [P.ps("pb%d" % i, [128, 512], F32) for i in range(7)]
        pbt = P.ps("pbt", [128, 1024], BF16)
        ysT = P.dram("ysT", [3, D, S], BF16)
        ysv = [[Buf(ysT.t, "ys%d_%d" % (n, c)) for c in range(8)] for n in range(3)]
        xcur = P.dram("xcur", [S, D], F32)
        xcv = [Buf(xcur.t, "xc%d" % i) for i in range(NT)]

        def pool(fn, r, w):
            P.op("pool", fn, r, w)

        pool(lambda e: e.memset(onesf.t[:], 1.0), [], [onesf])
        pool(lambda e: e.memset(onesb.t[:], 1.0), [], [onesb])
        pool(lambda e: e.memset(identf.t[:], 0.0), [], [identf])
        pool(lambda e: e.affine_select(identf.t[:], identf.t[:], pattern=[[-1, 128]], compare_op=ALU.not_equal, fill=1.0, base=0, channel_multiplier=1), [identf], [identf])
        pool(lambda e: e.tensor_copy(identb.t[:], identf.t[:]), [identf], [identb])
        pool(lambda e: e.affine_select(tincl.t[:], onesf.t[:], pattern=[[1, 128]], compare_op=ALU.is_ge, fill=0.0, base=0, channel_multiplier=-1), [onesf], [tincl])
        pool(lambda e: e.affine_select(tinclT.t[:], onesf.t[:], pattern=[[-1, 128]], compare_op=ALU.is_ge, fill=0.0, base=0, channel_multiplier=1), [onesf], [tinclT])
        pool(lambda e: e.memset(mlo.t[:], 0.0), [], [mlo])
        pool(lambda e: e.affine_select(mlo.t[:], mlo.t[:], pattern=[[1, 128]], compare_op=ALU.is_ge, fill=NEG, base=0, channel_multiplier=-1), [mlo], [mlo])
        pool(lambda e: e.memset(mup.t[:], 0.0), [], [mup])
        pool(lambda e: e.affine_select(mup.t[:], mup.t[:], pattern=[[-1, 128]], compare_op=ALU.is_ge, fill=NEG, base=0, channel_multiplier=1), [mup], [mup])
        pool(lambda e: e.affine_select(slf.t[:], onesf.t[:], pattern=[[-1, 128]], compare_op=ALU.is_gt, fill=0.0, base=0, channel_multiplier=1), [onesf], [slf])
        pool(lambda e: e.affine_select(suf.t[:], onesf.t[:], pattern=[[1, 128]], compare_op=ALU.is_gt, fill=0.0, base=0, channel_multiplier=-1), [onesf], [suf])

        def dump(name, buf, ap, shape, dt=F32):
            if ("D_" + name) in dbg:
                dd = nc.dram_tensor("D_" + name, list(shape), dt, kind="ExternalOutput").ap()
                P.dma(dd, ap, [buf], [])

        uid = [0]

        def scope():
            class _S:
                def __enter__(s_):
                    s_.es = ExitStack()
                    s_.es.__enter__()
                    return s_

                def sb(s_, name, shape, dt):
                    uid[0] += 1
                    return Buf(s_.es.enter_context(nc.sbuf_tensor("%s_u%d" % (name, uid[0]), list(shape), dt)), name)

                def __exit__(s_, *a):
                    P.barrier()
                    return s_.es.__exit__(*a)
            return _S()

        evi = [0]

        def evac_copy(out_ap, in_ap, reads, writes, scale=None):
            evi[0] += 1
            if evi[0] % 2 == 0:
                if scale is None:
                    P.op("act", lambda e: e.copy(out_ap, in_ap), reads, writes)
                else:
                    P.op("act", lambda e: e.mul(out_ap, in_ap, scale), reads, writes)
            else:
                if scale is None:
                    P.op("dve", lambda e: e.tensor_copy(out_ap, in_ap), reads, writes)
                else:
                    P.op("dve", lambda e: e.tensor_scalar_mul(out_ap, in_ap, scale), reads, writes)

        def wload(dst, col0, src2d, c0, n):
            P.dma(dst.t[:, :, col0:col0 + n], src2d.rearrange("(kc p) n -> p kc n", p=128)[:, :, c0:c0 + n], [], [dst], q="pool")

        def rsqrt_(dst_ap, src_ap, scale, reads, writes):
            P.op("act", lambda e: e.activation(dst_ap, src_ap, AF.Ln, bias=EPS, scale=scale), reads, writes)
            P.op("act", lambda e: e.activation(dst_ap, dst_ap, AF.Exp, scale=-0.5), writes, writes)

        def norm_tile(xt_ap, xbuf, tt, gcol):
            gbuf, gap = gcol
            nrm_i[0] += 1
            nsq, nss, nxn = nsq_l[nrm_i[0] % 2], nss_l[nrm_i[0] % 2], nxn_l[nrm_i[0] % 2]
            pbo = (nrm_i[0] % 2) * 0
            P.op("act", lambda e: e.activation(nsq.t[:], xt_ap, AF.Square, accum_out=nss.t[:, 0:1]), [xbuf], [nsq, nss])
            rsqrt_(nss.t[:, 1:2], nss.t[:, 0:1], 1.0 / D, [nss], [nss])
            P.op("dve", lambda e: e.tensor_scalar(nxn.t[:], xt_ap, nss.t[:, 1:2], None, ALU.mult), [xbuf, nss], [nxn])
            for c in range(8):
                P.op("pe", lambda e: e.transpose(pbt.t[:, c * 128:(c + 1) * 128], nxn.t[:, c * 128:(c + 1) * 128], identb.t[:]), [nxn, identb], [pbt], inc=(c == 7))
            P.op("dve", lambda e: e.tensor_tensor(hT.t[:, :, tt * 128:(tt + 1) * 128], pbt.t[:].rearrange("p (c t) -> p c t", c=8),
                                                  gap.unsqueeze(2).to_broadcast([128, 8, 128]), ALU.mult), [pbt, gbuf], [hTv[tt]])

        def softplus_(dst, src, t1, t2, n, sign_logsig=False):
            P.op("dve", lambda e: e.scalar_tensor_tensor(t1.t[:, 0:n], src.t[:, 0:n], -1.0, src.t[:, 0:n], ALU.mult, ALU.max), [src], [t1])
            P.op("act", lambda e: e.activation(t2.t[:, 0:n], t1.t[:, 0:n], AF.Exp, scale=-1.0), [t1], [t2])
            P.op("act", lambda e: e.activation(t2.t[:, 0:n], t2.t[:, 0:n], AF.Ln, bias=1.0), [t2], [t2])
            if sign_logsig:
                P.op("dve", lambda e: e.scalar_tensor_tensor(dst.t[:, 0:n], src.t[:, 0:n], 0.0, t2.t[:, 0:n], ALU.min, ALU.subtract), [src, t2], [dst])
            else:
                P.op("dve", lambda e: e.scalar_tensor_tensor(dst.t[:, 0:n], src.t[:, 0:n], 0.0, t2.t[:, 0:n], ALU.max, ALU.add), [src, t2], [dst])

        def decay(out, negc_ap, bias_ap, cbufs, mask, bank, diag):
            P.op("dve", lambda e: e.tensor_scalar(diag.t[:], identf.t[:], negc_ap, None, ALU.mult), [identf] + cbufs, [diag])
            P.op("pe", lambda e: e.matmul(bank.t[:, 0:128], onesf.t[:], diag.t[:], start=True, stop=False), [onesf, diag], [bank], inc=False)
            P.op("pe", lambda e: e.matmul(bank.t[:, 0:128], identf.t[:], mask.t[:], start=False, stop=True), [identf, mask], [bank])
            P.op("act", lambda e: e.activation(out.t[:], bank.t[:, 0:128], AF.Exp, bias=bias_ap, scale=1.0), [bank] + cbufs, [out])

        def proj_fm(w, wcol, banks, evac, tbs=range(4)):
            for tb in tbs:
                bank = banks[tb % len(banks)]
                for kc in range(8):
                    P.op("pe", lambda e: e.matmul(bank.t[:, 0:512], w.t[:, kc, wcol:wcol + 128], hT.t[:, kc, tb * 512:(tb + 1) * 512],
                                                  start=(kc == 0), stop=(kc == 7)), [w] + hTv[tb * 4:tb * 4 + 4], [bank], inc=(kc == 7))
                evac(tb, bank)

        def proj_tm(w, wcol, ncols, tt, bank):
            for kc in range(8):
                P.op("pe", lambda e: e.matmul(bank.t[:, 0:ncols], hT.t[:, kc, tt * 128:(tt + 1) * 128], w.t[:, kc, wcol:wcol + ncols],
                                              start=(kc == 0), stop=(kc == 7)), [w, hTv[tt]], [bank], inc=(kc == 7))

        def dump_ys():
            if "d_ys" in dbg_d:
                for n in range(3):
                    for c in range(8):
                        P.dma(dbg_d["d_ys"][n, c * 128:(c + 1) * 128, :], ysT.t[n, c * 128:(c + 1) * 128, :], [ysv[n][c]], [])

        stopped = False
        P.dma(colps[0].t[:], colp_d[0], [], [colps[0]])
        for l in range(nlayers):
            w_in = w_in_d[l]
            colp = colps[l % 2]
            P.dma(rowp.t[:], rowp_d[l].partition_broadcast(128), [], [rowp])
            wrr = w_in.rearrange("(kc p) n -> p kc n", p=128)
            P.dma(wsf.t[:, :, 0:16], wrr[:, :, O_MG:O_MG + 16], [], [wsf])
            P.dma(wsf.t[:, :, 16:48], wrr[:, :, O_DG:O_DG + 32], [], [wsf])
            P.op("dve", lambda e: e.tensor_copy(wsm.t[:], wsf.t[:]), [wsf], [wsm])
            if l == 0:
                with scope() as sc:
                    xin = [sc.sb("xin%d" % i, [128, D], F32) for i in range(2)]
                    for tt in range(NT):
                        xb_ = xin[tt % 2]
                        P.dma(xb_.t[:], x_d[tt * 128:(tt + 1) * 128, :], [], [xb_])
                        P.dma(xcur.t[tt * 128:(tt + 1) * 128, :], xb_.t[:], [xb_], [xcv[tt]])
                        norm_tile(xb_.t[:], xb_, tt, (colp, colp.t[:, 0:8]))

            with scope() as sc:
                sb1 = sc.sb
                gm = sb1("gm", [128, 256], F32)
                ipre = sb1("ipre", [128, 128], F32)
                fpre = sb1("fpre", [128, 128], F32)
                lf = sb1("lf", [128, 128], F32)
                t1 = sb1("t1", [128, 256], F32)
                t2 = sb1("t2", [128, 256], F32)
                bcs = sb1("bcs", [128, 128], F32)
                totb = sb1("totb", [128, 128], F32)
                biasc = sb1("biasc", [128, 128], F32)
                expb = sb1("expb", [128, 128], F32)
                wgt = sb1("wgt", [128, 128], F32)
                dec = sb1("dec", [128, 128], F32)
                for tt in range(NT):
                    for kc in range(8):
                        P.op("pe", lambda e: e.matmul(pb[0].t[:, tt * 16:(tt + 1) * 16], hT.t[:, kc, tt * 128:(tt + 1) * 128], wsm.t[:, kc, 0:16],
                                                      start=(kc == 0), stop=(kc == 7)), [wsm, hTv[tt]], [pb[0]], inc=(kc == 7 and tt == NT - 1))
                P.op("dve", lambda e: e.tensor_tensor(gm.t[:].rearrange("p (t g) -> p t g", g=16), pb[0].t[:, 0:256].rearrange("p (t g) -> p t g", g=16),
                                                      rowp.t[:, 1152:1168].unsqueeze(1).to_broadcast([128, 16, 16]), ALU.add), [pb[0], rowp], [gm])
                gm5 = gm.t[:].rearrange("p (t d w h) -> p t d w h", d=2, w=2, h=4)
                v4 = lambda b_: b_.t[:].rearrange("p (t d h) -> p t d h", d=2, h=4)
                P.op("dve", lambda e: e.tensor_copy(v4(ipre), gm5[:, :, :, 0, :]), [gm], [ipre])
                P.op("dve", lambda e: e.tensor_copy(v4(fpre), gm5[:, :, :, 1, :]), [gm], [fpre])
                softplus_(lf, fpre, t1, t2, 128, sign_logsig=True)
                P.op("pe", lambda e: e.matmul(pb[1].t[:, 0:128], tincl.t[:], lf.t[:], start=True, stop=True), [tincl, lf], [pb[1]], inc=False)
                P.op("pe", lambda e: e.matmul(pb[1].t[:, 128:256], tinclT.t[:], lf.t[:], start=True, stop=True), [tinclT, lf], [pb[1]], inc=False)
                P.op("pe", lambda e: e.matmul(pb[1].t[:, 256:384], onesf.t[:], lf.t[:], start=True, stop=True), [onesf, lf], [pb[1]])
                pv = lambda a, b: pb[1].t[:, a:b].rearrange("p (t d h) -> p t d h", d=2, h=4)
                P.op("dve", lambda e: e.tensor_copy(v4(bcs)[:, :, 0, :], pv(0, 128)[:, :, 0, :]), [pb[1]], [bcs])
                P.op("dve", lambda e: e.tensor_copy(v4(bcs)[:, :, 1, :], pv(128, 256)[:, :, 1, :]), [pb[1]], [bcs])
                P.op("dve", lambda e: e.tensor_copy(totb.t[:], pb[1].t[:, 256:384]), [pb[1]], [totb])
                P.op("dve", lambda e: e.tensor_sub(biasc.t[:], ipre.t[:], bcs.t[:]), [ipre, bcs], [biasc])
                P.op("act", lambda e: e.activation(expb.t[:], bcs.t[:], AF.Exp), [bcs], [expb])
                P.op("dve", lambda e: e.tensor_add(t1.t[:, 0:128], biasc.t[:], totb.t[:]), [biasc, totb], [t1])
                P.op("act", lambda e: e.activation(wgt.t[:], t1.t[:, 0:128], AF.Exp), [t1], [wgt])
                P.op("act", lambda e: e.activation(dec.t[:], totb.t[:], AF.Exp), [totb], [dec])

                qT = sb1("qT", [128, 2, S], BF16)
                kT = sb1("kT", [128, 2, S], BF16)
                ktm = sb1("ktm", [128, NT, 256], BF16)
                vtm = sb1("vtm", [128, NT, 257], BF16)
                hm = sb1("hm", [128, NT, 256], F32)
                hmv = [Buf(hm.t, "hm%d" % i) for i in range(NT)]
                Cst = [sb1("Cst%d" % d_, [128, 2, 257], F32) for d_ in range(2)]
                Cb = [sb1("Cb%d" % d_, [128, 2, 257], BF16) for d_ in range(2)]
                DT = [sb1("DT%d" % d_, [128, 128], F32) for d_ in range(2)]
                diag = [sb1("diag%d" % d_, [128, 128], F32) for d_ in range(2)]
                scT = [sb1("scT%d" % d_, [128, 128], BF16) for d_ in range(2)]
                Asb = [sb1("Asb%d" % d_, [128, 257], F32) for d_ in range(2)]
                comb = [sb1("comb%d" % d_, [128, 257], F32) for d_ in range(2)]
                rr = [sb1("rr%d" % d_, [128, 2], F32) for d_ in range(2)]
                kw = [sb1("kw%d" % d_, [128, 256], BF16) for d_ in range(2)]
                og_l = [sb1("og%d" % i, [128, 256], F32) for i in range(2)]
                ty_l = [sb1("ty%d" % i, [128, 256], F32) for i in range(2)]
                yb_l = [sb1("yb%d" % i, [128, 256], BF16) for i in range(2)]
                ty = ty_l[0]
                ssq = sb1("ssq", [128, 2 * NT], F32)
                ymT = sb1("ymT", [128, 2, S], BF16)
                P.op("dve", lambda e: e.memset(vtm.t[:, :, 256:257], 1.0), [], [vtm])

                for h in range(4):
                    wqk, wkv, wo = nextw(), nextw(), nextw()
                    wload(wqk, 0, w_in, O_MQ + h * 256, 256)
                    wload(wqk, 256, w_in, O_MK + h * 256, 256)
                    wload(wkv, 0, w_in, O_MK + h * 256, 256)
                    wload(wkv, 256, w_in, O_MV + h * 256, 256)
                    wload(wo, 0, w_in, O_MO + h * 256, 256)
                    for ft in range(2):
                        proj_fm(wqk, ft * 128, pb[0:4], lambda tb, bank: evac_copy(qT.t[:, ft, tb * 512:(tb + 1) * 512], bank.t[:, 0:512], [bank], [qT], scale=0.0625))
                        proj_fm(wqk, 256 + ft * 128, pb[0:4], lambda tb, bank: evac_copy(kT.t[:, ft, tb * 512:(tb + 1) * 512], bank.t[:, 0:512], [bank], [kT]))
                    for tt in range(NT):
                        bank = pb[tt % 4]
                        proj_tm(wkv, 0, 512, tt, bank)
                        P.op("act", lambda e: e.copy(ktm.t[:, tt, :], bank.t[:, 0:256]), [bank], [ktm])
                        P.op("dve", lambda e: e.tensor_copy(vtm.t[:, tt, 0:256], bank.t[:, 256:512]), [bank], [vtm])
                    def gen_mrec(d_, h=h):
                        bRQ, bS, bA = (pb[0], pb[3])[d_], (pb[1], pb[4])[d_], (pb[2], pb[5])[d_]
                        mask = mlo if d_ == 0 else mup
                        for step in range(NT):
                            c = step if d_ == 0 else NT - 1 - step
                            cs = slice(c * 128, (c + 1) * 128)
                            gi = (c * 2 + d_) * 4 + h
                            col = lambda b_: b_.t[:, gi:gi + 1]
                            P.op("dve", lambda e: e.tensor_scalar(diag[d_].t[:], identf.t[:], col(bcs), None, ALU.mult), [identf, bcs], [diag[d_]])
                            if step < NT - 1:
                                P.op("act", lambda e: e.activation(kw[d_].t[:], ktm.t[:, c, :], AF.Copy, scale=col(wgt)), [ktm, wgt], [kw[d_]])
                            yield
                            P.op("pe", lambda e: e.matmul(bRQ.t[:, 0:128], onesf.t[:], diag[d_].t[:], start=True, stop=False), [onesf, diag[d_]], [bRQ], inc=False)
                            P.op("pe", lambda e: e.matmul(bRQ.t[:, 0:128], identf.t[:], mask.t[:], start=False, stop=True), [identf, mask], [bRQ], inc=False)
                            for kc in range(2):
                                P.op("pe", lambda e: e.matmul(bRQ.t[:, 128:256], kT.t[:, kc, cs], qT.t[:, kc, cs], start=(kc == 0), stop=(kc == 1)), [kT, qT], [bRQ], inc=(kc == 1))
                            if step > 0:
                                for kc in range(2):
                                    P.op("pe", lambda e: e.matmul(bA.t[:, 0:257], qT.t[:, kc, cs], Cb[d_].t[:, kc, :], start=(kc == 0), stop=(kc == 1)), [qT, Cb[d_]], [bA], inc=(kc == 1))
                            yield
                            P.op("act", lambda e: e.activation(DT[d_].t[:], bRQ.t[:, 0:128], AF.Exp, bias=col(biasc), scale=1.0), [bRQ, biasc], [DT[d_]])
                            if step > 0:
                                P.op("act", lambda e: e.activation(Asb[d_].t[:], bA.t[:, 0:257], AF.Copy, scale=col(expb)), [bA, expb], [Asb[d_]])
                            yield
                            P.op("dve", lambda e: e.tensor_tensor(scT[d_].t[:], bRQ.t[:, 128:256], DT[d_].t[:], ALU.mult), [bRQ, DT[d_]], [scT[d_]])
                            yield
                            P.op("pe", lambda e: e.matmul(bS.t[:, 0:257], scT[d_].t[:], vtm.t[:, c, :], start=True, stop=True), [scT[d_], vtm], [bS])
                            yield
                            if step > 0:
                                P.op("dve", lambda e: e.tensor_tensor(comb[d_].t[:], Asb[d_].t[:], bS.t[:, 0:257], ALU.add), [Asb[d_], bS], [comb[d_]])
                            else:
                                P.op("dve", lambda e: e.tensor_copy(comb[d_].t[:], bS.t[:, 0:257]), [bS], [comb[d_]])
                            den = comb[d_].t[:, 256:257]
                            P.op("dve", lambda e: e.scalar_tensor_tensor(rr[d_].t[:, 0:1], den, -1.0, den, ALU.mult, ALU.max), [comb[d_]], [rr[d_]])
                            P.op("dve", lambda e: e.tensor_scalar_max(rr[d_].t[:, 0:1], rr[d_].t[:, 0:1], 1.0), [rr[d_]], [rr[d_]])
                            P.op("dve", lambda e: e.reciprocal(rr[d_].t[:, 1:2], rr[d_].t[:, 0:1]), [rr[d_]], [rr[d_]])
                            if step < NT - 1:
                                P.op("pe", lambda e: e.matmul(bS.t[:, 0:257], kw[d_].t[:, 0:128], vtm.t[:, c, :], start=True, stop=True), [kw[d_], vtm], [bS])
                                P.op("pe", lambda e: e.matmul(bA.t[:, 0:257], kw[d_].t[:, 128:256], vtm.t[:, c, :], start=True, stop=True), [kw[d_], vtm], [bA])
                            yield
                            first = (d_ == 0 and c < 8) or (d_ == 1 and c >= 8)
                            if first:
                                P.op("act", lambda e: e.activation(hm.t[:, c, :], comb[d_].t[:, 0:256], AF.Copy, scale=rr[d_].t[:, 1:2]), [comb[d_], rr[d_]], [hmv[c]])
                            else:
                                P.op("dve", lambda e: e.scalar_tensor_tensor(hm.t[:, c, :], comb[d_].t[:, 0:256], rr[d_].t[:, 1:2], hm.t[:, c, :], ALU.mult, ALU.add), [comb[d_], rr[d_]], [hmv[c]])
                            if step < NT - 1:
                                for m, bC in enumerate((bS, bA)):
                                    if step == 0:
                                        P.op("dve", lambda e: e.tensor_copy(Cst[d_].t[:, m, :], bC.t[:, 0:257]), [bC], [Cst[d_]])
                                    else:
                                        P.op("dve", lambda e: e.scalar_tensor_tensor(Cst[d_].t[:, m, :], Cst[d_].t[:, m, :], col(dec), bC.t[:, 0:257], ALU.mult, ALU.add), [bC, dec], [Cst[d_]])
                                yield
                                P.op("act", lambda e: e.copy(Cb[d_].t[:], Cst[d_].t[:]), [Cst[d_]], [Cb[d_]])
                            yield

                    gens = [gen_mrec(0), gen_mrec(1)]
                    while gens:
                        for g_ in list(gens):
                            try:
                                next(g_)
                            except StopIteration:
                                gens.remove(g_)
                    for tt in range(NT):
                        P.op("act", lambda e: e.activation(ty.t[:], hm.t[:, tt, :], AF.Square, accum_out=ssq.t[:, tt:tt + 1]), [hmv[tt]], [ty, ssq])
                    rsqrt_(ssq.t[:, NT:2 * NT], ssq.t[:, 0:NT], 1.0 / 256.0, [ssq], [ssq])
                    for tt in range(NT):
                        bank = pb[tt % 4]
                        og, ty, yb = og_l[tt % 2], ty_l[tt % 2], yb_l[tt % 2]
                        proj_tm(wo, 0, 256, tt, bank)
                        P.op("act", lambda e: e.activation(og.t[:], bank.t[:, 0:256], AF.Sigmoid), [bank], [og])
                        P.op("dve", lambda e: e.scalar_tensor_tensor(ty.t[:], hm.t[:, tt, :], ssq.t[:, NT + tt:NT + tt + 1], rowp.t[:, h * 256:(h + 1) * 256], ALU.mult, ALU.mult), [hmv[tt], ssq, rowp], [ty])
                        P.op("dve", lambda e: e.tensor_tensor(yb.t[:], ty.t[:], og.t[:], ALU.mult), [ty, og], [yb])
                        for j in range(2):
                            P.op("pe", lambda e: e.transpose(pbt.t[:, j * 128:(j + 1) * 128], yb.t[:, j * 128:(j + 1) * 128], identb.t[:]), [yb, identb], [pbt], inc=(j == 1))
                        P.op("act", lambda e: e.copy(ymT.t[:, :, tt * 128:(tt + 1) * 128], pbt.t[:, 0:256].rearrange("p (j t) -> p j t", j=2)), [pbt], [ymT])
                    P.dma(ysT.t[0, h * 256:(h + 1) * 256, :].rearrange("(j p) t -> p j t", p=128), ymT.t[:], [ymT], [ysv[0][2 * h], ysv[0][2 * h + 1]])
                    if stop == "m0":
                        break
            if stop in ("m0", "mlstm"):
                stopped = True
                break

            with scope() as sc:
                sb2 = sc.sb
                A2 = lambda name: sb2(name, [128, 256], F32)
                v4 = lambda b_: b_.t[:].rearrange("p (t d h) -> p t d h", d=2, h=8)
                Gc, negG, expG, bexpG, kdw, glb, negbeta, beta = [A2(n) for n in ("Gc", "negG", "expG", "bexpG", "kdw", "glb", "negbeta", "beta")]
                with scope() as sct:
                    dg = sct.sb("dg", [128, 512], F32)
                    apre, spl, gg, t1, t2 = [sct.sb(n, [128, 256], F32) for n in ("apre", "spl", "gg", "t1d", "t2d")]
                    ea = sct.sb("ea", [128, 16], F32)
                    for tt in range(NT):
                        for kc in range(8):
                            P.op("pe", lambda e: e.matmul(pb[0].t[:, tt * 32:(tt + 1) * 32], hT.t[:, kc, tt * 128:(tt + 1) * 128], wsm.t[:, kc, 16:48],
                                                          start=(kc == 0), stop=(kc == 7)), [wsm, hTv[tt]], [pb[0]], inc=(kc == 7 and tt == NT - 1))
                    P.op("act", lambda e: e.copy(dg.t[:], pb[0].t[:, 0:512]), [pb[0]], [dg])
                    dg5 = dg.t[:].rearrange("p (t d w h) -> p t d w h", d=2, w=2, h=8)
                    P.op("act", lambda e: e.activation(v4(beta), dg5[:, :, :, 0, :], AF.Sigmoid), [dg], [beta])
                    P.op("dve", lambda e: e.tensor_tensor(v4(apre), dg5[:, :, :, 1, :],
                                                          rowp.t[:, 1184:1200].rearrange("p (d h) -> p d h", d=2).unsqueeze(1).to_broadcast([128, 16, 2, 8]), ALU.add), [dg, rowp], [apre])
                    softplus_(spl, apre, t1, t2, 256)
                    P.op("act", lambda e: e.activation(ea.t[:], rowp.t[:, 1168:1184], AF.Exp), [rowp], [ea])
                    P.op("dve", lambda e: e.scalar_tensor_tensor(v4(gg), v4(spl), -1.0,
                                                                 ea.t[:].rearrange("p (d h) -> p d h", d=2).unsqueeze(1).to_broadcast([128, 16, 2, 8]), ALU.mult, ALU.mult), [spl, ea], [gg])
                    P.op("pe", lambda e: e.matmul(pb[1].t[:, 0:256], tincl.t[:], gg.t[:], start=True, stop=True), [tincl, gg], [pb[1]], inc=False)
                    P.op("pe", lambda e: e.matmul(pb[1].t[:, 256:512], tinclT.t[:], gg.t[:], start=True, stop=True), [tinclT, gg], [pb[1]], inc=False)
                    P.op("pe", lambda e: e.matmul(pb[2].t[:, 0:256], onesf.t[:], gg.t[:], start=True, stop=True), [onesf, gg], [pb[2]])
                    pv = lambda a, b: pb[1].t[:, a:b].rearrange("p (t d h) -> p t d h", d=2, h=8)
                    P.op("dve", lambda e: e.tensor_copy(v4(Gc)[:, :, 0, :], pv(0, 256)[:, :, 0, :]), [pb[1]], [Gc])
                    P.op("dve", lambda e: e.tensor_copy(v4(Gc)[:, :, 1, :], pv(256, 512)[:, :, 1, :]), [pb[1]], [Gc])
                    P.op("dve", lambda e: e.tensor_scalar_mul(negG.t[:], Gc.t[:], -1.0), [Gc], [negG])
                    P.op("act", lambda e: e.activation(expG.t[:], Gc.t[:], AF.Exp), [Gc], [expG])
                    P.op("dve", lambda e: e.tensor_mul(bexpG.t[:], beta.t[:], expG.t[:]), [beta, expG], [bexpG])
                    P.op("dve", lambda e: e.tensor_tensor(t1.t[:], pb[2].t[:, 0:256], Gc.t[:], ALU.subtract), [pb[2], Gc], [t1])
                    P.op("act", lambda e: e.activation(kdw.t[:], t1.t[:], AF.Exp), [t1], [kdw])
                    P.op("act", lambda e: e.activation(glb.t[:], pb[2].t[:, 0:256], AF.Exp), [pb[2]], [glb])
                    P.op("dve", lambda e: e.tensor_scalar_mul(negbeta.t[:], beta.t[:], -1.0), [beta], [negbeta])
                for nm, b_ in (("Gc", Gc), ("expG", expG), ("kdw", kdw), ("glb", glb), ("beta", beta)):
                    dump(nm, b_, b_.t[:], [128, 256])

                pc = sb2("pc", [128, S + 2], F32)
                cv = sb2("cv", [128, S], F32)
                sq = sb2("sq", [128, S], BF16)
                rin = sb2("rin", [128, 512], F32)
                qnT = sb2("qnT", [128, S], BF16)
                knT = sb2("knT", [128, S], BF16)
                vT = sb2("vT", [128, S], BF16)
                ktm = sb2("ktm2", [128, NT, 128], BF16)
                vtm = sb2("vtm2", [128, NT, 128], BF16)
                WTs = sb2("WTs", [128, 32, 128], BF16)
                Us = sb2("Us", [128, 32, 128], F32)
                ATs = sb2("ATs", [128, 32, 128], BF16)
                stv = [Buf(None, "st%d" % i) for i in range(32)]
                osb = sb2("osb", [128, NT, 128], F32)
                osv = [Buf(osb.t, "os%d" % i) for i in range(NT)]
                NS = NS_CFG[0]
                STAG = STAG_CFG[0]
                wlim[0] = 3 if NS_CFG[0] > 4 else NWB
                _flat = wb[3].t[:].rearrange("p a b -> p (a b)")
                _off = [0]

                def slot_tile(i, name, shape, dt):
                    if i < 4:
                        return sb2("%s%d" % (name, i), shape, dt)
                    n = shape[1] * (2 if dt == F32 else 1)
                    ap = _flat[:, _off[0]:_off[0] + n]
                    _off[0] += n
                    if dt == F32:
                        ap = ap.bitcast(F32)
                    return Buf(ap, "%s%d" % (name, i))
                Dm = [slot_tile(i, "Dm", [128, 128], F32) for i in range(NS)]
                dgl = [slot_tile(i, "dgl", [128, 128], F32) for i in range(NS)]
                Do1 = [slot_tile(i, "Do1", [128, 128], F32) for i in range(NS)]
                Do2 = [slot_tile(i, "Do2", [128, 128], F32) for i in range(NS)]
                CH = [[slot_tile(i, "CH%d_" % j, [128, 384], BF16) for j in range(2)] for i in range(NS)]
                Ao = [slot_tile(i, "Ao", [128, 256], BF16) for i in range(NS)]
                AoT = [slot_tile(i, "AoT", [128, 256], BF16) for i in range(NS)]
                attn_t = [slot_tile(i, "attn", [128, 128], BF16) for i in range(NS)]
                X0b = [slot_tile(i, "X0b", [128, 256], BF16) for i in range(NS)]
                X1b = [slot_tile(i, "X1b", [128, 256], BF16) for i in range(NS)]
                R1b = X0b
                Tmb = Ao
                Wtm = Do1
                Wb = [slot_tile(i, "Wb", [128, 128], BF16) for i in range(NS)]
                mbd = [sb2("mbd%d" % i, [128, 128], F32) for i in range(2)]
                mo1 = [sb2("mo1%d" % i, [128, 128], F32) for i in range(2)]
                mo2 = [sb2("mo2%d" % i, [128, 128], F32) for i in range(2)]
                with scope() as scm:
                    E32 = scm.sb("E32", [4, 128], F32)
                    E64 = scm.sb("E64", [2, 128], F32)
                    b32 = scm.sb("b32", [128, 128], F32)
                    b64 = scm.sb("b64", [128, 128], F32)
                    tmk = scm.sb("tmk", [128, 128], F32)
                    for E_, w_ in ((E32, 32), (E64, 64)):
                        np_ = 128 // w_
                        P.op("pool", lambda e: e.memset(E_.t[:], 1.0), [], [E_])
                        P.op("pool", lambda e: e.affine_select(E_.t[:], E_.t[:], pattern=[[1, 128]], compare_op=ALU.is_ge, fill=0.0, base=0, channel_multiplier=-w_), [E_], [E_])
                        P.op("pool", lambda e: e.affine_select(E_.t[:], E_.t[:], pattern=[[-1, 128]], compare_op=ALU.is_ge, fill=0.0, base=w_ - 1, channel_multiplier=w_), [E_], [E_])
                    P.op("pe", lambda e: e.matmul(pb[0].t[:, 0:128], E32.t[:], E32.t[:], start=True, stop=True), [E32], [pb[0]], inc=False)
                    P.op("pe", lambda e: e.matmul(pb[0].t[:, 128:256], E64.t[:], E64.t[:], start=True, stop=True), [E64], [pb[0]])
                    P.op("dve", lambda e: e.tensor_copy(b32.t[:], pb[0].t[:, 0:128]), [pb[0]], [b32])
                    P.op("dve", lambda e: e.tensor_copy(b64.t[:], pb[0].t[:, 128:256]), [pb[0]], [b64])
                    for d_, tri in ((0, slf), (1, suf)):
                        P.op("dve", lambda e: e.tensor_tensor(mbd[d_].t[:], tri.t[:], b32.t[:], ALU.mult), [tri, b32], [mbd[d_]])
                        P.op("dve", lambda e: e.tensor_tensor(tmk.t[:], b64.t[:], b32.t[:], ALU.subtract), [b64, b32], [tmk])
                        P.op("dve", lambda e: e.tensor_tensor(mo1[d_].t[:], tri.t[:], tmk.t[:], ALU.mult), [tri, tmk], [mo1[d_]])
                        P.op("dve", lambda e: e.tensor_tensor(tmk.t[:], tri.t[:], b64.t[:], ALU.mult), [tri, b64], [tmk])
                        P.op("dve", lambda e: e.tensor_tensor(mo2[d_].t[:], tri.t[:], tmk.t[:], ALU.subtract), [tri, tmk], [mo2[d_]])
                Sst = [sb2("Sst%d" % d_, [128, 128], F32) for d_ in range(2)]
                Sbb = [sb2("Sbb%d" % d_, [128, 128], BF16) for d_ in range(2)]
                vnb = [sb2("vnb%d" % d_, [128, 128], BF16) for d_ in range(2)]
                kdt = [sb2("kdt%d" % d_, [128, 128], BF16) for d_ in range(2)]
                tmo = [sb2("tmo%d" % d_, [128, 128], F32) for d_ in range(2)]
                zg_l = [sb2("zg%d" % i, [128, 128], F32) for i in range(2)]
                ty_l = [sb2("ty2%d" % i, [128, 128], F32) for i in range(2)]
                yb_l = [sb2("yb2%d" % i, [128, 128], BF16) for i in range(2)]
                ty = ty_l[0]
                ssq = sb2("ssq2", [128, 2 * NT], F32)
                ydT = sq
                P.op("dve", lambda e: e.memset(pc.t[:, 0:1], 0.0), [], [pc])
                P.op("dve", lambda e: e.memset(pc.t[:, S + 1:S + 2], 0.0), [], [pc])
                wA = wB = None
                for h in range(8):
                    hh = h % 2
                    if hh == 0:
                        wA, wB = nextw(), nextw()
                        wload(wA, 0, w_in, O_DQ + h * 128, 256)
                        wload(wA, 256, w_in, O_DK + h * 128, 256)
                        wload(wB, 0, w_in, O_DV + h * 128, 256)
                        wload(wB, 256, w_in, O_DZ + h * 128, 256)
                    for j, (wsrc, off) in enumerate(((wA, hh * 128), (wA, 256 + hh * 128), (wB, hh * 128))):
                        proj_fm(wsrc, off, pb[0:4], lambda tb, bank: evac_copy(pc.t[:, 1 + tb * 512:1 + (tb + 1) * 512], bank.t[:, 0:512], [bank], [pc]))
                        cw = lambda k: colp.t[:, 16 + k * 24 + j * 8 + h:16 + k * 24 + j * 8 + h + 1]
                        P.op("dve", lambda e: e.tensor_scalar(cv.t[:], pc.t[:, 0:S], cw(0), None, ALU.mult), [pc, colp], [cv])
                        P.op("dve", lambda e: e.scalar_tensor_tensor(cv.t[:], pc.t[:, 1:S + 1], cw(1), cv.t[:], ALU.mult, ALU.add), [pc, colp], [cv])
                        P.op("dve", lambda e: e.scalar_tensor_tensor(cv.t[:], pc.t[:, 2:S + 2], cw(2), cv.t[:], ALU.mult, ALU.add), [pc, colp], [cv])
                        P.op("act", lambda e: e.activation(cv.t[:], cv.t[:], AF.Silu), [cv], [cv])
                        if j == 2:
                            P.op("act", lambda e: e.copy(vT.t[:], cv.t[:]), [cv], [vT])
                        else:
                            dst = qnT if j == 0 else knT
                            P.op("act", lambda e: e.activation(sq.t[:], cv.t[:], AF.Square), [cv], [sq])
                            for tb in range(4):
                                bs = slice(tb * 512, (tb + 1) * 512)
                                bank = pb[4 + tb % 2]
                                P.op("pe", lambda e: e.matmul(bank.t[:, 0:512], onesb.t[:], sq.t[:, bs], start=True, stop=True), [onesb, sq], [bank])
                                rsqrt_(rin.t[:], bank.t[:, 0:512], 1.0, [bank], [rin])
                                if j == 0:
                                    P.op("dve", lambda e: e.scalar_tensor_tensor(dst.t[:, bs], cv.t[:, bs], 128.0 ** -0.5, rin.t[:], ALU.mult, ALU.mult), [cv, rin], [dst])
                                else:
                                    P.op("dve", lambda e: e.tensor_tensor(dst.t[:, bs], cv.t[:, bs], rin.t[:], ALU.mult), [cv, rin], [dst])
                    if h == 0:
                        dump("qnT", qnT, qnT.t[:], [128, S], BF16)
                        dump("knT", knT, knT.t[:], [128, S], BF16)
                        dump("vT", vT, vT.t[:], [128, S], BF16)
                    for src, dstm in ((knT, ktm), (vT, vtm)):
                        for g4 in range(2):
                            for i in range(8):
                                tt = g4 * 8 + i
                                P.op("pe", lambda e: e.transpose(pbt.t[:, i * 128:(i + 1) * 128], src.t[:, tt * 128:(tt + 1) * 128], identb.t[:]), [src, identb], [pbt], inc=(i == 7))
                            evac_copy(dstm.t[:, g4 * 8:(g4 + 1) * 8, :], pbt.t[:].rearrange("p (i t) -> p i t", i=8), [pbt], [dstm])
                    if stop == "dn0a":
                        break
                    def gen_prep(si, c, d_, h=h):
                        cs = slice(c * 128, (c + 1) * 128)
                        gi = (c * 2 + d_) * 8 + h
                        col = lambda b_: b_.t[:, gi:gi + 1]
                        e_ = c * 2 + d_
                        ch0, ch1 = CH[si][0], CH[si][1]
                        bank = pb[1 + si]
                        mask = mup if d_ == 0 else mlo
                        P.op("dve", lambda e: e.tensor_scalar(dgl[si].t[:], identf.t[:], col(negG), None, ALU.mult), [identf, negG], [dgl[si]])
                        yield
                        while lock["pb0"] is not None:
                            yield
                        lock["pb0"] = si
                        P.op("pe", lambda e: e.matmul(pb[0].t[:, 0:128], onesf.t[:], dgl[si].t[:], start=True, stop=False), [onesf, dgl[si]], [pb[0]], inc=False)
                        P.op("pe", lambda e: e.matmul(pb[0].t[:, 0:128], identf.t[:], mask.t[:], start=False, stop=True), [identf, mask], [pb[0]], inc=False)
                        P.op("pe", lambda e: e.matmul(pb[0].t[:, 128:256], knT.t[:, cs], knT.t[:, cs], start=True, stop=True), [knT], [pb[0]], inc=False)
                        P.op("pe", lambda e: e.matmul(pb[0].t[:, 256:384], qnT.t[:, cs], knT.t[:, cs], start=True, stop=True), [qnT, knT], [pb[0]])
                        yield
                        P.op("act", lambda e: e.activation(Dm[si].t[:], pb[0].t[:, 0:128], AF.Exp, bias=col(Gc), scale=1.0), [pb[0], Gc], [Dm[si]])
                        P.op("act", lambda e: e.activation(X0b[si].t[:, 0:128], ktm.t[:, c, :], AF.Copy, scale=col(bexpG)), [ktm, bexpG], [X0b[si]])
                        P.op("act", lambda e: e.activation(X0b[si].t[:, 128:256], vtm.t[:, c, :], AF.Copy, scale=col(beta)), [vtm, beta], [X0b[si]])
                        yield
                        P.op("pool", lambda e: e.tensor_tensor(dgl[si].t[:], Dm[si].t[:], mbd[d_].t[:], ALU.mult), [Dm[si], mbd[d_]], [dgl[si]])
                        P.op("pool", lambda e: e.tensor_tensor(Do1[si].t[:], Dm[si].t[:], mo1[d_].t[:], ALU.mult), [Dm[si], mo1[d_]], [Do1[si]])
                        P.op("pool", lambda e: e.tensor_tensor(Do2[si].t[:], Dm[si].t[:], mo2[d_].t[:], ALU.mult), [Dm[si], mo2[d_]], [Do2[si]])
                        P.op("dve", lambda e: e.tensor_tensor(attn_t[si].t[:], pb[0].t[:, 256:384], Dm[si].t[:], ALU.mult), [pb[0], Dm[si]], [attn_t[si]])
                        yield
                        P.op("dve", lambda e: e.scalar_tensor_tensor(ch0.t[:, 256:384], pb[0].t[:, 128:256], col(negbeta), dgl[si].t[:], ALU.mult, ALU.mult), [pb[0], negbeta, dgl[si]], [ch0])
                        P.op("dve", lambda e: e.scalar_tensor_tensor(Ao[si].t[:, 0:128], pb[0].t[:, 128:256], col(beta), Do1[si].t[:], ALU.mult, ALU.mult), [pb[0], beta, Do1[si]], [Ao[si]])
                        P.op("dve", lambda e: e.scalar_tensor_tensor(Ao[si].t[:, 128:256], pb[0].t[:, 128:256], col(beta), Do2[si].t[:], ALU.mult, ALU.mult), [pb[0], beta, Do2[si]], [Ao[si]])
                        lock["pb0"] = None
                        yield
                        while lock["pbt"] is not None:
                            yield
                        lock["pbt"] = si
                        P.op("pe", lambda e: e.transpose(pbt.t[:, 0:128], ch0.t[:, 256:384], identb.t[:]), [ch0, identb], [pbt], inc=False)
                        P.op("pe", lambda e: e.transpose(pbt.t[:, 128:256], Ao[si].t[:, 0:128], identb.t[:]), [Ao[si], identb], [pbt], inc=False)
                        P.op("pe", lambda e: e.transpose(pbt.t[:, 256:384], Ao[si].t[:, 128:256], identb.t[:]), [Ao[si], identb], [pbt], inc=False)
                        P.op("pe", lambda e: e.transpose(pbt.t[:, 384:512], attn_t[si].t[:], identb.t[:]), [attn_t[si], identb], [pbt])
                        yield
                        P.op("act", lambda e: e.copy(ch0.t[:, 0:128], pbt.t[:, 0:128]), [pbt], [ch0])
                        P.op("dve", lambda e: e.tensor_tensor(ch1.t[:, 128:256], pbt.t[:, 0:128], identb.t[:], ALU.add), [pbt, identb], [ch1])
                        P.op("act", lambda e: e.copy(AoT[si].t[:], pbt.t[:, 128:384]), [pbt], [AoT[si]])
                        P.op("dve", lambda e: e.tensor_copy(ATs.t[:, e_, :], pbt.t[:, 384:512]), [pbt], [stv[e_]])
                        lock["pbt"] = None
                        yield
                        P.op("pe", lambda e: e.matmul(bank.t[:, 0:128], ch0.t[:, 256:384], ch0.t[:, 0:128], start=True, stop=True), [ch0], [bank], inc=False)
                        P.op("pe", lambda e: e.matmul(bank.t[:, 256:384], ch0.t[:, 0:128], ch0.t[:, 256:384], start=True, stop=True), [ch0], [bank])
                        yield
                        P.op("act", lambda e: e.copy(ch1.t[:].rearrange("p (a b) -> p a b", b=128)[:, 0::2, :], bank.t[:, 0:384].rearrange("p (a b) -> p a b", b=128)[:, 0::2, :]), [bank], [ch1])
                        yield
                        for j in range(1, 5):
                            cur = CH[si][j % 2]
                            nxt = CH[si][(j + 1) % 2]
                            if j < 4:
                                P.op("pe", lambda e: e.matmul(bank.t[:, 0:256], cur.t[:, 256:384], cur.t[:, 0:256], start=True, stop=True), [cur], [bank], inc=False)
                                P.op("pe", lambda e: e.matmul(bank.t[:, 128:256], identb.t[:], cur.t[:, 128:256], start=False, stop=True), [cur, identb], [bank], inc=False)
                                P.op("pe", lambda e: e.matmul(bank.t[:, 256:384], cur.t[:, 0:128], cur.t[:, 256:384], start=True, stop=True), [cur], [bank])
                                yield
                                evac_copy(nxt.t[:], bank.t[:, 0:384], [bank], [nxt])
                                yield
                            else:
                                P.op("pe", lambda e: e.matmul(bank.t[:, 128:256], cur.t[:, 256:384], cur.t[:, 128:256], start=True, stop=False), [cur], [bank], inc=False)
                                P.op("pe", lambda e: e.matmul(bank.t[:, 128:256], identb.t[:], cur.t[:, 128:256], start=False, stop=True), [cur, identb], [bank])
                                yield
                                evac_copy(nxt.t[:, 128:256], bank.t[:, 128:256], [bank], [nxt])
                                yield
                        fin = CH[si][1]
                        PTf = fin.t[:, 128:256]
                        A1T, A2T = AoT[si].t[:, 0:128], AoT[si].t[:, 128:256]
                        lo, hi = bank.t[:, 0:256], bank.t[:, 256:512]
                        P.op("pe", lambda e: e.matmul(lo, PTf, X0b[si].t[:], start=True, stop=True), [fin, X0b[si]], [bank])
                        yield
                        P.op("act", lambda e: e.copy(R1b[si].t[:], lo), [bank], [R1b[si]])
                        P.op("dve", lambda e: e.tensor_copy(Us.t[:, e_, :], bank.t[:, 128:256]), [bank], [stv[e_]])
                        yield
                        P.op("pe", lambda e: e.matmul(hi, A1T, R1b[si].t[:], start=True, stop=True), [AoT[si], R1b[si]], [bank])
                        yield
                        P.op("act", lambda e: e.copy(Tmb[si].t[:], hi), [bank], [Tmb[si]])
                        yield
                        P.op("pe", lambda e: e.matmul(lo, PTf, Tmb[si].t[:], start=True, stop=True), [fin, Tmb[si]], [bank])
                        yield
                        P.op("dve", lambda e: e.tensor_tensor(X1b[si].t[:], R1b[si].t[:], lo, ALU.subtract), [R1b[si], bank], [X1b[si]])
                        P.op("dve", lambda e: e.tensor_tensor(Us.t[:, e_, :], Us.t[:, e_, :], bank.t[:, 128:256], ALU.subtract), [bank], [stv[e_]])
                        yield
                        P.op("pe", lambda e: e.matmul(hi, A2T, X1b[si].t[:], start=True, stop=True), [AoT[si], X1b[si]], [bank])
                        yield
                        P.op("act", lambda e: e.copy(Tmb[si].t[:], hi), [bank], [Tmb[si]])
                        yield
                        P.op("pe", lambda e: e.matmul(lo, PTf, Tmb[si].t[:], start=True, stop=True), [fin, Tmb[si]], [bank])
                        yield
                        P.op("act", lambda e: e.copy(R1b[si].t[:], lo), [bank], [R1b[si]])
                        P.op("dve", lambda e: e.tensor_tensor(Us.t[:, e_, :], Us.t[:, e_, :], bank.t[:, 128:256], ALU.subtract), [bank], [stv[e_]])
                        yield
                        P.op("pe", lambda e: e.matmul(hi, A1T, R1b[si].t[:], start=True, stop=True), [AoT[si], R1b[si]], [bank])
                        P.op("pool", lambda e: e.tensor_tensor(Wtm[si].t[:], X1b[si].t[:, 0:128], R1b[si].t[:, 0:128], ALU.subtract), [X1b[si], R1b[si]], [Wtm[si]])
                        yield
                        P.op("act", lambda e: e.copy(Tmb[si].t[:], hi), [bank], [Tmb[si]])
                        yield
                        P.op("pe", lambda e: e.matmul(lo, PTf, Tmb[si].t[:], start=True, stop=True), [fin, Tmb[si]], [bank])
                        yield
                        P.op("dve", lambda e: e.tensor_tensor(Wb[si].t[:], Wtm[si].t[:], bank.t[:, 0:128], ALU.add), [Wtm[si], bank], [Wb[si]])
                        P.op("dve", lambda e: e.tensor_tensor(Us.t[:, e_, :], Us.t[:, e_, :], bank.t[:, 128:256], ALU.add), [bank], [stv[e_]])
                        yield
                        while lock["pbt"] is not None:
                            yield
                        lock["pbt"] = si
                        P.op("pe", lambda e: e.transpose(pbt.t[:, 0:128], Wb[si].t[:], identb.t[:]), [Wb[si], identb], [pbt])
                        yield
                        P.op("act", lambda e: e.copy(WTs.t[:, e_, :], pbt.t[:, 0:128]), [pbt], [stv[e_]])
                        lock["pbt"] = None
                        yield

                    def gen_rec(h=h):
                        bA, bB = (pb[6], pb[6]) if NS_CFG[0] > 4 else (pb[5], pb[6])
                        for step in range(NT):
                            info = []
                            for d_ in range(2):
                                c = step if d_ == 0 else NT - 1 - step
                                info.append((d_, c, slice(c * 128, (c + 1) * 128), (c * 2 + d_) * 8 + h, c * 2 + d_))
                            for d_, c, cs, gi, e_ in info:
                                if step > 0:
                                    P.op("pe", lambda e: e.matmul(bA.t[:, d_ * 256:d_ * 256 + 128], WTs.t[:, e_, :], Sbb[d_].t[:], start=True, stop=True), [stv[e_], Sbb[d_]], [bA], inc=False)
                                    P.op("pe", lambda e: e.matmul(bA.t[:, d_ * 256 + 128:d_ * 256 + 256], qnT.t[:, cs], Sbb[d_].t[:], start=True, stop=True), [qnT, Sbb[d_]], [bA])
                                if step < NT - 1:
                                    P.op("act", lambda e: e.activation(kdt[d_].t[:], ktm.t[:, c, :], AF.Copy, scale=kdw.t[:, gi:gi + 1]), [ktm, kdw], [kdt[d_]])
                            yield
                            for d_, c, cs, gi, e_ in info:
                                if step > 0:
                                    P.op("dve", lambda e: e.tensor_tensor(vnb[d_].t[:], Us.t[:, e_, :], bA.t[:, d_ * 256:d_ * 256 + 128], ALU.subtract), [stv[e_], bA], [vnb[d_]])
                                    P.op("act", lambda e: e.activation(tmo[d_].t[:], bA.t[:, d_ * 256 + 128:d_ * 256 + 256], AF.Copy, scale=expG.t[:, gi:gi + 1]), [bA, expG], [tmo[d_]])
                                else:
                                    P.op("dve", lambda e: e.tensor_copy(vnb[d_].t[:], Us.t[:, e_, :]), [stv[e_]], [vnb[d_]])
                            yield
                            for d_, c, cs, gi, e_ in info:
                                P.op("pe", lambda e: e.matmul(bB.t[:, d_ * 256:d_ * 256 + 128], ATs.t[:, e_, :], vnb[d_].t[:], start=True, stop=True), [stv[e_], vnb[d_]], [bB], inc=(step == NT - 1))
                                if step < NT - 1:
                                    P.op("pe", lambda e: e.matmul(bB.t[:, d_ * 256 + 128:d_ * 256 + 256], kdt[d_].t[:], vnb[d_].t[:], start=True, stop=True), [kdt[d_], vnb[d_]], [bB])
                            yield
                            for d_, c, cs, gi, e_ in info:
                                if step < NT - 1:
                                    if step == 0:
                                        P.op("dve", lambda e: e.tensor_copy(Sst[d_].t[:], bB.t[:, d_ * 256 + 128:d_ * 256 + 256]), [bB], [Sst[d_]])
                                    else:
                                        P.op("dve", lambda e: e.scalar_tensor_tensor(Sst[d_].t[:], Sst[d_].t[:], glb.t[:, gi:gi + 1], bB.t[:, d_ * 256 + 128:d_ * 256 + 256], ALU.mult, ALU.add), [bB, glb], [Sst[d_]])
                                    P.op("act", lambda e: e.copy(Sbb[d_].t[:], Sst[d_].t[:]), [Sst[d_]], [Sbb[d_]])
                            for d_, c, cs, gi, e_ in info:
                                first = (d_ == 0 and c < 8) or (d_ == 1 and c >= 8)
                                if step > 0:
                                    P.op("dve", lambda e: e.tensor_tensor(tmo[d_].t[:], tmo[d_].t[:], bB.t[:, d_ * 256:d_ * 256 + 128], ALU.add), [bB], [tmo[d_]])
                                    src_ap, src_b = tmo[d_].t[:], tmo[d_]
                                    if first:
                                        P.op("act", lambda e: e.copy(osb.t[:, c, :], src_ap), [src_b], [osv[c]])
                                    else:
                                        P.op("dve", lambda e: e.tensor_tensor(osb.t[:, c, :], osb.t[:, c, :], src_ap, ALU.add), [src_b], [osv[c]])
                                else:
                                    if first:
                                        P.op("dve", lambda e: e.tensor_copy(osb.t[:, c, :], bB.t[:, d_ * 256:d_ * 256 + 128]), [bB], [osv[c]])
                                    else:
                                        P.op("dve", lambda e: e.tensor_tensor(osb.t[:, c, :], osb.t[:, c, :], bB.t[:, d_ * 256:d_ * 256 + 128], ALU.add), [bB], [osv[c]])
                            yield

                    lock = {"pb0": None, "pbt": None}
                    order = []
                    for i in range(NT):
                        order.append((i, 0))
                        order.append((NT - 1 - i, 1))
                    active = [None] * NS
                    nstarted = 0
                    nfinished = 0
                    finished = [False] * 32
                    rec = gen_rec()
                    rec_step = 0
                    rec_hop = 0
                    rec_done = False
                    tick = 0
                    while nfinished < 32 or not rec_done:
                        if nstarted < 32 and tick % STAG == 0:
                            for si in range(NS):
                                if active[si] is None:
                                    c, d_ = order[nstarted]
                                    active[si] = (gen_prep(si, c, d_), nstarted)
                                    nstarted += 1
                                    break
                        for si in range(NS):
                            if active[si] is not None:
                                g_, idx = active[si]
                                try:
                                    next(g_)
                                except StopIteration:
                                    finished[idx] = True
                                    nfinished += 1
                                    active[si] = None
                        if not rec_done and (stop != "dn0b"):
                            if rec_hop > 0 or (finished[2 * rec_step] and finished[2 * rec_step + 1]):
                                try:
                                    next(rec)
                                    rec_hop += 1
                                    if rec_hop == 4:
                                        rec_hop = 0
                                        rec_step += 1
                                        if rec_step == NT:
                                            rec_done = True
                                except StopIteration:
                                    rec_done = True
                        elif stop == "dn0b":
                            rec_done = True
                        tick += 1
                    if stop == "dn0c":
                        break
                    if h == 0:
                        dump("osb", osv[0], osb.t[:], [128, NT, 128])
                    for tt in range(NT):
                        P.op("act", lambda e: e.activation(ty.t[:], osb.t[:, tt, :], AF.Square, accum_out=ssq.t[:, tt:tt + 1]), [osv[tt]], [ty, ssq])
                    rsqrt_(ssq.t[:, NT:2 * NT], ssq.t[:, 0:NT], 1.0 / 128.0, [ssq], [ssq])
                    for tt in range(NT):
                        bank = pb[4 + tt % 2]
                        zg, ty, yb = zg_l[tt % 2], ty_l[tt % 2], yb_l[tt % 2]
                        proj_tm(wB, 256 + hh * 128, 128, tt, bank)
                        P.op("act", lambda e: e.activation(zg.t[:], bank.t[:, 0:128], AF.Silu), [bank], [zg])
                        P.op("dve", lambda e: e.scalar_tensor_tensor(ty.t[:], osb.t[:, tt, :], ssq.t[:, NT + tt:NT + tt + 1], rowp.t[:, 1024:1152], ALU.mult, ALU.mult), [osv[tt], ssq, rowp], [ty])
                        P.op("dve", lambda e: e.tensor_tensor(yb.t[:], ty.t[:], zg.t[:], ALU.mult), [ty, zg], [yb])
                        P.op("pe", lambda e: e.transpose(pbt.t[:, 0:128], yb.t[:], identb.t[:]), [yb, identb], [pbt])
                        P.op("act", lambda e: e.copy(ydT.t[:, tt * 128:(tt + 1) * 128], pbt.t[:, 0:128]), [pbt], [ydT])
                    P.dma(ysT.t[1, h * 128:(h + 1) * 128, :], ydT.t[:], [ydT], [ysv[1][h]])
                    if stop == "dn0":
                        break
            if stop in ("dn0", "dn", "dn0a", "dn0b", "dn0c"):
                stopped = True
                break

            wlim[0] = NWB
            with scope() as sc:
                cx = sc.sb("cx", [128, S + 2], F32)
                Bsb = sc.sb("Bsb", [128, S], F32)
                ycv = sc.sb("ycv", [128, S], F32)
                tmx_l = [sc.sb("tmx%d" % i, [128, 512], F32) for i in range(2)]
                ycT = sc.sb("ycT", [128, S], BF16)
                P.op("dve", lambda e: e.memset(cx.t[:, 0:1], 0.0), [], [cx])
                P.op("dve", lambda e: e.memset(cx.t[:, S + 1:S + 2], 0.0), [], [cx])
                wA = wB = None
                for dc in range(8):
                    dd = dc % 2
                    if dd == 0:
                        wA, wB = nextw(), nextw()
                        wload(wA, 0, w_in, O_SB + dc * 128, 256)
                        wload(wA, 256, w_in, O_SC + dc * 128, 256)
                        wload(wB, 0, w_in, O_SX + dc * 128, 256)
                    for tb in range(4):
                        bs = slice(tb * 512, (tb + 1) * 512)
                        tmx = tmx_l[tb % 2]
                        for j, (wsrc, off) in enumerate(((wA, dd * 128), (wA, 256 + dd * 128), (wB, dd * 128))):
                            bank = pb[j + 3 * (tb % 2)]
                            for kc in range(8):
                                P.op("pe", lambda e: e.matmul(bank.t[:, 0:512], wsrc.t[:, kc, off:off + 128], hT.t[:, kc, bs], start=(kc == 0), stop=(kc == 7)),
                                     [wsrc] + hTv[tb * 4:tb * 4 + 4], [bank], inc=(kc == 7))
                        o3 = 3 * (tb % 2)
                        P.op("act", lambda e: e.copy(Bsb.t[:, bs], pb[o3].t[:, 0:512]), [pb[o3]], [Bsb])
                        P.op("act", lambda e: e.copy(tmx.t[:], pb[o3 + 2].t[:, 0:512]), [pb[o3 + 2]], [tmx])
                        P.op("dve", lambda e: e.tensor_tensor(cx.t[:, 1 + tb * 512:1 + (tb + 1) * 512], pb[o3 + 1].t[:, 0:512], tmx.t[:], ALU.mult), [pb[o3 + 1], tmx], [cx])
                    cw = lambda k: colp.t[:, 88 + k * 8 + dc:88 + k * 8 + dc + 1]
                    P.op("dve", lambda e: e.tensor_scalar(ycv.t[:], cx.t[:, 0:S], cw(0), None, ALU.mult), [cx, colp], [ycv])
                    P.op("dve", lambda e: e.scalar_tensor_tensor(ycv.t[:], cx.t[:, 1:S + 1], cw(1), ycv.t[:], ALU.mult, ALU.add), [cx, colp], [ycv])
                    P.op("dve", lambda e: e.scalar_tensor_tensor(ycv.t[:], cx.t[:, 2:S + 2], cw(2), ycv.t[:], ALU.mult, ALU.add), [cx, colp], [ycv])
                    P.op("dve", lambda e: e.tensor_tensor(ycT.t[:], ycv.t[:], Bsb.t[:], ALU.mult), [ycv, Bsb], [ycT])
                    P.dma(ysT.t[2, dc * 128:(dc + 1) * 128, :], ycT.t[:], [ycT], [ysv[2][dc]])
            if stop == "sc":
                stopped = True
                break

            if l + 1 < nlayers:
                P.dma(colps[(l + 1) % 2].t[:], colp_d[l + 1], [], [colps[(l + 1) % 2]])
            last = (l == DEPTH - 1)
            for half in range(2):
                with scope() as sch:
                    xres = sch.sb("xres", [128, 8, D], F32)
                    xrv = [Buf(xres.t, "xr%d" % i) for i in range(8)]
                    with scope() as sc:
                        ys_sb = [sc.sb("ys_sb%d" % n, [128, 8, 1024], BF16) for n in range(3)]
                        sg_l = [[sc.sb("sg%d_%d" % (n, i), [128, 512], F32) for n in range(3)] for i in range(2)]
                        acc_l = [sc.sb("acc%d" % i, [128, 512], F32) for i in range(2)]
                        tmm_l = [sc.sb("tmm", [128, 512], F32)] * 2
                        mixT = sc.sb("mixT", [128, 8, 1024], BF16)
                        for n in range(3):
                            P.dma(ys_sb[n].t[:], ysT.t[n].rearrange("(kc p) t -> p kc t", p=128)[:, :, half * 1024:(half + 1) * 1024], ysv[n], [ys_sb[n]])
                        wA = wB = wC = None
                        for dc in range(8):
                            dd = dc % 2
                            if dd == 0:
                                wA, wB, wC = nextw(), nextw(), nextw()
                                wload(wA, 0, w_br_d[l, 0], dc * 128, 256)
                                wload(wA, 256, w_br_d[l, 1], dc * 128, 256)
                                wload(wB, 0, w_br_d[l, 2], dc * 128, 256)
                                wload(wB, 256, w_in, O_MRG + dc * 128, 256)
                                wload(wC, 0, w_in, O_MRG + 1024 + dc * 128, 256)
                                wload(wC, 256, w_in, O_MRG + 2048 + dc * 128, 256)
                            wbr = ((wA, dd * 128), (wA, 256 + dd * 128), (wB, dd * 128))
                            wgt_ = ((wB, 256 + dd * 128), (wC, dd * 128), (wC, 256 + dd * 128))
                            for tbh in range(2):
                                tb = half * 2 + tbh
                                bs = slice(tb * 512, (tb + 1) * 512)
                                bsh = slice(tbh * 512, (tbh + 1) * 512)
                                sg, acc, tmm = sg_l[tbh], acc_l[tbh], tmm_l[tbh]
                                for n in range(3):
                                    wsrc, off = wgt_[n]
                                    for kc in range(8):
                                        P.op("pe", lambda e: e.matmul(pb[3 + n].t[:, 0:512], wsrc.t[:, kc, off:off + 128], hT.t[:, kc, bs], start=(kc == 0), stop=(kc == 7)),
                                             [wsrc] + hTv[tb * 4:tb * 4 + 4], [pb[3 + n]], inc=(kc == 7))
                                    P.op("act", lambda e: e.activation(sg[n].t[:], pb[3 + n].t[:, 0:512], AF.Sigmoid), [pb[3 + n]], [sg[n]])
                                for n in range(3):
                                    wsrc, off = wbr[n]
                                    for kc in range(8):
                                        P.op("pe", lambda e: e.matmul(pb[n].t[:, 0:512], wsrc.t[:, kc, off:off + 128], ys_sb[n].t[:, kc, bsh], start=(kc == 0), stop=(kc == 7)),
                                             [wsrc, ys_sb[n]], [pb[n]], inc=(kc == 7))
                                P.op("dve", lambda e: e.tensor_tensor(acc.t[:], sg[0].t[:], pb[0].t[:, 0:512], ALU.mult), [sg[0], pb[0]], [acc])
                                P.op("dve", lambda e: e.tensor_tensor(tmm.t[:], sg[1].t[:], pb[1].t[:, 0:512], ALU.mult), [sg[1], pb[1]], [tmm])
                                P.op("dve", lambda e: e.tensor_tensor(acc.t[:], acc.t[:], tmm.t[:], ALU.add), [tmm], [acc])
                                P.op("dve", lambda e: e.tensor_tensor(tmm.t[:], sg[2].t[:], pb[2].t[:, 0:512], ALU.mult), [sg[2], pb[2]], [tmm])
                                P.op("dve", lambda e: e.tensor_tensor(mixT.t[:, dc, bsh], acc.t[:], tmm.t[:], ALU.add), [acc, tmm], [mixT])
                        wo0, wo1 = nextw(), nextw()
                        wload(wo0, 0, w_out_d[l], 0, 512)
                        wload(wo1, 0, w_out_d[l], 512, 512)
                        for t8 in range(8):
                            tt = half * 8 + t8
                            P.dma(xres.t[:, t8, :], xcur.t[tt * 128:(tt + 1) * 128, :], [xcv[tt]], [xrv[t8]])
                            for nb, wsrc in enumerate((wo0, wo1)):
                                bank = pb[(t8 * 2 + nb) % 4]
                                for dc in range(8):
                                    P.op("pe", lambda e: e.matmul(bank.t[:, 0:512], mixT.t[:, dc, t8 * 128:(t8 + 1) * 128], wsrc.t[:, dc, :], start=(dc == 0), stop=(dc == 7)),
                                         [mixT, wsrc], [bank], inc=(dc == 7))
                                P.op("dve", lambda e: e.tensor_tensor(xres.t[:, t8, nb * 512:(nb + 1) * 512], xres.t[:, t8, nb * 512:(nb + 1) * 512], bank.t[:, 0:512], ALU.add), [bank], [xrv[t8]])
                            norm_tile(xres.t[:, t8, :], xrv[t8], tt, (colp, colp.t[:, 8:16]))
                            if stop == "mix" and "d_xm" in dbg_d:
                                P.dma(dbg_d["d_xm"][tt * 128:(tt + 1) * 128, :], xres.t[:, t8, :], [xrv[t8]], [])
                    if stop == "mix":
                        continue
                    with scope() as sc:
                        upT = sc.sb("upT", [128, 8, 1024], BF16)
                        wbx = wb + [sc.sb("wbx%d" % i, [128, 8, 512], BF16) for i in range(4)]
                        wxi = [0]

                        def nextwx():
                            b = wbx[wxi[0] % len(wbx)]
                            wxi[0] += 1
                            return b
                        relu_t = [sc.sb("relu_t%d" % i, [128, 512], F32) for i in range(2)]
                        otile = [sc.sb("otile%d" % i, [128, D], F32) for i in range(2)] if last else None
                        gfin = sc.sb("gfin_sb", [128, D], F32) if last else None
                        if last:
                            P.dma(gfin.t[:], gfin_d.partition_broadcast(128), [], [gfin])
                        for fb in range(4):
                            wu = [nextwx(), nextwx()]
                            wd = [nextwx(), nextwx()]
                            wload(wu[0], 0, w_up_d[l], fb * 1024, 512)
                            wload(wu[1], 0, w_up_d[l], fb * 1024 + 512, 512)
                            wload(wd[0], 0, w_dn_d[l, fb * 1024:(fb + 1) * 1024, :], 0, 512)
                            wload(wd[1], 0, w_dn_d[l, fb * 1024:(fb + 1) * 1024, :], 512, 512)
                            for fc in range(8):
                                def ev(tb, bank):
                                    bsh = slice((tb - half * 2) * 512, (tb - half * 2 + 1) * 512)
                                    rl = relu_t[tb % 2]
                                    P.op("act", lambda e: e.activation(rl.t[:], bank.t[:, 0:512], AF.Relu), [bank], [rl])
                                    P.op("dve", lambda e: e.tensor_tensor(upT.t[:, fc, bsh], rl.t[:], rl.t[:], ALU.mult), [rl], [upT])
                                proj_fm(wu[fc // 4], (fc % 4) * 128, pb[0:4], ev, tbs=(half * 2, half * 2 + 1))
                            for t8 in range(8):
                                tt = half * 8 + t8
                                for nb in range(2):
                                    bank = pb[4 + (t8 * 2 + nb) % 3]
                                    for fc in range(8):
                                        P.op("pe", lambda e: e.matmul(bank.t[:, 0:512], upT.t[:, fc, t8 * 128:(t8 + 1) * 128], wd[nb].t[:, fc, :], start=(fc == 0), stop=(fc == 7)),
                                             [upT, wd[nb]], [bank], inc=(fc == 7))
                                    P.op("dve", lambda e: e.tensor_tensor(xres.t[:, t8, nb * 512:(nb + 1) * 512], xres.t[:, t8, nb * 512:(nb + 1) * 512], bank.t[:, 0:512], ALU.add), [bank], [xrv[t8]])
                                if fb == 3:
                                    if stop == "mlp" and "d_xm" in dbg_d:
                                        P.dma(dbg_d["d_xm"][tt * 128:(tt + 1) * 128, :], xres.t[:, t8, :], [xrv[t8]], [])
                                    if not last:
                                        P.dma(xcur.t[tt * 128:(tt + 1) * 128, :], xres.t[:, t8, :], [xrv[t8]], [xcv[tt]])
                                        if l + 1 < nlayers:
                                            cn = colps[(l + 1) % 2]
                                            norm_tile(xres.t[:, t8, :], xrv[t8], tt, (cn, cn.t[:, 0:8]))
                                    else:
                                        ot = otile[t8 % 2]
                                        nsq, nss = nsq_l[t8 % 2], nss_l[t8 % 2]
                                        P.op("act", lambda e: e.activation(nsq.t[:], xres.t[:, t8, :], AF.Square, accum_out=nss.t[:, 0:1]), [xrv[t8]], [nsq, nss])
                                        rsqrt_(nss.t[:, 1:2], nss.t[:, 0:1], 1.0 / D, [nss], [nss])
                                        P.op("dve", lambda e: e.scalar_tensor_tensor(ot.t[:], xres.t[:, t8, :], nss.t[:, 1:2], gfin.t[:], ALU.mult, ALU.mult), [xrv[t8], nss, gfin], [ot])
                                        P.dma(out_d[tt * 128:(tt + 1) * 128, :], ot.t[:], [ot], [])
            if stop in ("mix", "mlp"):
                stopped = True
                break
        if stopped:
            dump_ys()
        P.finish()
        print("build: ops", P.nops, "waits", P.nwait, {k: P.cnt[k] for k in P.cnt})
    return nc


def make_params(inp):
    colp = np.zeros((DEPTH, 128, NCOL), np.float32)
    rowp = np.zeros((DEPTH, NROW), np.float32)
    for l in range(DEPTH):
        colp[l, :, 0:8] = inp["norm_mix_g"][l].reshape(8, 128).T
        colp[l, :, 8:16] = inp["norm_mlp_g"][l].reshape(8, 128).T
        colp[l, :, 16:88] = inp["dn_conv_w"][l].reshape(3, 24, 128).transpose(2, 0, 1).reshape(128, 72)
        colp[l, :, 88:112] = inp["sc_conv_w"][l].reshape(3, 8, 128).transpose(2, 0, 1).reshape(128, 24)
        rowp[l, 0:1024] = inp["m_norm_g"][l]
        rowp[l, 1024:1152] = inp["dn_norm_g"][l]
        rowp[l, 1152:1168] = inp["m_gate_b"][l].reshape(16)
        rowp[l, 1168:1184] = inp["dn_a_log"][l].reshape(16)
        rowp[l, 1184:1200] = inp["dn_dt_bias"][l].reshape(16)
    return colp, rowp


def make_in_maps(inp, cores):
    colp, rowp = make_params(inp)
    shared = {"w_in": np.ascontiguousarray(inp["w_in"]), "w_branch": np.ascontiguousarray(inp["w_branch"]),
              "w_out": np.ascontiguousarray(inp["w_out"]), "w_up": np.ascontiguousarray(inp["w_up"]),
              "w_down": np.ascontiguousarray(inp["w_down"]), "colp": colp, "rowp": rowp,
              "gfin": np.ascontiguousarray(inp["norm_final_g"])}
    return [dict(shared, x=np.ascontiguousarray(inp["x"][b])) for b in cores]


def kernel(**inputs):
    inp = {k: np.asarray(v, dtype=np.float32) for k, v in inputs.items()}
    nc = build()
    in_maps = make_in_maps(inp, list(range(8)))
    res = run_bass_kernel_spmd(nc, in_maps, core_ids=list(range(8)))
    return np.stack([r["out"] for r in res.results], axis=0).astype(np.float32)
```

```python
import numpy as np
import concourse.bass as bass
import concourse.mybir as mybir
from concourse.bass_utils import run_bass_kernel_spmd
from contextlib import ExitStack

F32 = mybir.dt.float32
BF16 = mybir.dt.bfloat16
ALU = mybir.AluOpType
AF = mybir.ActivationFunctionType
AX = mybir.AxisListType

S = 2048
D = 1024
NT = 16
DEPTH = 4
NPROJ = 14384
DFF = 4096
EPS = 1e-6
NCOL = 112
NROW = 1200
O_MQ, O_MK, O_MV, O_MO, O_MG = 0, 1024, 2048, 3072, 4096
O_DQ, O_DK, O_DV, O_DZ, O_DG = 4112, 5136, 6160, 7184, 8208
O_SB, O_SC, O_SX, O_MRG = 8240, 9264, 10288, 11312
NEG = -30000.0


class Tok:
    __slots__ = ("sem", "val", "clk")

    def __init__(self, sem, val, clk):
        self.sem, self.val, self.clk = sem, val, clk


class Buf:
    __slots__ = ("t", "w", "r", "name", "excl")

    def __init__(self, t, name, excl=False):
        self.t, self.name = t, name
        self.w = None
        self.r = []
        self.excl = excl


class Prog:
    NDMA = 12

    def __init__(self, nc, es, same_engine_sync=True):
        self.nc, self.es = nc, es
        self.same = same_engine_sync
        self.eng = {"pe": nc.tensor, "act": nc.scalar, "dve": nc.vector, "pool": nc.gpsimd, "sp": nc.sync}
        self.sem = {k: es.enter_context(nc.semaphore("s_" + k)) for k in self.eng}
        self.cnt = {k: 0 for k in self.eng}
        self.clk = {k: {} for k in self.eng}
        self.pend = {k: [] for k in self.eng}
        self.dsem = [es.enter_context(nc.semaphore("d%d" % i)) for i in range(2 * self.NDMA)]
        self.dcnt = [0] * (2 * self.NDMA)
        self.dnext = {"sp": 0, "pool": 0}
        self.nwait = 0
        self.nops = 0

    def sb(self, name, shape, dt):
        t = self.es.enter_context(self.nc.sbuf_tensor(name, list(shape), dt))
        return Buf(t, name)

    def ps(self, name, shape, dt):
        t = self.es.enter_context(self.nc.psum_tensor(name, list(shape), dt))
        return Buf(t, name, excl=True)

    def dram(self, name, shape, dt):
        t = self.nc.dram_tensor(name, list(shape), dt, kind="Internal").ap()
        return Buf(t, name)

    def _need(self, reads, writes):
        toks = []
        for b in reads:
            if b.w is not None:
                toks.append(b.w)
        for b in writes:
            if b.w is not None:
                toks.append(b.w)
            toks.extend(b.r)
        return toks

    def _wait(self, e, toks):
        clk = self.clk[e]
        eng = self.eng[e]
        own = self.sem[e]
        best = {}
        for t in toks:
            if t.sem is own and (e == "pe" or not self.same):
                continue
            k = id(t.sem)
            if clk.get(k, 0) >= t.val:
                continue
            if k not in best or best[k].val < t.val:
                best[k] = t
        for k, t in best.items():
            if clk.get(k, 0) >= t.val:
                continue
            eng.wait_ge(t.sem, t.val)
            self.nwait += 1
            clk[k] = t.val
            for kk, vv in t.clk.items():
                if clk.get(kk, 0) < vv:
                    clk[kk] = vv

    def op(self, e, fn, reads, writes, inc=True):
        if any(b.excl for b in reads):
            writes = list(writes) + [b for b in reads if b.excl]
            reads = [b for b in reads if not b.excl]
        self._wait(e, self._need(reads, writes))
        ins = fn(self.eng[e])
        self.nops += 1
        if not inc:
            self.pend[e].append((reads, writes))
            return ins
        self.cnt[e] += 1
        ins.then_inc(self.sem[e], 1)
        tok = Tok(self.sem[e], self.cnt[e], dict(self.clk[e]))
        tok.clk[id(self.sem[e])] = self.cnt[e]
        for (rs, ws) in self.pend[e] + [(reads, writes)]:
            for b in rs:
                b.r.append(tok)
            for b in ws:
                b.w = tok
                b.r = []
        self.pend[e] = []
        return ins

    def dma(self, out, in_, reads, writes, q="sp"):
        toks = self._need(reads, writes)
        i = self.dnext[q] + (self.NDMA if q == "pool" else 0)
        self.dnext[q] = (self.dnext[q] + 1) % self.NDMA
        s = self.dsem[i]
        if self.dcnt[i] > 0:
            toks.append(Tok(s, self.dcnt[i], {}))
        self._wait(q, toks)
        ins = self.eng[q].dma_start(out=out, in_=in_)
        self.dcnt[i] += 16
        ins.then_inc(s, 16)
        tok = Tok(s, self.dcnt[i], dict(self.clk[q]))
        for b in reads:
            b.r.append(tok)
        for b in writes:
            b.w = tok
            b.r = []
        self.nops += 1
        return ins

    def barrier(self):
        toks = []
        for i in range(2 * self.NDMA):
            if self.dcnt[i] > 0:
                toks.append(Tok(self.dsem[i], self.dcnt[i], {}))
        for k in self.eng:
            if self.cnt[k] > 0:
                toks.append(Tok(self.sem[k], self.cnt[k], {}))
        for e in self.eng:
            self._wait(e, [t for t in toks if t.sem is not self.sem[e]])

    def finish(self):
        toks = []
        for i in range(2 * self.NDMA):
            if self.dcnt[i] > 0:
                toks.append(Tok(self.dsem[i], self.dcnt[i], {}))
        for k in self.eng:
            if self.cnt[k] > 0 and k != "sp":
                toks.append(Tok(self.sem[k], self.cnt[k], {}))
        self._wait("sp", toks)


STAG_CFG = [2]
NS_CFG = [5]


def build(nlayers=DEPTH, dbg=(), stop=None):
    nc = bass.Bass("TRN2", target_bir_lowering=False)

    def din(name, shape):
        return nc.dram_tensor(name, list(shape), F32, kind="ExternalInput").ap()

    x_d = din("x", [S, D])
    w_in_d = din("w_in", [DEPTH, D, NPROJ])
    w_br_d = din("w_branch", [DEPTH, 3, D, D])
    w_out_d = din("w_out", [DEPTH, D, D])
    w_up_d = din("w_up", [DEPTH, D, DFF])
    w_dn_d = din("w_down", [DEPTH, DFF, D])
    colp_d = din("colp", [DEPTH, 128, NCOL])
    rowp_d = din("rowp", [DEPTH, NROW])
    gfin_d = din("gfin", [D])
    out_d = nc.dram_tensor("out", [S, D], F32, kind="ExternalOutput").ap()
    dbg_d = {}
    for name, shape, dt in (("d_ys", [3, D, S], BF16), ("d_xm", [S, D], F32)):
        if name in dbg:
            dbg_d[name] = nc.dram_tensor(name, shape, dt, kind="ExternalOutput").ap()

    with ExitStack() as es:
        P = Prog(nc, es)
        hT = P.sb("hT", [128, 8, S], BF16)
        hTv = [Buf(hT.t, "hT%d" % i) for i in range(NT)]
        NWB = 4
        wb = [P.sb("wb%d" % i, [128, 8, 512], BF16) for i in range(NWB)]
        wbi = [0]

        wlim = [NWB]

        def nextw():
            b = wb[wbi[0] % wlim[0]]
            wbi[0] += 1
            return b

        identb = P.sb("identb", [128, 128], BF16)
        identf = P.sb("identf", [128, 128], F32)
        onesb = P.sb("onesb", [128, 128], BF16)
        onesf = P.sb("onesf", [128, 128], F32)
        tincl = P.sb("tincl", [128, 128], F32)
        tinclT = P.sb("tinclT", [128, 128], F32)
        mlo = P.sb("mlo", [128, 128], F32)
        mup = P.sb("mup", [128, 128], F32)
        slf = P.sb("slf", [128, 128], F32)
        suf = P.sb("suf", [128, 128], F32)
        colps = [P.sb("colp_sb%d" % i, [128, NCOL], F32) for i in range(2)]
        rowp = P.sb("rowp_sb", [128, NROW], F32)
        wsm = P.sb("wsm", [128, 8, 48], BF16)
        wsf = P.sb("wsf", [128, 8, 48], F32)
        nsq_l = [P.sb("nsq%d" % i, [128, D], BF16) for i in range(2)]
        nss_l = [P.sb("nss%d" % i, [128, 2], F32) for i in range(2)]
        nxn_l = [P.sb("nxn%d" % i, [128, D], BF16) for i in range(2)]
        nrm_i = [0]
        pb = [P.ps("pb%d" % i, [128, 512], F32) for i in range(7)]
        pbt = P.ps("pbt", [128, 1024], BF16)
        ysT = P.dram("ysT", [3, D, S], BF16)
        ysv = [[Buf(ysT.t, "ys%d_%d" % (n, c)) for c in range(8)] for n in range(3)]
        xcur = P.dram("xcur", [S, D], F32)
        xcv = [Buf(xcur.t, "xc%d" % i) for i in range(NT)]

        def pool(fn, r, w):
            P.op("pool", fn, r, w)

        pool(lambda e: e.memset(onesf.t[:], 1.0), [], [onesf])
        pool(lambda e: e.memset(onesb.t[:], 1.0), [], [onesb])
        pool(lambda e: e.memset(identf.t[:], 0.0), [], [identf])
        pool(lambda e: e.affine_select(identf.t[:], identf.t[:], pattern=[[-1, 128]], compare_op=ALU.not_equal, fill=1.0, base=0, channel_multiplier=1), [identf], [identf])
        pool(lambda e: e.tensor_copy(identb.t[:], identf.t[:]), [identf], [identb])
        pool(lambda e: e.affine_select(tincl.t[:], onesf.t[:], pattern=[[1, 128]], compare_op=ALU.is_ge, fill=0.0, base=0, channel_multiplier=-1), [onesf], [tincl])
        pool(lambda e: e.affine_select(tinclT.t[:], onesf.t[:], pattern=[[-1, 128]], compare_op=ALU.is_ge, fill=0.0, base=0, channel_multiplier=1), [onesf], [tinclT])
        pool(lambda e: e.memset(mlo.t[:], 0.0), [], [mlo])
        pool(lambda e: e.affine_select(mlo.t[:], mlo.t[:], pattern=[[1, 128]], compare_op=ALU.is_ge, fill=NEG, base=0, channel_multiplier=-1), [mlo], [mlo])
        pool(lambda e: e.memset(mup.t[:], 0.0), [], [mup])
        pool(lambda e: e.affine_select(mup.t[:], mup.t[:], pattern=[[-1, 128]], compare_op=ALU.is_ge, fill=NEG, base=0, channel_multiplier=1), [mup], [mup])
        pool(lambda e: e.affine_select(slf.t[:], onesf.t[:], pattern=[[-1, 128]], compare_op=ALU.is_gt, fill=0.0, base=0, channel_multiplier=1), [onesf], [slf])
        pool(lambda e: e.affine_select(suf.t[:], onesf.t[:], pattern=[[1, 128]], compare_op=ALU.is_gt, fill=0.0, base=0, channel_multiplier=-1), [onesf], [suf])

        def dump(name, buf, ap, shape, dt=F32):
            if ("D_" + name) in dbg:
                dd = nc.dram_tensor("D_" + name, list(shape), dt, kind="ExternalOutput").ap()
                P.dma(dd, ap, [buf], [])

        uid = [0]

        def scope():
            class _S:
                def __enter__(s_):
                    s_.es = ExitStack()
                    s_.es.__enter__()
                    return s_

                def sb(s_, name, shape, dt):
                    uid[0] += 1
                    return Buf(s_.es.enter_context(nc.sbuf_tensor("%s_u%d" % (name, uid[0]), list(shape), dt)), name)

                def __exit__(s_, *a):
                    P.barrier()
                    return s_.es.__exit__(*a)
            return _S()

        evi = [0]

        def evac_copy(out_ap, in_ap, reads, writes, scale=None):
            evi[0] += 1
            if evi[0] % 2 == 0:
                if scale is None:
                    P.op("act", lambda e: e.copy(out_ap, in_ap), reads, writes)
                else:
                    P.op("act", lambda e: e.mul(out_ap, in_ap, scale), reads, writes)
            else:
                if scale is None:
                    P.op("dve", lambda e: e.tensor_copy(out_ap, in_ap), reads, writes)
                else:
                    P.op("dve", lambda e: e.tensor_scalar_mul(out_ap, in_ap, scale), reads, writes)

        def wload(dst, col0, src2d, c0, n):
            P.dma(dst.t[:, :, col0:col0 + n], src2d.rearrange("(kc p) n -> p kc n", p=128)[:, :, c0:c0 + n], [], [dst], q="pool")

        def rsqrt_(dst_ap, src_ap, scale, reads, writes):
            P.op("act", lambda e: e.activation(dst_ap, src_ap, AF.Ln, bias=EPS, scale=scale), reads, writes)
            P.op("act", lambda e: e.activation(dst_ap, dst_ap, AF.Exp, scale=-0.5), writes, writes)

        def norm_tile(xt_ap, xbuf, tt, gcol):
            gbuf, gap = gcol
            nrm_i[0] += 1
            nsq, nss, nxn = nsq_l[nrm_i[0] % 2], nss_l[nrm_i[0] % 2], nxn_l[nrm_i[0] % 2]
            pbo = (nrm_i[0] % 2) * 0
            P.op("act", lambda e: e.activation(nsq.t[:], xt_ap, AF.Square, accum_out=nss.t[:, 0:1]), [xbuf], [nsq, nss])
            rsqrt_(nss.t[:, 1:2], nss.t[:, 0:1], 1.0 / D, [nss], [nss])
            P.op("dve", lambda e: e.tensor_scalar(nxn.t[:], xt_ap, nss.t[:, 1:2], None, ALU.mult), [xbuf, nss], [nxn])
            for c in range(8):
                P.op("pe", lambda e: e.transpose(pbt.t[:, c * 128:(c + 1) * 128], nxn.t[:, c * 128:(c + 1) * 128], identb.t[:]), [nxn, identb], [pbt], inc=(c == 7))
            P.op("dve", lambda e: e.tensor_tensor(hT.t[:, :, tt * 128:(tt + 1) * 128], pbt.t[:].rearrange("p (c t) -> p c t", c=8),
                                                  gap.unsqueeze(2).to_broadcast([128, 8, 128]), ALU.mult), [pbt, gbuf], [hTv[tt]])

        def softplus_(dst, src, t1, t2, n, sign_logsig=False):
            P.op("dve", lambda e: e.scalar_tensor_tensor(t1.t[:, 0:n], src.t[:, 0:n], -1.0, src.t[:, 0:n], ALU.mult, ALU.max), [src], [t1])
            P.op("act", lambda e: e.activation(t2.t[:, 0:n], t1.t[:, 0:n], AF.Exp, scale=-1.0), [t1], [t2])
            P.op("act", lambda e: e.activation(t2.t[:, 0:n], t2.t[:, 0:n], AF.Ln, bias=1.0), [t2], [t2])
            if sign_logsig:
                P.op("dve", lambda e: e.scalar_tensor_tensor(dst.t[:, 0:n], src.t[:, 0:n], 0.0, t2.t[:, 0:n], ALU.min, ALU.subtract), [src, t2], [dst])
            else:
                P.op("dve", lambda e: e.scalar_tensor_tensor(dst.t[:, 0:n], src.t[:, 0:n], 0.0, t2.t[:, 0:n], ALU.max, ALU.add), [src, t2], [dst])

        def decay(out, negc_ap, bias_ap, cbufs, mask, bank, diag):
            P.op("dve", lambda e: e.tensor_scalar(diag.t[:], identf.t[:], negc_ap, None, ALU.mult), [identf] + cbufs, [diag])
            P.op("pe", lambda e: e.matmul(bank.t[:, 0:128], onesf.t[:], diag.t[:], start=True, stop=False), [onesf, diag], [bank], inc=False)
            P.op("pe", lambda e: e.matmul(bank.t[:, 0:128], identf.t[:], mask.t[:], start=False, stop=True), [identf, mask], [bank])
            P.op("act", lambda e: e.activation(out.t[:], bank.t[:, 0:128], AF.Exp, bias=bias_ap, scale=1.0), [bank] + cbufs, [out])

        def proj_fm(w, wcol, banks, evac, tbs=range(4)):
            for tb in tbs:
                bank = banks[tb % len(banks)]
                for kc in range(8):
                    P.op("pe", lambda e: e.matmul(bank.t[:, 0:512], w.t[:, kc, wcol:wcol + 128], hT.t[:, kc, tb * 512:(tb + 1) * 512],
                                                  start=(kc == 0), stop=(kc == 7)), [w] + hTv[tb * 4:tb * 4 + 4], [bank], inc=(kc == 7))
                evac(tb, bank)

        def proj_tm(w, wcol, ncols, tt, bank):
            for kc in range(8):
                P.op("pe", lambda e: e.matmul(bank.t[:, 0:ncols], hT.t[:, kc, tt * 128:(tt + 1) * 128], w.t[:, kc, wcol:wcol + ncols],
                                              start=(kc == 0), stop=(kc == 7)), [w, hTv[tt]], [bank], inc=(kc == 7))

        def dump_ys():
            if "d_ys" in dbg_d:
                for n in range(3):
                    for c in range(8):
                        P.dma(dbg_d["d_ys"][n, c * 128:(c + 1) * 128, :], ysT.t[n, c * 128:(c + 1) * 128, :], [ysv[n][c]], [])

        stopped = False
        P.dma(colps[0].t[:], colp_d[0], [], [colps[0]])
        for l in range(nlayers):
            w_in = w_in_d[l]
            colp = colps[l % 2]
            P.dma(rowp.t[:], rowp_d[l].partition_broadcast(128), [], [rowp])
            wrr = w_in.rearrange("(kc p) n -> p kc n", p=128)
            P.dma(wsf.t[:, :, 0:16], wrr[:, :, O_MG:O_MG + 16], [], [wsf])
            P.dma(wsf.t[:, :, 16:48], wrr[:, :, O_DG:O_DG + 32], [], [wsf])
            P.op("dve", lambda e: e.tensor_copy(wsm.t[:], wsf.t[:]), [wsf], [wsm])
            if l == 0:
                with scope() as sc:
                    xin = [sc.sb("xin%d" % i, [128, D], F32) for i in range(2)]
                    for tt in range(NT):
                        xb_ = xin[tt % 2]
                        P.dma(xb_.t[:], x_d[tt * 128:(tt + 1) * 128, :], [], [xb_])
                        P.dma(xcur.t[tt * 128:(tt + 1) * 128, :], xb_.t[:], [xb_], [xcv[tt]])
                        norm_tile(xb_.t[:], xb_, tt, (colp, colp.t[:, 0:8]))

            with scope() as sc:
                sb1 = sc.sb
                gm = sb1("gm", [128, 256], F32)
                ipre = sb1("ipre", [128, 128], F32)
                fpre = sb1("fpre", [128, 128], F32)
                lf = sb1("lf", [128, 128], F32)
                t1 = sb1("t1", [128, 256], F32)
                t2 = sb1("t2", [128, 256], F32)
                bcs = sb1("bcs", [128, 128], F32)
                totb = sb1("totb", [128, 128], F32)
                biasc = sb1("biasc", [128, 128], F32)
                expb = sb1("expb", [128, 128], F32)
                wgt = sb1("wgt", [128, 128], F32)
                dec = sb1("dec", [128, 128], F32)
                for tt in range(NT):
                    for kc in range(8):
                        P.op("pe", lambda e: e.matmul(pb[0].t[:, tt * 16:(tt + 1) * 16], hT.t[:, kc, tt * 128:(tt + 1) * 128], wsm.t[:, kc, 0:16],
                                                      start=(kc == 0), stop=(kc == 7)), [wsm, hTv[tt]], [pb[0]], inc=(kc == 7 and tt == NT - 1))
                P.op("dve", lambda e: e.tensor_tensor(gm.t[:].rearrange("p (t g) -> p t g", g=16), pb[0].t[:, 0:256].rearrange("p (t g) -> p t g", g=16),
                                                      rowp.t[:, 1152:1168].unsqueeze(1).to_broadcast([128, 16, 16]), ALU.add), [pb[0], rowp], [gm])
                gm5 = gm.t[:].rearrange("p (t d w h) -> p t d w h", d=2, w=2, h=4)
                v4 = lambda b_: b_.t[:].rearrange("p (t d h) -> p t d h", d=2, h=4)
                P.op("dve", lambda e: e.tensor_copy(v4(ipre), gm5[:, :, :, 0, :]), [gm], [ipre])
                P.op("dve", lambda e: e.tensor_copy(v4(fpre), gm5[:, :, :, 1, :]), [gm], [fpre])
                softplus_(lf, fpre, t1, t2, 128, sign_logsig=True)
                P.op("pe", lambda e: e.matmul(pb[1].t[:, 0:128], tincl.t[:], lf.t[:], start=True, stop=True), [tincl, lf], [pb[1]], inc=False)
                P.op("pe", lambda e: e.matmul(pb[1].t[:, 128:256], tinclT.t[:], lf.t[:], start=True, stop=True), [tinclT, lf], [pb[1]], inc=False)
                P.op("pe", lambda e: e.matmul(pb[1].t[:, 256:384], onesf.t[:], lf.t[:], start=True, stop=True), [onesf, lf], [pb[1]])
                pv = lambda a, b: pb[1].t[:, a:b].rearrange("p (t d h) -> p t d h", d=2, h=4)
                P.op("dve", lambda e: e.tensor_copy(v4(bcs)[:, :, 0, :], pv(0, 128)[:, :, 0, :]), [pb[1]], [bcs])
                P.op("dve", lambda e: e.tensor_copy(v4(bcs)[:, :, 1, :], pv(128, 256)[:, :, 1, :]), [pb[1]], [bcs])
                P.op("dve", lambda e: e.tensor_copy(totb.t[:], pb[1].t[:, 256:384]), [pb[1]], [totb])
                P.op("dve", lambda e: e.tensor_sub(biasc.t[:], ipre.t[:], bcs.t[:]), [ipre, bcs], [biasc])
                P.op("act", lambda e: e.activation(expb.t[:], bcs.t[:], AF.Exp), [bcs], [expb])
                P.op("dve", lambda e: e.tensor_add(t1.t[:, 0:128], biasc.t[:], totb.t[:]), [biasc, totb], [t1])
                P.op("act", lambda e: e.activation(wgt.t[:], t1.t[:, 0:128], AF.Exp), [t1], [wgt])
                P.op("act", lambda e: e.activation(dec.t[:], totb.t[:], AF.Exp), [totb], [dec])

                qT = sb1("qT", [128, 2, S], BF16)
                kT = sb1("kT", [128, 2, S], BF16)
                ktm = sb1("ktm", [128, NT, 256], BF16)
                vtm = sb1("vtm", [128, NT, 257], BF16)
                hm = sb1("hm", [128, NT, 256], F32)
                hmv = [Buf(hm.t, "hm%d" % i) for i in range(NT)]
                Cst = [sb1("Cst%d" % d_, [128, 2, 257], F32) for d_ in range(2)]
                Cb = [sb1("Cb%d" % d_, [128, 2, 257], BF16) for d_ in range(2)]
                DT = [sb1("DT%d" % d_, [128, 128], F32) for d_ in range(2)]
                diag = [sb1("diag%d" % d_, [128, 128], F32) for d_ in range(2)]
                scT = [sb1("scT%d" % d_, [128, 128], BF16) for d_ in range(2)]
                Asb = [sb1("Asb%d" % d_, [128, 257], F32) for d_ in range(2)]
                comb = [sb1("comb%d" % d_, [128, 257], F32) for d_ in range(2)]
                rr = [sb1("rr%d" % d_, [128, 2], F32) for d_ in range(2)]
                kw = [sb1("kw%d" % d_, [128, 256], BF16) for d_ in range(2)]
                og_l = [sb1("og%d" % i, [128, 256], F32) for i in range(2)]
                ty_l = [sb1("ty%d" % i, [128, 256], F32) for i in range(2)]
                yb_l = [sb1("yb%d" % i, [128, 256], BF16) for i in range(2)]
                ty = ty_l[0]
                ssq = sb1("ssq", [128, 2 * NT], F32)
                ymT = sb1("ymT", [128, 2, S], BF16)
                P.op("dve", lambda e: e.memset(vtm.t[:, :, 256:257], 1.0), [], [vtm])

                for h in range(4):
                    wqk, wkv, wo = nextw(), nextw(), nextw()
                    wload(wqk, 0, w_in, O_MQ + h * 256, 256)
                    wload(wqk, 256, w_in, O_MK + h * 256, 256)
                    wload(wkv, 0, w_in, O_MK + h * 256, 256)
                    wload(wkv, 256, w_in, O_MV + h * 256, 256)
                    wload(wo, 0, w_in, O_MO + h * 256, 256)
                    for ft in range(2):
                        proj_fm(wqk, ft * 128, pb[0:4], lambda tb, bank: evac_copy(qT.t[:, ft, tb * 512:(tb + 1) * 512], bank.t[:, 0:512], [bank], [qT], scale=0.0625))
                        proj_fm(wqk, 256 + ft * 128, pb[0:4], lambda tb, bank: evac_copy(kT.t[:, ft, tb * 512:(tb + 1) * 512], bank.t[:, 0:512], [bank], [kT]))
                    for tt in range(NT):
                        bank = pb[tt % 4]
                        proj_tm(wkv, 0, 512, tt, bank)
                        P.op("act", lambda e: e.copy(ktm.t[:, tt, :], bank.t[:, 0:256]), [bank], [ktm])
                        P.op("dve", lambda e: e.tensor_copy(vtm.t[:, tt, 0:256], bank.t[:, 256:512]), [bank], [vtm])
                    def gen_mrec(d_, h=h):
                        bRQ, bS, bA = (pb[0], pb[3])[d_], (pb[1], pb[4])[d_], (pb[2], pb[5])[d_]
                        mask = mlo if d_ == 0 else mup
                        for step in range(NT):
                            c = step if d_ == 0 else NT - 1 - step
                            cs = slice(c * 128, (c + 1) * 128)
                            gi = (c * 2 + d_) * 4 + h
                            col = lambda b_: b_.t[:, gi:gi + 1]
                            P.op("dve", lambda e: e.tensor_scalar(diag[d_].t[:], identf.t[:], col(bcs), None, ALU.mult), [identf, bcs], [diag[d_]])
                            if step < NT - 1:
                                P.op("act", lambda e: e.activation(kw[d_].t[:], ktm.t[:, c, :], AF.Copy, scale=col(wgt)), [ktm, wgt], [kw[d_]])
                            yield
                            P.op("pe", lambda e: e.matmul(bRQ.t[:, 0:128], onesf.t[:], diag[d_].t[:], start=True, stop=False), [onesf, diag[d_]], [bRQ], inc=False)
                            P.op("pe", lambda e: e.matmul(bRQ.t[:, 0:128], identf.t[:], mask.t[:], start=False, stop=True), [identf, mask], [bRQ], inc=False)
                            for kc in range(2):
                                P.op("pe", lambda e: e.matmul(bRQ.t[:, 128:256], kT.t[:, kc, cs], qT.t[:, kc, cs], start=(kc == 0), stop=(kc == 1)), [kT, qT], [bRQ], inc=(kc == 1))
                            if step > 0:
                                for kc in range(2):
                                    P.op("pe", lambda e: e.matmul(bA.t[:, 0:257], qT.t[:, kc, cs], Cb[d_].t[:, kc, :], start=(kc == 0), stop=(kc == 1)), [qT, Cb[d_]], [bA], inc=(kc == 1))
                            yield
                            P.op("act", lambda e: e.activation(DT[d_].t[:], bRQ.t[:, 0:128], AF.Exp, bias=col(biasc), scale=1.0), [bRQ, biasc], [DT[d_]])
                            if step > 0:
                                P.op("act", lambda e: e.activation(Asb[d_].t[:], bA.t[:, 0:257], AF.Copy, scale=col(expb)), [bA, expb], [Asb[d_]])
                            yield
                            P.op("dve", lambda e: e.tensor_tensor(scT[d_].t[:], bRQ.t[:, 128:256], DT[d_].t[:], ALU.mult), [bRQ, DT[d_]], [scT[d_]])
                            yield
                            P.op("pe", lambda e: e.matmul(bS.t[:, 0:257], scT[d_].t[:], vtm.t[:, c, :], start=True, stop=True), [scT[d_], vtm], [bS])
                            yield
                            if step > 0:
                                P.op("dve", lambda e: e.tensor_tensor(comb[d_].t[:], Asb[d_].t[:], bS.t[:, 0:257], ALU.add), [Asb[d_], bS], [comb[d_]])
                            else:
                                P.op("dve", lambda e: e.tensor_copy(comb[d_].t[:], bS.t[:, 0:257]), [bS], [comb[d_]])
                            den = comb[d_].t[:, 256:257]
                            P.op("dve", lambda e: e.scalar_tensor_tensor(rr[d_].t[:, 0:1], den, -1.0, den, ALU.mult, ALU.max), [comb[d_]], [rr[d_]])
                            P.op("dve", lambda e: e.tensor_scalar_max(rr[d_].t[:, 0:1], rr[d_].t[:, 0:1], 1.0), [rr[d_]], [rr[d_]])
                            P.op("dve", lambda e: e.reciprocal(rr[d_].t[:, 1:2], rr[d_].t[:, 0:1]), [rr[d_]], [rr[d_]])
                            if step < NT - 1:
                                P.op("pe", lambda e: e.matmul(bS.t[:, 0:257], kw[d_].t[:, 0:128], vtm.t[:, c, :], start=True, stop=True), [kw[d_], vtm], [bS])
                                P.op("pe", lambda e: e.matmul(bA.t[:, 0:257], kw[d_].t[:, 128:256], vtm.t[:, c, :], start=True, stop=True), [kw[d_], vtm], [bA])
                            yield
                            first = (d_ == 0 and c < 8) or (d_ == 1 and c >= 8)
                            if first:
                                P.op("act", lambda e: e.activation(hm.t[:, c, :], comb[d_].t[:, 0:256], AF.Copy, scale=rr[d_].t[:, 1:2]), [comb[d_], rr[d_]], [hmv[c]])
                            else:
                                P.op("dve", lambda e: e.scalar_tensor_tensor(hm.t[:, c, :], comb[d_].t[:, 0:256], rr[d_].t[:, 1:2], hm.t[:, c, :], ALU.mult, ALU.add), [comb[d_], rr[d_]], [hmv[c]])
                            if step < NT - 1:
                                for m, bC in enumerate((bS, bA)):
                                    if step == 0:
                                        P.op("dve", lambda e: e.tensor_copy(Cst[d_].t[:, m, :], bC.t[:, 0:257]), [bC], [Cst[d_]])
                                    else:
                                        P.op("dve", lambda e: e.scalar_tensor_tensor(Cst[d_].t[:, m, :], Cst[d_].t[:, m, :], col(dec), bC.t[:, 0:257], ALU.mult, ALU.add), [bC, dec], [Cst[d_]])
                                yield
                                P.op("act", lambda e: e.copy(Cb[d_].t[:], Cst[d_].t[:]), [Cst[d_]], [Cb[d_]])
                            yield

                    gens = [gen_mrec(0), gen_mrec(1)]
                    while gens:
                        for g_ in list(gens):
                            try:
                                next(g_)
                            except StopIteration:
                                gens.remove(g_)
                    for tt in range(NT):
                        P.op("act", lambda e: e.activation(ty.t[:], hm.t[:, tt, :], AF.Square, accum_out=ssq.t[:, tt:tt + 1]), [hmv[tt]], [ty, ssq])
                    rsqrt_(ssq.t[:, NT:2 * NT], ssq.t[:, 0:NT], 1.0 / 256.0, [ssq], [ssq])
                    for tt in range(NT):
                        bank = pb[tt % 4]
                        og, ty, yb = og_l[tt % 2], ty_l[tt % 2], yb_l[tt % 2]
                        proj_tm(wo, 0, 256, tt, bank)
                        P.op("act", lambda e: e.activation(og.t[:], bank.t[:, 0:256], AF.Sigmoid), [bank], [og])
                        P.op("dve", lambda e: e.scalar_tensor_tensor(ty.t[:], hm.t[:, tt, :], ssq.t[:, NT + tt:NT + tt + 1], rowp.t[:, h * 256:(h + 1) * 256], ALU.mult, ALU.mult), [hmv[tt], ssq, rowp], [ty])
                        P.op("dve", lambda e: e.tensor_tensor(yb.t[:], ty.t[:], og.t[:], ALU.mult), [ty, og], [yb])
                        for j in range(2):
                            P.op("pe", lambda e: e.transpose(pbt.t[:, j * 128:(j + 1) * 128], yb.t[:, j * 128:(j + 1) * 128], identb.t[:]), [yb, identb], [pbt], inc=(j == 1))
                        P.op("act", lambda e: e.copy(ymT.t[:, :, tt * 128:(tt + 1) * 128], pbt.t[:, 0:256].rearrange("p (j t) -> p j t", j=2)), [pbt], [ymT])
                    P.dma(ysT.t[0, h * 256:(h + 1) * 256, :].rearrange("(j p) t -> p j t", p=128), ymT.t[:], [ymT], [ysv[0][2 * h], ysv[0][2 * h + 1]])
                    if stop == "m0":
                        break
            if stop in ("m0", "mlstm"):
                stopped = True
                break

            with scope() as sc:
                sb2 = sc.sb
                A2 = lambda name: sb2(name, [128, 256], F32)
                v4 = lambda b_: b_.t[:].rearrange("p (t d h) -> p t d h", d=2, h=8)
                Gc, negG, expG, bexpG, kdw, glb, negbeta, beta = [A2(n) for n in ("Gc", "negG", "expG", "bexpG", "kdw", "glb", "negbeta", "beta")]
                with scope() as sct:
                    dg = sct.sb("dg", [128, 512], F32)
                    apre, spl, gg, t1, t2 = [sct.sb(n, [128, 256], F32) for n in ("apre", "spl", "gg", "t1d", "t2d")]
                    ea = sct.sb("ea", [128, 16], F32)
                    for tt in range(NT):
                        for kc in range(8):
                            P.op("pe", lambda e: e.matmul(pb[0].t[:, tt * 32:(tt + 1) * 32], hT.t[:, kc, tt * 128:(tt + 1) * 128], wsm.t[:, kc, 16:48],
                                                          start=(kc == 0), stop=(kc == 7)), [wsm, hTv[tt]], [pb[0]], inc=(kc == 7 and tt == NT - 1))
                    P.op("act", lambda e: e.copy(dg.t[:], pb[0].t[:, 0:512]), [pb[0]], [dg])
                    dg5 = dg.t[:].rearrange("p (t d w h) -> p t d w h", d=2, w=2, h=8)
                    P.op("act", lambda e: e.activation(v4(beta), dg5[:, :, :, 0, :], AF.Sigmoid), [dg], [beta])
                    P.op("dve", lambda e: e.tensor_tensor(v4(apre), dg5[:, :, :, 1, :],
                                                          rowp.t[:, 1184:1200].rearrange("p (d h) -> p d h", d=2).unsqueeze(1).to_broadcast([128, 16, 2, 8]), ALU.add), [dg, rowp], [apre])
                    softplus_(spl, apre, t1, t2, 256)
                    P.op("act", lambda e: e.activation(ea.t[:], rowp.t[:, 1168:1184], AF.Exp), [rowp], [ea])
                    P.op("dve", lambda e: e.scalar_tensor_tensor(v4(gg), v4(spl), -1.0,
                                                                 ea.t[:].rearrange("p (d h) -> p d h", d=2).unsqueeze(1).to_broadcast([128, 16, 2, 8]), ALU.mult, ALU.mult), [spl, ea], [gg])
                    P.op("pe", lambda e: e.matmul(pb[1].t[:, 0:256], tincl.t[:], gg.t[:], start=True, stop=True), [tincl, gg], [pb[1]], inc=False)
                    P.op("pe", lambda e: e.matmul(pb[1].t[:, 256:512], tinclT.t[:], gg.t[:], start=True, stop=True), [tinclT, gg], [pb[1]], inc=False)
                    P.op("pe", lambda e: e.matmul(pb[2].t[:, 0:256], onesf.t[:], gg.t[:], start=True, stop=True), [onesf, gg], [pb[2]])
                    pv = lambda a, b: pb[1].t[:, a:b].rearrange("p (t d h) -> p t d h", d=2, h=8)
                    P.op("dve", lambda e: e.tensor_copy(v4(Gc)[:, :, 0, :], pv(0, 256)[:, :, 0, :]), [pb[1]], [Gc])
                    P.op("dve", lambda e: e.tensor_copy(v4(Gc)[:, :, 1, :], pv(256, 512)[:, :, 1, :]), [pb[1]], [Gc])
                    P.op("dve", lambda e: e.tensor_scalar_mul(negG.t[:], Gc.t[:], -1.0), [Gc], [negG])
                    P.op("act", lambda e: e.activation(expG.t[:], Gc.t[:], AF.Exp), [Gc], [expG])
                    P.op("dve", lambda e: e.tensor_mul(bexpG.t[:], beta.t[:], expG.t[:]), [beta, expG], [bexpG])
                    P.op("dve", lambda e: e.tensor_tensor(t1.t[:], pb[2].t[:, 0:256], Gc.t[:], ALU.subtract), [pb[2], Gc], [t1])
                    P.op("act", lambda e: e.activation(kdw.t[:], t1.t[:], AF.Exp), [t1], [kdw])
                    P.op("act", lambda e: e.activation(glb.t[:], pb[2].t[:, 0:256], AF.Exp), [pb[2]], [glb])
                    P.op("dve", lambda e: e.tensor_scalar_mul(negbeta.t[:], beta.t[:], -1.0), [beta], [negbeta])
                for nm, b_ in (("Gc", Gc), ("expG", expG), ("kdw", kdw), ("glb", glb), ("beta", beta)):
                    dump(nm, b_, b_.t[:], [128, 256])

                pc = sb2("pc", [128, S + 2], F32)
                cv = sb2("cv", [128, S], F32)
                sq = sb2("sq", [128, S], BF16)
                rin = sb2("rin", [128, 512], F32)
                qnT = sb2("qnT", [128, S], BF16)
                knT = sb2("knT", [128, S], BF16)
                vT = sb2("vT", [128, S], BF16)
                ktm = sb2("ktm2", [128, NT, 128], BF16)
                vtm = sb2("vtm2", [128, NT, 128], BF16)
                WTs = sb2("WTs", [128, 32, 128], BF16)
                Us = sb2("Us", [128, 32, 128], F32)
                ATs = sb2("ATs", [128, 32, 128], BF16)
                stv = [Buf(None, "st%d" % i) for i in range(32)]
                osb = sb2("osb", [128, NT, 128], F32)
                osv = [Buf(osb.t, "os%d" % i) for i in range(NT)]
                NS = NS_CFG[0]
                STAG = STAG_CFG[0]
                wlim[0] = 3 if NS_CFG[0] > 4 else NWB
                _flat = wb[3].t[:].rearrange("p a b -> p (a b)")
                _off = [0]

                def slot_tile(i, name, shape, dt):
                    if i < 4:
                        return sb2("%s%d" % (name, i), shape, dt)
                    n = shape[1] * (2 if dt == F32 else 1)
                    ap = _flat[:, _off[0]:_off[0] + n]
                    _off[0] += n
                    if dt == F32:
                        ap = ap.bitcast(F32)
                    return Buf(ap, "%s%d" % (name, i))
                Dm = [slot_tile(i, "Dm", [128, 128], F32) for i in range(NS)]
                dgl = [slot_tile(i, "dgl", [128, 128], F32) for i in range(NS)]
                Do1 = [slot_tile(i, "Do1", [128, 128], F32) for i in range(NS)]
                Do2 = [slot_tile(i, "Do2", [128, 128], F32) for i in range(NS)]
                CH = [[slot_tile(i, "CH%d_" % j, [128, 384], BF16) for j in range(2)] for i in range(NS)]
                Ao = [slot_tile(i, "Ao", [128, 256], BF16) for i in range(NS)]
                AoT = [slot_tile(i, "AoT", [128, 256], BF16) for i in range(NS)]
                attn_t = [slot_tile(i, "attn", [128, 128], BF16) for i in range(NS)]
                X0b = [slot_tile(i, "X0b", [128, 256], BF16) for i in range(NS)]
                X1b = [slot_tile(i, "X1b", [128, 256], BF16) for i in range(NS)]
                R1b = X0b
                Tmb = Ao
                Wtm = Do1
                Wb = [slot_tile(i, "Wb", [128, 128], BF16) for i in range(NS)]
                mbd = [sb2("mbd%d" % i, [128, 128], F32) for i in range(2)]
                mo1 = [sb2("mo1%d" % i, [128, 128], F32) for i in range(2)]
                mo2 = [sb2("mo2%d" % i, [128, 128], F32) for i in range(2)]
                with scope() as scm:
                    E32 = scm.sb("E32", [4, 128], F32)
                    E64 = scm.sb("E64", [2, 128], F32)
                    b32 = scm.sb("b32", [128, 128], F32)
                    b64 = scm.sb("b64", [128, 128], F32)
                    tmk = scm.sb("tmk", [128, 128], F32)
                    for E_, w_ in ((E32, 32), (E64, 64)):
                        np_ = 128 // w_
                        P.op("pool", lambda e: e.memset(E_.t[:], 1.0), [], [E_])
                        P.op("pool", lambda e: e.affine_select(E_.t[:], E_.t[:], pattern=[[1, 128]], compare_op=ALU.is_ge, fill=0.0, base=0, channel_multiplier=-w_), [E_], [E_])
                        P.op("pool", lambda e: e.affine_select(E_.t[:], E_.t[:], pattern=[[-1, 128]], compare_op=ALU.is_ge, fill=0.0, base=w_ - 1, channel_multiplier=w_), [E_], [E_])
                    P.op("pe", lambda e: e.matmul(pb[0].t[:, 0:128], E32.t[:], E32.t[:], start=True, stop=True), [E32], [pb[0]], inc=False)
                    P.op("pe", lambda e: e.matmul(pb[0].t[:, 128:256], E64.t[:], E64.t[:], start=True, stop=True), [E64], [pb[0]])
                    P.op("dve", lambda e: e.tensor_copy(b32.t[:], pb[0].t[:, 0:128]), [pb[0]], [b32])
                    P.op("dve", lambda e: e.tensor_copy(b64.t[:], pb[0].t[:, 128:256]), [pb[0]], [b64])
                    for d_, tri in ((0, slf), (1, suf)):
                        P.op("dve", lambda e: e.tensor_tensor(mbd[d_].t[:], tri.t[:], b32.t[:], ALU.mult), [tri, b32], [mbd[d_]])
                        P.op("dve", lambda e: e.tensor_tensor(tmk.t[:], b64.t[:], b32.t[:], ALU.subtract), [b64, b32], [tmk])
                        P.op("dve", lambda e: e.tensor_tensor(mo1[d_].t[:], tri.t[:], tmk.t[:], ALU.mult), [tri, tmk], [mo1[d_]])
                        P.op("dve", lambda e: e.tensor_tensor(tmk.t[:], tri.t[:], b64.t[:], ALU.mult), [tri, b64], [tmk])
                        P.op("dve", lambda e: e.tensor_tensor(mo2[d_].t[:], tri.t[:], tmk.t[:], ALU.subtract), [tri, tmk], [mo2[d_]])
                Sst = [sb2("Sst%d" % d_, [128, 128], F32) for d_ in range(2)]
                Sbb = [sb2("Sbb%d" % d_, [128, 128], BF16) for d_ in range(2)]
                vnb = [sb2("vnb%d" % d_, [128, 128], BF16) for d_ in range(2)]
                kdt = [sb2("kdt%d" % d_, [128, 128], BF16) for d_ in range(2)]
                tmo = [sb2("tmo%d" % d_, [128, 128], F32) for d_ in range(2)]
                zg_l = [sb2("zg%d" % i, [128, 128], F32) for i in range(2)]
                ty_l = [sb2("ty2%d" % i, [128, 128], F32) for i in range(2)]
                yb_l = [sb2("yb2%d" % i, [128, 128], BF16) for i in range(2)]
                ty = ty_l[0]
                ssq = sb2("ssq2", [128, 2 * NT], F32)
                ydT = sq
                P.op("dve", lambda e: e.memset(pc.t[:, 0:1], 0.0), [], [pc])
                P.op("dve", lambda e: e.memset(pc.t[:, S + 1:S + 2], 0.0), [], [pc])
                wA = wB = None
                for h in range(8):
                    hh = h % 2
                    if hh == 0:
                        wA, wB = nextw(), nextw()
                        wload(wA, 0, w_in, O_DQ + h * 128, 256)
                        wload(wA, 256, w_in, O_DK + h * 128, 256)
                        wload(wB, 0, w_in, O_DV + h * 128, 256)
                        wload(wB, 256, w_in, O_DZ + h * 128, 256)
                    for j, (wsrc, off) in enumerate(((wA, hh * 128), (wA, 256 + hh * 128), (wB, hh * 128))):
                        proj_fm(wsrc, off, pb[0:4], lambda tb, bank: evac_copy(pc.t[:, 1 + tb * 512:1 + (tb + 1) * 512], bank.t[:, 0:512], [bank], [pc]))
                        cw = lambda k: colp.t[:, 16 + k * 24 + j * 8 + h:16 + k * 24 + j * 8 + h + 1]
                        P.op("dve", lambda e: e.tensor_scalar(cv.t[:], pc.t[:, 0:S], cw(0), None, ALU.mult), [pc, colp], [cv])
                        P.op("dve", lambda e: e.scalar_tensor_tensor(cv.t[:], pc.t[:, 1:S + 1], cw(1), cv.t[:], ALU.mult, ALU.add), [pc, colp], [cv])
                        P.op("dve", lambda e: e.scalar_tensor_tensor(cv.t[:], pc.t[:, 2:S + 2], cw(2), cv.t[:], ALU.mult, ALU.add), [pc, colp], [cv])
                        P.op("act", lambda e: e.activation(cv.t[:], cv.t[:], AF.Silu), [cv], [cv])
                        if j == 2:
                            P.op("act", lambda e: e.copy(vT.t[:], cv.t[:]), [cv], [vT])
                        else:
                            dst = qnT if j == 0 else knT
                            P.op("act", lambda e: e.activation(sq.t[:], cv.t[:], AF.Square), [cv], [sq])
                            for tb in range(4):
                                bs = slice(tb * 512, (tb + 1) * 512)
                                bank = pb[4 + tb % 2]
                                P.op("pe", lambda e: e.matmul(bank.t[:, 0:512], onesb.t[:], sq.t[:, bs], start=True, stop=True), [onesb, sq], [bank])
                                rsqrt_(rin.t[:], bank.t[:, 0:512], 1.0, [bank], [rin])
                                if j == 0:
                                    P.op("dve", lambda e: e.scalar_tensor_tensor(dst.t[:, bs], cv.t[:, bs], 128.0 ** -0.5, rin.t[:], ALU.mult, ALU.mult), [cv, rin], [dst])
                                else:
                                    P.op("dve", lambda e: e.tensor_tensor(dst.t[:, bs], cv.t[:, bs], rin.t[:], ALU.mult), [cv, rin], [dst])
                    if h == 0:
                        dump("qnT", qnT, qnT.t[:], [128, S], BF16)
                        dump("knT", knT, knT.t[:], [128, S], BF16)
                        dump("vT", vT, vT.t[:], [128, S], BF16)
                    for src, dstm in ((knT, ktm), (vT, vtm)):
                        for g4 in range(2):
                            for i in range(8):
                                tt = g4 * 8 + i
                                P.op("pe", lambda e: e.transpose(pbt.t[:, i * 128:(i + 1) * 128], src.t[:, tt * 128:(tt + 1) * 128], identb.t[:]), [src, identb], [pbt], inc=(i == 7))
                            evac_copy(dstm.t[:, g4 * 8:(g4 + 1) * 8, :], pbt.t[:].rearrange("p (i t) -> p i t", i=8), [pbt], [dstm])
                    if stop == "dn0a":
                        break
                    def gen_prep(si, c, d_, h=h):
                        cs = slice(c * 128, (c + 1) * 128)
                        gi = (c * 2 + d_) * 8 + h
                        col = lambda b_: b_.t[:, gi:gi + 1]
                        e_ = c * 2 + d_
                        ch0, ch1 = CH[si][0], CH[si][1]
                        bank = pb[1 + si]
                        mask = mup if d_ == 0 else mlo
                        P.op("dve", lambda e: e.tensor_scalar(dgl[si].t[:], identf.t[:], col(negG), None, ALU.mult), [identf, negG], [dgl[si]])
                        yield
                        while lock["pb0"] is not None:
                            yield
                        lock["pb0"] = si
                        P.op("pe", lambda e: e.matmul(pb[0].t[:, 0:128], onesf.t[:], dgl[si].t[:], start=True, stop=False), [onesf, dgl[si]], [pb[0]], inc=False)
                        P.op("pe", lambda e: e.matmul(pb[0].t[:, 0:128], identf.t[:], mask.t[:], start=False, stop=True), [identf, mask], [pb[0]], inc=False)
                        P.op("pe", lambda e: e.matmul(pb[0].t[:, 128:256], knT.t[:, cs], knT.t[:, cs], start=True, stop=True), [knT], [pb[0]], inc=False)
                        P.op("pe", lambda e: e.matmul(pb[0].t[:, 256:384], qnT.t[:, cs], knT.t[:, cs], start=True, stop=True), [qnT, knT], [pb[0]])
                        yield
                        P.op("act", lambda e: e.activation(Dm[si].t[:], pb[0].t[:, 0:128], AF.Exp, bias=col(Gc), scale=1.0), [pb[0], Gc], [Dm[si]])
                        P.op("act", lambda e: e.activation(X0b[si].t[:, 0:128], ktm.t[:, c, :], AF.Copy, scale=col(bexpG)), [ktm, bexpG], [X0b[si]])
                        P.op("act", lambda e: e.activation(X0b[si].t[:, 128:256], vtm.t[:, c, :], AF.Copy, scale=col(beta)), [vtm, beta], [X0b[si]])
                        yield
                        P.op("pool", lambda e: e.tensor_tensor(dgl[si].t[:], Dm[si].t[:], mbd[d_].t[:], ALU.mult), [Dm[si], mbd[d_]], [dgl[si]])
                        P.op("pool", lambda e: e.tensor_tensor(Do1[si].t[:], Dm[si].t[:], mo1[d_].t[:], ALU.mult), [Dm[si], mo1[d_]], [Do1[si]])
                        P.op("pool", lambda e: e.tensor_tensor(Do2[si].t[:], Dm[si].t[:], mo2[d_].t[:], ALU.mult), [Dm[si], mo2[d_]], [Do2[si]])
                        P.op("dve", lambda e: e.tensor_tensor(attn_t[si].t[:], pb[0].t[:, 256:384], Dm[si].t[:], ALU.mult), [pb[0], Dm[si]], [attn_t[si]])
                        yield
                        P.op("dve", lambda e: e.scalar_tensor_tensor(ch0.t[:, 256:384], pb[0].t[:, 128:256], col(negbeta), dgl[si].t[:], ALU.mult, ALU.mult), [pb[0], negbeta, dgl[si]], [ch0])
                        P.op("dve", lambda e: e.scalar_tensor_tensor(Ao[si].t[:, 0:128], pb[0].t[:, 128:256], col(beta), Do1[si].t[:], ALU.mult, ALU.mult), [pb[0], beta, Do1[si]], [Ao[si]])
                        P.op("dve", lambda e: e.scalar_tensor_tensor(Ao[si].t[:, 128:256], pb[0].t[:, 128:256], col(beta), Do2[si].t[:], ALU.mult, ALU.mult), [pb[0], beta, Do2[si]], [Ao[si]])
                        lock["pb0"] = None
                        yield
                        while lock["pbt"] is not None:
                            yield
                        lock["pbt"] = si
                        P.op("pe", lambda e: e.transpose(pbt.t[:, 0:128], ch0.t[:, 256:384], identb.t[:]), [ch0, identb], [pbt], inc=False)
                        P.op("pe", lambda e: e.transpose(pbt.t[:, 128:256], Ao[si].t[:, 0:128], identb.t[:]), [Ao[si], identb], [pbt], inc=False)
                        P.op("pe", lambda e: e.transpose(pbt.t[:, 256:384], Ao[si].t[:, 128:256], identb.t[:]), [Ao[si], identb], [pbt], inc=False)
                        P.op("pe", lambda e: e.transpose(pbt.t[:, 384:512], attn_t[si].t[:], identb.t[:]), [attn_t[si], identb], [pbt])
                        yield
                        P.op("act", lambda e: e.copy(ch0.t[:, 0:128], pbt.t[:, 0:128]), [pbt], [ch0])
                        P.op("dve", lambda e: e.tensor_tensor(ch1.t[:, 128:256], pbt.t[:, 0:128], identb.t[:], ALU.add), [pbt, identb], [ch1])
                        P.op("act", lambda e: e.copy(AoT[si].t[:], pbt.t[:, 128:384]), [pbt], [AoT[si]])
                        P.op("dve", lambda e: e.tensor_copy(ATs.t[:, e_, :], pbt.t[:, 384:512]), [pbt], [stv[e_]])
                        lock["pbt"] = None
                        yield
                        P.op("pe", lambda e: e.matmul(bank.t[:, 0:128], ch0.t[:, 256:384], ch0.t[:, 0:128], start=True, stop=True), [ch0], [bank], inc=False)
                        P.op("pe", lambda e: e.matmul(bank.t[:, 256:384], ch0.t[:, 0:128], ch0.t[:, 256:384], start=True, stop=True), [ch0], [bank])
                        yield
                        P.op("act", lambda e: e.copy(ch1.t[:].rearrange("p (a b) -> p a b", b=128)[:, 0::2, :], bank.t[:, 0:384].rearrange("p (a b) -> p a b", b=128)[:, 0::2, :]), [bank], [ch1])
                        yield
                        for j in range(1, 5):
                            cur = CH[si][j % 2]
                            nxt = CH[si][(j + 1) % 2]
                            if j < 4:
                                P.op("pe", lambda e: e.matmul(bank.t[:, 0:256], cur.t[:, 256:384], cur.t[:, 0:256], start=True, stop=False), [cur], [bank], inc=False)
                                P.op("pe", lambda e: e.matmul(bank.t[:, 128:256], identb.t[:], cur.t[:, 128:256], start=False, stop=True), [cur, identb], [bank], inc=False)
                                P.op("pe", lambda e: e.matmul(bank.t[:, 256:384], cur.t[:, 0:128], cur.t[:, 256:384], start=True, stop=True), [cur], [bank])
                                yield
                                evac_copy(nxt.t[:], bank.t[:, 0:384], [bank], [nxt])
                                yield
                            else:
                                P.op("pe", lambda e: e.matmul(bank.t[:, 128:256], cur.t[:, 256:384], cur.t[:, 128:256], start=True, stop=False), [cur], [bank], inc=False)
                                P.op("pe", lambda e: e.matmul(bank.t[:, 128:256], identb.t[:], cur.t[:, 128:256], start=False, stop=True), [cur, identb], [bank])
                                yield
                                evac_copy(nxt.t[:, 128:256], bank.t[:, 128:256], [bank], [nxt])
                                yield
                        fin = CH[si][1]
                        PTf = fin.t[:, 128:256]
                        A1T, A2T = AoT[si].t[:, 0:128], AoT[si].t[:, 128:256]
                        lo, hi = bank.t[:, 0:256], bank.t[:, 256:512]
                        P.op("pe", lambda e: e.matmul(lo, PTf, X0b[si].t[:], start=True, stop=True), [fin, X0b[si]], [bank])
                        yield
                        P.op("act", lambda e: e.copy(R1b[si].t[:], lo), [bank], [R1b[si]])
                        P.op("dve", lambda e: e.tensor_copy(Us.t[:, e_, :], bank.t[:, 128:256]), [bank], [stv[e_]])
                        yield
                        P.op("pe", lambda e: e.matmul(hi, A1T, R1b[si].t[:], start=True, stop=True), [AoT[si], R1b[si]], [bank])
                        yield
                        P.op("act", lambda e: e.copy(Tmb[si].t[:], hi), [bank], [Tmb[si]])
                        yield
                        P.op("pe", lambda e: e.matmul(lo, PTf, Tmb[si].t[:], start=True, stop=True), [fin, Tmb[si]], [bank])
                        yield
                        P.op("dve", lambda e: e.tensor_tensor(X1b[si].t[:], R1b[si].t[:], lo, ALU.subtract), [R1b[si], bank], [X1b[si]])
                        P.op("dve", lambda e: e.tensor_tensor(Us.t[:, e_, :], Us.t[:, e_, :], bank.t[:, 128:256], ALU.subtract), [bank], [stv[e_]])
                        yield
                        P.op("pe", lambda e: e.matmul(hi, A2T, X1b[si].t[:], start=True, stop=True), [AoT[si], X1b[si]], [bank])
                        yield
                        P.op("act", lambda e: e.copy(Tmb[si].t[:], hi), [bank], [Tmb[si]])
                        yield
                        P.op("pe", lambda e: e.matmul(lo, PTf, Tmb[si].t[:], start=True, stop=True), [fin, Tmb[si]], [bank])
                        yield
                        P.op("act", lambda e: e.copy(R1b[si].t[:], lo), [bank], [R1b[si]])
                        P.op("dve", lambda e: e.tensor_tensor(Us.t[:, e_, :], Us.t[:, e_, :], bank.t[:, 128:256], ALU.subtract), [bank], [stv[e_]])
                        yield
                        P.op("pe", lambda e: e.matmul(hi, A1T, R1b[si].t[:], start=True, stop=True), [AoT[si], R1b[si]], [bank])
                        P.op("pool", lambda e: e.tensor_tensor(Wtm[si].t[:], X1b[si].t[:, 0:128], R1b[si].t[:, 0:128], ALU.subtract), [X1b[si], R1b[si]], [Wtm[si]])
                        yield
                        P.op("act", lambda e: e.copy(Tmb[si].t[:], hi), [bank], [Tmb[si]])
                        yield
                        P.op("pe", lambda e: e.matmul(lo, PTf, Tmb[si].t[:], start=True, stop=True), [fin, Tmb[si]], [bank])
                        yield
                        P.op("dve", lambda e: e.tensor_tensor(Wb[si].t[:], Wtm[si].t[:], bank.t[:, 0:128], ALU.add), [Wtm[si], bank], [Wb[si]])
                        P.op("dve", lambda e: e.tensor_tensor(Us.t[:, e_, :], Us.t[:, e_, :], bank.t[:, 128:256], ALU.add), [bank], [stv[e_]])
                        yield
                        while lock["pbt"] is not None:
                            yield
                        lock["pbt"] = si
                        P.op("pe", lambda e: e.transpose(pbt.t[:, 0:128], Wb[si].t[:], identb.t[:]), [Wb[si], identb], [pbt])
                        yield
                        P.op("act", lambda e: e.copy(WTs.t[:, e_, :], pbt.t[:, 0:128]), [pbt], [stv[e_]])
                        lock["pbt"] = None
                        yield

                    def gen_rec(h=h):
                        bA, bB = (pb[6], pb[6]) if NS_CFG[0] > 4 else (pb[5], pb[6])
                        for step in range(NT):
                            info = []
                            for d_ in range(2):
                                c = step if d_ == 0 else NT - 1 - step
                                info.append((d_, c, slice(c * 128, (c + 1) * 128), (c * 2 + d_) * 8 + h, c * 2 + d_))
                            for d_, c, cs, gi, e_ in info:
                                if step > 0:
                                    P.op("pe", lambda e: e.matmul(bA.t[:, d_ * 256:d_ * 256 + 128], WTs.t[:, e_, :], Sbb[d_].t[:], start=True, stop=True), [stv[e_], Sbb[d_]], [bA], inc=False)
                                    P.op("pe", lambda e: e.matmul(bA.t[:, d_ * 256 + 128:d_ * 256 + 256], qnT.t[:, cs], Sbb[d_].t[:], start=True, stop=True), [qnT, Sbb[d_]], [bA])
                                if step < NT - 1:
                                    P.op("act", lambda e: e.activation(kdt[d_].t[:], ktm.t[:, c, :], AF.Copy, scale=kdw.t[:, gi:gi + 1]), [ktm, kdw], [kdt[d_]])
                            yield
                            for d_, c, cs, gi, e_ in info:
                                if step > 0:
                                    P.op("dve", lambda e: e.tensor_tensor(vnb[d_].t[:], Us.t[:, e_, :], bA.t[:, d_ * 256:d_ * 256 + 128], ALU.subtract), [stv[e_], bA], [vnb[d_]])
                                    P.op("act", lambda e: e.activation(tmo[d_].t[:], bA.t[:, d_ * 256 + 128:d_ * 256 + 256], AF.Copy, scale=expG.t[:, gi:gi + 1]), [bA, expG], [tmo[d_]])
                                else:
                                    P.op("dve", lambda e: e.tensor_copy(vnb[d_].t[:], Us.t[:, e_, :]), [stv[e_]], [vnb[d_]])
                            yield
                            for d_, c, cs, gi, e_ in info:
                                P.op("pe", lambda e: e.matmul(bB.t[:, d_ * 256:d_ * 256 + 128], ATs.t[:, e_, :], vnb[d_].t[:], start=True, stop=True), [stv[e_], vnb[d_]], [bB], inc=(step == NT - 1))
                                if step < NT - 1:
                                    P.op("pe", lambda e: e.matmul(bB.t[:, d_ * 256 + 128:d_ * 256 + 256], kdt[d_].t[:], vnb[d_].t[:], start=True, stop=True), [kdt[d_], vnb[d_]], [bB])
                            yield
                            for d_, c, cs, gi, e_ in info:
                                if step < NT - 1:
                                    if step == 0:
                                        P.op("dve", lambda e: e.tensor_copy(Sst[d_].t[:], bB.t[:, d_ * 256 + 128:d_ * 256 + 256]), [bB], [Sst[d_]])
                                    else:
                                        P.op("dve", lambda e: e.scalar_tensor_tensor(Sst[d_].t[:], Sst[d_].t[:], glb.t[:, gi:gi + 1], bB.t[:, d_ * 256 + 128:d_ * 256 + 256], ALU.mult, ALU.add), [bB, glb], [Sst[d_]])
                                    P.op("act", lambda e: e.copy(Sbb[d_].t[:], Sst[d_].t[:]), [Sst[d_]], [Sbb[d_]])
                            for d_, c, cs, gi, e_ in info:
                                first = (d_ == 0 and c < 8) or (d_ == 1 and c >= 8)
                                if step > 0:
                                    P.op("dve", lambda e: e.tensor_tensor(tmo[d_].t[:], tmo[d_].t[:], bB.t[:, d_ * 256:d_ * 256 + 128], ALU.add), [bB], [tmo[d_]])
                                    src_ap, src_b = tmo[d_].t[:], tmo[d_]
                                    if first:
                                        P.op("act", lambda e: e.copy(osb.t[:, c, :], src_ap), [src_b], [osv[c]])
                                    else:
                                        P.op("dve", lambda e: e.tensor_tensor(osb.t[:, c, :], osb.t[:, c, :], src_ap, ALU.add), [src_b], [osv[c]])
                                else:
                                    if first:
                                        P.op("dve", lambda e: e.tensor_copy(osb.t[:, c, :], bB.t[:, d_ * 256:d_ * 256 + 128]), [bB], [osv[c]])
                                    else:
                                        P.op("dve", lambda e: e.tensor_tensor(osb.t[:, c, :], osb.t[:, c, :], bB.t[:, d_ * 256:d_ * 256 + 128], ALU.add), [bB], [osv[c]])
                            yield

                    lock = {"pb0": None, "pbt": None}
                    order = []
                    for i in range(NT):
                        order.append((i, 0))
                        order.append((NT - 1 - i, 1))
                    active = [None] * NS
                    nstarted = 0
                    nfinished = 0
                    finished = [False] * 32
                    rec = gen_rec()
                    rec_step = 0
                    rec_hop = 0
                    rec_done = False
                    tick = 0
                    while nfinished < 32 or not rec_done:
                        if nstarted < 32 and tick % STAG == 0:
                            for si in range(NS):
                                if active[si] is None:
                                    c, d_ = order[nstarted]
                                    active[si] = (gen_prep(si, c, d_), nstarted)
                                    nstarted += 1
                                    break
                        for si in range(NS):
                            if active[si] is not None:
                                g_, idx = active[si]
                                try:
                                    next(g_)
                                except StopIteration:
                                    finished[idx] = True
                                    nfinished += 1
                                    active[si] = None
                        if not rec_done and (stop != "dn0b"):
                            if rec_hop > 0 or (finished[2 * rec_step] and finished[2 * rec_step + 1]):
                                try:
                                    next(rec)
                                    rec_hop += 1
                                    if rec_hop == 4:
                                        rec_hop = 0
                                        rec_step += 1
                                        if rec_step == NT:
                                            rec_done = True
                                except StopIteration:
                                    rec_done = True
                        elif stop == "dn0b":
                            rec_done = True
                        tick += 1
                    if stop == "dn0c":
                        break
                    if h == 0:
                        dump("osb", osv[0], osb.t[:], [128, NT, 128])
                    for tt in range(NT):
                        P.op("act", lambda e: e.activation(ty.t[:], osb.t[:, tt, :], AF.Square, accum_out=ssq.t[:, tt:tt + 1]), [osv[tt]], [ty, ssq])
                    rsqrt_(ssq.t[:, NT:2 * NT], ssq.t[:, 0:NT], 1.0 / 128.0, [ssq], [ssq])
                    for tt in range(NT):
                        bank = pb[4 + tt % 2]
                        zg, ty, yb = zg_l[tt % 2], ty_l[tt % 2], yb_l[tt % 2]
                        proj_tm(wB, 256 + hh * 128, 128, tt, bank)
                        P.op("act", lambda e: e.activation(zg.t[:], bank.t[:, 0:128], AF.Silu), [bank], [zg])
                        P.op("dve", lambda e: e.scalar_tensor_tensor(ty.t[:], osb.t[:, tt, :], ssq.t[:, NT + tt:NT + tt + 1], rowp.t[:, 1024:1152], ALU.mult, ALU.mult), [osv[tt], ssq, rowp], [ty])
                        P.op("dve", lambda e: e.tensor_tensor(yb.t[:], ty.t[:], zg.t[:], ALU.mult), [ty, zg], [yb])
                        P.op("pe", lambda e: e.transpose(pbt.t[:, 0:128], yb.t[:], identb.t[:]), [yb, identb], [pbt])
                        P.op("act", lambda e: e.copy(ydT.t[:, tt * 128:(tt + 1) * 128], pbt.t[:, 0:128]), [pbt], [ydT])
                    P.dma(ysT.t[1, h * 128:(h + 1) * 128, :], ydT.t[:], [ydT], [ysv[1][h]])
                    if stop == "dn0":
                        break
            if stop in ("dn0", "dn", "dn0a", "dn0b", "dn0c"):
                stopped = True
                break

            wlim[0] = NWB
            with scope() as sc:
                cx = sc.sb("cx", [128, S + 2], F32)
                Bsb = sc.sb("Bsb", [128, S], F32)
                ycv = sc.sb("ycv", [128, S], F32)
                tmx_l = [sc.sb("tmx%d" % i, [128, 512], F32) for i in range(2)]
                ycT = sc.sb("ycT", [128, S], BF16)
                P.op("dve", lambda e: e.memset(cx.t[:, 0:1], 0.0), [], [cx])
                P.op("dve", lambda e: e.memset(cx.t[:, S + 1:S + 2], 0.0), [], [cx])
                wA = wB = None
                for dc in range(8):
                    dd = dc % 2
                    if dd == 0:
                        wA, wB = nextw(), nextw()
                        wload(wA, 0, w_in, O_SB + dc * 128, 256)
                        wload(wA, 256, w_in, O_SC + dc * 128, 256)
                        wload(wB, 0, w_in, O_SX + dc * 128, 256)
                    for tb in range(4):
                        bs = slice(tb * 512, (tb + 1) * 512)
                        tmx = tmx_l[tb % 2]
                        for j, (wsrc, off) in enumerate(((wA, dd * 128), (wA, 256 + dd * 128), (wB, dd * 128))):
                            bank = pb[j + 3 * (tb % 2)]
                            for kc in range(8):
                                P.op("pe", lambda e: e.matmul(bank.t[:, 0:512], wsrc.t[:, kc, off:off + 128], hT.t[:, kc, bs], start=(kc == 0), stop=(kc == 7)),
                                     [wsrc] + hTv[tb * 4:tb * 4 + 4], [bank], inc=(kc == 7))
                        o3 = 3 * (tb % 2)
                        P.op("act", lambda e: e.copy(Bsb.t[:, bs], pb[o3].t[:, 0:512]), [pb[o3]], [Bsb])
                        P.op("act", lambda e: e.copy(tmx.t[:], pb[o3 + 2].t[:, 0:512]), [pb[o3 + 2]], [tmx])
                        P.op("dve", lambda e: e.tensor_tensor(cx.t[:, 1 + tb * 512:1 + (tb + 1) * 512], pb[o3 + 1].t[:, 0:512], tmx.t[:], ALU.mult), [pb[o3 + 1], tmx], [cx])
                    cw = lambda k: colp.t[:, 88 + k * 8 + dc:88 + k * 8 + dc + 1]
                    P.op("dve", lambda e: e.tensor_scalar(ycv.t[:], cx.t[:, 0:S], cw(0), None, ALU.mult), [cx, colp], [ycv])
                    P.op("dve", lambda e: e.scalar_tensor_tensor(ycv.t[:], cx.t[:, 1:S + 1], cw(1), ycv.t[:], ALU.mult, ALU.add), [cx, colp], [ycv])
                    P.op("dve", lambda e: e.scalar_tensor_tensor(ycv.t[:], cx.t[:, 2:S + 2], cw(2), ycv.t[:], ALU.mult, ALU.add), [cx, colp], [ycv])
                    P.op("dve", lambda e: e.tensor_tensor(ycT.t[:], ycv.t[:], Bsb.t[:], ALU.mult), [ycv, Bsb], [ycT])
                    P.dma(ysT.t[2, dc * 128:(dc + 1) * 128, :], ycT.t[:], [ycT], [ysv[2][dc]])
            if stop == "sc":
                stopped = True
                break

            if l + 1 < nlayers:
                P.dma(colps[(l + 1) % 2].t[:], colp_d[l + 1], [], [colps[(l + 1) % 2]])
            last = (l == DEPTH - 1)
            for half in range(2):
                with scope() as sch:
                    xres = sch.sb("xres", [128, 8, D], F32)
                    xrv = [Buf(xres.t, "xr%d" % i) for i in range(8)]
                    with scope() as sc:
                        ys_sb = [sc.sb("ys_sb%d" % n, [128, 8, 1024], BF16) for n in range(3)]
                        sg_l = [[sc.sb("sg%d_%d" % (n, i), [128, 512], F32) for n in range(3)] for i in range(2)]
                        acc_l = [sc.sb("acc%d" % i, [128, 512], F32) for i in range(2)]
                        tmm_l = [sc.sb("tmm", [128, 512], F32)] * 2
                        mixT = sc.sb("mixT", [128, 8, 1024], BF16)
                        for n in range(3):
                            P.dma(ys_sb[n].t[:], ysT.t[n].rearrange("(kc p) t -> p kc t", p=128)[:, :, half * 1024:(half + 1) * 1024], ysv[n], [ys_sb[n]])
                        wA = wB = wC = None
                        for dc in range(8):
                            dd = dc % 2
                            if dd == 0:
                                wA, wB, wC = nextw(), nextw(), nextw()
                                wload(wA, 0, w_br_d[l, 0], dc * 128, 256)
                                wload(wA, 256, w_br_d[l, 1], dc * 128, 256)
                                wload(wB, 0, w_br_d[l, 2], dc * 128, 256)
                                wload(wB, 256, w_in, O_MRG + dc * 128, 256)
                                wload(wC, 0, w_in, O_MRG + 1024 + dc * 128, 256)
                                wload(wC, 256, w_in, O_MRG + 2048 + dc * 128, 256)
                            wbr = ((wA, dd * 128), (wA, 256 + dd * 128), (wB, dd * 128))
                            wgt_ = ((wB, 256 + dd * 128), (wC, dd * 128), (wC, 256 + dd * 128))
                            for tbh in range(2):
                                tb = half * 2 + tbh
                                bs = slice(tb * 512, (tb + 1) * 512)
                                bsh = slice(tbh * 512, (tbh + 1) * 512)
                                sg, acc, tmm = sg_l[tbh], acc_l[tbh], tmm_l[tbh]
                                for n in range(3):
                                    wsrc, off = wgt_[n]
                                    for kc in range(8):
                                        P.op("pe", lambda e: e.matmul(pb[3 + n].t[:, 0:512], wsrc.t[:, kc, off:off + 128], hT.t[:, kc, bs], start=(kc == 0), stop=(kc == 7)),
                                             [wsrc] + hTv[tb * 4:tb * 4 + 4], [pb[3 + n]], inc=(kc == 7))
                                    P.op("act", lambda e: e.activation(sg[n].t[:], pb[3 + n].t[:, 0:512], AF.Sigmoid), [pb[3 + n]], [sg[n]])
                                for n in range(3):
                                    wsrc, off = wbr[n]
                                    for kc in range(8):
                                        P.op("pe", lambda e: e.matmul(pb[n].t[:, 0:512], wsrc.t[:, kc, off:off + 128], ys_sb[n].t[:, kc, bsh], start=(kc == 0), stop=(kc == 7)),
                                             [wsrc, ys_sb[n]], [pb[n]], inc=(kc == 7))
                                P.op("dve", lambda e: e.tensor_tensor(acc.t[:], sg[0].t[:], pb[0].t[:, 0:512], ALU.mult), [sg[0], pb[0]], [acc])
                                P.op("dve", lambda e: e.tensor_tensor(tmm.t[:], sg[1].t[:], pb[1].t[:, 0:512], ALU.mult), [sg[1], pb[1]], [tmm])
                                P.op("dve", lambda e: e.tensor_tensor(acc.t[:], acc.t[:], tmm.t[:], ALU.add), [tmm], [acc])
                                P.op("dve", lambda e: e.tensor_tensor(tmm.t[:], sg[2].t[:], pb[2].t[:, 0:512], ALU.mult), [sg[2], pb[2]], [tmm])
                                P.op("dve", lambda e: e.tensor_tensor(mixT.t[:, dc, bsh], acc.t[:], tmm.t[:], ALU.add), [acc, tmm], [mixT])
                        wo0, wo1 = nextw(), nextw()
                        wload(wo0, 0, w_out_d[l], 0, 512)
                        wload(wo1, 0, w_out_d[l], 512, 512)
                        for t8 in range(8):
                            tt = half * 8 + t8
                            P.dma(xres.t[:, t8, :], xcur.t[tt * 128:(tt + 1) * 128, :], [xcv[tt]], [xrv[t8]])
                            for nb, wsrc in enumerate((wo0, wo1)):
                                bank = pb[(t8 * 2 + nb) % 4]
                                for dc in range(8):
                                    P.op("pe", lambda e: e.matmul(bank.t[:, 0:512], mixT.t[:, dc, t8 * 128:(t8 + 1) * 128], wsrc.t[:, dc, :], start=(dc == 0), stop=(dc == 7)),
                                         [mixT, wsrc], [bank], inc=(dc == 7))
                                P.op("dve", lambda e: e.tensor_tensor(xres.t[:, t8, nb * 512:(nb + 1) * 512], xres.t[:, t8, nb * 512:(nb + 1) * 512], bank.t[:, 0:512], ALU.add), [bank], [xrv[t8]])
                            norm_tile(xres.t[:, t8, :], xrv[t8], tt, (colp, colp.t[:, 8:16]))
                            if stop == "mix" and "d_xm" in dbg_d:
                                P.dma(dbg_d["d_xm"][tt * 128:(tt + 1) * 128, :], xres.t[:, t8, :], [xrv[t8]], [])
                    if stop == "mix":
                        continue
                    with scope() as sc:
                        upT = sc.sb("upT", [128, 8, 1024], BF16)
                        relu_t = [sc.sb("relu_t%d" % i, [128, 512], F32) for i in range(2)]
                        otile = [sc.sb("otile%d" % i, [128, D], F32) for i in range(2)] if last else None
                        gfin = sc.sb("gfin_sb", [128, D], F32) if last else None
                        if last:
                            P.dma(gfin.t[:], gfin_d.partition_broadcast(128), [], [gfin])
                        for fb in range(4):
                            wu = [nextw(), nextw()]
                            wd = [nextw(), nextw()]
                            wload(wu[0], 0, w_up_d[l], fb * 1024, 512)
                            wload(wu[1], 0, w_up_d[l], fb * 1024 + 512, 512)
                            wload(wd[0], 0, w_dn_d[l, fb * 1024:(fb + 1) * 1024, :], 0, 512)
                            wload(wd[1], 0, w_dn_d[l, fb * 1024:(fb + 1) * 1024, :], 512, 512)
                            for fc in range(8):
                                def ev(tb, bank):
                                    bsh = slice((tb - half * 2) * 512, (tb - half * 2 + 1) * 512)
                                    rl = relu_t[tb % 2]
                                    P.op("act", lambda e: e.activation(rl.t[:], bank.t[:, 0:512], AF.Relu), [bank], [rl])
                                    P.op("dve", lambda e: e.tensor_tensor(upT.t[:, fc, bsh], rl.t[:], rl.t[:], ALU.mult), [rl], [upT])
                                proj_fm(wu[fc // 4], (fc % 4) * 128, pb[0:4], ev, tbs=(half * 2, half * 2 + 1))
                            for t8 in range(8):
                                tt = half * 8 + t8
                                for nb in range(2):
                                    bank = pb[4 + (t8 * 2 + nb) % 3]
                                    for fc in range(8):
                                        P.op("pe", lambda e: e.matmul(bank.t[:, 0:512], upT.t[:, fc, t8 * 128:(t8 + 1) * 128], wd[nb].t[:, fc, :], start=(fc == 0), stop=(fc == 7)),
                                             [upT, wd[nb]], [bank], inc=(fc == 7))
                                    P.op("dve", lambda e: e.tensor_tensor(xres.t[:, t8, nb * 512:(nb + 1) * 512], xres.t[:, t8, nb * 512:(nb + 1) * 512], bank.t[:, 0:512], ALU.add), [bank], [xrv[t8]])
                                if fb == 3:
                                    if stop == "mlp" and "d_xm" in dbg_d:
                                        P.dma(dbg_d["d_xm"][tt * 128:(tt + 1) * 128, :], xres.t[:, t8, :], [xrv[t8]], [])
                                    if not last:
                                        P.dma(xcur.t[tt * 128:(tt + 1) * 128, :], xres.t[:, t8, :], [xrv[t8]], [xcv[tt]])
                                        if l + 1 < nlayers:
                                            cn = colps[(l + 1) % 2]
                                            norm_tile(xres.t[:, t8, :], xrv[t8], tt, (cn, cn.t[:, 0:8]))
                                    else:
                                        ot = otile[t8 % 2]
                                        nsq, nss = nsq_l[t8 % 2], nss_l[t8 % 2]
                                        P.op("act", lambda e: e.activation(nsq.t[:], xres.t[:, t8, :], AF.Square, accum_out=nss.t[:, 0:1]), [xrv[t8]], [nsq, nss])
                                        rsqrt_(nss.t[:, 1:2], nss.t[:, 0:1], 1.0 / D, [nss], [nss])
                                        P.op("dve", lambda e: e.scalar_tensor_tensor(ot.t[:], xres.t[:, t8, :], nss.t[:, 1:2], gfin.t[:], ALU.mult, ALU.mult), [xrv[t8], nss, gfin], [ot])
                                        P.dma(out_d[tt * 128:(tt + 1) * 128, :], ot.t[:], [ot], [])
            if stop in ("mix", "mlp"):
                stopped = True
                break
        if stopped:
            dump_ys()
        P.finish()
        print("build: ops", P.nops, "waits", P.nwait, {k: P.cnt[k] for k in P.cnt})
    return nc


def make_params(inp):
    colp = np.zeros((DEPTH, 128, NCOL), np.float32)
    rowp = np.zeros((DEPTH, NROW), np.float32)
    for l in range(DEPTH):
        colp[l, :, 0:8] = inp["norm_mix_g"][l].reshape(8, 128).T
        colp[l, :, 8:16] = inp["norm_mlp_g"][l].reshape(8, 128).T
        colp[l, :, 16:88] = inp["dn_conv_w"][l].reshape(3, 24, 128).transpose(2, 0, 1).reshape(128, 72)
        colp[l, :, 88:112] = inp["sc_conv_w"][l].reshape(3, 8, 128).transpose(2, 0, 1).reshape(128, 24)
        rowp[l, 0:1024] = inp["m_norm_g"][l]
        rowp[l, 1024:1152] = inp["dn_norm_g"][l]
        rowp[l, 1152:1168] = inp["m_gate_b"][l].reshape(16)
        rowp[l, 1168:1184] = inp["dn_a_log"][l].reshape(16)
        rowp[l, 1184:1200] = inp["dn_dt_bias"][l].reshape(16)
    return colp, rowp


def make_in_maps(inp, cores):
    colp, rowp = make_params(inp)
    shared = {"w_in": np.ascontiguousarray(inp["w_in"]), "w_branch": np.ascontiguousarray(inp["w_branch"]),
              "w_out": np.ascontiguousarray(inp["w_out"]), "w_up": np.ascontiguousarray(inp["w_up"]),
              "w_down": np.ascontiguousarray(inp["w_down"]), "colp": colp, "rowp": rowp,
              "gfin": np.ascontiguousarray(inp["norm_final_g"])}
    return [dict(shared, x=np.ascontiguousarray(inp["x"][b])) for b in cores]


def kernel(**inputs):
    inp = {k: np.asarray(v, dtype=np.float32) for k, v in inputs.items()}
    nc = build()
    in_maps = make_in_maps(inp, list(range(8)))
    res = run_bass_kernel_spmd(nc, in_maps, core_ids=list(range(8)))
    return np.stack([r["out"] for r in res.results], axis=0).astype(np.float32)
```

```python
import numpy as np
import concourse.bass as bass
import concourse.mybir as mybir
from concourse.bass_utils import run_bass_kernel_spmd
from contextlib import ExitStack

F32 = mybir.dt.float32
BF16 = mybir.dt.bfloat16
ALU = mybir.AluOpType
AF = mybir.ActivationFunctionType
AX = mybir.AxisListType

S = 2048
D = 1024
NT = 16
DEPTH = 4
NPROJ = 14384
DFF = 4096
EPS = 1e-6
NCOL = 112
NROW = 1200
O_MQ, O_MK, O_MV, O_MO, O_MG = 0, 1024, 2048, 3072, 4096
O_DQ, O_DK, O_DV, O_DZ, O_DG = 4112, 5136, 6160, 7184, 8208
O_SB, O_SC, O_SX, O_MRG = 8240, 9264, 10288, 11312
NEG = -30000.0


class Tok:
    __slots__ = ("sem", "val", "clk")

    def __init__(self, sem, val, clk):
        self.sem, self.val, self.clk = sem, val, clk


class Buf:
    __slots__ = ("t", "w", "r", "name", "excl")

    def __init__(self, t, name, excl=False):
        self.t, self.name = t, name
        self.w = None
        self.r = []
        self.excl = excl


class Prog:
    NDMA = 12

    def __init__(self, nc, es, same_engine_sync=True):
        self.nc, self.es = nc, es
        self.same = same_engine_sync
        self.eng = {"pe": nc.tensor, "act": nc.scalar, "dve": nc.vector, "pool": nc.gpsimd, "sp": nc.sync}
        self.sem = {k: es.enter_context(nc.semaphore("s_" + k)) for k in self.eng}
        self.cnt = {k: 0 for k in self.eng}
        self.clk = {k: {} for k in self.eng}
        self.pend = {k: [] for k in self.eng}
        self.dsem = [es.enter_context(nc.semaphore("d%d" % i)) for i in range(2 * self.NDMA)]
        self.dcnt = [0] * (2 * self.NDMA)
        self.dnext = {"sp": 0, "pool": 0}
        self.nwait = 0
        self.nops = 0

    def sb(self, name, shape, dt):
        t = self.es.enter_context(self.nc.sbuf_tensor(name, list(shape), dt))
        return Buf(t, name)

    def ps(self, name, shape, dt):
        t = self.es.enter_context(self.nc.psum_tensor(name, list(shape), dt))
        return Buf(t, name, excl=True)

    def dram(self, name, shape, dt):
        t = self.nc.dram_tensor(name, list(shape), dt, kind="Internal").ap()
        return Buf(t, name)

    def _need(self, reads, writes):
        toks = []
        for b in reads:
            if b.w is not None:
                toks.append(b.w)
        for b in writes:
            if b.w is not None:
                toks.append(b.w)
            toks.extend(b.r)
        return toks

    def _wait(self, e, toks):
        clk = self.clk[e]
        eng = self.eng[e]
        own = self.sem[e]
        best = {}
        for t in toks:
            if t.sem is own and (e == "pe" or not self.same):
                continue
            k = id(t.sem)
            if clk.get(k, 0) >= t.val:
                continue
            if k not in best or best[k].val < t.val:
                best[k] = t
        for k, t in best.items():
            if clk.get(k, 0) >= t.val:
                continue
            eng.wait_ge(t.sem, t.val)
            self.nwait += 1
            clk[k] = t.val
            for kk, vv in t.clk.items():
                if clk.get(kk, 0) < vv:
                    clk[kk] = vv

    def op(self, e, fn, reads, writes, inc=True):
        if any(b.excl for b in reads):
            writes = list(writes) + [b for b in reads if b.excl]
            reads = [b for b in reads if not b.excl]
        self._wait(e, self._need(reads, writes))
        ins = fn(self.eng[e])
        self.nops += 1
        if not inc:
            self.pend[e].append((reads, writes))
            return ins
        self.cnt[e] += 1
        ins.then_inc(self.sem[e], 1)
        tok = Tok(self.sem[e], self.cnt[e], dict(self.clk[e]))
        tok.clk[id(self.sem[e])] = self.cnt[e]
        for (rs, ws) in self.pend[e] + [(reads, writes)]:
            for b in rs:
                b.r.append(tok)
            for b in ws:
                b.w = tok
                b.r = []
        self.pend[e] = []
        return ins

    def dma(self, out, in_, reads, writes, q="sp"):
        toks = self._need(reads, writes)
        i = self.dnext[q] + (self.NDMA if q == "pool" else 0)
        self.dnext[q] = (self.dnext[q] + 1) % self.NDMA
        s = self.dsem[i]
        if self.dcnt[i] > 0:
            toks.append(Tok(s, self.dcnt[i], {}))
        self._wait(q, toks)
        ins = self.eng[q].dma_start(out=out, in_=in_)
        self.dcnt[i] += 16
        ins.then_inc(s, 16)
        tok = Tok(s, self.dcnt[i], dict(self.clk[q]))
        for b in reads:
            b.r.append(tok)
        for b in writes:
            b.w = tok
            b.r = []
        self.nops += 1
        return ins

    def barrier(self):
        toks = []
        for i in range(2 * self.NDMA):
            if self.dcnt[i] > 0:
                toks.append(Tok(self.dsem[i], self.dcnt[i], {}))
        for k in self.eng:
            if self.cnt[k] > 0:
                toks.append(Tok(self.sem[k], self.cnt[k], {}))
        for e in self.eng:
            self._wait(e, [t for t in toks if t.sem is not self.sem[e]])

    def finish(self):
        toks = []
        for i in range(2 * self.NDMA):
            if self.dcnt[i] > 0:
                toks.append(Tok(self.dsem[i], self.dcnt[i], {}))
        for k in self.eng:
            if self.cnt[k] > 0 and k != "sp":
                toks.append(Tok(self.sem[k], self.cnt[k], {}))
        self._wait("sp", toks)


STAG_CFG = [4]
NS_CFG = [4]


def build(nlayers=DEPTH, dbg=(), stop=None):
    nc = bass.Bass("TRN2", target_bir_lowering=False)

    def din(name, shape):
        return nc.dram_tensor(name, list(shape), F32, kind="ExternalInput").ap()

    x_d = din("x", [S, D])
    w_in_d = din("w_in", [DEPTH, D, NPROJ])
    w_br_d = din("w_branch", [DEPTH, 3, D, D])
    w_out_d = din("w_out", [DEPTH, D, D])
    w_up_d = din("w_up", [DEPTH, D, DFF])
    w_dn_d = din("w_down", [DEPTH, DFF, D])
    colp_d = din("colp", [DEPTH, 128, NCOL])
    rowp_d = din("rowp", [DEPTH, NROW])
    gfin_d = din("gfin", [D])
    out_d = nc.dram_tensor("out", [S, D], F32, kind="ExternalOutput").ap()
    dbg_d = {}
    for name, shape, dt in (("d_ys", [3, D, S], BF16), ("d_xm", [S, D], F32)):
        if name in dbg:
            dbg_d[name] = nc.dram_tensor(name, shape, dt, kind="ExternalOutput").ap()

    with ExitStack() as es:
        P = Prog(nc, es)
        hT = P.sb("hT", [128, 8, S], BF16)
        hTv = [Buf(hT.t, "hT%d" % i) for i in range(NT)]
        NWB = 4
        wb = [P.sb("wb%d" % i, [128, 8, 512], BF16) for i in range(NWB)]
        wbi = [0]

        wlim = [NWB]

        def nextw():
            b = wb[wbi[0] % wlim[0]]
            wbi[0] += 1
            return b

        identb = P.sb("identb", [128, 128], BF16)
        identf = P.sb("identf", [128, 128], F32)
        onesb = P.sb("onesb", [128, 128], BF16)
        onesf = P.sb("onesf", [128, 128], F32)
        tincl = P.sb("tincl", [128, 128], F32)
        tinclT = P.sb("tinclT", [128, 128], F32)
        mlo = P.sb("mlo", [128, 128], F32)
        mup = P.sb("mup", [128, 128], F32)
        slf = P.sb("slf", [128, 128], F32)
        suf = P.sb("suf", [128, 128], F32)
        colps = [P.sb("colp_sb%d" % i, [128, NCOL], F32) for i in range(2)]
        rowp = P.sb("rowp_sb", [128, NROW], F32)
        wsm = P.sb("wsm", [128, 8, 48], BF16)
        wsf = P.sb("wsf", [128, 8, 48], F32)
        nsq_l = [P.sb("nsq%d" % i, [128, D], BF16) for i in range(2)]
        nss_l = [P.sb("nss%d" % i, [128, 2], F32) for i in range(2)]
        nxn_l = [P.sb("nxn%d" % i, [128, D], BF16) for i in range(2)]
        nrm_i = [0]
        pb = [P.ps("pb%d" % i, [128, 512], F32) for i in range(7)]
        pbt = P.ps("pbt", [128, 1024], BF16)
        ysT = P.dram("ysT", [3, D, S], BF16)
        ysv = [[Buf(ysT.t, "ys%d_%d" % (n, c)) for c in range(8)] for n in range(3)]
        xcur = P.dram("xcur", [S, D], F32)
        xcv = [Buf(xcur.t, "xc%d" % i) for i in range(NT)]

        def pool(fn, r, w):
            P.op("pool", fn, r, w)

        pool(lambda e: e.memset(onesf.t[:], 1.0), [], [onesf])
        pool(lambda e: e.memset(onesb.t[:], 1.0), [], [onesb])
        pool(lambda e: e.memset(identf.t[:], 0.0), [], [identf])
        pool(lambda e: e.affine_select(identf.t[:], identf.t[:], pattern=[[-1, 128]], compare_op=ALU.not_equal, fill=1.0, base=0, channel_multiplier=1), [identf], [identf])
        pool(lambda e: e.tensor_copy(identb.t[:], identf.t[:]), [identf], [identb])
        pool(lambda e: e.affine_select(tincl.t[:], onesf.t[:], pattern=[[1, 128]], compare_op=ALU.is_ge, fill=0.0, base=0, channel_multiplier=-1), [onesf], [tincl])
        pool(lambda e: e.affine_select(tinclT.t[:], onesf.t[:], pattern=[[-1, 128]], compare_op=ALU.is_ge, fill=0.0, base=0, channel_multiplier=1), [onesf], [tinclT])
        pool(lambda e: e.memset(mlo.t[:], 0.0), [], [mlo])
        pool(lambda e: e.affine_select(mlo.t[:], mlo.t[:], pattern=[[1, 128]], compare_op=ALU.is_ge, fill=NEG, base=0, channel_multiplier=-1), [mlo], [mlo])
        pool(lambda e: e.memset(mup.t[:], 0.0), [], [mup])
        pool(lambda e: e.affine_select(mup.t[:], mup.t[:], pattern=[[-1, 128]], compare_op=ALU.is_ge, fill=NEG, base=0, channel_multiplier=1), [mup], [mup])
        pool(lambda e: e.affine_select(slf.t[:], onesf.t[:], pattern=[[-1, 128]], compare_op=ALU.is_gt, fill=0.0, base=0, channel_multiplier=1), [onesf], [slf])
        pool(lambda e: e.affine_select(suf.t[:], onesf.t[:], pattern=[[1, 128]], compare_op=ALU.is_gt, fill=0.0, base=0, channel_multiplier=-1), [onesf], [suf])

        def dump(name, buf, ap, shape, dt=F32):
            if ("D_" + name) in dbg:
                dd = nc.dram_tensor("D_" + name, list(shape), dt, kind="ExternalOutput").ap()
                P.dma(dd, ap, [buf], [])

        uid = [0]

        def scope():
            class _S:
                def __enter__(s_):
                    s_.es = ExitStack()
                    s_.es.__enter__()
                    return s_

                def sb(s_, name, shape, dt):
                    uid[0] += 1
                    return Buf(s_.es.enter_context(nc.sbuf_tensor("%s_u%d" % (name, uid[0]), list(shape), dt)), name)

                def __exit__(s_, *a):
                    P.barrier()
                    return s_.es.__exit__(*a)
            return _S()

        evi = [0]

        def evac_copy(out_ap, in_ap, reads, writes, scale=None):
            evi[0] += 1
            if evi[0] % 2 == 0:
                if scale is None:
                    P.op("act", lambda e: e.copy(out_ap, in_ap), reads, writes)
                else:
                    P.op("act", lambda e: e.mul(out_ap, in_ap, scale), reads, writes)
            else:
                if scale is None:
                    P.op("dve", lambda e: e.tensor_copy(out_ap, in_ap), reads, writes)
                else:
                    P.op("dve", lambda e: e.tensor_scalar_mul(out_ap, in_ap, scale), reads, writes)

        def wload(dst, col0, src2d, c0, n):
            P.dma(dst.t[:, :, col0:col0 + n], src2d.rearrange("(kc p) n -> p kc n", p=128)[:, :, c0:c0 + n], [], [dst], q="pool")

        def rsqrt_(dst_ap, src_ap, scale, reads, writes):
            P.op("act", lambda e: e.activation(dst_ap, src_ap, AF.Ln, bias=EPS, scale=scale), reads, writes)
            P.op("act", lambda e: e.activation(dst_ap, dst_ap, AF.Exp, scale=-0.5), writes, writes)

        def norm_tile(xt_ap, xbuf, tt, gcol):
            gbuf, gap = gcol
            nrm_i[0] += 1
            nsq, nss, nxn = nsq_l[nrm_i[0] % 2], nss_l[nrm_i[0] % 2], nxn_l[nrm_i[0] % 2]
            pbo = (nrm_i[0] % 2) * 0
            P.op("act", lambda e: e.activation(nsq.t[:], xt_ap, AF.Square, accum_out=nss.t[:, 0:1]), [xbuf], [nsq, nss])
            rsqrt_(nss.t[:, 1:2], nss.t[:, 0:1], 1.0 / D, [nss], [nss])
            P.op("dve", lambda e: e.tensor_scalar(nxn.t[:], xt_ap, nss.t[:, 1:2], None, ALU.mult), [xbuf, nss], [nxn])
            for c in range(8):
                P.op("pe", lambda e: e.transpose(pbt.t[:, c * 128:(c + 1) * 128], nxn.t[:, c * 128:(c + 1) * 128], identb.t[:]), [nxn, identb], [pbt], inc=(c == 7))
            P.op("dve", lambda e: e.tensor_tensor(hT.t[:, :, tt * 128:(tt + 1) * 128], pbt.t[:].rearrange("p (c t) -> p c t", c=8),
                                                  gap.unsqueeze(2).to_broadcast([128, 8, 128]), ALU.mult), [pbt, gbuf], [hTv[tt]])

        def softplus_(dst, src, t1, t2, n, sign_logsig=False):
            P.op("dve", lambda e: e.scalar_tensor_tensor(t1.t[:, 0:n], src.t[:, 0:n], -1.0, src.t[:, 0:n], ALU.mult, ALU.max), [src], [t1])
            P.op("act", lambda e: e.activation(t2.t[:, 0:n], t1.t[:, 0:n], AF.Exp, scale=-1.0), [t1], [t2])
            P.op("act", lambda e: e.activation(t2.t[:, 0:n], t2.t[:, 0:n], AF.Ln, bias=1.0), [t2], [t2])
            if sign_logsig:
                P.op("dve", lambda e: e.scalar_tensor_tensor(dst.t[:, 0:n], src.t[:, 0:n], 0.0, t2.t[:, 0:n], ALU.min, ALU.subtract), [src, t2], [dst])
            else:
                P.op("dve", lambda e: e.scalar_tensor_tensor(dst.t[:, 0:n], src.t[:, 0:n], 0.0, t2.t[:, 0:n], ALU.max, ALU.add), [src, t2], [dst])

        def decay(out, negc_ap, bias_ap, cbufs, mask, bank, diag):
            P.op("dve", lambda e: e.tensor_scalar(diag.t[:], identf.t[:], negc_ap, None, ALU.mult), [identf] + cbufs, [diag])
            P.op("pe", lambda e: e.matmul(bank.t[:, 0:128], onesf.t[:], diag.t[:], start=True, stop=False), [onesf, diag], [bank], inc=False)
            P.op("pe", lambda e: e.matmul(bank.t[:, 0:128], identf.t[:], mask.t[:], start=False, stop=True), [identf, mask], [bank])
            P.op("act", lambda e: e.activation(out.t[:], bank.t[:, 0:128], AF.Exp, bias=bias_ap, scale=1.0), [bank] + cbufs, [out])

        def proj_fm(w, wcol, banks, evac, tbs=range(4)):
            for tb in tbs:
                bank = banks[tb % len(banks)]
                for kc in range(8):
                    P.op("pe", lambda e: e.matmul(bank.t[:, 0:512], w.t[:, kc, wcol:wcol + 128], hT.t[:, kc, tb * 512:(tb + 1) * 512],
                                                  start=(kc == 0), stop=(kc == 7)), [w] + hTv[tb * 4:tb * 4 + 4], [bank], inc=(kc == 7))
                evac(tb, bank)

        def proj_tm(w, wcol, ncols, tt, bank):
            for kc in range(8):
                P.op("pe", lambda e: e.matmul(bank.t[:, 0:ncols], hT.t[:, kc, tt * 128:(tt + 1) * 128], w.t[:, kc, wcol:wcol + ncols],
                                              start=(kc == 0), stop=(kc == 7)), [w, hTv[tt]], [bank], inc=(kc == 7))

        def dump_ys():
            if "d_ys" in dbg_d:
                for n in range(3):
                    for c in range(8):
                        P.dma(dbg_d["d_ys"][n, c * 128:(c + 1) * 128, :], ysT.t[n, c * 128:(c + 1) * 128, :], [ysv[n][c]], [])

        stopped = False
        P.dma(colps[0].t[:], colp_d[0], [], [colps[0]])
        for l in range(nlayers):
            w_in = w_in_d[l]
            colp = colps[l % 2]
            P.dma(rowp.t[:], rowp_d[l].partition_broadcast(128), [], [rowp])
            wrr = w_in.rearrange("(kc p) n -> p kc n", p=128)
            P.dma(wsf.t[:, :, 0:16], wrr[:, :, O_MG:O_MG + 16], [], [wsf])
            P.dma(wsf.t[:, :, 16:48], wrr[:, :, O_DG:O_DG + 32], [], [wsf])
            P.op("dve", lambda e: e.tensor_copy(wsm.t[:], wsf.t[:]), [wsf], [wsm])
            if l == 0:
                with scope() as sc:
                    xin = [sc.sb("xin%d" % i, [128, D], F32) for i in range(2)]
                    for tt in range(NT):
                        xb_ = xin[tt % 2]
                        P.dma(xb_.t[:], x_d[tt * 128:(tt + 1) * 128, :], [], [xb_])
                        P.dma(xcur.t[tt * 128:(tt + 1) * 128, :], xb_.t[:], [xb_], [xcv[tt]])
                        norm_tile(xb_.t[:], xb_, tt, (colp, colp.t[:, 0:8]))

            with scope() as sc:
                sb1 = sc.sb
                gm = sb1("gm", [128, 256], F32)
                ipre = sb1("ipre", [128, 128], F32)
                fpre = sb1("fpre", [128, 128], F32)
                lf = sb1("lf", [128, 128], F32)
                t1 = sb1("t1", [128, 256], F32)
                t2 = sb1("t2", [128, 256], F32)
                bcs = sb1("bcs", [128, 128], F32)
                totb = sb1("totb", [128, 128], F32)
                biasc = sb1("biasc", [128, 128], F32)
                expb = sb1("expb", [128, 128], F32)
                wgt = sb1("wgt", [128, 128], F32)
                dec = sb1("dec", [128, 128], F32)
                for tt in range(NT):
                    for kc in range(8):
                        P.op("pe", lambda e: e.matmul(pb[0].t[:, tt * 16:(tt + 1) * 16], hT.t[:, kc, tt * 128:(tt + 1) * 128], wsm.t[:, kc, 0:16],
                                                      start=(kc == 0), stop=(kc == 7)), [wsm, hTv[tt]], [pb[0]], inc=(kc == 7 and tt == NT - 1))
                P.op("dve", lambda e: e.tensor_tensor(gm.t[:].rearrange("p (t g) -> p t g", g=16), pb[0].t[:, 0:256].rearrange("p (t g) -> p t g", g=16),
                                                      rowp.t[:, 1152:1168].unsqueeze(1).to_broadcast([128, 16, 16]), ALU.add), [pb[0], rowp], [gm])
                gm5 = gm.t[:].rearrange("p (t d w h) -> p t d w h", d=2, w=2, h=4)
                v4 = lambda b_: b_.t[:].rearrange("p (t d h) -> p t d h", d=2, h=4)
                P.op("dve", lambda e: e.tensor_copy(v4(ipre), gm5[:, :, :, 0, :]), [gm], [ipre])
                P.op("dve", lambda e: e.tensor_copy(v4(fpre), gm5[:, :, :, 1, :]), [gm], [fpre])
                softplus_(lf, fpre, t1, t2, 128, sign_logsig=True)
                P.op("pe", lambda e: e.matmul(pb[1].t[:, 0:128], tincl.t[:], lf.t[:], start=True, stop=True), [tincl, lf], [pb[1]], inc=False)
                P.op("pe", lambda e: e.matmul(pb[1].t[:, 128:256], tinclT.t[:], lf.t[:], start=True, stop=True), [tinclT, lf], [pb[1]], inc=False)
                P.op("pe", lambda e: e.matmul(pb[1].t[:, 256:384], onesf.t[:], lf.t[:], start=True, stop=True), [onesf, lf], [pb[1]])
                pv = lambda a, b: pb[1].t[:, a:b].rearrange("p (t d h) -> p t d h", d=2, h=4)
                P.op("dve", lambda e: e.tensor_copy(v4(bcs)[:, :, 0, :], pv(0, 128)[:, :, 0, :]), [pb[1]], [bcs])
                P.op("dve", lambda e: e.tensor_copy(v4(bcs)[:, :, 1, :], pv(128, 256)[:, :, 1, :]), [pb[1]], [bcs])
                P.op("dve", lambda e: e.tensor_copy(totb.t[:], pb[1].t[:, 256:384]), [pb[1]], [totb])
                P.op("dve", lambda e: e.tensor_sub(biasc.t[:], ipre.t[:], bcs.t[:]), [ipre, bcs], [biasc])
                P.op("act", lambda e: e.activation(expb.t[:], bcs.t[:], AF.Exp), [bcs], [expb])
                P.op("dve", lambda e: e.tensor_add(t1.t[:, 0:128], biasc.t[:], totb.t[:]), [biasc, totb], [t1])
                P.op("act", lambda e: e.activation(wgt.t[:], t1.t[:, 0:128], AF.Exp), [t1], [wgt])
                P.op("act", lambda e: e.activation(dec.t[:], totb.t[:], AF.Exp), [totb], [dec])

                qT = sb1("qT", [128, 2, S], BF16)
                kT = sb1("kT", [128, 2, S], BF16)
                ktm = sb1("ktm", [128, NT, 256], BF16)
                vtm = sb1("vtm", [128, NT, 257], BF16)
                hm = sb1("hm", [128, NT, 256], F32)
                hmv = [Buf(hm.t, "hm%d" % i) for i in range(NT)]
                Cst = [sb1("Cst%d" % d_, [128, 2, 257], F32) for d_ in range(2)]
                Cb = [sb1("Cb%d" % d_, [128, 2, 257], BF16) for d_ in range(2)]
                DT = [sb1("DT%d" % d_, [128, 128], F32) for d_ in range(2)]
                diag = [sb1("diag%d" % d_, [128, 128], F32) for d_ in range(2)]
                scT = [sb1("scT%d" % d_, [128, 128], BF16) for d_ in range(2)]
                Asb = [sb1("Asb%d" % d_, [128, 257], F32) for d_ in range(2)]
                comb = [sb1("comb%d" % d_, [128, 257], F32) for d_ in range(2)]
                rr = [sb1("rr%d" % d_, [128, 2], F32) for d_ in range(2)]
                kw = [sb1("kw%d" % d_, [128, 256], BF16) for d_ in range(2)]
                og_l = [sb1("og%d" % i, [128, 256], F32) for i in range(2)]
                ty_l = [sb1("ty%d" % i, [128, 256], F32) for i in range(2)]
                yb_l = [sb1("yb%d" % i, [128, 256], BF16) for i in range(2)]
                ty = ty_l[0]
                ssq = sb1("ssq", [128, 2 * NT], F32)
                ymT = sb1("ymT", [128, 2, S], BF16)
                P.op("dve", lambda e: e.memset(vtm.t[:, :, 256:257], 1.0), [], [vtm])

                for h in range(4):
                    wqk, wkv, wo = nextw(), nextw(), nextw()
                    wload(wqk, 0, w_in, O_MQ + h * 256, 256)
                    wload(wqk, 256, w_in, O_MK + h * 256, 256)
                    wload(wkv, 0, w_in, O_MK + h * 256, 256)
                    wload(wkv, 256, w_in, O_MV + h * 256, 256)
                    wload(wo, 0, w_in, O_MO + h * 256, 256)
                    for ft in range(2):
                        proj_fm(wqk, ft * 128, pb[0:4], lambda tb, bank: evac_copy(qT.t[:, ft, tb * 512:(tb + 1) * 512], bank.t[:, 0:512], [bank], [qT], scale=0.0625))
                        proj_fm(wqk, 256 + ft * 128, pb[0:4], lambda tb, bank: evac_copy(kT.t[:, ft, tb * 512:(tb + 1) * 512], bank.t[:, 0:512], [bank], [kT]))
                    for tt in range(NT):
                        bank = pb[tt % 4]
                        proj_tm(wkv, 0, 512, tt, bank)
                        P.op("act", lambda e: e.copy(ktm.t[:, tt, :], bank.t[:, 0:256]), [bank], [ktm])
                        P.op("dve", lambda e: e.tensor_copy(vtm.t[:, tt, 0:256], bank.t[:, 256:512]), [bank], [vtm])
                    def gen_mrec(d_, h=h):
                        bRQ, bS, bA = (pb[0], pb[3])[d_], (pb[1], pb[4])[d_], (pb[2], pb[5])[d_]
                        mask = mlo if d_ == 0 else mup
                        for step in range(NT):
                            c = step if d_ == 0 else NT - 1 - step
                            cs = slice(c * 128, (c + 1) * 128)
                            gi = (c * 2 + d_) * 4 + h
                            col = lambda b_: b_.t[:, gi:gi + 1]
                            P.op("dve", lambda e: e.tensor_scalar(diag[d_].t[:], identf.t[:], col(bcs), None, ALU.mult), [identf, bcs], [diag[d_]])
                            if step < NT - 1:
                                P.op("act", lambda e: e.activation(kw[d_].t[:], ktm.t[:, c, :], AF.Copy, scale=col(wgt)), [ktm, wgt], [kw[d_]])
                            yield
                            P.op("pe", lambda e: e.matmul(bRQ.t[:, 0:128], onesf.t[:], diag[d_].t[:], start=True, stop=False), [onesf, diag[d_]], [bRQ], inc=False)
                            P.op("pe", lambda e: e.matmul(bRQ.t[:, 0:128], identf.t[:], mask.t[:], start=False, stop=True), [identf, mask], [bRQ], inc=False)
                            for kc in range(2):
                                P.op("pe", lambda e: e.matmul(bRQ.t[:, 128:256], kT.t[:, kc, cs], qT.t[:, kc, cs], start=(kc == 0), stop=(kc == 1)), [kT, qT], [bRQ], inc=(kc == 1))
                            if step > 0:
                                for kc in range(2):
                                    P.op("pe", lambda e: e.matmul(bA.t[:, 0:257], qT.t[:, kc, cs], Cb[d_].t[:, kc, :], start=(kc == 0), stop=(kc == 1)), [qT, Cb[d_]], [bA], inc=(kc == 1))
                            yield
                            P.op("act", lambda e: e.activation(DT[d_].t[:], bRQ.t[:, 0:128], AF.Exp, bias=col(biasc), scale=1.0), [bRQ, biasc], [DT[d_]])
                            if step > 0:
                                P.op("act", lambda e: e.activation(Asb[d_].t[:], bA.t[:, 0:257], AF.Copy, scale=col(expb)), [bA, expb], [Asb[d_]])
                            yield
                            P.op("dve", lambda e: e.tensor_tensor(scT[d_].t[:], bRQ.t[:, 128:256], DT[d_].t[:], ALU.mult), [bRQ, DT[d_]], [scT[d_]])
                            yield
                            P.op("pe", lambda e: e.matmul(bS.t[:, 0:257], scT[d_].t[:], vtm.t[:, c, :], start=True, stop=True), [scT[d_], vtm], [bS])
                            yield
                            if step > 0:
                                P.op("dve", lambda e: e.tensor_tensor(comb[d_].t[:], Asb[d_].t[:], bS.t[:, 0:257], ALU.add), [Asb[d_], bS], [comb[d_]])
                            else:
                                P.op("dve", lambda e: e.tensor_copy(comb[d_].t[:], bS.t[:, 0:257]), [bS], [comb[d_]])
                            den = comb[d_].t[:, 256:257]
                            P.op("dve", lambda e: e.scalar_tensor_tensor(rr[d_].t[:, 0:1], den, -1.0, den, ALU.mult, ALU.max), [comb[d_]], [rr[d_]])
                            P.op("dve", lambda e: e.tensor_scalar_max(rr[d_].t[:, 0:1], rr[d_].t[:, 0:1], 1.0), [rr[d_]], [rr[d_]])
                            P.op("dve", lambda e: e.reciprocal(rr[d_].t[:, 1:2], rr[d_].t[:, 0:1]), [rr[d_]], [rr[d_]])
                            if step < NT - 1:
                                P.op("pe", lambda e: e.matmul(bS.t[:, 0:257], kw[d_].t[:, 0:128], vtm.t[:, c, :], start=True, stop=True), [kw[d_], vtm], [bS])
                                P.op("pe", lambda e: e.matmul(bA.t[:, 0:257], kw[d_].t[:, 128:256], vtm.t[:, c, :], start=True, stop=True), [kw[d_], vtm], [bA])
                            yield
                            first = (d_ == 0 and c < 8) or (d_ == 1 and c >= 8)
                            if first:
                                P.op("act", lambda e: e.activation(hm.t[:, c, :], comb[d_].t[:, 0:256], AF.Copy, scale=rr[d_].t[:, 1:2]), [comb[d_], rr[d_]], [hmv[c]])
                            else:
                                P.op("dve", lambda e: e.scalar_tensor_tensor(hm.t[:, c, :], comb[d_].t[:, 0:256], rr[d_].t[:, 1:2], hm.t[:, c, :], ALU.mult, ALU.add), [comb[d_], rr[d_]], [hmv[c]])
                            if step < NT - 1:
                                for m, bC in enumerate((bS, bA)):
                                    if step == 0:
                                        P.op("dve", lambda e: e.tensor_copy(Cst[d_].t[:, m, :], bC.t[:, 0:257]), [bC], [Cst[d_]])
                                    else:
                                        P.op("dve", lambda e: e.scalar_tensor_tensor(Cst[d_].t[:, m, :], Cst[d_].t[:, m, :], col(dec), bC.t[:, 0:257], ALU.mult, ALU.add), [bC, dec], [Cst[d_]])
                                yield
                                P.op("act", lambda e: e.copy(Cb[d_].t[:], Cst[d_].t[:]), [Cst[d_]], [Cb[d_]])
                            yield

                    gens = [gen_mrec(0), gen_mrec(1)]
                    while gens:
                        for g_ in list(gens):
                            try:
                                next(g_)
                            except StopIteration:
                                gens.remove(g_)
                    for tt in range(NT):
                        P.op("act", lambda e: e.activation(ty.t[:], hm.t[:, tt, :], AF.Square, accum_out=ssq.t[:, tt:tt + 1]), [hmv[tt]], [ty, ssq])
                    rsqrt_(ssq.t[:, NT:2 * NT], ssq.t[:, 0:NT], 1.0 / 256.0, [ssq], [ssq])
                    for tt in range(NT):
                        bank = pb[tt % 4]
                        og, ty, yb = og_l[tt % 2], ty_l[tt % 2], yb_l[tt % 2]
                        proj_tm(wo, 0, 256, tt, bank)
                        P.op("act", lambda e: e.activation(og.t[:], bank.t[:, 0:256], AF.Sigmoid), [bank], [og])
                        P.op("dve", lambda e: e.scalar_tensor_tensor(ty.t[:], hm.t[:, tt, :], ssq.t[:, NT + tt:NT + tt + 1], rowp.t[:, h * 256:(h + 1) * 256], ALU.mult, ALU.mult), [hmv[tt], ssq, rowp], [ty])
                        P.op("dve", lambda e: e.tensor_tensor(yb.t[:], ty.t[:], og.t[:], ALU.mult), [ty, og], [yb])
                        for j in range(2):
                            P.op("pe", lambda e: e.transpose(pbt.t[:, j * 128:(j + 1) * 128], yb.t[:, j * 128:(j + 1) * 128], identb.t[:]), [yb, identb], [pbt], inc=(j == 1))
                        P.op("act", lambda e: e.copy(ymT.t[:, :, tt * 128:(tt + 1) * 128], pbt.t[:, 0:256].rearrange("p (j t) -> p j t", j=2)), [pbt], [ymT])
                    P.dma(ysT.t[0, h * 256:(h + 1) * 256, :].rearrange("(j p) t -> p j t", p=128), ymT.t[:], [ymT], [ysv[0][2 * h], ysv[0][2 * h + 1]])
                    if stop == "m0":
                        break
            if stop in ("m0", "mlstm"):
                stopped = True
                break

            with scope() as sc:
                sb2 = sc.sb
                A2 = lambda name: sb2(name, [128, 256], F32)
                v4 = lambda b_: b_.t[:].rearrange("p (t d h) -> p t d h", d=2, h=8)
                Gc, negG, expG, bexpG, kdw, glb, negbeta, beta = [A2(n) for n in ("Gc", "negG", "expG", "bexpG", "kdw", "glb", "negbeta", "beta")]
                with scope() as sct:
                    dg = sct.sb("dg", [128, 512], F32)
                    apre, spl, gg, t1, t2 = [sct.sb(n, [128, 256], F32) for n in ("apre", "spl", "gg", "t1d", "t2d")]
                    ea = sct.sb("ea", [128, 16], F32)
                    for tt in range(NT):
                        for kc in range(8):
                            P.op("pe", lambda e: e.matmul(pb[0].t[:, tt * 32:(tt + 1) * 32], hT.t[:, kc, tt * 128:(tt + 1) * 128], wsm.t[:, kc, 16:48],
                                                          start=(kc == 0), stop=(kc == 7)), [wsm, hTv[tt]], [pb[0]], inc=(kc == 7 and tt == NT - 1))
                    P.op("act", lambda e: e.copy(dg.t[:], pb[0].t[:, 0:512]), [pb[0]], [dg])
                    dg5 = dg.t[:].rearrange("p (t d w h) -> p t d w h", d=2, w=2, h=8)
                    P.op("act", lambda e: e.activation(v4(beta), dg5[:, :, :, 0, :], AF.Sigmoid), [dg], [beta])
                    P.op("dve", lambda e: e.tensor_tensor(v4(apre), dg5[:, :, :, 1, :],
                                                          rowp.t[:, 1184:1200].rearrange("p (d h) -> p d h", d=2).unsqueeze(1).to_broadcast([128, 16, 2, 8]), ALU.add), [dg, rowp], [apre])
                    softplus_(spl, apre, t1, t2, 256)
                    P.op("act", lambda e: e.activation(ea.t[:], rowp.t[:, 1168:1184], AF.Exp), [rowp], [ea])
                    P.op("dve", lambda e: e.scalar_tensor_tensor(v4(gg), v4(spl), -1.0,
                                                                 ea.t[:].rearrange("p (d h) -> p d h", d=2).unsqueeze(1).to_broadcast([128, 16, 2, 8]), ALU.mult, ALU.mult), [spl, ea], [gg])
                    P.op("pe", lambda e: e.matmul(pb[1].t[:, 0:256], tincl.t[:], gg.t[:], start=True, stop=True), [tincl, gg], [pb[1]], inc=False)
                    P.op("pe", lambda e: e.matmul(pb[1].t[:, 256:512], tinclT.t[:], gg.t[:], start=True, stop=True), [tinclT, gg], [pb[1]], inc=False)
                    P.op("pe", lambda e: e.matmul(pb[2].t[:, 0:256], onesf.t[:], gg.t[:], start=True, stop=True), [onesf, gg], [pb[2]])
                    pv = lambda a, b: pb[1].t[:, a:b].rearrange("p (t d h) -> p t d h", d=2, h=8)
                    P.op("dve", lambda e: e.tensor_copy(v4(Gc)[:, :, 0, :], pv(0, 256)[:, :, 0, :]), [pb[1]], [Gc])
                    P.op("dve", lambda e: e.tensor_copy(v4(Gc)[:, :, 1, :], pv(256, 512)[:, :, 1, :]), [pb[1]], [Gc])
                    P.op("dve", lambda e: e.tensor_scalar_mul(negG.t[:], Gc.t[:], -1.0), [Gc], [negG])
                    P.op("act", lambda e: e.activation(expG.t[:], Gc.t[:], AF.Exp), [Gc], [expG])
                    P.op("dve", lambda e: e.tensor_mul(bexpG.t[:], beta.t[:], expG.t[:]), [beta, expG], [bexpG])
                    P.op("dve", lambda e: e.tensor_tensor(t1.t[:], pb[2].t[:, 0:256], Gc.t[:], ALU.subtract), [pb[2], Gc], [t1])
                    P.op("act", lambda e: e.activation(kdw.t[:], t1.t[:], AF.Exp), [t1], [kdw])
                    P.op("act", lambda e: e.activation(glb.t[:], pb[2].t[:, 0:256], AF.Exp), [pb[2]], [glb])
                    P.op("dve", lambda e: e.tensor_scalar_mul(negbeta.t[:], beta.t[:], -1.0), [beta], [negbeta])
                for nm, b_ in (("Gc", Gc), ("expG", expG), ("kdw", kdw), ("glb", glb), ("beta", beta)):
                    dump(nm, b_, b_.t[:], [128, 256])

                pc = sb2("pc", [128, S + 2], F32)
                cv = sb2("cv", [128, S], F32)
                sq = sb2("sq", [128, S], BF16)
                rin = sb2("rin", [128, 512], F32)
                qnT = sb2("qnT", [128, S], BF16)
                knT = sb2("knT", [128, S], BF16)
                vT = sb2("vT", [128, S], BF16)
                ktm = sb2("ktm2", [128, NT, 128], BF16)
                vtm = sb2("vtm2", [128, NT, 128], BF16)
                WTs = sb2("WTs", [128, 32, 128], BF16)
                Us = sb2("Us", [128, 32, 128], F32)
                ATs = sb2("ATs", [128, 32, 128], BF16)
                stv = [Buf(None, "st%d" % i) for i in range(32)]
                osb = sb2("osb", [128, NT, 128], F32)
                osv = [Buf(osb.t, "os%d" % i) for i in range(NT)]
                NS = NS_CFG[0]
                STAG = STAG_CFG[0]
                wlim[0] = 3 if NS_CFG[0] > 4 else NWB
                _flat = wb[3].t[:].rearrange("p a b -> p (a b)")
                _off = [0]

                def slot_tile(i, name, shape, dt):
                    if i < 4:
                        return sb2("%s%d" % (name, i), shape, dt)
                    n = shape[1] * (2 if dt == F32 else 1)
                    ap = _flat[:, _off[0]:_off[0] + n]
                    _off[0] += n
                    if dt == F32:
                        ap = ap.bitcast(F32)
                    return Buf(ap, "%s%d" % (name, i))
                Dm = [slot_tile(i, "Dm", [128, 128], F32) for i in range(NS)]
                dgl = [slot_tile(i, "dgl", [128, 128], F32) for i in range(NS)]
                Do1 = [slot_tile(i, "Do1", [128, 128], F32) for i in range(NS)]
                Do2 = [slot_tile(i, "Do2", [128, 128], F32) for i in range(NS)]
                CH = [[slot_tile(i, "CH%d_" % j, [128, 384], BF16) for j in range(2)] for i in range(NS)]
                Ao = [slot_tile(i, "Ao", [128, 256], BF16) for i in range(NS)]
                AoT = [slot_tile(i, "AoT", [128, 256], BF16) for i in range(NS)]
                attn_t = [slot_tile(i, "attn", [128, 128], BF16) for i in range(NS)]
                X0b = [slot_tile(i, "X0b", [128, 256], BF16) for i in range(NS)]
                X1b = [slot_tile(i, "X1b", [128, 256], BF16) for i in range(NS)]
                R1b = X0b
                Tmb = Ao
                Wtm = Do1
                Wb = [slot_tile(i, "Wb", [128, 128], BF16) for i in range(NS)]
                mbd = [sb2("mbd%d" % i, [128, 128], F32) for i in range(2)]
                mo1 = [sb2("mo1%d" % i, [128, 128], F32) for i in range(2)]
                mo2 = [sb2("mo2%d" % i, [128, 128], F32) for i in range(2)]
                with scope() as scm:
                    E32 = scm.sb("E32", [4, 128], F32)
                    E64 = scm.sb("E64", [2, 128], F32)
                    b32 = scm.sb("b32", [128, 128], F32)
                    b64 = scm.sb("b64", [128, 128], F32)
                    tmk = scm.sb("tmk", [128, 128], F32)
                    for E_, w_ in ((E32, 32), (E64, 64)):
                        np_ = 128 // w_
                        P.op("pool", lambda e: e.memset(E_.t[:], 1.0), [], [E_])
                        P.op("pool", lambda e: e.affine_select(E_.t[:], E_.t[:], pattern=[[1, 128]], compare_op=ALU.is_ge, fill=0.0, base=0, channel_multiplier=-w_), [E_], [E_])
                        P.op("pool", lambda e: e.affine_select(E_.t[:], E_.t[:], pattern=[[-1, 128]], compare_op=ALU.is_ge, fill=0.0, base=w_ - 1, channel_multiplier=w_), [E_], [E_])
                    P.op("pe", lambda e: e.matmul(pb[0].t[:, 0:128], E32.t[:], E32.t[:], start=True, stop=True), [E32], [pb[0]], inc=False)
                    P.op("pe", lambda e: e.matmul(pb[0].t[:, 128:256], E64.t[:], E64.t[:], start=True, stop=True), [E64], [pb[0]])
                    P.op("dve", lambda e: e.tensor_copy(b32.t[:], pb[0].t[:, 0:128]), [pb[0]], [b32])
                    P.op("dve", lambda e: e.tensor_copy(b64.t[:], pb[0].t[:, 128:256]), [pb[0]], [b64])
                    for d_, tri in ((0, slf), (1, suf)):
                        P.op("dve", lambda e: e.tensor_tensor(mbd[d_].t[:], tri.t[:], b32.t[:], ALU.mult), [tri, b32], [mbd[d_]])
                        P.op("dve", lambda e: e.tensor_tensor(tmk.t[:], b64.t[:], b32.t[:], ALU.subtract), [b64, b32], [tmk])
                        P.op("dve", lambda e: e.tensor_tensor(mo1[d_].t[:], tri.t[:], tmk.t[:], ALU.mult), [tri, tmk], [mo1[d_]])
                        P.op("dve", lambda e: e.tensor_tensor(tmk.t[:], tri.t[:], b64.t[:], ALU.mult), [tri, b64], [tmk])
                        P.op("dve", lambda e: e.tensor_tensor(mo2[d_].t[:], tri.t[:], tmk.t[:], ALU.subtract), [tri, tmk], [mo2[d_]])
                Sst = [sb2("Sst%d" % d_, [128, 128], F32) for d_ in range(2)]
                Sbb = [sb2("Sbb%d" % d_, [128, 128], BF16) for d_ in range(2)]
                vnb = [sb2("vnb%d" % d_, [128, 128], BF16) for d_ in range(2)]
                kdt = [sb2("kdt%d" % d_, [128, 128], BF16) for d_ in range(2)]
                tmo = [sb2("tmo%d" % d_, [128, 128], F32) for d_ in range(2)]
                zg_l = [sb2("zg%d" % i, [128, 128], F32) for i in range(2)]
                ty_l = [sb2("ty2%d" % i, [128, 128], F32) for i in range(2)]
                yb_l = [sb2("yb2%d" % i, [128, 128], BF16) for i in range(2)]
                ty = ty_l[0]
                ssq = sb2("ssq2", [128, 2 * NT], F32)
                ydT = sq
                P.op("dve", lambda e: e.memset(pc.t[:, 0:1], 0.0), [], [pc])
                P.op("dve", lambda e: e.memset(pc.t[:, S + 1:S + 2], 0.0), [], [pc])
                wA = wB = None
                for h in range(8):
                    hh = h % 2
                    if hh == 0:
                        wA, wB = nextw(), nextw()
                        wload(wA, 0, w_in, O_DQ + h * 128, 256)
                        wload(wA, 256, w_in, O_DK + h * 128, 256)
                        wload(wB, 0, w_in, O_DV + h * 128, 256)
                        wload(wB, 256, w_in, O_DZ + h * 128, 256)
                    for j, (wsrc, off) in enumerate(((wA, hh * 128), (wA, 256 + hh * 128), (wB, hh * 128))):
                        proj_fm(wsrc, off, pb[0:4], lambda tb, bank: evac_copy(pc.t[:, 1 + tb * 512:1 + (tb + 1) * 512], bank.t[:, 0:512], [bank], [pc]))
                        cw = lambda k: colp.t[:, 16 + k * 24 + j * 8 + h:16 + k * 24 + j * 8 + h + 1]
                        P.op("dve", lambda e: e.tensor_scalar(cv.t[:], pc.t[:, 0:S], cw(0), None, ALU.mult), [pc, colp], [cv])
                        P.op("dve", lambda e: e.scalar_tensor_tensor(cv.t[:], pc.t[:, 1:S + 1], cw(1), cv.t[:], ALU.mult, ALU.add), [pc, colp], [cv])
                        P.op("dve", lambda e: e.scalar_tensor_tensor(cv.t[:], pc.t[:, 2:S + 2], cw(2), cv.t[:], ALU.mult, ALU.add), [pc, colp], [cv])
                        P.op("act", lambda e: e.activation(cv.t[:], cv.t[:], AF.Silu), [cv], [cv])
                        if j == 2:
                            P.op("act", lambda e: e.copy(vT.t[:], cv.t[:]), [cv], [vT])
                        else:
                            dst = qnT if j == 0 else knT
                            P.op("act", lambda e: e.activation(sq.t[:], cv.t[:], AF.Square), [cv], [sq])
                            for tb in range(4):
                                bs = slice(tb * 512, (tb + 1) * 512)
                                bank = pb[4 + tb % 2]
                                P.op("pe", lambda e: e.matmul(bank.t[:, 0:512], onesb.t[:], sq.t[:, bs], start=True, stop=True), [onesb, sq], [bank])
                                rsqrt_(rin.t[:], bank.t[:, 0:512], 1.0, [bank], [rin])
                                if j == 0:
                                    P.op("dve", lambda e: e.scalar_tensor_tensor(dst.t[:, bs], cv.t[:, bs], 128.0 ** -0.5, rin.t[:], ALU.mult, ALU.mult), [cv, rin], [dst])
                                else:
                                    P.op("dve", lambda e: e.tensor_tensor(dst.t[:, bs], cv.t[:, bs], rin.t[:], ALU.mult), [cv, rin], [dst])
                    if h == 0:
                        dump("qnT", qnT, qnT.t[:], [128, S], BF16)
                        dump("knT", knT, knT.t[:], [128, S], BF16)
                        dump("vT", vT, vT.t[:], [128, S], BF16)
                    for src, dstm in ((knT, ktm), (vT, vtm)):
                        for g4 in range(2):
                            for i in range(8):
                                tt = g4 * 8 + i
                                P.op("pe", lambda e: e.transpose(pbt.t[:, i * 128:(i + 1) * 128], src.t[:, tt * 128:(tt + 1) * 128], identb.t[:]), [src, identb], [pbt], inc=(i == 7))
                            evac_copy(dstm.t[:, g4 * 8:(g4 + 1) * 8, :], pbt.t[:].rearrange("p (i t) -> p i t", i=8), [pbt], [dstm])
                    if stop == "dn0a":
                        break
                    def gen_prep(si, c, d_, h=h):
                        cs = slice(c * 128, (c + 1) * 128)
                        gi = (c * 2 + d_) * 8 + h
                        col = lambda b_: b_.t[:, gi:gi + 1]
                        e_ = c * 2 + d_
                        ch0, ch1 = CH[si][0], CH[si][1]
                        bank = pb[1 + si]
                        mask = mup if d_ == 0 else mlo
                        P.op("dve", lambda e: e.tensor_scalar(dgl[si].t[:], identf.t[:], col(negG), None, ALU.mult), [identf, negG], [dgl[si]])
                        yield
                        while lock["pb0"] is not None:
                            yield
                        lock["pb0"] = si
                        P.op("pe", lambda e: e.matmul(pb[0].t[:, 0:128], onesf.t[:], dgl[si].t[:], start=True, stop=False), [onesf, dgl[si]], [pb[0]], inc=False)
                        P.op("pe", lambda e: e.matmul(pb[0].t[:, 0:128], identf.t[:], mask.t[:], start=False, stop=True), [identf, mask], [pb[0]], inc=False)
                        P.op("pe", lambda e: e.matmul(pb[0].t[:, 128:256], knT.t[:, cs], knT.t[:, cs], start=True, stop=True), [knT], [pb[0]], inc=False)
                        P.op("pe", lambda e: e.matmul(pb[0].t[:, 256:384], qnT.t[:, cs], knT.t[:, cs], start=True, stop=True), [qnT, knT], [pb[0]])
                        yield
                        P.op("act", lambda e: e.activation(Dm[si].t[:], pb[0].t[:, 0:128], AF.Exp, bias=col(Gc), scale=1.0), [pb[0], Gc], [Dm[si]])
                        P.op("act", lambda e: e.activation(X0b[si].t[:, 0:128], ktm.t[:, c, :], AF.Copy, scale=col(bexpG)), [ktm, bexpG], [X0b[si]])
                        P.op("act", lambda e: e.activation(X0b[si].t[:, 128:256], vtm.t[:, c, :], AF.Copy, scale=col(beta)), [vtm, beta], [X0b[si]])
                        yield
                        P.op("pool", lambda e: e.tensor_tensor(dgl[si].t[:], Dm[si].t[:], mbd[d_].t[:], ALU.mult), [Dm[si], mbd[d_]], [dgl[si]])
                        P.op("pool", lambda e: e.tensor_tensor(Do1[si].t[:], Dm[si].t[:], mo1[d_].t[:], ALU.mult), [Dm[si], mo1[d_]], [Do1[si]])
                        P.op("pool", lambda e: e.tensor_tensor(Do2[si].t[:], Dm[si].t[:], mo2[d_].t[:], ALU.mult), [Dm[si], mo2[d_]], [Do2[si]])
                        P.op("dve", lambda e: e.tensor_tensor(attn_t[si].t[:], pb[0].t[:, 256:384], Dm[si].t[:], ALU.mult), [pb[0], Dm[si]], [attn_t[si]])
                        yield
                        P.op("dve", lambda e: e.scalar_tensor_tensor(ch0.t[:, 256:384], pb[0].t[:, 128:256], col(negbeta), dgl[si].t[:], ALU.mult, ALU.mult), [pb[0], negbeta, dgl[si]], [ch0])
                        P.op("dve", lambda e: e.scalar_tensor_tensor(Ao[si].t[:, 0:128], pb[0].t[:, 128:256], col(beta), Do1[si].t[:], ALU.mult, ALU.mult), [pb[0], beta, Do1[si]], [Ao[si]])
                        P.op("dve", lambda e: e.scalar_tensor_tensor(Ao[si].t[:, 128:256], pb[0].t[:, 128:256], col(beta), Do2[si].t[:], ALU.mult, ALU.mult), [pb[0], beta, Do2[si]], [Ao[si]])
                        lock["pb0"] = None
                        yield
                        while lock["pbt"] is not None:
                            yield
                        lock["pbt"] = si
                        P.op("pe", lambda e: e.transpose(pbt.t[:, 0:128], ch0.t[:, 256:384], identb.t[:]), [ch0, identb], [pbt], inc=False)
                        P.op("pe", lambda e: e.transpose(pbt.t[:, 128:256], Ao[si].t[:, 0:128], identb.t[:]), [Ao[si], identb], [pbt], inc=False)
                        P.op("pe", lambda e: e.transpose(pbt.t[:, 256:384], Ao[si].t[:, 128:256], identb.t[:]), [Ao[si], identb], [pbt], inc=False)
                        P.op("pe", lambda e: e.transpose(pbt.t[:, 384:512], attn_t[si].t[:], identb.t[:]), [attn_t[si], identb], [pbt])
                        yield
                        P.op("act", lambda e: e.copy(ch0.t[:, 0:128], pbt.t[:, 0:128]), [pbt], [ch0])
                        P.op("dve", lambda e: e.tensor_tensor(ch1.t[:, 128:256], pbt.t[:, 0:128], identb.t[:], ALU.add), [pbt, identb], [ch1])
                        P.op("act", lambda e: e.copy(AoT[si].t[:], pbt.t[:, 128:384]), [pbt], [AoT[si]])
                        P.op("dve", lambda e: e.tensor_copy(ATs.t[:, e_, :], pbt.t[:, 384:512]), [pbt], [stv[e_]])
                        lock["pbt"] = None
                        yield
                        P.op("pe", lambda e: e.matmul(bank.t[:, 0:128], ch0.t[:, 256:384], ch0.t[:, 0:128], start=True, stop=True), [ch0], [bank], inc=False)
                        P.op("pe", lambda e: e.matmul(bank.t[:, 256:384], ch0.t[:, 0:128], ch0.t[:, 256:384], start=True, stop=True), [ch0], [bank])
                        yield
                        P.op("act", lambda e: e.copy(ch1.t[:].rearrange("p (a b) -> p a b", b=128)[:, 0::2, :], bank.t[:, 0:384].rearrange("p (a b) -> p a b", b=128)[:, 0::2, :]), [bank], [ch1])
                        yield
                        for j in range(1, 5):
                            cur = CH[si][j % 2]
                            nxt = CH[si][(j + 1) % 2]
                            if j < 4:
                                P.op("pe", lambda e: e.matmul(bank.t[:, 0:256], cur.t[:, 256:384], cur.t[:, 0:256], start=True, stop=False), [cur], [bank], inc=False)
                                P.op("pe", lambda e: e.matmul(bank.t[:, 128:256], identb.t[:], cur.t[:, 128:256], start=False, stop=True), [cur, identb], [bank], inc=False)
                                P.op("pe", lambda e: e.matmul(bank.t[:, 256:384], cur.t[:, 0:128], cur.t[:, 256:384], start=True, stop=True), [cur], [bank])
                                yield
                                evac_copy(nxt.t[:], bank.t[:, 0:384], [bank], [nxt])
                                yield
                            else:
                                P.op("pe", lambda e: e.matmul(bank.t[:, 128:256], cur.t[:, 256:384], cur.t[:, 128:256], start=True, stop=False), [cur], [bank], inc=False)
                                P.op("pe", lambda e: e.matmul(bank.t[:, 128:256], identb.t[:], cur.t[:, 128:256], start=False, stop=True), [cur, identb], [bank])
                                yield
                                evac_copy(nxt.t[:, 128:256], bank.t[:, 128:256], [bank], [nxt])
                                yield
                        fin = CH[si][1]
                        PTf = fin.t[:, 128:256]
                        A1T, A2T = AoT[si].t[:, 0:128], AoT[si].t[:, 128:256]
                        lo, hi = bank.t[:, 0:256], bank.t[:, 256:512]
                        P.op("pe", lambda e: e.matmul(lo, PTf, X0b[si].t[:], start=True, stop=True), [fin, X0b[si]], [bank])
                        yield
                        P.op("act", lambda e: e.copy(R1b[si].t[:], lo), [bank], [R1b[si]])
                        P.op("dve", lambda e: e.tensor_copy(Us.t[:, e_, :], bank.t[:, 128:256]), [bank], [stv[e_]])
                        yield
                        P.op("pe", lambda e: e.matmul(hi, A1T, R1b[si].t[:], start=True, stop=True), [AoT[si], R1b[si]], [bank])
                        yield
                        P.op("act", lambda e: e.copy(Tmb[si].t[:], hi), [bank], [Tmb[si]])
                        yield
                        P.op("pe", lambda e: e.matmul(lo, PTf, Tmb[si].t[:], start=True, stop=True), [fin, Tmb[si]], [bank])
                        yield
                        P.op("dve", lambda e: e.tensor_tensor(X1b[si].t[:], R1b[si].t[:], lo, ALU.subtract), [R1b[si], bank], [X1b[si]])
                        P.op("dve", lambda e: e.tensor_tensor(Us.t[:, e_, :], Us.t[:, e_, :], bank.t[:, 128:256], ALU.subtract), [bank], [stv[e_]])
                        yield
                        P.op("pe", lambda e: e.matmul(hi, A2T, X1b[si].t[:], start=True, stop=True), [AoT[si], X1b[si]], [bank])
                        yield
                        P.op("act", lambda e: e.copy(Tmb[si].t[:], hi), [bank], [Tmb[si]])
                        yield
                        P.op("pe", lambda e: e.matmul(lo, PTf, Tmb[si].t[:], start=True, stop=True), [fin, Tmb[si]], [bank])
                        yield
                        P.op("act", lambda e: e.copy(R1b[si].t[:], lo), [bank], [R1b[si]])
                        P.op("dve", lambda e: e.tensor_tensor(Us.t[:, e_, :], Us.t[:, e_, :], bank.t[:, 128:256], ALU.subtract), [bank], [stv[e_]])
                        yield
                        P.op("pe", lambda e: e.matmul(hi, A1T, R1b[si].t[:], start=True, stop=True), [AoT[si], R1b[si]], [bank])
                        P.op("pool", lambda e: e.tensor_tensor(Wtm[si].t[:], X1b[si].t[:, 0:128], R1b[si].t[:, 0:128], ALU.subtract), [X1b[si], R1b[si]], [Wtm[si]])
                        yield
                        P.op("act", lambda e: e.copy(Tmb[si].t[:], hi), [bank], [Tmb[si]])
                        yield
                        P.op("pe", lambda e: e.matmul(lo, PTf, Tmb[si].t[:], start=True, stop=True), [fin, Tmb[si]], [bank])
                        yield
                        P.op("dve", lambda e: e.tensor_tensor(Wb[si].t[:], Wtm[si].t[:], bank.t[:, 0:128], ALU.add), [Wtm[si], bank], [Wb[si]])
                        P.op("dve", lambda e: e.tensor_tensor(Us.t[:, e_, :], Us.t[:, e_, :], bank.t[:, 128:256], ALU.add), [bank], [stv[e_]])
                        yield
                        while lock["pbt"] is not None:
                            yield
                        lock["pbt"] = si
                        P.op("pe", lambda e: e.transpose(pbt.t[:, 0:128], Wb[si].t[:], identb.t[:]), [Wb[si], identb], [pbt])
                        yield
                        P.op("act", lambda e: e.copy(WTs.t[:, e_, :], pbt.t[:, 0:128]), [pbt], [stv[e_]])
                        lock["pbt"] = None
                        yield

                    def gen_rec(h=h):
                        bA, bB = (pb[6], pb[6]) if NS_CFG[0] > 4 else (pb[5], pb[6])
                        for step in range(NT):
                            info = []
                            for d_ in range(2):
                                c = step if d_ == 0 else NT - 1 - step
                                info.append((d_, c, slice(c * 128, (c + 1) * 128), (c * 2 + d_) * 8 + h, c * 2 + d_))
                            for d_, c, cs, gi, e_ in info:
                                if step > 0:
                                    P.op("pe", lambda e: e.matmul(bA.t[:, d_ * 256:d_ * 256 + 128], WTs.t[:, e_, :], Sbb[d_].t[:], start=True, stop=True), [stv[e_], Sbb[d_]], [bA], inc=False)
                                    P.op("pe", lambda e: e.matmul(bA.t[:, d_ * 256 + 128:d_ * 256 + 256], qnT.t[:, cs], Sbb[d_].t[:], start=True, stop=True), [qnT, Sbb[d_]], [bA])
                                if step < NT - 1:
                                    P.op("act", lambda e: e.activation(kdt[d_].t[:], ktm.t[:, c, :], AF.Copy, scale=kdw.t[:, gi:gi + 1]), [ktm, kdw], [kdt[d_]])
                            yield
                            for d_, c, cs, gi, e_ in info:
                                if step > 0:
                                    P.op("dve", lambda e: e.tensor_tensor(vnb[d_].t[:], Us.t[:, e_, :], bA.t[:, d_ * 256:d_ * 256 + 128], ALU.subtract), [stv[e_], bA], [vnb[d_]])
                                    P.op("act", lambda e: e.activation(tmo[d_].t[:], bA.t[:, d_ * 256 + 128:d_ * 256 + 256], AF.Copy, scale=expG.t[:, gi:gi + 1]), [bA, expG], [tmo[d_]])
                                else:
                                    P.op("dve", lambda e: e.tensor_copy(vnb[d_].t[:], Us.t[:, e_, :]), [stv[e_]], [vnb[d_]])
                            yield
                            for d_, c, cs, gi, e_ in info:
                                P.op("pe", lambda e: e.matmul(bB.t[:, d_ * 256:d_ * 256 + 128], ATs.t[:, e_, :], vnb[d_].t[:], start=True, stop=True), [stv[e_], vnb[d_]], [bB], inc=(step == NT - 1))
                                if step < NT - 1:
                                    P.op("pe", lambda e: e.matmul(bB.t[:, d_ * 256 + 128:d_ * 256 + 256], kdt[d_].t[:], vnb[d_].t[:], start=True, stop=True), [kdt[d_], vnb[d_]], [bB])
                            yield
                            for d_, c, cs, gi, e_ in info:
                                if step < NT - 1:
                                    if step == 0:
                                        P.op("dve", lambda e: e.tensor_copy(Sst[d_].t[:], bB.t[:, d_ * 256 + 128:d_ * 256 + 256]), [bB], [Sst[d_]])
                                    else:
                                        P.op("dve", lambda e: e.scalar_tensor_tensor(Sst[d_].t[:], Sst[d_].t[:], glb.t[:, gi:gi + 1], bB.t[:, d_ * 256 + 128:d_ * 256 + 256], ALU.mult, ALU.add), [bB, glb], [Sst[d_]])
                                    P.op("act", lambda e: e.copy(Sbb[d_].t[:], Sst[d_].t[:]), [Sst[d_]], [Sbb[d_]])
                            for d_, c, cs, gi, e_ in info:
                                first = (d_ == 0 and c < 8) or (d_ == 1 and c >= 8)
                                if step > 0:
                                    P.op("dve", lambda e: e.tensor_tensor(tmo[d_].t[:], tmo[d_].t[:], bB.t[:, d_ * 256:d_ * 256 + 128], ALU.add), [bB], [tmo[d_]])
                                    src_ap, src_b = tmo[d_].t[:], tmo[d_]
                                    if first:
                                        P.op("act", lambda e: e.copy(osb.t[:, c, :], src_ap), [src_b], [osv[c]])
                                    else:
                                        P.op("dve", lambda e: e.tensor_tensor(osb.t[:, c, :], osb.t[:, c, :], src_ap, ALU.add), [src_b], [osv[c]])
                                else:
                                    if first:
                                        P.op("dve", lambda e: e.tensor_copy(osb.t[:, c, :], bB.t[:, d_ * 256:d_ * 256 + 128]), [bB], [osv[c]])
                                    else:
                                        P.op("dve", lambda e: e.tensor_tensor(osb.t[:, c, :], osb.t[:, c, :], bB.t[:, d_ * 256:d_ * 256 + 128], ALU.add), [bB], [osv[c]])
                            yield

                    lock = {"pb0": None, "pbt": None}
                    order = []
                    for i in range(NT):
                        order.append((i, 0))
                        order.append((NT - 1 - i, 1))
                    active = [None] * NS
                    nstarted = 0
                    nfinished = 0
                    finished = [False] * 32
                    rec = gen_rec()
                    rec_step = 0
                    rec_hop = 0
                    rec_done = False
                    tick = 0
                    while nfinished < 32 or not rec_done:
                        if nstarted < 32 and tick % STAG == 0:
                            for si in range(NS):
                                if active[si] is None:
                                    c, d_ = order[nstarted]
                                    active[si] = (gen_prep(si, c, d_), nstarted)
                                    nstarted += 1
                                    break
                        for si in range(NS):
                            if active[si] is not None:
                                g_, idx = active[si]
                                try:
                                    next(g_)
                                except StopIteration:
                                    finished[idx] = True
                                    nfinished += 1
                                    active[si] = None
                        if not rec_done and (stop != "dn0b"):
                            if rec_hop > 0 or (finished[2 * rec_step] and finished[2 * rec_step + 1]):
                                try:
                                    next(rec)
                                    rec_hop += 1
                                    if rec_hop == 4:
                                        rec_hop = 0
                                        rec_step += 1
                                        if rec_step == NT:
                                            rec_done = True
                                except StopIteration:
                                    rec_done = True
                        elif stop == "dn0b":
                            rec_done = True
                        tick += 1
                    if stop == "dn0c":
                        break
                    if h == 0:
                        dump("osb", osv[0], osb.t[:], [128, NT, 128])
                    for tt in range(NT):
                        P.op("act", lambda e: e.activation(ty.t[:], osb.t[:, tt, :], AF.Square, accum_out=ssq.t[:, tt:tt + 1]), [osv[tt]], [ty, ssq])
                    rsqrt_(ssq.t[:, NT:2 * NT], ssq.t[:, 0:NT], 1.0 / 128.0, [ssq], [ssq])
                    for tt in range(NT):
                        bank = pb[4 + tt % 2]
                        zg, ty, yb = zg_l[tt % 2], ty_l[tt % 2], yb_l[tt % 2]
                        proj_tm(wB, 256 + hh * 128, 128, tt, bank)
                        P.op("act", lambda e: e.activation(zg.t[:], bank.t[:, 0:128], AF.Silu), [bank], [zg])
                        P.op("dve", lambda e: e.scalar_tensor_tensor(ty.t[:], osb.t[:, tt, :], ssq.t[:, NT + tt:NT + tt + 1], rowp.t[:, 1024:1152], ALU.mult, ALU.mult), [osv[tt], ssq, rowp], [ty])
                        P.op("dve", lambda e: e.tensor_tensor(yb.t[:], ty.t[:], zg.t[:], ALU.mult), [ty, zg], [yb])
                        P.op("pe", lambda e: e.transpose(pbt.t[:, 0:128], yb.t[:], identb.t[:]), [yb, identb], [pbt])
                        P.op("act", lambda e: e.copy(ydT.t[:, tt * 128:(tt + 1) * 128], pbt.t[:, 0:128]), [pbt], [ydT])
                    P.dma(ysT.t[1, h * 128:(h + 1) * 128, :], ydT.t[:], [ydT], [ysv[1][h]])
                    if stop == "dn0":
                        break
            if stop in ("dn0", "dn", "dn0a", "dn0b", "dn0c"):
                stopped = True
                break

            wlim[0] = NWB
            with scope() as sc:
                cx = sc.sb("cx", [128, S + 2], F32)
                Bsb = sc.sb("Bsb", [128, S], F32)
                ycv = sc.sb("ycv", [128, S], F32)
                tmx_l = [sc.sb("tmx%d" % i, [128, 512], F32) for i in range(2)]
                ycT = sc.sb("ycT", [128, S], BF16)
                P.op("dve", lambda e: e.memset(cx.t[:, 0:1], 0.0), [], [cx])
                P.op("dve", lambda e: e.memset(cx.t[:, S + 1:S + 2], 0.0), [], [cx])
                wA = wB = None
                for dc in range(8):
                    dd = dc % 2
                    if dd == 0:
                        wA, wB = nextw(), nextw()
                        wload(wA, 0, w_in, O_SB + dc * 128, 256)
                        wload(wA, 256, w_in, O_SC + dc * 128, 256)
                        wload(wB, 0, w_in, O_SX + dc * 128, 256)
                    for tb in range(4):
                        bs = slice(tb * 512, (tb + 1) * 512)
                        tmx = tmx_l[tb % 2]
                        for j, (wsrc, off) in enumerate(((wA, dd * 128), (wA, 256 + dd * 128), (wB, dd * 128))):
                            bank = pb[j + 3 * (tb % 2)]
                            for kc in range(8):
                                P.op("pe", lambda e: e.matmul(bank.t[:, 0:512], wsrc.t[:, kc, off:off + 128], hT.t[:, kc, bs], start=(kc == 0), stop=(kc == 7)),
                                     [wsrc] + hTv[tb * 4:tb * 4 + 4], [bank], inc=(kc == 7))
                        o3 = 3 * (tb % 2)
                        P.op("act", lambda e: e.copy(Bsb.t[:, bs], pb[o3].t[:, 0:512]), [pb[o3]], [Bsb])
                        P.op("act", lambda e: e.copy(tmx.t[:], pb[o3 + 2].t[:, 0:512]), [pb[o3 + 2]], [tmx])
                        P.op("dve", lambda e: e.tensor_tensor(cx.t[:, 1 + tb * 512:1 + (tb + 1) * 512], pb[o3 + 1].t[:, 0:512], tmx.t[:], ALU.mult), [pb[o3 + 1], tmx], [cx])
                    cw = lambda k: colp.t[:, 88 + k * 8 + dc:88 + k * 8 + dc + 1]
                    P.op("dve", lambda e: e.tensor_scalar(ycv.t[:], cx.t[:, 0:S], cw(0), None, ALU.mult), [cx, colp], [ycv])
                    P.op("dve", lambda e: e.scalar_tensor_tensor(ycv.t[:], cx.t[:, 1:S + 1], cw(1), ycv.t[:], ALU.mult, ALU.add), [cx, colp], [ycv])
                    P.op("dve", lambda e: e.scalar_tensor_tensor(ycv.t[:], cx.t[:, 2:S + 2], cw(2), ycv.t[:], ALU.mult, ALU.add), [cx, colp], [ycv])
                    P.op("dve", lambda e: e.tensor_tensor(ycT.t[:], ycv.t[:], Bsb.t[:], ALU.mult), [ycv, Bsb], [ycT])
                    P.dma(ysT.t[2, dc * 128:(dc + 1) * 128, :], ycT.t[:], [ycT], [ysv[2][dc]])
            if stop == "sc":
                stopped = True
                break

            if l + 1 < nlayers:
                P.dma(colps[(l + 1) % 2].t[:], colp_d[l + 1], [], [colps[(l + 1) % 2]])
            last = (l == DEPTH - 1)
            for half in range(2):
                with scope() as sch:
                    xres = sch.sb("xres", [128, 8, D], F32)
                    xrv = [Buf(xres.t, "xr%d" % i) for i in range(8)]
                    with scope() as sc:
                        ys_sb = [sc.sb("ys_sb%d" % n, [128, 8, 1024], BF16) for n in range(3)]
                        sg_l = [[sc.sb("sg%d_%d" % (n, i), [128, 512], F32) for n in range(3)] for i in range(2)]
                        acc_l = [sc.sb("acc%d" % i, [128, 512], F32) for i in range(2)]
                        tmm_l = [sc.sb("tmm", [128, 512], F32)] * 2
                        mixT = sc.sb("mixT", [128, 8, 1024], BF16)
                        for n in range(3):
                            P.dma(ys_sb[n].t[:], ysT.t[n].rearrange("(kc p) t -> p kc t", p=128)[:, :, half * 1024:(half + 1) * 1024], ysv[n], [ys_sb[n]])
                        wA = wB = wC = None
                        for dc in range(8):
                            dd = dc % 2
                            if dd == 0:
                                wA, wB, wC = nextw(), nextw(), nextw()
                                wload(wA, 0, w_br_d[l, 0], dc * 128, 256)
                                wload(wA, 256, w_br_d[l, 1], dc * 128, 256)
                                wload(wB, 0, w_br_d[l, 2], dc * 128, 256)
                                wload(wB, 256, w_in, O_MRG + dc * 128, 256)
                                wload(wC, 0, w_in, O_MRG + 1024 + dc * 128, 256)
                                wload(wC, 256, w_in, O_MRG + 2048 + dc * 128, 256)
                            wbr = ((wA, dd * 128), (wA, 256 + dd * 128), (wB, dd * 128))
                            wgt_ = ((wB, 256 + dd * 128), (wC, dd * 128), (wC, 256 + dd * 128))
                            for tbh in range(2):
                                tb = half * 2 + tbh
                                bs = slice(tb * 512, (tb + 1) * 512)
                                bsh = slice(tbh * 512, (tbh + 1) * 512)
                                sg, acc, tmm = sg_l[tbh], acc_l[tbh], tmm_l[tbh]
                                for n in range(3):
                                    wsrc, off = wgt_[n]
                                    for kc in range(8):
                                        P.op("pe", lambda e: e.matmul(pb[3 + n].t[:, 0:512], wsrc.t[:, kc, off:off + 128], hT.t[:, kc, bs], start=(kc == 0), stop=(kc == 7)),
                                             [wsrc] + hTv[tb * 4:tb * 4 + 4], [pb[3 + n]], inc=(kc == 7))
                                    P.op("act", lambda e: e.activation(sg[n].t[:], pb[3 + n].t[:, 0:512], AF.Sigmoid), [pb[3 + n]], [sg[n]])
                                for n in range(3):
                                    wsrc, off = wbr[n]
                                    for kc in range(8):
                                        P.op("pe", lambda e: e.matmul(pb[n].t[:, 0:512], wsrc.t[:, kc, off:off + 128], ys_sb[n].t[:, kc, bsh], start=(kc == 0), stop=(kc == 7)),
                                             [wsrc, ys_sb[n]], [pb[n]], inc=(kc == 7))
                                P.op("dve", lambda e: e.tensor_tensor(acc.t[:], sg[0].t[:], pb[0].t[:, 0:512], ALU.mult), [sg[0], pb[0]], [acc])
                                P.op("dve", lambda e: e.tensor_tensor(tmm.t[:], sg[1].t[:], pb[1].t[:, 0:512], ALU.mult), [sg[1], pb[1]], [tmm])
                                P.op("dve", lambda e: e.tensor_tensor(acc.t[:], acc.t[:], tmm.t[:], ALU.add), [tmm], [acc])
                                P.op("dve", lambda e: e.tensor_tensor(tmm.t[:], sg[2].t[:], pb[2].t[:, 0:512], ALU.mult), [sg[2], pb[2]], [tmm])
                                P.op("dve", lambda e: e.tensor_tensor(mixT.t[:, dc, bsh], acc.t[:], tmm.t[:], ALU.add), [acc, tmm], [mixT])
                        wo0, wo1 = nextw(), nextw()
                        wload(wo0, 0, w_out_d[l], 0, 512)
                        wload(wo1, 0, w_out_d[l], 512, 512)
                        for t8 in range(8):
                            tt = half * 8 + t8
                            P.dma(xres.t[:, t8, :], xcur.t[tt * 128:(tt + 1) * 128, :], [xcv[tt]], [xrv[t8]])
                            for nb, wsrc in enumerate((wo0, wo1)):
                                bank = pb[(t8 * 2 + nb) % 4]
                                for dc in range(8):
                                    P.op("pe", lambda e: e.matmul(bank.t[:, 0:512], mixT.t[:, dc, t8 * 128:(t8 + 1) * 128], wsrc.t[:, dc, :], start=(dc == 0), stop=(dc == 7)),
                                         [mixT, wsrc], [bank], inc=(dc == 7))
                                P.op("dve", lambda e: e.tensor_tensor(xres.t[:, t8, nb * 512:(nb + 1) * 512], xres.t[:, t8, nb * 512:(nb + 1) * 512], bank.t[:, 0:512], ALU.add), [bank], [xrv[t8]])
                            norm_tile(xres.t[:, t8, :], xrv[t8], tt, (colp, colp.t[:, 8:16]))
                            if stop == "mix" and "d_xm" in dbg_d:
                                P.dma(dbg_d["d_xm"][tt * 128:(tt + 1) * 128, :], xres.t[:, t8, :], [xrv[t8]], [])
                    if stop == "mix":
                        continue
                    with scope() as sc:
                        upT = sc.sb("upT", [128, 8, 1024], BF16)
                        relu_t = [sc.sb("relu_t%d" % i, [128, 512], F32) for i in range(2)]
                        otile = [sc.sb("otile%d" % i, [128, D], F32) for i in range(2)] if last else None
                        gfin = sc.sb("gfin_sb", [128, D], F32) if last else None
                        if last:
                            P.dma(gfin.t[:], gfin_d.partition_broadcast(128), [], [gfin])
                        for fb in range(4):
                            wu = [nextw(), nextw()]
                            wd = [nextw(), nextw()]
                            wload(wu[0], 0, w_up_d[l], fb * 1024, 512)
                            wload(wu[1], 0, w_up_d[l], fb * 1024 + 512, 512)
                            wload(wd[0], 0, w_dn_d[l, fb * 1024:(fb + 1) * 1024, :], 0, 512)
                            wload(wd[1], 0, w_dn_d[l, fb * 1024:(fb + 1) * 1024, :], 512, 512)
                            for fc in range(8):
                                def ev(tb, bank):
                                    bsh = slice((tb - half * 2) * 512, (tb - half * 2 + 1) * 512)
                                    rl = relu_t[tb % 2]
                                    P.op("act", lambda e: e.activation(rl.t[:], bank.t[:, 0:512], AF.Relu), [bank], [rl])
                                    P.op("dve", lambda e: e.tensor_tensor(upT.t[:, fc, bsh], rl.t[:], rl.t[:], ALU.mult), [rl], [upT])
                                proj_fm(wu[fc // 4], (fc % 4) * 128, pb[0:4], ev, tbs=(half * 2, half * 2 + 1))
                            for t8 in range(8):
                                tt = half * 8 + t8
                                for nb in range(2):
                                    bank = pb[4 + (t8 * 2 + nb) % 3]
                                    for fc in range(8):
                                        P.op("pe", lambda e: e.matmul(bank.t[:, 0:512], upT.t[:, fc, t8 * 128:(t8 + 1) * 128], wd[nb].t[:, fc, :], start=(fc == 0), stop=(fc == 7)),
                                             [upT, wd[nb]], [bank], inc=(fc == 7))
                                    P.op("dve", lambda e: e.tensor_tensor(xres.t[:, t8, nb * 512:(nb + 1) * 512], xres.t[:, t8, nb * 512:(nb + 1) * 512], bank.t[:, 0:512], ALU.add), [bank], [xrv[t8]])
                                if fb == 3:
                                    if stop == "mlp" and "d_xm" in dbg_d:
                                        P.dma(dbg_d["d_xm"][tt * 128:(tt + 1) * 128, :], xres.t[:, t8, :], [xrv[t8]], [])
                                    if not last:
                                        P.dma(xcur.t[tt * 128:(tt + 1) * 128, :], xres.t[:, t8, :], [xrv[t8]], [xcv[tt]])
                                        if l + 1 < nlayers:
                                            cn = colps[(l + 1) % 2]
                                            norm_tile(xres.t[:, t8, :], xrv[t8], tt, (cn, cn.t[:, 0:8]))
                                    else:
                                        ot = otile[t8 % 2]
                                        nsq, nss = nsq_l[t8 % 2], nss_l[t8 % 2]
                                        P.op("act", lambda e: e.activation(nsq.t[:], xres.t[:, t8, :], AF.Square, accum_out=nss.t[:, 0:1]), [xrv[t8]], [nsq, nss])
                                        rsqrt_(nss.t[:, 1:2], nss.t[:, 0:1], 1.0 / D, [nss], [nss])
                                        P.op("dve", lambda e: e.scalar_tensor_tensor(ot.t[:], xres.t[:, t8, :], nss.t[:, 1:2], gfin.t[:], ALU.mult, ALU.mult), [xrv[t8], nss, gfin], [ot])
                                        P.dma(out_d[tt * 128:(tt + 1) * 128, :], ot.t[:], [ot], [])
            if stop in ("mix", "mlp"):
                stopped = True
                break
        if stopped:
            dump_ys()
        P.finish()
        print("build: ops", P.nops, "waits", P.nwait, {k: P.cnt[k] for k in P.cnt})
    return nc


def make_params(inp):
    colp = np.zeros((DEPTH, 128, NCOL), np.float32)
    rowp = np.zeros((DEPTH, NROW), np.float32)
    for l in range(DEPTH):
        colp[l, :, 0:8] = inp["norm_mix_g"][l].reshape(8, 128).T
        colp[l, :, 8:16] = inp["norm_mlp_g"][l].reshape(8, 128).T
        colp[l, :, 16:88] = inp["dn_conv_w"][l].reshape(3, 24, 128).transpose(2, 0, 1).reshape(128, 72)
        colp[l, :, 88:112] = inp["sc_conv_w"][l].reshape(3, 8, 128).transpose(2, 0, 1).reshape(128, 24)
        rowp[l, 0:1024] = inp["m_norm_g"][l]
        rowp[l, 1024:1152] = inp["dn_norm_g"][l]
        rowp[l, 1152:1168] = inp["m_gate_b"][l].reshape(16)
        rowp[l, 1168:1184] = inp["dn_a_log"][l].reshape(16)
        rowp[l, 1184:1200] = inp["dn_dt_bias"][l].reshape(16)
    return colp, rowp


def make_in_maps(inp, cores):
    colp, rowp = make_params(inp)
    shared = {"w_in": np.ascontiguousarray(inp["w_in"]), "w_branch": np.ascontiguousarray(inp["w_branch"]),
              "w_out": np.ascontiguousarray(inp["w_out"]), "w_up": np.ascontiguousarray(inp["w_up"]),
              "w_down": np.ascontiguousarray(inp["w_down"]), "colp": colp, "rowp": rowp,
              "gfin": np.ascontiguousarray(inp["norm_final_g"])}
    return [dict(shared, x=np.ascontiguousarray(inp["x"][b])) for b in cores]


def kernel(**inputs):
    inp = {k: np.asarray(v, dtype=np.float32) for k, v in inputs.items()}
    nc = build()
    in_maps = make_in_maps(inp, list(range(8)))
    res = run_bass_kernel_spmd(nc, in_maps, core_ids=list(range(8)))
    return np.stack([r["out"] for r in res.results], axis=0).astype(np.float32)
```

```python
import numpy as np
import concourse.bass as bass
import concourse.mybir as mybir
from concourse.bass_utils import run_bass_kernel_spmd
from contextlib import ExitStack

F32 = mybir.dt.float32
BF16 = mybir.dt.bfloat16
ALU = mybir.AluOpType
AF = mybir.ActivationFunctionType
AX = mybir.AxisListType

S = 2048
D = 1024
NT = 16
DEPTH = 4
NPROJ = 14384
DFF = 4096
EPS = 1e-6
NCOL = 112
NROW = 1200
O_MQ, O_MK, O_MV, O_MO, O_MG = 0, 1024, 2048, 3072, 4096
O_DQ, O_DK, O_DV, O_DZ, O_DG = 4112, 5136, 6160, 7184, 8208
O_SB, O_SC, O_SX, O_MRG = 8240, 9264, 10288, 11312
NEG = -30000.0


class Tok:
    __slots__ = ("sem", "val", "clk")

    def __init__(self, sem, val, clk):
        self.sem, self.val, self.clk = sem, val, clk


class Buf:
    __slots__ = ("t", "w", "r", "name", "excl", "lastrd")

    def __init__(self, t, name, excl=False):
        self.t, self.name = t, name
        self.w = None
        self.r = []
        self.excl = excl
        self.lastrd = None


class Prog:
    NDMA = 12

    def __init__(self, nc, es, same_engine_sync=True):
        self.nc, self.es = nc, es
        self.same = same_engine_sync
        self.eng = {"pe": nc.tensor, "act": nc.scalar, "dve": nc.vector, "pool": nc.gpsimd, "sp": nc.sync}
        self.sem = {k: es.enter_context(nc.semaphore("s_" + k)) for k in self.eng}
        self.cnt = {k: 0 for k in self.eng}
        self.clk = {k: {} for k in self.eng}
        self.pend = {k: [] for k in self.eng}
        self.dsem = [es.enter_context(nc.semaphore("d%d" % i)) for i in range(2 * self.NDMA)]
        self.dcnt = [0] * (2 * self.NDMA)
        self.dnext = {"sp": 0, "pool": 0}
        self.nwait = 0
        self.nops = 0

    def sb(self, name, shape, dt):
        t = self.es.enter_context(self.nc.sbuf_tensor(name, list(shape), dt))
        return Buf(t, name)

    def ps(self, name, shape, dt):
        t = self.es.enter_context(self.nc.psum_tensor(name, list(shape), dt))
        return Buf(t, name, excl=True)

    def dram(self, name, shape, dt):
        t = self.nc.dram_tensor(name, list(shape), dt, kind="Internal").ap()
        return Buf(t, name)

    def _need(self, reads, writes):
        toks = []
        for b in reads:
            if b.w is not None:
                toks.append(b.w)
        for b in writes:
            if b.w is not None:
                toks.append(b.w)
            toks.extend(b.r)
        return toks

    def _wait(self, e, toks):
        clk = self.clk[e]
        eng = self.eng[e]
        own = self.sem[e]
        best = {}
        for t in toks:
            if t.sem is own and (e == "pe" or not self.same):
                continue
            k = id(t.sem)
            if clk.get(k, 0) >= t.val:
                continue
            if k not in best or best[k].val < t.val:
                best[k] = t
        for k, t in best.items():
            if clk.get(k, 0) >= t.val:
                continue
            eng.wait_ge(t.sem, t.val)
            self.nwait += 1
            clk[k] = t.val
            for kk, vv in t.clk.items():
                if clk.get(kk, 0) < vv:
                    clk[kk] = vv

    def op(self, e, fn, reads, writes, inc=True):
        xr = [b for b in reads if b.excl]
        if xr:
            reads = [b for b in reads if not b.excl]
        toks = self._need(reads, writes)
        for b in xr:
            if b.w is not None and not (b.lastrd == e and b.w.sem is self.sem[e]):
                toks.append(b.w)
            toks.extend(b.r)
        self._wait(e, toks)
        for b in writes:
            b.lastrd = None
        if xr:
            writes = list(writes) + xr
        ins = fn(self.eng[e])
        self.nops += 1
        if not inc:
            self.pend[e].append((reads, writes))
            return ins
        self.cnt[e] += 1
        ins.then_inc(self.sem[e], 1)
        tok = Tok(self.sem[e], self.cnt[e], dict(self.clk[e]))
        tok.clk[id(self.sem[e])] = self.cnt[e]
        for (rs, ws) in self.pend[e] + [(reads, writes)]:
            for b in rs:
                b.r.append(tok)
            for b in ws:
                b.w = tok
                b.r = []
        for b in xr:
            b.lastrd = e
        self.pend[e] = []
        return ins

    def dma(self, out, in_, reads, writes, q="sp"):
        toks = self._need(reads, writes)
        i = self.dnext[q] + (self.NDMA if q == "pool" else 0)
        self.dnext[q] = (self.dnext[q] + 1) % self.NDMA
        s = self.dsem[i]
        if self.dcnt[i] > 0:
            toks.append(Tok(s, self.dcnt[i], {}))
        self._wait(q, toks)
        ins = self.eng[q].dma_start(out=out, in_=in_)
        self.dcnt[i] += 16
        ins.then_inc(s, 16)
        tok = Tok(s, self.dcnt[i], dict(self.clk[q]))
        for b in reads:
            b.r.append(tok)
        for b in writes:
            b.w = tok
            b.r = []
        self.nops += 1
        return ins

    def barrier(self):
        toks = []
        for i in range(2 * self.NDMA):
            if self.dcnt[i] > 0:
                toks.append(Tok(self.dsem[i], self.dcnt[i], {}))
        for k in self.eng:
            if self.cnt[k] > 0:
                toks.append(Tok(self.sem[k], self.cnt[k], {}))
        for e in self.eng:
            self._wait(e, [t for t in toks if t.sem is not self.sem[e]])

    def finish(self):
        toks = []
        for i in range(2 * self.NDMA):
            if self.dcnt[i] > 0:
                toks.append(Tok(self.dsem[i], self.dcnt[i], {}))
        for k in self.eng:
            if self.cnt[k] > 0 and k != "sp":
                toks.append(Tok(self.sem[k], self.cnt[k], {}))
        self._wait("sp", toks)


STAG_CFG = [4]
NS_CFG = [4]


def build(nlayers=DEPTH, dbg=(), stop=None):
    nc = bass.Bass("TRN2", target_bir_lowering=False)

    def din(name, shape):
        return nc.dram_tensor(name, list(shape), F32, kind="ExternalInput").ap()

    x_d = din("x", [S, D])
    w_in_d = din("w_in", [DEPTH, D, NPROJ])
    w_br_d = din("w_branch", [DEPTH, 3, D, D])
    w_out_d = din("w_out", [DEPTH, D, D])
    w_up_d = din("w_up", [DEPTH, D, DFF])
    w_dn_d = din("w_down", [DEPTH, DFF, D])
    colp_d = din("colp", [DEPTH, 128, NCOL])
    rowp_d = din("rowp", [DEPTH, NROW])
    gfin_d = din("gfin", [D])
    out_d = nc.dram_tensor("out", [S, D], F32, kind="ExternalOutput").ap()
    dbg_d = {}
    for name, shape, dt in (("d_ys", [3, D, S], BF16), ("d_xm", [S, D], F32)):
        if name in dbg:
            dbg_d[name] = nc.dram_tensor(name, shape, dt, kind="ExternalOutput").ap()

    with ExitStack() as es:
        P = Prog(nc, es)
        hT = P.sb("hT", [128, 8, S], BF16)
        hTv = [Buf(hT.t, "hT%d" % i) for i in range(NT)]
        NWB = 4
        wb = [P.sb("wb%d" % i, [128, 8, 512], BF16) for i in range(NWB)]
        wbi = [0]

        wlim = [NWB]

        def nextw():
            b = wb[wbi[0] % wlim[0]]
            wbi[0] += 1
            return b

        identb = P.sb("identb", [128, 128], BF16)
        identf = P.sb("identf", [128, 128], F32)
        onesb = P.sb("onesb", [128, 128], BF16)
        onesf = P.sb("onesf", [128, 128], F32)
        tincl = P.sb("tincl", [128, 128], F32)
        tinclT = P.sb("tinclT", [128, 128], F32)
        mlo = P.sb("mlo", [128, 128], F32)
        mup = P.sb("mup", [128, 128], F32)
        slf = P.sb("slf", [128, 128], F32)
        suf = P.sb("suf", [128, 128], F32)
        colps = [P.sb("colp_sb%d" % i, [128, NCOL], F32) for i in range(2)]
        rowp = P.sb("rowp_sb", [128, NROW], F32)
        wsm = P.sb("wsm", [128, 8, 48], BF16)
        wsf = P.sb("wsf", [128, 8, 48], F32)
        nsq_l = [P.sb("nsq%d" % i, [128, D], BF16) for i in range(2)]
        nss_l = [P.sb("nss%d" % i, [128, 2], F32) for i in range(2)]
        nxn_l = [P.sb("nxn%d" % i, [128, D], BF16) for i in range(2)]
        nrm_i = [0]
        pb = [P.ps("pb%d" % i, [128, 512], F32) for i in range(7)]
        pbt = P.ps("pbt", [128, 1024], BF16)
        ysT = P.dram("ysT", [3, D, S], BF16)
        ysv = [[Buf(ysT.t, "ys%d_%d" % (n, c)) for c in range(8)] for n in range(3)]
        xcur = P.dram("xcur", [S, D], F32)
        xcv = [Buf(xcur.t, "xc%d" % i) for i in range(NT)]

        def pool(fn, r, w):
            P.op("pool", fn, r, w)

        pool(lambda e: e.memset(onesf.t[:], 1.0), [], [onesf])
        pool(lambda e: e.memset(onesb.t[:], 1.0), [], [onesb])
        pool(lambda e: e.memset(identf.t[:], 0.0), [], [identf])
        pool(lambda e: e.affine_select(identf.t[:], identf.t[:], pattern=[[-1, 128]], compare_op=ALU.not_equal, fill=1.0, base=0, channel_multiplier=1), [identf], [identf])
        pool(lambda e: e.tensor_copy(identb.t[:], identf.t[:]), [identf], [identb])
        pool(lambda e: e.affine_select(tincl.t[:], onesf.t[:], pattern=[[1, 128]], compare_op=ALU.is_ge, fill=0.0, base=0, channel_multiplier=-1), [onesf], [tincl])
        pool(lambda e: e.affine_select(tinclT.t[:], onesf.t[:], pattern=[[-1, 128]], compare_op=ALU.is_ge, fill=0.0, base=0, channel_multiplier=1), [onesf], [tinclT])
        pool(lambda e: e.memset(mlo.t[:], 0.0), [], [mlo])
        pool(lambda e: e.affine_select(mlo.t[:], mlo.t[:], pattern=[[1, 128]], compare_op=ALU.is_ge, fill=NEG, base=0, channel_multiplier=-1), [mlo], [mlo])
        pool(lambda e: e.memset(mup.t[:], 0.0), [], [mup])
        pool(lambda e: e.affine_select(mup.t[:], mup.t[:], pattern=[[-1, 128]], compare_op=ALU.is_ge, fill=NEG, base=0, channel_multiplier=1), [mup], [mup])
        pool(lambda e: e.affine_select(slf.t[:], onesf.t[:], pattern=[[-1, 128]], compare_op=ALU.is_gt, fill=0.0, base=0, channel_multiplier=1), [onesf], [slf])
        pool(lambda e: e.affine_select(suf.t[:], onesf.t[:], pattern=[[1, 128]], compare_op=ALU.is_gt, fill=0.0, base=0, channel_multiplier=-1), [onesf], [suf])

        def dump(name, buf, ap, shape, dt=F32):
            if ("D_" + name) in dbg:
                dd = nc.dram_tensor("D_" + name, list(shape), dt, kind="ExternalOutput").ap()
                P.dma(dd, ap, [buf], [])

        uid = [0]

        def scope():
            class _S:
                def __enter__(s_):
                    s_.es = ExitStack()
                    s_.es.__enter__()
                    return s_

                def sb(s_, name, shape, dt):
                    uid[0] += 1
                    return Buf(s_.es.enter_context(nc.sbuf_tensor("%s_u%d" % (name, uid[0]), list(shape), dt)), name)

                def __exit__(s_, *a):
                    P.barrier()
                    return s_.es.__exit__(*a)
            return _S()

        evi = [0]

        def evac_copy(out_ap, in_ap, reads, writes, scale=None):
            evi[0] += 1
            if evi[0] % 2 == 0:
                if scale is None:
                    P.op("act", lambda e: e.copy(out_ap, in_ap), reads, writes)
                else:
                    P.op("act", lambda e: e.mul(out_ap, in_ap, scale), reads, writes)
            else:
                if scale is None:
                    P.op("dve", lambda e: e.tensor_copy(out_ap, in_ap), reads, writes)
                else:
                    P.op("dve", lambda e: e.tensor_scalar_mul(out_ap, in_ap, scale), reads, writes)

        def wload(dst, col0, src2d, c0, n):
            P.dma(dst.t[:, :, col0:col0 + n], src2d.rearrange("(kc p) n -> p kc n", p=128)[:, :, c0:c0 + n], [], [dst], q="pool")

        def rsqrt_(dst_ap, src_ap, scale, reads, writes):
            P.op("act", lambda e: e.activation(dst_ap, src_ap, AF.Ln, bias=EPS, scale=scale), reads, writes)
            P.op("act", lambda e: e.activation(dst_ap, dst_ap, AF.Exp, scale=-0.5), writes, writes)

        def norm_tile(xt_ap, xbuf, tt, gcol):
            gbuf, gap = gcol
            nrm_i[0] += 1
            nsq, nss, nxn = nsq_l[nrm_i[0] % 2], nss_l[nrm_i[0] % 2], nxn_l[nrm_i[0] % 2]
            pbo = (nrm_i[0] % 2) * 0
            P.op("act", lambda e: e.activation(nsq.t[:], xt_ap, AF.Square, accum_out=nss.t[:, 0:1]), [xbuf], [nsq, nss])
            rsqrt_(nss.t[:, 1:2], nss.t[:, 0:1], 1.0 / D, [nss], [nss])
            P.op("dve", lambda e: e.tensor_scalar(nxn.t[:], xt_ap, nss.t[:, 1:2], None, ALU.mult), [xbuf, nss], [nxn])
            for c in range(8):
                P.op("pe", lambda e: e.transpose(pbt.t[:, c * 128:(c + 1) * 128], nxn.t[:, c * 128:(c + 1) * 128], identb.t[:]), [nxn, identb], [pbt], inc=(c == 7))
            P.op("dve", lambda e: e.tensor_tensor(hT.t[:, :, tt * 128:(tt + 1) * 128], pbt.t[:].rearrange("p (c t) -> p c t", c=8),
                                                  gap.unsqueeze(2).to_broadcast([128, 8, 128]), ALU.mult), [pbt, gbuf], [hTv[tt]])

        def softplus_(dst, src, t1, t2, n, sign_logsig=False):
            P.op("dve", lambda e: e.scalar_tensor_tensor(t1.t[:, 0:n], src.t[:, 0:n], -1.0, src.t[:, 0:n], ALU.mult, ALU.max), [src], [t1])
            P.op("act", lambda e: e.activation(t2.t[:, 0:n], t1.t[:, 0:n], AF.Exp, scale=-1.0), [t1], [t2])
            P.op("act", lambda e: e.activation(t2.t[:, 0:n], t2.t[:, 0:n], AF.Ln, bias=1.0), [t2], [t2])
            if sign_logsig:
                P.op("dve", lambda e: e.scalar_tensor_tensor(dst.t[:, 0:n], src.t[:, 0:n], 0.0, t2.t[:, 0:n], ALU.min, ALU.subtract), [src, t2], [dst])
            else:
                P.op("dve", lambda e: e.scalar_tensor_tensor(dst.t[:, 0:n], src.t[:, 0:n], 0.0, t2.t[:, 0:n], ALU.max, ALU.add), [src, t2], [dst])

        def decay(out, negc_ap, bias_ap, cbufs, mask, bank, diag):
            P.op("dve", lambda e: e.tensor_scalar(diag.t[:], identf.t[:], negc_ap, None, ALU.mult), [identf] + cbufs, [diag])
            P.op("pe", lambda e: e.matmul(bank.t[:, 0:128], onesf.t[:], diag.t[:], start=True, stop=False), [onesf, diag], [bank], inc=False)
            P.op("pe", lambda e: e.matmul(bank.t[:, 0:128], identf.t[:], mask.t[:], start=False, stop=True), [identf, mask], [bank])
            P.op("act", lambda e: e.activation(out.t[:], bank.t[:, 0:128], AF.Exp, bias=bias_ap, scale=1.0), [bank] + cbufs, [out])

        def proj_fm(w, wcol, banks, evac, tbs=range(4)):
            for tb in tbs:
                bank = banks[tb % len(banks)]
                for kc in range(8):
                    P.op("pe", lambda e: e.matmul(bank.t[:, 0:512], w.t[:, kc, wcol:wcol + 128], hT.t[:, kc, tb * 512:(tb + 1) * 512],
                                                  start=(kc == 0), stop=(kc == 7)), [w] + hTv[tb * 4:tb * 4 + 4], [bank], inc=(kc == 7))
                evac(tb, bank)

        def proj_tm(w, wcol, ncols, tt, bank):
            for kc in range(8):
                P.op("pe", lambda e: e.matmul(bank.t[:, 0:ncols], hT.t[:, kc, tt * 128:(tt + 1) * 128], w.t[:, kc, wcol:wcol + ncols],
                                              start=(kc == 0), stop=(kc == 7)), [w, hTv[tt]], [bank], inc=(kc == 7))

        def dump_ys():
            if "d_ys" in dbg_d:
                for n in range(3):
                    for c in range(8):
                        P.dma(dbg_d["d_ys"][n, c * 128:(c + 1) * 128, :], ysT.t[n, c * 128:(c + 1) * 128, :], [ysv[n][c]], [])

        stopped = False
        P.dma(colps[0].t[:], colp_d[0], [], [colps[0]])
        for l in range(nlayers):
            w_in = w_in_d[l]
            colp = colps[l % 2]
            P.dma(rowp.t[:], rowp_d[l].partition_broadcast(128), [], [rowp])
            wrr = w_in.rearrange("(kc p) n -> p kc n", p=128)
            P.dma(wsf.t[:, :, 0:16], wrr[:, :, O_MG:O_MG + 16], [], [wsf])
            P.dma(wsf.t[:, :, 16:48], wrr[:, :, O_DG:O_DG + 32], [], [wsf])
            P.op("dve", lambda e: e.tensor_copy(wsm.t[:], wsf.t[:]), [wsf], [wsm])
            if l == 0:
                with scope() as sc:
                    xin = [sc.sb("xin%d" % i, [128, D], F32) for i in range(2)]
                    for tt in range(NT):
                        xb_ = xin[tt % 2]
                        P.dma(xb_.t[:], x_d[tt * 128:(tt + 1) * 128, :], [], [xb_])
                        P.dma(xcur.t[tt * 128:(tt + 1) * 128, :], xb_.t[:], [xb_], [xcv[tt]])
                        norm_tile(xb_.t[:], xb_, tt, (colp, colp.t[:, 0:8]))

            with scope() as sc:
                sb1 = sc.sb
                gm = sb1("gm", [128, 256], F32)
                ipre = sb1("ipre", [128, 128], F32)
                fpre = sb1("fpre", [128, 128], F32)
                lf = sb1("lf", [128, 128], F32)
                t1 = sb1("t1", [128, 256], F32)
                t2 = sb1("t2", [128, 256], F32)
                bcs = sb1("bcs", [128, 128], F32)
                totb = sb1("totb", [128, 128], F32)
                biasc = sb1("biasc", [128, 128], F32)
                expb = sb1("expb", [128, 128], F32)
                wgt = sb1("wgt", [128, 128], F32)
                dec = sb1("dec", [128, 128], F32)
                for tt in range(NT):
                    for kc in range(8):
                        P.op("pe", lambda e: e.matmul(pb[0].t[:, tt * 16:(tt + 1) * 16], hT.t[:, kc, tt * 128:(tt + 1) * 128], wsm.t[:, kc, 0:16],
                                                      start=(kc == 0), stop=(kc == 7)), [wsm, hTv[tt]], [pb[0]], inc=(kc == 7 and tt == NT - 1))
                P.op("dve", lambda e: e.tensor_tensor(gm.t[:].rearrange("p (t g) -> p t g", g=16), pb[0].t[:, 0:256].rearrange("p (t g) -> p t g", g=16),
                                                      rowp.t[:, 1152:1168].unsqueeze(1).to_broadcast([128, 16, 16]), ALU.add), [pb[0], rowp], [gm])
                gm5 = gm.t[:].rearrange("p (t d w h) -> p t d w h", d=2, w=2, h=4)
                v4 = lambda b_: b_.t[:].rearrange("p (t d h) -> p t d h", d=2, h=4)
                P.op("dve", lambda e: e.tensor_copy(v4(ipre), gm5[:, :, :, 0, :]), [gm], [ipre])
                P.op("dve", lambda e: e.tensor_copy(v4(fpre), gm5[:, :, :, 1, :]), [gm], [fpre])
                softplus_(lf, fpre, t1, t2, 128, sign_logsig=True)
                P.op("pe", lambda e: e.matmul(pb[1].t[:, 0:128], tincl.t[:], lf.t[:], start=True, stop=True), [tincl, lf], [pb[1]], inc=False)
                P.op("pe", lambda e: e.matmul(pb[1].t[:, 128:256], tinclT.t[:], lf.t[:], start=True, stop=True), [tinclT, lf], [pb[1]], inc=False)
                P.op("pe", lambda e: e.matmul(pb[1].t[:, 256:384], onesf.t[:], lf.t[:], start=True, stop=True), [onesf, lf], [pb[1]])
                pv = lambda a, b: pb[1].t[:, a:b].rearrange("p (t d h) -> p t d h", d=2, h=4)
                P.op("dve", lambda e: e.tensor_copy(v4(bcs)[:, :, 0, :], pv(0, 128)[:, :, 0, :]), [pb[1]], [bcs])
                P.op("dve", lambda e: e.tensor_copy(v4(bcs)[:, :, 1, :], pv(128, 256)[:, :, 1, :]), [pb[1]], [bcs])
                P.op("dve", lambda e: e.tensor_copy(totb.t[:], pb[1].t[:, 256:384]), [pb[1]], [totb])
                P.op("dve", lambda e: e.tensor_sub(biasc.t[:], ipre.t[:], bcs.t[:]), [ipre, bcs], [biasc])
                P.op("act", lambda e: e.activation(expb.t[:], bcs.t[:], AF.Exp), [bcs], [expb])
                P.op("dve", lambda e: e.tensor_add(t1.t[:, 0:128], biasc.t[:], totb.t[:]), [biasc, totb], [t1])
                P.op("act", lambda e: e.activation(wgt.t[:], t1.t[:, 0:128], AF.Exp), [t1], [wgt])
                P.op("act", lambda e: e.activation(dec.t[:], totb.t[:], AF.Exp), [totb], [dec])

                qT = sb1("qT", [128, 2, S], BF16)
                kT = sb1("kT", [128, 2, S], BF16)
                ktm = sb1("ktm", [128, NT, 256], BF16)
                vtm = sb1("vtm", [128, NT, 257], BF16)
                hm = sb1("hm", [128, NT, 256], F32)
                hmv = [Buf(hm.t, "hm%d" % i) for i in range(NT)]
                Cst = [sb1("Cst%d" % d_, [128, 2, 257], F32) for d_ in range(2)]
                Cb = [sb1("Cb%d" % d_, [128, 2, 257], BF16) for d_ in range(2)]
                DT = [sb1("DT%d" % d_, [128, 128], F32) for d_ in range(2)]
                diag = [sb1("diag%d" % d_, [128, 128], F32) for d_ in range(2)]
                scT = [sb1("scT%d" % d_, [128, 128], BF16) for d_ in range(2)]
                Asb = [sb1("Asb%d" % d_, [128, 257], F32) for d_ in range(2)]
                comb = [sb1("comb%d" % d_, [128, 257], F32) for d_ in range(2)]
                rr = [sb1("rr%d" % d_, [128, 2], F32) for d_ in range(2)]
                kw = [sb1("kw%d" % d_, [128, 256], BF16) for d_ in range(2)]
                og_l = [sb1("og%d" % i, [128, 256], F32) for i in range(2)]
                ty_l = [sb1("ty%d" % i, [128, 256], F32) for i in range(2)]
                yb_l = [sb1("yb%d" % i, [128, 256], BF16) for i in range(2)]
                ty = ty_l[0]
                ssq = sb1("ssq", [128, 2 * NT], F32)
                ymT = sb1("ymT", [128, 2, S], BF16)
                P.op("dve", lambda e: e.memset(vtm.t[:, :, 256:257], 1.0), [], [vtm])

                for h in range(4):
                    wqk, wkv, wo = nextw(), nextw(), nextw()
                    wload(wqk, 0, w_in, O_MQ + h * 256, 256)
                    wload(wqk, 256, w_in, O_MK + h * 256, 256)
                    wload(wkv, 0, w_in, O_MK + h * 256, 256)
                    wload(wkv, 256, w_in, O_MV + h * 256, 256)
                    wload(wo, 0, w_in, O_MO + h * 256, 256)
                    for ft in range(2):
                        proj_fm(wqk, ft * 128, pb[0:4], lambda tb, bank: evac_copy(qT.t[:, ft, tb * 512:(tb + 1) * 512], bank.t[:, 0:512], [bank], [qT], scale=0.0625))
                        proj_fm(wqk, 256 + ft * 128, pb[0:4], lambda tb, bank: evac_copy(kT.t[:, ft, tb * 512:(tb + 1) * 512], bank.t[:, 0:512], [bank], [kT]))
                    for tt in range(NT):
                        bank = pb[tt % 4]
                        proj_tm(wkv, 0, 512, tt, bank)
                        P.op("act", lambda e: e.copy(ktm.t[:, tt, :], bank.t[:, 0:256]), [bank], [ktm])
                        P.op("dve", lambda e: e.tensor_copy(vtm.t[:, tt, 0:256], bank.t[:, 256:512]), [bank], [vtm])
                    def gen_mrec(d_, h=h):
                        bRQ, bS, bA = (pb[0], pb[3])[d_], (pb[1], pb[4])[d_], (pb[2], pb[5])[d_]
                        mask = mlo if d_ == 0 else mup
                        for step in range(NT):
                            c = step if d_ == 0 else NT - 1 - step
                            cs = slice(c * 128, (c + 1) * 128)
                            gi = (c * 2 + d_) * 4 + h
                            col = lambda b_: b_.t[:, gi:gi + 1]
                            P.op("dve", lambda e: e.tensor_scalar(diag[d_].t[:], identf.t[:], col(bcs), None, ALU.mult), [identf, bcs], [diag[d_]])
                            if step < NT - 1:
                                P.op("act", lambda e: e.activation(kw[d_].t[:], ktm.t[:, c, :], AF.Copy, scale=col(wgt)), [ktm, wgt], [kw[d_]])
                            yield
                            P.op("pe", lambda e: e.matmul(bRQ.t[:, 0:128], onesf.t[:], diag[d_].t[:], start=True, stop=False), [onesf, diag[d_]], [bRQ], inc=False)
                            P.op("pe", lambda e: e.matmul(bRQ.t[:, 0:128], identf.t[:], mask.t[:], start=False, stop=True), [identf, mask], [bRQ], inc=False)
                            for kc in range(2):
                                P.op("pe", lambda e: e.matmul(bRQ.t[:, 128:256], kT.t[:, kc, cs], qT.t[:, kc, cs], start=(kc == 0), stop=(kc == 1)), [kT, qT], [bRQ], inc=(kc == 1))
                            if step > 0:
                                for kc in range(2):
                                    P.op("pe", lambda e: e.matmul(bA.t[:, 0:257], qT.t[:, kc, cs], Cb[d_].t[:, kc, :], start=(kc == 0), stop=(kc == 1)), [qT, Cb[d_]], [bA], inc=(kc == 1))
                            yield
                            P.op("act", lambda e: e.activation(DT[d_].t[:], bRQ.t[:, 0:128], AF.Exp, bias=col(biasc), scale=1.0), [bRQ, biasc], [DT[d_]])
                            if step > 0:
                                P.op("act", lambda e: e.activation(Asb[d_].t[:], bA.t[:, 0:257], AF.Copy, scale=col(expb)), [bA, expb], [Asb[d_]])
                            yield
                            P.op("dve", lambda e: e.tensor_tensor(scT[d_].t[:], bRQ.t[:, 128:256], DT[d_].t[:], ALU.mult), [bRQ, DT[d_]], [scT[d_]])
                            yield
                            P.op("pe", lambda e: e.matmul(bS.t[:, 0:257], scT[d_].t[:], vtm.t[:, c, :], start=True, stop=True), [scT[d_], vtm], [bS])
                            yield
                            if step > 0:
                                P.op("dve", lambda e: e.tensor_tensor(comb[d_].t[:], Asb[d_].t[:], bS.t[:, 0:257], ALU.add), [Asb[d_], bS], [comb[d_]])
                            else:
                                P.op("dve", lambda e: e.tensor_copy(comb[d_].t[:], bS.t[:, 0:257]), [bS], [comb[d_]])
                            den = comb[d_].t[:, 256:257]
                            P.op("dve", lambda e: e.scalar_tensor_tensor(rr[d_].t[:, 0:1], den, -1.0, den, ALU.mult, ALU.max), [comb[d_]], [rr[d_]])
                            P.op("dve", lambda e: e.tensor_scalar_max(rr[d_].t[:, 0:1], rr[d_].t[:, 0:1], 1.0), [rr[d_]], [rr[d_]])
                            P.op("dve", lambda e: e.reciprocal(rr[d_].t[:, 1:2], rr[d_].t[:, 0:1]), [rr[d_]], [rr[d_]])
                            if step < NT - 1:
                                P.op("pe", lambda e: e.matmul(bS.t[:, 0:257], kw[d_].t[:, 0:128], vtm.t[:, c, :], start=True, stop=True), [kw[d_], vtm], [bS])
                                P.op("pe", lambda e: e.matmul(bA.t[:, 0:257], kw[d_].t[:, 128:256], vtm.t[:, c, :], start=True, stop=True), [kw[d_], vtm], [bA])
                            yield
                            first = (d_ == 0 and c < 8) or (d_ == 1 and c >= 8)
                            if first:
                                P.op("act", lambda e: e.activation(hm.t[:, c, :], comb[d_].t[:, 0:256], AF.Copy, scale=rr[d_].t[:, 1:2]), [comb[d_], rr[d_]], [hmv[c]])
                            else:
                                P.op("dve", lambda e: e.scalar_tensor_tensor(hm.t[:, c, :], comb[d_].t[:, 0:256], rr[d_].t[:, 1:2], hm.t[:, c, :], ALU.mult, ALU.add), [comb[d_], rr[d_]], [hmv[c]])
                            if step < NT - 1:
                                for m, bC in enumerate((bS, bA)):
                                    if step == 0:
                                        P.op("dve", lambda e: e.tensor_copy(Cst[d_].t[:, m, :], bC.t[:, 0:257]), [bC], [Cst[d_]])
                                    else:
                                        P.op("dve", lambda e: e.scalar_tensor_tensor(Cst[d_].t[:, m, :], Cst[d_].t[:, m, :], col(dec), bC.t[:, 0:257], ALU.mult, ALU.add), [bC, dec], [Cst[d_]])
                                yield
                                P.op("act", lambda e: e.copy(Cb[d_].t[:], Cst[d_].t[:]), [Cst[d_]], [Cb[d_]])
                            yield

                    gens = [gen_mrec(0), gen_mrec(1)]
                    while gens:
                        for g_ in list(gens):
                            try:
                                next(g_)
                            except StopIteration:
                                gens.remove(g_)
                    for tt in range(NT):
                        P.op("act", lambda e: e.activation(ty.t[:], hm.t[:, tt, :], AF.Square, accum_out=ssq.t[:, tt:tt + 1]), [hmv[tt]], [ty, ssq])
                    rsqrt_(ssq.t[:, NT:2 * NT], ssq.t[:, 0:NT], 1.0 / 256.0, [ssq], [ssq])
                    for tt in range(NT):
                        bank = pb[tt % 4]
                        og, ty, yb = og_l[tt % 2], ty_l[tt % 2], yb_l[tt % 2]
                        proj_tm(wo, 0, 256, tt, bank)
                        P.op("act", lambda e: e.activation(og.t[:], bank.t[:, 0:256], AF.Sigmoid), [bank], [og])
                        P.op("dve", lambda e: e.scalar_tensor_tensor(ty.t[:], hm.t[:, tt, :], ssq.t[:, NT + tt:NT + tt + 1], rowp.t[:, h * 256:(h + 1) * 256], ALU.mult, ALU.mult), [hmv[tt], ssq, rowp], [ty])
                        P.op("dve", lambda e: e.tensor_tensor(yb.t[:], ty.t[:], og.t[:], ALU.mult), [ty, og], [yb])
                        for j in range(2):
                            P.op("pe", lambda e: e.transpose(pbt.t[:, j * 128:(j + 1) * 128], yb.t[:, j * 128:(j + 1) * 128], identb.t[:]), [yb, identb], [pbt], inc=(j == 1))
                        P.op("act", lambda e: e.copy(ymT.t[:, :, tt * 128:(tt + 1) * 128], pbt.t[:, 0:256].rearrange("p (j t) -> p j t", j=2)), [pbt], [ymT])
                    P.dma(ysT.t[0, h * 256:(h + 1) * 256, :].rearrange("(j p) t -> p j t", p=128), ymT.t[:], [ymT], [ysv[0][2 * h], ysv[0][2 * h + 1]])
                    if stop == "m0":
                        break
            if stop in ("m0", "mlstm"):
                stopped = True
                break

            with scope() as sc:
                sb2 = sc.sb
                A2 = lambda name: sb2(name, [128, 256], F32)
                v4 = lambda b_: b_.t[:].rearrange("p (t d h) -> p t d h", d=2, h=8)
                Gc, negG, expG, bexpG, kdw, glb, negbeta, beta = [A2(n) for n in ("Gc", "negG", "expG", "bexpG", "kdw", "glb", "negbeta", "beta")]
                with scope() as sct:
                    dg = sct.sb("dg", [128, 512], F32)
                    apre, spl, gg, t1, t2 = [sct.sb(n, [128, 256], F32) for n in ("apre", "spl", "gg", "t1d", "t2d")]
                    ea = sct.sb("ea", [128, 16], F32)
                    for tt in range(NT):
                        for kc in range(8):
                            P.op("pe", lambda e: e.matmul(pb[0].t[:, tt * 32:(tt + 1) * 32], hT.t[:, kc, tt * 128:(tt + 1) * 128], wsm.t[:, kc, 16:48],
                                                          start=(kc == 0), stop=(kc == 7)), [wsm, hTv[tt]], [pb[0]], inc=(kc == 7 and tt == NT - 1))
                    P.op("act", lambda e: e.copy(dg.t[:], pb[0].t[:, 0:512]), [pb[0]], [dg])
                    dg5 = dg.t[:].rearrange("p (t d w h) -> p t d w h", d=2, w=2, h=8)
                    P.op("act", lambda e: e.activation(v4(beta), dg5[:, :, :, 0, :], AF.Sigmoid), [dg], [beta])
                    P.op("dve", lambda e: e.tensor_tensor(v4(apre), dg5[:, :, :, 1, :],
                                                          rowp.t[:, 1184:1200].rearrange("p (d h) -> p d h", d=2).unsqueeze(1).to_broadcast([128, 16, 2, 8]), ALU.add), [dg, rowp], [apre])
                    softplus_(spl, apre, t1, t2, 256)
                    P.op("act", lambda e: e.activation(ea.t[:], rowp.t[:, 1168:1184], AF.Exp), [rowp], [ea])
                    P.op("dve", lambda e: e.scalar_tensor_tensor(v4(gg), v4(spl), -1.0,
                                                                 ea.t[:].rearrange("p (d h) -> p d h", d=2).unsqueeze(1).to_broadcast([128, 16, 2, 8]), ALU.mult, ALU.mult), [spl, ea], [gg])
                    P.op("pe", lambda e: e.matmul(pb[1].t[:, 0:256], tincl.t[:], gg.t[:], start=True, stop=True), [tincl, gg], [pb[1]], inc=False)
                    P.op("pe", lambda e: e.matmul(pb[1].t[:, 256:512], tinclT.t[:], gg.t[:], start=True, stop=True), [tinclT, gg], [pb[1]], inc=False)
                    P.op("pe", lambda e: e.matmul(pb[2].t[:, 0:256], onesf.t[:], gg.t[:], start=True, stop=True), [onesf, gg], [pb[2]])
                    pv = lambda a, b: pb[1].t[:, a:b].rearrange("p (t d h) -> p t d h", d=2, h=8)
                    P.op("dve", lambda e: e.tensor_copy(v4(Gc)[:, :, 0, :], pv(0, 256)[:, :, 0, :]), [pb[1]], [Gc])
                    P.op("dve", lambda e: e.tensor_copy(v4(Gc)[:, :, 1, :], pv(256, 512)[:, :, 1, :]), [pb[1]], [Gc])
                    P.op("dve", lambda e: e.tensor_scalar_mul(negG.t[:], Gc.t[:], -1.0), [Gc], [negG])
                    P.op("act", lambda e: e.activation(expG.t[:], Gc.t[:], AF.Exp), [Gc], [expG])
                    P.op("dve", lambda e: e.tensor_mul(bexpG.t[:], beta.t[:], expG.t[:]), [beta, expG], [bexpG])
                    P.op("dve", lambda e: e.tensor_tensor(t1.t[:], pb[2].t[:, 0:256], Gc.t[:], ALU.subtract), [pb[2], Gc], [t1])
                    P.op("act", lambda e: e.activation(kdw.t[:], t1.t[:], AF.Exp), [t1], [kdw])
                    P.op("act", lambda e: e.activation(glb.t[:], pb[2].t[:, 0:256], AF.Exp), [pb[2]], [glb])
                    P.op("dve", lambda e: e.tensor_scalar_mul(negbeta.t[:], beta.t[:], -1.0), [beta], [negbeta])
                for nm, b_ in (("Gc", Gc), ("expG", expG), ("kdw", kdw), ("glb", glb), ("beta", beta)):
                    dump(nm, b_, b_.t[:], [128, 256])

                pc = sb2("pc", [128, S + 2], F32)
                cv = sb2("cv", [128, S], F32)
                sq = sb2("sq", [128, S], BF16)
                rin = sb2("rin", [128, 512], F32)
                qnT = sb2("qnT", [128, S], BF16)
                knT = sb2("knT", [128, S], BF16)
                vT = sb2("vT", [128, S], BF16)
                ktm = sb2("ktm2", [128, NT, 128], BF16)
                vtm = sb2("vtm2", [128, NT, 128], BF16)
                WTs = sb2("WTs", [128, 32, 128], BF16)
                Us = sb2("Us", [128, 32, 128], F32)
                ATs = sb2("ATs", [128, 32, 128], BF16)
                stv = [Buf(None, "st%d" % i) for i in range(32)]
                osb = sb2("osb", [128, NT, 128], F32)
                osv = [Buf(osb.t, "os%d" % i) for i in range(NT)]
                NS = NS_CFG[0]
                STAG = STAG_CFG[0]
                wlim[0] = 3 if NS_CFG[0] > 4 else NWB
                _flat = wb[3].t[:].rearrange("p a b -> p (a b)")
                _off = [0]

                def slot_tile(i, name, shape, dt):
                    if i < 4:
                        return sb2("%s%d" % (name, i), shape, dt)
                    n = shape[1] * (2 if dt == F32 else 1)
                    ap = _flat[:, _off[0]:_off[0] + n]
                    _off[0] += n
                    if dt == F32:
                        ap = ap.bitcast(F32)
                    return Buf(ap, "%s%d" % (name, i))
                Dm = [slot_tile(i, "Dm", [128, 128], F32) for i in range(NS)]
                dgl = [slot_tile(i, "dgl", [128, 128], F32) for i in range(NS)]
                Do1 = [slot_tile(i, "Do1", [128, 128], F32) for i in range(NS)]
                Do2 = [slot_tile(i, "Do2", [128, 128], F32) for i in range(NS)]
                CH = [[slot_tile(i, "CH%d_" % j, [128, 384], BF16) for j in range(2)] for i in range(NS)]
                Ao = [slot_tile(i, "Ao", [128, 256], BF16) for i in range(NS)]
                AoT = [slot_tile(i, "AoT", [128, 256], BF16) for i in range(NS)]
                attn_t = [slot_tile(i, "attn", [128, 128], BF16) for i in range(NS)]
                X0b = [slot_tile(i, "X0b", [128, 256], BF16) for i in range(NS)]
                X1b = [slot_tile(i, "X1b", [128, 256], BF16) for i in range(NS)]
                R1b = X0b
                Tmb = Ao
                Wtm = Do1
                Wb = [slot_tile(i, "Wb", [128, 128], BF16) for i in range(NS)]
                mbd = [sb2("mbd%d" % i, [128, 128], F32) for i in range(2)]
                mo1 = [sb2("mo1%d" % i, [128, 128], F32) for i in range(2)]
                mo2 = [sb2("mo2%d" % i, [128, 128], F32) for i in range(2)]
                with scope() as scm:
                    E32 = scm.sb("E32", [4, 128], F32)
                    E64 = scm.sb("E64", [2, 128], F32)
                    b32 = scm.sb("b32", [128, 128], F32)
                    b64 = scm.sb("b64", [128, 128], F32)
                    tmk = scm.sb("tmk", [128, 128], F32)
                    for E_, w_ in ((E32, 32), (E64, 64)):
                        np_ = 128 // w_
                        P.op("pool", lambda e: e.memset(E_.t[:], 1.0), [], [E_])
                        P.op("pool", lambda e: e.affine_select(E_.t[:], E_.t[:], pattern=[[1, 128]], compare_op=ALU.is_ge, fill=0.0, base=0, channel_multiplier=-w_), [E_], [E_])
                        P.op("pool", lambda e: e.affine_select(E_.t[:], E_.t[:], pattern=[[-1, 128]], compare_op=ALU.is_ge, fill=0.0, base=w_ - 1, channel_multiplier=w_), [E_], [E_])
                    P.op("pe", lambda e: e.matmul(pb[0].t[:, 0:128], E32.t[:], E32.t[:], start=True, stop=True), [E32], [pb[0]], inc=False)
                    P.op("pe", lambda e: e.matmul(pb[0].t[:, 128:256], E64.t[:], E64.t[:], start=True, stop=True), [E64], [pb[0]])
                    P.op("dve", lambda e: e.tensor_copy(b32.t[:], pb[0].t[:, 0:128]), [pb[0]], [b32])
                    P.op("dve", lambda e: e.tensor_copy(b64.t[:], pb[0].t[:, 128:256]), [pb[0]], [b64])
                    for d_, tri in ((0, slf), (1, suf)):
                        P.op("dve", lambda e: e.tensor_tensor(mbd[d_].t[:], tri.t[:], b32.t[:], ALU.mult), [tri, b32], [mbd[d_]])
                        P.op("dve", lambda e: e.tensor_tensor(tmk.t[:], b64.t[:], b32.t[:], ALU.subtract), [b64, b32], [tmk])
                        P.op("dve", lambda e: e.tensor_tensor(mo1[d_].t[:], tri.t[:], tmk.t[:], ALU.mult), [tri, tmk], [mo1[d_]])
                        P.op("dve", lambda e: e.tensor_tensor(tmk.t[:], tri.t[:], b64.t[:], ALU.mult), [tri, b64], [tmk])
                        P.op("dve", lambda e: e.tensor_tensor(mo2[d_].t[:], tri.t[:], tmk.t[:], ALU.subtract), [tri, tmk], [mo2[d_]])
                Sst = [sb2("Sst%d" % d_, [128, 128], F32) for d_ in range(2)]
                Sbb = [sb2("Sbb%d" % d_, [128, 128], BF16) for d_ in range(2)]
                vnb = [sb2("vnb%d" % d_, [128, 128], BF16) for d_ in range(2)]
                kdt = [sb2("kdt%d" % d_, [128, 128], BF16) for d_ in range(2)]
                tmo = [sb2("tmo%d" % d_, [128, 128], F32) for d_ in range(2)]
                zg_l = [sb2("zg%d" % i, [128, 128], F32) for i in range(2)]
                ty_l = [sb2("ty2%d" % i, [128, 128], F32) for i in range(2)]
                yb_l = [sb2("yb2%d" % i, [128, 128], BF16) for i in range(2)]
                ty = ty_l[0]
                ssq = sb2("ssq2", [128, 2 * NT], F32)
                ydT = sq
                P.op("dve", lambda e: e.memset(pc.t[:, 0:1], 0.0), [], [pc])
                P.op("dve", lambda e: e.memset(pc.t[:, S + 1:S + 2], 0.0), [], [pc])
                wA = wB = None
                for h in range(8):
                    hh = h % 2
                    if hh == 0:
                        wA, wB = nextw(), nextw()
                        wload(wA, 0, w_in, O_DQ + h * 128, 256)
                        wload(wA, 256, w_in, O_DK + h * 128, 256)
                        wload(wB, 0, w_in, O_DV + h * 128, 256)
                        wload(wB, 256, w_in, O_DZ + h * 128, 256)
                    for j, (wsrc, off) in enumerate(((wA, hh * 128), (wA, 256 + hh * 128), (wB, hh * 128))):
                        proj_fm(wsrc, off, pb[0:4], lambda tb, bank: evac_copy(pc.t[:, 1 + tb * 512:1 + (tb + 1) * 512], bank.t[:, 0:512], [bank], [pc]))
                        cw = lambda k: colp.t[:, 16 + k * 24 + j * 8 + h:16 + k * 24 + j * 8 + h + 1]
                        P.op("dve", lambda e: e.tensor_scalar(cv.t[:], pc.t[:, 0:S], cw(0), None, ALU.mult), [pc, colp], [cv])
                        P.op("dve", lambda e: e.scalar_tensor_tensor(cv.t[:], pc.t[:, 1:S + 1], cw(1), cv.t[:], ALU.mult, ALU.add), [pc, colp], [cv])
                        P.op("dve", lambda e: e.scalar_tensor_tensor(cv.t[:], pc.t[:, 2:S + 2], cw(2), cv.t[:], ALU.mult, ALU.add), [pc, colp], [cv])
                        P.op("act", lambda e: e.activation(cv.t[:], cv.t[:], AF.Silu), [cv], [cv])
                        if j == 2:
                            P.op("act", lambda e: e.copy(vT.t[:], cv.t[:]), [cv], [vT])
                        else:
                            dst = qnT if j == 0 else knT
                            P.op("act", lambda e: e.activation(sq.t[:], cv.t[:], AF.Square), [cv], [sq])
                            for tb in range(4):
                                bs = slice(tb * 512, (tb + 1) * 512)
                                bank = pb[4 + tb % 2]
                                P.op("pe", lambda e: e.matmul(bank.t[:, 0:512], onesb.t[:], sq.t[:, bs], start=True, stop=True), [onesb, sq], [bank])
                                rsqrt_(rin.t[:], bank.t[:, 0:512], 1.0, [bank], [rin])
                                if j == 0:
                                    P.op("dve", lambda e: e.scalar_tensor_tensor(dst.t[:, bs], cv.t[:, bs], 128.0 ** -0.5, rin.t[:], ALU.mult, ALU.mult), [cv, rin], [dst])
                                else:
                                    P.op("dve", lambda e: e.tensor_tensor(dst.t[:, bs], cv.t[:, bs], rin.t[:], ALU.mult), [cv, rin], [dst])
                    if h == 0:
                        dump("qnT", qnT, qnT.t[:], [128, S], BF16)
                        dump("knT", knT, knT.t[:], [128, S], BF16)
                        dump("vT", vT, vT.t[:], [128, S], BF16)
                    for src, dstm in ((knT, ktm), (vT, vtm)):
                        for g4 in range(2):
                            for i in range(8):
                                tt = g4 * 8 + i
                                P.op("pe", lambda e: e.transpose(pbt.t[:, i * 128:(i + 1) * 128], src.t[:, tt * 128:(tt + 1) * 128], identb.t[:]), [src, identb], [pbt], inc=(i == 7))
                            evac_copy(dstm.t[:, g4 * 8:(g4 + 1) * 8, :], pbt.t[:].rearrange("p (i t) -> p i t", i=8), [pbt], [dstm])
                    if stop == "dn0a":
                        break
                    def gen_prep(si, c, d_, h=h):
                        cs = slice(c * 128, (c + 1) * 128)
                        gi = (c * 2 + d_) * 8 + h
                        col = lambda b_: b_.t[:, gi:gi + 1]
                        e_ = c * 2 + d_
                        ch0, ch1 = CH[si][0], CH[si][1]
                        bank = pb[1 + si]
                        mask = mup if d_ == 0 else mlo
                        P.op("dve", lambda e: e.tensor_scalar(dgl[si].t[:], identf.t[:], col(negG), None, ALU.mult), [identf, negG], [dgl[si]])
                        yield
                        while lock["pb0"] is not None:
                            yield
                        lock["pb0"] = si
                        P.op("pe", lambda e: e.matmul(pb[0].t[:, 0:128], onesf.t[:], dgl[si].t[:], start=True, stop=False), [onesf, dgl[si]], [pb[0]], inc=False)
                        P.op("pe", lambda e: e.matmul(pb[0].t[:, 0:128], identf.t[:], mask.t[:], start=False, stop=True), [identf, mask], [pb[0]], inc=False)
                        P.op("pe", lambda e: e.matmul(pb[0].t[:, 128:256], knT.t[:, cs], knT.t[:, cs], start=True, stop=True), [knT], [pb[0]], inc=False)
                        P.op("pe", lambda e: e.matmul(pb[0].t[:, 256:384], qnT.t[:, cs], knT.t[:, cs], start=True, stop=True), [qnT, knT], [pb[0]])
                        yield
                        P.op("act", lambda e: e.activation(Dm[si].t[:], pb[0].t[:, 0:128], AF.Exp, bias=col(Gc), scale=1.0), [pb[0], Gc], [Dm[si]])
                        P.op("act", lambda e: e.activation(X0b[si].t[:, 0:128], ktm.t[:, c, :], AF.Copy, scale=col(bexpG)), [ktm, bexpG], [X0b[si]])
                        P.op("act", lambda e: e.activation(X0b[si].t[:, 128:256], vtm.t[:, c, :], AF.Copy, scale=col(beta)), [vtm, beta], [X0b[si]])
                        yield
                        P.op("pool", lambda e: e.tensor_tensor(dgl[si].t[:], Dm[si].t[:], mbd[d_].t[:], ALU.mult), [Dm[si], mbd[d_]], [dgl[si]])
                        P.op("pool", lambda e: e.tensor_tensor(Do1[si].t[:], Dm[si].t[:], mo1[d_].t[:], ALU.mult), [Dm[si], mo1[d_]], [Do1[si]])
                        P.op("pool", lambda e: e.tensor_tensor(Do2[si].t[:], Dm[si].t[:], mo2[d_].t[:], ALU.mult), [Dm[si], mo2[d_]], [Do2[si]])
                        P.op("dve", lambda e: e.tensor_tensor(attn_t[si].t[:], pb[0].t[:, 256:384], Dm[si].t[:], ALU.mult), [pb[0], Dm[si]], [attn_t[si]])
                        yield
                        P.op("dve", lambda e: e.scalar_tensor_tensor(ch0.t[:, 256:384], pb[0].t[:, 128:256], col(negbeta), dgl[si].t[:], ALU.mult, ALU.mult), [pb[0], negbeta, dgl[si]], [ch0])
                        P.op("dve", lambda e: e.scalar_tensor_tensor(Ao[si].t[:, 0:128], pb[0].t[:, 128:256], col(beta), Do1[si].t[:], ALU.mult, ALU.mult), [pb[0], beta, Do1[si]], [Ao[si]])
                        P.op("dve", lambda e: e.scalar_tensor_tensor(Ao[si].t[:, 128:256], pb[0].t[:, 128:256], col(beta), Do2[si].t[:], ALU.mult, ALU.mult), [pb[0], beta, Do2[si]], [Ao[si]])
                        lock["pb0"] = None
                        yield
                        while lock["pbt"] is not None:
                            yield
                        lock["pbt"] = si
                        P.op("pe", lambda e: e.transpose(pbt.t[:, 0:128], ch0.t[:, 256:384], identb.t[:]), [ch0, identb], [pbt], inc=False)
                        P.op("pe", lambda e: e.transpose(pbt.t[:, 128:256], Ao[si].t[:, 0:128], identb.t[:]), [Ao[si], identb], [pbt], inc=False)
                        P.op("pe", lambda e: e.transpose(pbt.t[:, 256:384], Ao[si].t[:, 128:256], identb.t[:]), [Ao[si], identb], [pbt], inc=False)
                        P.op("pe", lambda e: e.transpose(pbt.t[:, 384:512], attn_t[si].t[:], identb.t[:]), [attn_t[si], identb], [pbt])
                        yield
                        P.op("act", lambda e: e.copy(ch0.t[:, 0:128], pbt.t[:, 0:128]), [pbt], [ch0])
                        P.op("dve", lambda e: e.tensor_tensor(ch1.t[:, 128:256], pbt.t[:, 0:128], identb.t[:], ALU.add), [pbt, identb], [ch1])
                        P.op("act", lambda e: e.copy(AoT[si].t[:], pbt.t[:, 128:384]), [pbt], [AoT[si]])
                        P.op("dve", lambda e: e.tensor_copy(ATs.t[:, e_, :], pbt.t[:, 384:512]), [pbt], [stv[e_]])
                        lock["pbt"] = None
                        yield
                        P.op("pe", lambda e: e.matmul(bank.t[:, 0:128], ch0.t[:, 256:384], ch0.t[:, 0:128], start=True, stop=True), [ch0], [bank], inc=False)
                        P.op("pe", lambda e: e.matmul(bank.t[:, 256:384], ch0.t[:, 0:128], ch0.t[:, 256:384], start=True, stop=True), [ch0], [bank])
                        yield
                        P.op("act", lambda e: e.copy(ch1.t[:].rearrange("p (a b) -> p a b", b=128)[:, 0::2, :], bank.t[:, 0:384].rearrange("p (a b) -> p a b", b=128)[:, 0::2, :]), [bank], [ch1])
                        yield
                        for j in range(1, 5):
                            cur = CH[si][j % 2]
                            nxt = CH[si][(j + 1) % 2]
                            if j < 4:
                                P.op("pe", lambda e: e.matmul(bank.t[:, 0:256], cur.t[:, 256:384], cur.t[:, 0:256], start=True, stop=False), [cur], [bank], inc=False)
                                P.op("pe", lambda e: e.matmul(bank.t[:, 128:256], identb.t[:], cur.t[:, 128:256], start=False, stop=True), [cur, identb], [bank], inc=False)
                                P.op("pe", lambda e: e.matmul(bank.t[:, 256:384], cur.t[:, 0:128], cur.t[:, 256:384], start=True, stop=True), [cur], [bank])
                                yield
                                evac_copy(nxt.t[:], bank.t[:, 0:384], [bank], [nxt])
                                yield
                            else:
                                P.op("pe", lambda e: e.matmul(bank.t[:, 128:256], cur.t[:, 256:384], cur.t[:, 128:256], start=True, stop=False), [cur], [bank], inc=False)
                                P.op("pe", lambda e: e.matmul(bank.t[:, 128:256], identb.t[:], cur.t[:, 128:256], start=False, stop=True), [cur, identb], [bank])
                                yield
                                evac_copy(nxt.t[:, 128:256], bank.t[:, 128:256], [bank], [nxt])
                                yield
                        fin = CH[si][1]
                        PTf = fin.t[:, 128:256]
                        A1T, A2T = AoT[si].t[:, 0:128], AoT[si].t[:, 128:256]
                        lo, hi = bank.t[:, 0:256], bank.t[:, 256:512]
                        P.op("pe", lambda e: e.matmul(lo, PTf, X0b[si].t[:], start=True, stop=True), [fin, X0b[si]], [bank])
                        yield
                        P.op("act", lambda e: e.copy(R1b[si].t[:], lo), [bank], [R1b[si]])
                        P.op("dve", lambda e: e.tensor_copy(Us.t[:, e_, :], bank.t[:, 128:256]), [bank], [stv[e_]])
                        yield
                        P.op("pe", lambda e: e.matmul(hi, A1T, R1b[si].t[:], start=True, stop=True), [AoT[si], R1b[si]], [bank])
                        yield
                        P.op("act", lambda e: e.copy(Tmb[si].t[:], hi), [bank], [Tmb[si]])
                        yield
                        P.op("pe", lambda e: e.matmul(lo, PTf, Tmb[si].t[:], start=True, stop=True), [fin, Tmb[si]], [bank])
                        yield
                        P.op("dve", lambda e: e.tensor_tensor(X1b[si].t[:], R1b[si].t[:], lo, ALU.subtract), [R1b[si], bank], [X1b[si]])
                        P.op("dve", lambda e: e.tensor_tensor(Us.t[:, e_, :], Us.t[:, e_, :], bank.t[:, 128:256], ALU.subtract), [bank], [stv[e_]])
                        yield
                        P.op("pe", lambda e: e.matmul(hi, A2T, X1b[si].t[:], start=True, stop=True), [AoT[si], X1b[si]], [bank])
                        yield
                        P.op("act", lambda e: e.copy(Tmb[si].t[:], hi), [bank], [Tmb[si]])
                        yield
                        P.op("pe", lambda e: e.matmul(lo, PTf, Tmb[si].t[:], start=True, stop=True), [fin, Tmb[si]], [bank])
                        yield
                        P.op("act", lambda e: e.copy(R1b[si].t[:], lo), [bank], [R1b[si]])
                        P.op("dve", lambda e: e.tensor_tensor(Us.t[:, e_, :], Us.t[:, e_, :], bank.t[:, 128:256], ALU.subtract), [bank], [stv[e_]])
                        yield
                        P.op("pe", lambda e: e.matmul(hi, A1T, R1b[si].t[:], start=True, stop=True), [AoT[si], R1b[si]], [bank])
                        P.op("pool", lambda e: e.tensor_tensor(Wtm[si].t[:], X1b[si].t[:, 0:128], R1b[si].t[:, 0:128], ALU.subtract), [X1b[si], R1b[si]], [Wtm[si]])
                        yield
                        P.op("act", lambda e: e.copy(Tmb[si].t[:], hi), [bank], [Tmb[si]])
                        yield
                        P.op("pe", lambda e: e.matmul(lo, PTf, Tmb[si].t[:], start=True, stop=True), [fin, Tmb[si]], [bank])
                        yield
                        P.op("dve", lambda e: e.tensor_tensor(Wb[si].t[:], Wtm[si].t[:], bank.t[:, 0:128], ALU.add), [Wtm[si], bank], [Wb[si]])
                        P.op("dve", lambda e: e.tensor_tensor(Us.t[:, e_, :], Us.t[:, e_, :], bank.t[:, 128:256], ALU.add), [bank], [stv[e_]])
                        yield
                        while lock["pbt"] is not None:
                            yield
                        lock["pbt"] = si
                        P.op("pe", lambda e: e.transpose(pbt.t[:, 0:128], Wb[si].t[:], identb.t[:]), [Wb[si], identb], [pbt])
                        yield
                        P.op("act", lambda e: e.copy(WTs.t[:, e_, :], pbt.t[:, 0:128]), [pbt], [stv[e_]])
                        lock["pbt"] = None
                        yield

                    def gen_rec(h=h):
                        bA, bB = (pb[6], pb[6]) if NS_CFG[0] > 4 else (pb[5], pb[6])
                        for step in range(NT):
                            info = []
                            for d_ in range(2):
                                c = step if d_ == 0 else NT - 1 - step
                                info.append((d_, c, slice(c * 128, (c + 1) * 128), (c * 2 + d_) * 8 + h, c * 2 + d_))
                            for d_, c, cs, gi, e_ in info:
                                if step > 0:
                                    P.op("pe", lambda e: e.matmul(bA.t[:, d_ * 256:d_ * 256 + 128], WTs.t[:, e_, :], Sbb[d_].t[:], start=True, stop=True), [stv[e_], Sbb[d_]], [bA], inc=False)
                                    P.op("pe", lambda e: e.matmul(bA.t[:, d_ * 256 + 128:d_ * 256 + 256], qnT.t[:, cs], Sbb[d_].t[:], start=True, stop=True), [qnT, Sbb[d_]], [bA])
                                if step < NT - 1:
                                    P.op("act", lambda e: e.activation(kdt[d_].t[:], ktm.t[:, c, :], AF.Copy, scale=kdw.t[:, gi:gi + 1]), [ktm, kdw], [kdt[d_]])
                            yield
                            for d_, c, cs, gi, e_ in info:
                                if step > 0:
                                    P.op("dve", lambda e: e.tensor_tensor(vnb[d_].t[:], Us.t[:, e_, :], bA.t[:, d_ * 256:d_ * 256 + 128], ALU.subtract), [stv[e_], bA], [vnb[d_]])
                                    P.op("act", lambda e: e.activation(tmo[d_].t[:], bA.t[:, d_ * 256 + 128:d_ * 256 + 256], AF.Copy, scale=expG.t[:, gi:gi + 1]), [bA, expG], [tmo[d_]])
                                else:
                                    P.op("dve", lambda e: e.tensor_copy(vnb[d_].t[:], Us.t[:, e_, :]), [stv[e_]], [vnb[d_]])
                            yield
                            for d_, c, cs, gi, e_ in info:
                                P.op("pe", lambda e: e.matmul(bB.t[:, d_ * 256:d_ * 256 + 128], ATs.t[:, e_, :], vnb[d_].t[:], start=True, stop=True), [stv[e_], vnb[d_]], [bB], inc=(step == NT - 1))
                                if step < NT - 1:
                                    P.op("pe", lambda e: e.matmul(bB.t[:, d_ * 256 + 128:d_ * 256 + 256], kdt[d_].t[:], vnb[d_].t[:], start=True, stop=True), [kdt[d_], vnb[d_]], [bB])
                            yield
                            for d_, c, cs, gi, e_ in info:
                                if step < NT - 1:
                                    if step == 0:
                                        P.op("dve", lambda e: e.tensor_copy(Sst[d_].t[:], bB.t[:, d_ * 256 + 128:d_ * 256 + 256]), [bB], [Sst[d_]])
                                    else:
                                        P.op("dve", lambda e: e.scalar_tensor_tensor(Sst[d_].t[:], Sst[d_].t[:], glb.t[:, gi:gi + 1], bB.t[:, d_ * 256 + 128:d_ * 256 + 256], ALU.mult, ALU.add), [bB, glb], [Sst[d_]])
                                    P.op("act", lambda e: e.copy(Sbb[d_].t[:], Sst[d_].t[:]), [Sst[d_]], [Sbb[d_]])
                            for d_, c, cs, gi, e_ in info:
                                first = (d_ == 0 and c < 8) or (d_ == 1 and c >= 8)
                                if step > 0:
                                    P.op("dve", lambda e: e.tensor_tensor(tmo[d_].t[:], tmo[d_].t[:], bB.t[:, d_ * 256:d_ * 256 + 128], ALU.add), [bB], [tmo[d_]])
                                    src_ap, src_b = tmo[d_].t[:], tmo[d_]
                                    if first:
                                        P.op("act", lambda e: e.copy(osb.t[:, c, :], src_ap), [src_b], [osv[c]])
                                    else:
                                        P.op("dve", lambda e: e.tensor_tensor(osb.t[:, c, :], osb.t[:, c, :], src_ap, ALU.add), [src_b], [osv[c]])
                                else:
                                    if first:
                                        P.op("dve", lambda e: e.tensor_copy(osb.t[:, c, :], bB.t[:, d_ * 256:d_ * 256 + 128]), [bB], [osv[c]])
                                    else:
                                        P.op("dve", lambda e: e.tensor_tensor(osb.t[:, c, :], osb.t[:, c, :], bB.t[:, d_ * 256:d_ * 256 + 128], ALU.add), [bB], [osv[c]])
                            yield

                    lock = {"pb0": None, "pbt": None}
                    order = []
                    for i in range(NT):
                        order.append((i, 0))
                        order.append((NT - 1 - i, 1))
                    active = [None] * NS
                    nstarted = 0
                    nfinished = 0
                    finished = [False] * 32
                    rec = gen_rec()
                    rec_step = 0
                    rec_hop = 0
                    rec_done = False
                    tick = 0
                    while nfinished < 32 or not rec_done:
                        if nstarted < 32 and tick % STAG == 0:
                            for si in range(NS):
                                if active[si] is None:
                                    c, d_ = order[nstarted]
                                    active[si] = (gen_prep(si, c, d_), nstarted)
                                    nstarted += 1
                                    break
                        for si in range(NS):
                            if active[si] is not None:
                                g_, idx = active[si]
                                try:
                                    next(g_)
                                except StopIteration:
                                    finished[idx] = True
                                    nfinished += 1
                                    active[si] = None
                        if not rec_done and (stop != "dn0b"):
                            if rec_hop > 0 or (finished[2 * rec_step] and finished[2 * rec_step + 1]):
                                try:
                                    next(rec)
                                    rec_hop += 1
                                    if rec_hop == 4:
                                        rec_hop = 0
                                        rec_step += 1
                                        if rec_step == NT:
                                            rec_done = True
                                except StopIteration:
                                    rec_done = True
                        elif stop == "dn0b":
                            rec_done = True
                        tick += 1
                    if stop == "dn0c":
                        break
                    if h == 0:
                        dump("osb", osv[0], osb.t[:], [128, NT, 128])
                    for tt in range(NT):
                        P.op("act", lambda e: e.activation(ty.t[:], osb.t[:, tt, :], AF.Square, accum_out=ssq.t[:, tt:tt + 1]), [osv[tt]], [ty, ssq])
                    rsqrt_(ssq.t[:, NT:2 * NT], ssq.t[:, 0:NT], 1.0 / 128.0, [ssq], [ssq])
                    for tt in range(NT):
                        bank = pb[4 + tt % 2]
                        zg, ty, yb = zg_l[tt % 2], ty_l[tt % 2], yb_l[tt % 2]
                        proj_tm(wB, 256 + hh * 128, 128, tt, bank)
                        P.op("act", lambda e: e.activation(zg.t[:], bank.t[:, 0:128], AF.Silu), [bank], [zg])
                        P.op("dve", lambda e: e.scalar_tensor_tensor(ty.t[:], osb.t[:, tt, :], ssq.t[:, NT + tt:NT + tt + 1], rowp.t[:, 1024:1152], ALU.mult, ALU.mult), [osv[tt], ssq, rowp], [ty])
                        P.op("dve", lambda e: e.tensor_tensor(yb.t[:], ty.t[:], zg.t[:], ALU.mult), [ty, zg], [yb])
                        P.op("pe", lambda e: e.transpose(pbt.t[:, 0:128], yb.t[:], identb.t[:]), [yb, identb], [pbt])
                        P.op("act", lambda e: e.copy(ydT.t[:, tt * 128:(tt + 1) * 128], pbt.t[:, 0:128]), [pbt], [ydT])
                    P.dma(ysT.t[1, h * 128:(h + 1) * 128, :], ydT.t[:], [ydT], [ysv[1][h]])
                    if stop == "dn0":
                        break
            if stop in ("dn0", "dn", "dn0a", "dn0b", "dn0c"):
                stopped = True
                break

            wlim[0] = NWB
            with scope() as sc:
                cx = sc.sb("cx", [128, S + 2], F32)
                Bsb = sc.sb("Bsb", [128, S], F32)
                ycv = sc.sb("ycv", [128, S], F32)
                tmx_l = [sc.sb("tmx%d" % i, [128, 512], F32) for i in range(2)]
                ycT = sc.sb("ycT", [128, S], BF16)
                P.op("dve", lambda e: e.memset(cx.t[:, 0:1], 0.0), [], [cx])
                P.op("dve", lambda e: e.memset(cx.t[:, S + 1:S + 2], 0.0), [], [cx])
                wA = wB = None
                for dc in range(8):
                    dd = dc % 2
                    if dd == 0:
                        wA, wB = nextw(), nextw()
                        wload(wA, 0, w_in, O_SB + dc * 128, 256)
                        wload(wA, 256, w_in, O_SC + dc * 128, 256)
                        wload(wB, 0, w_in, O_SX + dc * 128, 256)
                    for tb in range(4):
                        bs = slice(tb * 512, (tb + 1) * 512)
                        tmx = tmx_l[tb % 2]
                        for j, (wsrc, off) in enumerate(((wA, dd * 128), (wA, 256 + dd * 128), (wB, dd * 128))):
                            bank = pb[j + 3 * (tb % 2)]
                            for kc in range(8):
                                P.op("pe", lambda e: e.matmul(bank.t[:, 0:512], wsrc.t[:, kc, off:off + 128], hT.t[:, kc, bs], start=(kc == 0), stop=(kc == 7)),
                                     [wsrc] + hTv[tb * 4:tb * 4 + 4], [bank], inc=(kc == 7))
                        o3 = 3 * (tb % 2)
                        P.op("act", lambda e: e.copy(Bsb.t[:, bs], pb[o3].t[:, 0:512]), [pb[o3]], [Bsb])
                        P.op("act", lambda e: e.copy(tmx.t[:], pb[o3 + 2].t[:, 0:512]), [pb[o3 + 2]], [tmx])
                        P.op("dve", lambda e: e.tensor_tensor(cx.t[:, 1 + tb * 512:1 + (tb + 1) * 512], pb[o3 + 1].t[:, 0:512], tmx.t[:], ALU.mult), [pb[o3 + 1], tmx], [cx])
                    cw = lambda k: colp.t[:, 88 + k * 8 + dc:88 + k * 8 + dc + 1]
                    P.op("dve", lambda e: e.tensor_scalar(ycv.t[:], cx.t[:, 0:S], cw(0), None, ALU.mult), [cx, colp], [ycv])
                    P.op("dve", lambda e: e.scalar_tensor_tensor(ycv.t[:], cx.t[:, 1:S + 1], cw(1), ycv.t[:], ALU.mult, ALU.add), [cx, colp], [ycv])
                    P.op("dve", lambda e: e.scalar_tensor_tensor(ycv.t[:], cx.t[:, 2:S + 2], cw(2), ycv.t[:], ALU.mult, ALU.add), [cx, colp], [ycv])
                    P.op("dve", lambda e: e.tensor_tensor(ycT.t[:], ycv.t[:], Bsb.t[:], ALU.mult), [ycv, Bsb], [ycT])
                    P.dma(ysT.t[2, dc * 128:(dc + 1) * 128, :], ycT.t[:], [ycT], [ysv[2][dc]])
            if stop == "sc":
                stopped = True
                break

            if l + 1 < nlayers:
                P.dma(colps[(l + 1) % 2].t[:], colp_d[l + 1], [], [colps[(l + 1) % 2]])
            last = (l == DEPTH - 1)
            for half in range(2):
                with scope() as sch:
                    xres = sch.sb("xres", [128, 8, D], F32)
                    xrv = [Buf(xres.t, "xr%d" % i) for i in range(8)]
                    with scope() as sc:
                        ys_sb = [sc.sb("ys_sb%d" % n, [128, 8, 1024], BF16) for n in range(3)]
                        sg_l = [[sc.sb("sg%d_%d" % (n, i), [128, 512], F32) for n in range(3)] for i in range(2)]
                        acc_l = [sc.sb("acc%d" % i, [128, 512], F32) for i in range(2)]
                        tmm_l = [sc.sb("tmm", [128, 512], F32)] * 2
                        mixT = sc.sb("mixT", [128, 8, 1024], BF16)
                        for n in range(3):
                            P.dma(ys_sb[n].t[:], ysT.t[n].rearrange("(kc p) t -> p kc t", p=128)[:, :, half * 1024:(half + 1) * 1024], ysv[n], [ys_sb[n]])
                        wA = wB = wC = None
                        for dc in range(8):
                            dd = dc % 2
                            if dd == 0:
                                wA, wB, wC = nextw(), nextw(), nextw()
                                wload(wA, 0, w_br_d[l, 0], dc * 128, 256)
                                wload(wA, 256, w_br_d[l, 1], dc * 128, 256)
                                wload(wB, 0, w_br_d[l, 2], dc * 128, 256)
                                wload(wB, 256, w_in, O_MRG + dc * 128, 256)
                                wload(wC, 0, w_in, O_MRG + 1024 + dc * 128, 256)
                                wload(wC, 256, w_in, O_MRG + 2048 + dc * 128, 256)
                            wbr = ((wA, dd * 128), (wA, 256 + dd * 128), (wB, dd * 128))
                            wgt_ = ((wB, 256 + dd * 128), (wC, dd * 128), (wC, 256 + dd * 128))
                            for tbh in range(2):
                                tb = half * 2 + tbh
                                bs = slice(tb * 512, (tb + 1) * 512)
                                bsh = slice(tbh * 512, (tbh + 1) * 512)
                                sg, acc, tmm = sg_l[tbh], acc_l[tbh], tmm_l[tbh]
                                for n in range(3):
                                    wsrc, off = wgt_[n]
                                    for kc in range(8):
                                        P.op("pe", lambda e: e.matmul(pb[3 + n].t[:, 0:512], wsrc.t[:, kc, off:off + 128], hT.t[:, kc, bs], start=(kc == 0), stop=(kc == 7)),
                                             [wsrc] + hTv[tb * 4:tb * 4 + 4], [pb[3 + n]], inc=(kc == 7))
                                    P.op("act", lambda e: e.activation(sg[n].t[:], pb[3 + n].t[:, 0:512], AF.Sigmoid), [pb[3 + n]], [sg[n]])
                                for n in range(3):
                                    wsrc, off = wbr[n]
                                    for kc in range(8):
                                        P.op("pe", lambda e: e.matmul(pb[n].t[:, 0:512], wsrc.t[:, kc, off:off + 128], ys_sb[n].t[:, kc, bsh], start=(kc == 0), stop=(kc == 7)),
                                             [wsrc, ys_sb[n]], [pb[n]], inc=(kc == 7))
                                P.op("dve", lambda e: e.tensor_tensor(acc.t[:], sg[0].t[:], pb[0].t[:, 0:512], ALU.mult), [sg[0], pb[0]], [acc])
                                P.op("dve", lambda e: e.tensor_tensor(tmm.t[:], sg[1].t[:], pb[1].t[:, 0:512], ALU.mult), [sg[1], pb[1]], [tmm])
                                P.op("dve", lambda e: e.tensor_tensor(sg[2].t[:], sg[2].t[:], pb[2].t[:, 0:512], ALU.mult), [pb[2]], [sg[2]])
                                P.op("dve", lambda e: e.tensor_tensor(acc.t[:], acc.t[:], tmm.t[:], ALU.add), [tmm], [acc])
                                P.op("dve", lambda e: e.tensor_tensor(mixT.t[:, dc, bsh], acc.t[:], sg[2].t[:], ALU.add), [acc, sg[2]], [mixT])
                        wo0, wo1 = nextw(), nextw()
                        wload(wo0, 0, w_out_d[l], 0, 512)
                        wload(wo1, 0, w_out_d[l], 512, 512)
                        for t8 in range(8):
                            tt = half * 8 + t8
                            P.dma(xres.t[:, t8, :], xcur.t[tt * 128:(tt + 1) * 128, :], [xcv[tt]], [xrv[t8]])
                            for nb, wsrc in enumerate((wo0, wo1)):
                                bank = pb[(t8 * 2 + nb) % 4]
                                for dc in range(8):
                                    P.op("pe", lambda e: e.matmul(bank.t[:, 0:512], mixT.t[:, dc, t8 * 128:(t8 + 1) * 128], wsrc.t[:, dc, :], start=(dc == 0), stop=(dc == 7)),
                                         [mixT, wsrc], [bank], inc=(dc == 7))
                                P.op("dve", lambda e: e.tensor_tensor(xres.t[:, t8, nb * 512:(nb + 1) * 512], xres.t[:, t8, nb * 512:(nb + 1) * 512], bank.t[:, 0:512], ALU.add), [bank], [xrv[t8]])
                            norm_tile(xres.t[:, t8, :], xrv[t8], tt, (colp, colp.t[:, 8:16]))
                            if stop == "mix" and "d_xm" in dbg_d:
                                P.dma(dbg_d["d_xm"][tt * 128:(tt + 1) * 128, :], xres.t[:, t8, :], [xrv[t8]], [])
                    if stop == "mix":
                        continue
                    with scope() as sc:
                        upT = sc.sb("upT", [128, 8, 1024], BF16)
                        relu_t = [sc.sb("relu_t%d" % i, [128, 512], F32) for i in range(2)]
                        otile = [sc.sb("otile%d" % i, [128, D], F32) for i in range(2)] if last else None
                        gfin = sc.sb("gfin_sb", [128, D], F32) if last else None
                        if last:
                            P.dma(gfin.t[:], gfin_d.partition_broadcast(128), [], [gfin])
                        for fb in range(4):
                            wu = [nextw(), nextw()]
                            wd = [nextw(), nextw()]
                            wload(wu[0], 0, w_up_d[l], fb * 1024, 512)
                            wload(wu[1], 0, w_up_d[l], fb * 1024 + 512, 512)
                            wload(wd[0], 0, w_dn_d[l, fb * 1024:(fb + 1) * 1024, :], 0, 512)
                            wload(wd[1], 0, w_dn_d[l, fb * 1024:(fb + 1) * 1024, :], 512, 512)
                            for fc in range(8):
                                def ev(tb, bank):
                                    bsh = slice((tb - half * 2) * 512, (tb - half * 2 + 1) * 512)
                                    rl = relu_t[tb % 2]
                                    P.op("act", lambda e: e.activation(rl.t[:], bank.t[:, 0:512], AF.Relu), [bank], [rl])
                                    P.op("dve", lambda e: e.tensor_tensor(upT.t[:, fc, bsh], rl.t[:], rl.t[:], ALU.mult), [rl], [upT])
                                proj_fm(wu[fc // 4], (fc % 4) * 128, pb[0:4], ev, tbs=(half * 2, half * 2 + 1))
                            for t8 in range(8):
                                tt = half * 8 + t8
                                for nb in range(2):
                                    bank = pb[4 + (t8 * 2 + nb) % 3]
                                    for fc in range(8):
                                        P.op("pe", lambda e: e.matmul(bank.t[:, 0:512], upT.t[:, fc, t8 * 128:(t8 + 1) * 128], wd[nb].t[:, fc, :], start=(fc == 0), stop=(fc == 7)),
                                             [upT, wd[nb]], [bank], inc=(fc == 7))
                                    P.op("dve", lambda e: e.tensor_tensor(xres.t[:, t8, nb * 512:(nb + 1) * 512], xres.t[:, t8, nb * 512:(nb + 1) * 512], bank.t[:, 0:512], ALU.add), [bank], [xrv[t8]])
                                if fb == 3:
                                    if stop == "mlp" and "d_xm" in dbg_d:
                                        P.dma(dbg_d["d_xm"][tt * 128:(tt + 1) * 128, :], xres.t[:, t8, :], [xrv[t8]], [])
                                    if not last:
                                        P.dma(xcur.t[tt * 128:(tt + 1) * 128, :], xres.t[:, t8, :], [xrv[t8]], [xcv[tt]])
                                        if l + 1 < nlayers:
                                            cn = colps[(l + 1) % 2]
                                            norm_tile(xres.t[:, t8, :], xrv[t8], tt, (cn, cn.t[:, 0:8]))
                                    else:
                                        ot = otile[t8 % 2]
                                        nsq, nss = nsq_l[t8 % 2], nss_l[t8 % 2]
                                        P.op("act", lambda e: e.activation(nsq.t[:], xres.t[:, t8, :], AF.Square, accum_out=nss.t[:, 0:1]), [xrv[t8]], [nsq, nss])
                                        rsqrt_(nss.t[:, 1:2], nss.t[:, 0:1], 1.0 / D, [nss], [nss])
                                        P.op("dve", lambda e: e.scalar_tensor_tensor(ot.t[:], xres.t[:, t8, :], nss.t[:, 1:2], gfin.t[:], ALU.mult, ALU.mult), [xrv[t8], nss, gfin], [ot])
                                        P.dma(out_d[tt * 128:(tt + 1) * 128, :], ot.t[:], [ot], [])
            if stop in ("mix", "mlp"):
                stopped = True
                break
        if stopped:
            dump_ys()
        P.finish()
        print("build: ops", P.nops, "waits", P.nwait, {k: P.cnt[k] for k in P.cnt})
    return nc


def make_params(inp):
    colp = np.zeros((DEPTH, 128, NCOL), np.float32)
    rowp = np.zeros((DEPTH, NROW), np.float32)
    for l in range(DEPTH):
        colp[l, :, 0:8] = inp["norm_mix_g"][l].reshape(8, 128).T
        colp[l, :, 8:16] = inp["norm_mlp_g"][l].reshape(8, 128).T
        colp[l, :, 16:88] = inp["dn_conv_w"][l].reshape(3, 24, 128).transpose(2, 0, 1).reshape(128, 72)
        colp[l, :, 88:112] = inp["sc_conv_w"][l].reshape(3, 8, 128).transpose(2, 0, 1).reshape(128, 24)
        rowp[l, 0:1024] = inp["m_norm_g"][l]
        rowp[l, 1024:1152] = inp["dn_norm_g"][l]
        rowp[l, 1152:1168] = inp["m_gate_b"][l].reshape(16)
        rowp[l, 1168:1184] = inp["dn_a_log"][l].reshape(16)
        rowp[l, 1184:1200] = inp["dn_dt_bias"][l].reshape(16)
    return colp, rowp


def make_in_maps(inp, cores):
    colp, rowp = make_params(inp)
    shared = {"w_in": np.ascontiguousarray(inp["w_in"]), "w_branch": np.ascontiguousarray(inp["w_branch"]),
              "w_out": np.ascontiguousarray(inp["w_out"]), "w_up": np.ascontiguousarray(inp["w_up"]),
              "w_down": np.ascontiguousarray(inp["w_down"]), "colp": colp, "rowp": rowp,
              "gfin": np.ascontiguousarray(inp["norm_final_g"])}
    return [dict(shared, x=np.ascontiguousarray(inp["x"][b])) for b in cores]


def kernel(**inputs):
    inp = {k: np.asarray(v, dtype=np.float32) for k, v in inputs.items()}
    nc = build()
    in_maps = make_in_maps(inp, list(range(8)))
    res = run_bass_kernel_spmd(nc, in_maps, core_ids=list(range(8)))
    return np.stack([r["out"] for r in res.results], axis=0).astype(np.float32)
```

```python
import numpy as np
import concourse.bass as bass
import concourse.mybir as mybir
from concourse.bass_utils import run_bass_kernel_spmd
from contextlib import ExitStack

F32 = mybir.dt.float32
BF16 = mybir.dt.bfloat16
ALU = mybir.AluOpType
AF = mybir.ActivationFunctionType
AX = mybir.AxisListType

S = 2048
D = 1024
NT = 16
DEPTH = 4
NPROJ = 14384
DFF = 4096
EPS = 1e-6
NCOL = 112
NROW = 1200
O_MQ, O_MK, O_MV, O_MO, O_MG = 0, 1024, 2048, 3072, 4096
O_DQ, O_DK, O_DV, O_DZ, O_DG = 4112, 5136, 6160, 7184, 8208
O_SB, O_SC, O_SX, O_MRG = 8240, 9264, 10288, 11312
NEG = -30000.0


class Tok:
    __slots__ = ("sem", "val", "clk")

    def __init__(self, sem, val, clk):
        self.sem, self.val, self.clk = sem, val, clk


class Buf:
    __slots__ = ("t", "w", "r", "name", "excl", "lastrd")

    def __init__(self, t, name, excl=False):
        self.t, self.name = t, name
        self.w = None
        self.r = []
        self.excl = excl
        self.lastrd = None


class Prog:
    NDMA = 12

    def __init__(self, nc, es, same_engine_sync=True):
        self.nc, self.es = nc, es
        self.same = same_engine_sync
        self.eng = {"pe": nc.tensor, "act": nc.scalar, "dve": nc.vector, "pool": nc.gpsimd, "sp": nc.sync}
        self.sem = {k: es.enter_context(nc.semaphore("s_" + k)) for k in self.eng}
        self.cnt = {k: 0 for k in self.eng}
        self.clk = {k: {} for k in self.eng}
        self.pend = {k: [] for k in self.eng}
        self.dsem = [es.enter_context(nc.semaphore("d%d" % i)) for i in range(2 * self.NDMA)]
        self.dcnt = [0] * (2 * self.NDMA)
        self.dnext = {"sp": 0, "pool": 0}
        self.nwait = 0
        self.nops = 0

    def sb(self, name, shape, dt):
        t = self.es.enter_context(self.nc.sbuf_tensor(name, list(shape), dt))
        return Buf(t, name)

    def ps(self, name, shape, dt):
        t = self.es.enter_context(self.nc.psum_tensor(name, list(shape), dt))
        return Buf(t, name, excl=True)

    def dram(self, name, shape, dt):
        t = self.nc.dram_tensor(name, list(shape), dt, kind="Internal").ap()
        return Buf(t, name)

    def _need(self, reads, writes):
        toks = []
        for b in reads:
            if b.w is not None:
                toks.append(b.w)
        for b in writes:
            if b.w is not None:
                toks.append(b.w)
            toks.extend(b.r)
        return toks

    def _wait(self, e, toks):
        clk = self.clk[e]
        eng = self.eng[e]
        own = self.sem[e]
        best = {}
        for t in toks:
            if t.sem is own and (e == "pe" or not self.same):
                continue
            k = id(t.sem)
            if clk.get(k, 0) >= t.val:
                continue
            if k not in best or best[k].val < t.val:
                best[k] = t
        for k, t in best.items():
            if clk.get(k, 0) >= t.val:
                continue
            eng.wait_ge(t.sem, t.val)
            self.nwait += 1
            clk[k] = t.val
            for kk, vv in t.clk.items():
                if clk.get(kk, 0) < vv:
                    clk[kk] = vv

    def op(self, e, fn, reads, writes, inc=True):
        xr = [b for b in reads if b.excl]
        if xr:
            reads = [b for b in reads if not b.excl]
        toks = self._need(reads, writes)
        for b in xr:
            if b.w is not None and not (b.lastrd == e and b.w.sem is self.sem[e]):
                toks.append(b.w)
            toks.extend(b.r)
        self._wait(e, toks)
        for b in writes:
            b.lastrd = None
        if xr:
            writes = list(writes) + xr
        ins = fn(self.eng[e])
        self.nops += 1
        if not inc:
            self.pend[e].append((reads, writes))
            return ins
        self.cnt[e] += 1
        ins.then_inc(self.sem[e], 1)
        tok = Tok(self.sem[e], self.cnt[e], dict(self.clk[e]))
        tok.clk[id(self.sem[e])] = self.cnt[e]
        for (rs, ws) in self.pend[e] + [(reads, writes)]:
            for b in rs:
                b.r.append(tok)
            for b in ws:
                b.w = tok
                b.r = []
        for b in xr:
            b.lastrd = e
        self.pend[e] = []
        return ins

    def dma(self, out, in_, reads, writes, q="sp"):
        toks = self._need(reads, writes)
        i = self.dnext[q] + (self.NDMA if q == "pool" else 0)
        self.dnext[q] = (self.dnext[q] + 1) % self.NDMA
        s = self.dsem[i]
        if self.dcnt[i] > 0:
            toks.append(Tok(s, self.dcnt[i], {}))
        self._wait(q, toks)
        ins = self.eng[q].dma_start(out=out, in_=in_)
        self.dcnt[i] += 16
        ins.then_inc(s, 16)
        tok = Tok(s, self.dcnt[i], dict(self.clk[q]))
        for b in reads:
            b.r.append(tok)
        for b in writes:
            b.w = tok
            b.r = []
        self.nops += 1
        return ins

    def barrier(self):
        toks = []
        for i in range(2 * self.NDMA):
            if self.dcnt[i] > 0:
                toks.append(Tok(self.dsem[i], self.dcnt[i], {}))
        for k in self.eng:
            if self.cnt[k] > 0:
                toks.append(Tok(self.sem[k], self.cnt[k], {}))
        for e in self.eng:
            self._wait(e, [t for t in toks if t.sem is not self.sem[e]])

    def finish(self):
        toks = []
        for i in range(2 * self.NDMA):
            if self.dcnt[i] > 0:
                toks.append(Tok(self.dsem[i], self.dcnt[i], {}))
        for k in self.eng:
            if self.cnt[k] > 0 and k != "sp":
                toks.append(Tok(self.sem[k], self.cnt[k], {}))
        self._wait("sp", toks)


STAG_CFG = [4]
NS_CFG = [4]


def build(nlayers=DEPTH, dbg=(), stop=None):
    nc = bass.Bass("TRN2", target_bir_lowering=False)

    def din(name, shape):
        return nc.dram_tensor(name, list(shape), F32, kind="ExternalInput").ap()

    x_d = din("x", [S, D])
    w_in_d = din("w_in", [DEPTH, D, NPROJ])
    w_br_d = din("w_branch", [DEPTH, 3, D, D])
    w_out_d = din("w_out", [DEPTH, D, D])
    w_up_d = din("w_up", [DEPTH, D, DFF])
    w_dn_d = din("w_down", [DEPTH, DFF, D])
    colp_d = din("colp", [DEPTH, 128, NCOL])
    rowp_d = din("rowp", [DEPTH, NROW])
    gfin_d = din("gfin", [D])
    out_d = nc.dram_tensor("out", [S, D], F32, kind="ExternalOutput").ap()
    dbg_d = {}
    for name, shape, dt in (("d_ys", [3, D, S], BF16), ("d_xm", [S, D], F32)):
        if name in dbg:
            dbg_d[name] = nc.dram_tensor(name, shape, dt, kind="ExternalOutput").ap()

    with ExitStack() as es:
        P = Prog(nc, es)
        hT = P.sb("hT", [128, 8, S], BF16)
        hTv = [Buf(hT.t, "hT%d" % i) for i in range(NT)]
        NWB = 4
        wb = [P.sb("wb%d" % i, [128, 8, 512], BF16) for i in range(NWB)]
        wbi = [0]

        wlim = [NWB]

        def nextw():
            b = wb[wbi[0] % wlim[0]]
            wbi[0] += 1
            return b

        identb = P.sb("identb", [128, 128], BF16)
        identf = P.sb("identf", [128, 128], F32)
        onesb = P.sb("onesb", [128, 128], BF16)
        onesf = P.sb("onesf", [128, 128], F32)
        tincl = P.sb("tincl", [128, 128], F32)
        tinclT = P.sb("tinclT", [128, 128], F32)
        mlo = P.sb("mlo", [128, 128], F32)
        mup = P.sb("mup", [128, 128], F32)
        slf = P.sb("slf", [128, 128], F32)
        suf = P.sb("suf", [128, 128], F32)
        colps = [P.sb("colp_sb%d" % i, [128, NCOL], F32) for i in range(2)]
        rowp = P.sb("rowp_sb", [128, NROW], F32)
        wsm = P.sb("wsm", [128, 8, 48], BF16)
        wsf = P.sb("wsf", [128, 8, 48], F32)
        nsq_l = [P.sb("nsq%d" % i, [128, D], BF16) for i in range(2)]
        nss_l = [P.sb("nss%d" % i, [128, 2], F32) for i in range(2)]
        nxn_l = [P.sb("nxn%d" % i, [128, D], BF16) for i in range(2)]
        nrm_i = [0]
        pb = [P.ps("pb%d" % i, [128, 512], F32) for i in range(7)]
        pbt = P.ps("pbt", [128, 1024], BF16)
        ysT = P.dram("ysT", [3, D, S], BF16)
        ysv = [[Buf(ysT.t, "ys%d_%d" % (n, c)) for c in range(8)] for n in range(3)]
        xcur = P.dram("xcur", [S, D], F32)
        xcv = [Buf(xcur.t, "xc%d" % i) for i in range(NT)]

        def pool(fn, r, w):
            P.op("pool", fn, r, w)

        pool(lambda e: e.memset(onesf.t[:], 1.0), [], [onesf])
        pool(lambda e: e.memset(onesb.t[:], 1.0), [], [onesb])
        pool(lambda e: e.memset(identf.t[:], 0.0), [], [identf])
        pool(lambda e: e.affine_select(identf.t[:], identf.t[:], pattern=[[-1, 128]], compare_op=ALU.not_equal, fill=1.0, base=0, channel_multiplier=1), [identf], [identf])
        pool(lambda e: e.tensor_copy(identb.t[:], identf.t[:]), [identf], [identb])
        pool(lambda e: e.affine_select(tincl.t[:], onesf.t[:], pattern=[[1, 128]], compare_op=ALU.is_ge, fill=0.0, base=0, channel_multiplier=-1), [onesf], [tincl])
        pool(lambda e: e.affine_select(tinclT.t[:], onesf.t[:], pattern=[[-1, 128]], compare_op=ALU.is_ge, fill=0.0, base=0, channel_multiplier=1), [onesf], [tinclT])
        pool(lambda e: e.memset(mlo.t[:], 0.0), [], [mlo])
        pool(lambda e: e.affine_select(mlo.t[:], mlo.t[:], pattern=[[1, 128]], compare_op=ALU.is_ge, fill=NEG, base=0, channel_multiplier=-1), [mlo], [mlo])
        pool(lambda e: e.memset(mup.t[:], 0.0), [], [mup])
        pool(lambda e: e.affine_select(mup.t[:], mup.t[:], pattern=[[-1, 128]], compare_op=ALU.is_ge, fill=NEG, base=0, channel_multiplier=1), [mup], [mup])
        pool(lambda e: e.affine_select(slf.t[:], onesf.t[:], pattern=[[-1, 128]], compare_op=ALU.is_gt, fill=0.0, base=0, channel_multiplier=1), [onesf], [slf])
        pool(lambda e: e.affine_select(suf.t[:], onesf.t[:], pattern=[[1, 128]], compare_op=ALU.is_gt, fill=0.0, base=0, channel_multiplier=-1), [onesf], [suf])

        def dump(name, buf, ap, shape, dt=F32):
            if ("D_" + name) in dbg:
                dd = nc.dram_tensor("D_" + name, list(shape), dt, kind="ExternalOutput").ap()
                P.dma(dd, ap, [buf], [])

        uid = [0]

        def scope():
            class _S:
                def __enter__(s_):
                    s_.es = ExitStack()
                    s_.es.__enter__()
                    return s_

                def sb(s_, name, shape, dt):
                    uid[0] += 1
                    return Buf(s_.es.enter_context(nc.sbuf_tensor("%s_u%d" % (name, uid[0]), list(shape), dt)), name)

                def __exit__(s_, *a):
                    P.barrier()
                    return s_.es.__exit__(*a)
            return _S()

        evi = [0]

        def evac_copy(out_ap, in_ap, reads, writes, scale=None):
            evi[0] += 1
            if evi[0] % 2 == 0:
                if scale is None:
                    P.op("act", lambda e: e.copy(out_ap, in_ap), reads, writes)
                else:
                    P.op("act", lambda e: e.mul(out_ap, in_ap, scale), reads, writes)
            else:
                if scale is None:
                    P.op("dve", lambda e: e.tensor_copy(out_ap, in_ap), reads, writes)
                else:
                    P.op("dve", lambda e: e.tensor_scalar_mul(out_ap, in_ap, scale), reads, writes)

        def wload(dst, col0, src2d, c0, n):
            P.dma(dst.t[:, :, col0:col0 + n], src2d.rearrange("(kc p) n -> p kc n", p=128)[:, :, c0:c0 + n], [], [dst], q="pool")

        def rsqrt_(dst_ap, src_ap, scale, reads, writes):
            P.op("act", lambda e: e.activation(dst_ap, src_ap, AF.Ln, bias=EPS, scale=scale), reads, writes)
            P.op("act", lambda e: e.activation(dst_ap, dst_ap, AF.Exp, scale=-0.5), writes, writes)

        def norm_tile(xt_ap, xbuf, tt, gcol):
            gbuf, gap = gcol
            nrm_i[0] += 1
            nsq, nss, nxn = nsq_l[nrm_i[0] % 2], nss_l[nrm_i[0] % 2], nxn_l[nrm_i[0] % 2]
            pbo = (nrm_i[0] % 2) * 0
            P.op("act", lambda e: e.activation(nsq.t[:], xt_ap, AF.Square, accum_out=nss.t[:, 0:1]), [xbuf], [nsq, nss])
            rsqrt_(nss.t[:, 1:2], nss.t[:, 0:1], 1.0 / D, [nss], [nss])
            P.op("dve", lambda e: e.tensor_scalar(nxn.t[:], xt_ap, nss.t[:, 1:2], None, ALU.mult), [xbuf, nss], [nxn])
            for c in range(8):
                P.op("pe", lambda e: e.transpose(pbt.t[:, c * 128:(c + 1) * 128], nxn.t[:, c * 128:(c + 1) * 128], identb.t[:]), [nxn, identb], [pbt], inc=(c == 7))
            P.op("dve", lambda e: e.tensor_tensor(hT.t[:, :, tt * 128:(tt + 1) * 128], pbt.t[:].rearrange("p (c t) -> p c t", c=8),
                                                  gap.unsqueeze(2).to_broadcast([128, 8, 128]), ALU.mult), [pbt, gbuf], [hTv[tt]])

        def softplus_(dst, src, t1, t2, n, sign_logsig=False):
            P.op("dve", lambda e: e.scalar_tensor_tensor(t1.t[:, 0:n], src.t[:, 0:n], -1.0, src.t[:, 0:n], ALU.mult, ALU.max), [src], [t1])
            P.op("act", lambda e: e.activation(t2.t[:, 0:n], t1.t[:, 0:n], AF.Exp, scale=-1.0), [t1], [t2])
            P.op("act", lambda e: e.activation(t2.t[:, 0:n], t2.t[:, 0:n], AF.Ln, bias=1.0), [t2], [t2])
            if sign_logsig:
                P.op("dve", lambda e: e.scalar_tensor_tensor(dst.t[:, 0:n], src.t[:, 0:n], 0.0, t2.t[:, 0:n], ALU.min, ALU.subtract), [src, t2], [dst])
            else:
                P.op("dve", lambda e: e.scalar_tensor_tensor(dst.t[:, 0:n], src.t[:, 0:n], 0.0, t2.t[:, 0:n], ALU.max, ALU.add), [src, t2], [dst])

        def decay(out, negc_ap, bias_ap, cbufs, mask, bank, diag):
            P.op("dve", lambda e: e.tensor_scalar(diag.t[:], identf.t[:], negc_ap, None, ALU.mult), [identf] + cbufs, [diag])
            P.op("pe", lambda e: e.matmul(bank.t[:, 0:128], onesf.t[:], diag.t[:], start=True, stop=False), [onesf, diag], [bank], inc=False)
            P.op("pe", lambda e: e.matmul(bank.t[:, 0:128], identf.t[:], mask.t[:], start=False, stop=True), [identf, mask], [bank])
            P.op("act", lambda e: e.activation(out.t[:], bank.t[:, 0:128], AF.Exp, bias=bias_ap, scale=1.0), [bank] + cbufs, [out])

        def proj_fm(w, wcol, banks, evac, tbs=range(4)):
            for tb in tbs:
                bank = banks[tb % len(banks)]
                for kc in range(8):
                    P.op("pe", lambda e: e.matmul(bank.t[:, 0:512], w.t[:, kc, wcol:wcol + 128], hT.t[:, kc, tb * 512:(tb + 1) * 512],
                                                  start=(kc == 0), stop=(kc == 7)), [w] + hTv[tb * 4:tb * 4 + 4], [bank], inc=(kc == 7))
                evac(tb, bank)

        def proj_tm(w, wcol, ncols, tt, bank):
            for kc in range(8):
                P.op("pe", lambda e: e.matmul(bank.t[:, 0:ncols], hT.t[:, kc, tt * 128:(tt + 1) * 128], w.t[:, kc, wcol:wcol + ncols],
                                              start=(kc == 0), stop=(kc == 7)), [w, hTv[tt]], [bank], inc=(kc == 7))

        def dump_ys():
            if "d_ys" in dbg_d:
                for n in range(3):
                    for c in range(8):
                        P.dma(dbg_d["d_ys"][n, c * 128:(c + 1) * 128, :], ysT.t[n, c * 128:(c + 1) * 128, :], [ysv[n][c]], [])

        stopped = False
        P.dma(colps[0].t[:], colp_d[0], [], [colps[0]])
        for l in range(nlayers):
            w_in = w_in_d[l]
            colp = colps[l % 2]
            P.dma(rowp.t[:], rowp_d[l].partition_broadcast(128), [], [rowp])
            wrr = w_in.rearrange("(kc p) n -> p kc n", p=128)
            P.dma(wsf.t[:, :, 0:16], wrr[:, :, O_MG:O_MG + 16], [], [wsf])
            P.dma(wsf.t[:, :, 16:48], wrr[:, :, O_DG:O_DG + 32], [], [wsf])
            P.op("dve", lambda e: e.tensor_copy(wsm.t[:], wsf.t[:]), [wsf], [wsm])
            if l == 0:
                with scope() as sc:
                    xin = [sc.sb("xin%d" % i, [128, D], F32) for i in range(2)]
                    for tt in range(NT):
                        xb_ = xin[tt % 2]
                        P.dma(xb_.t[:], x_d[tt * 128:(tt + 1) * 128, :], [], [xb_])
                        P.dma(xcur.t[tt * 128:(tt + 1) * 128, :], xb_.t[:], [xb_], [xcv[tt]])
                        norm_tile(xb_.t[:], xb_, tt, (colp, colp.t[:, 0:8]))

            with scope() as sc:
                sb1 = sc.sb
                gm = sb1("gm", [128, 256], F32)
                ipre = sb1("ipre", [128, 128], F32)
                fpre = sb1("fpre", [128, 128], F32)
                lf = sb1("lf", [128, 128], F32)
                t1 = sb1("t1", [128, 256], F32)
                t2 = sb1("t2", [128, 256], F32)
                bcs = sb1("bcs", [128, 128], F32)
                totb = sb1("totb", [128, 128], F32)
                biasc = sb1("biasc", [128, 128], F32)
                expb = sb1("expb", [128, 128], F32)
                wgt = sb1("wgt", [128, 128], F32)
                dec = sb1("dec", [128, 128], F32)
                for tt in range(NT):
                    for kc in range(8):
                        P.op("pe", lambda e: e.matmul(pb[0].t[:, tt * 16:(tt + 1) * 16], hT.t[:, kc, tt * 128:(tt + 1) * 128], wsm.t[:, kc, 0:16],
                                                      start=(kc == 0), stop=(kc == 7)), [wsm, hTv[tt]], [pb[0]], inc=(kc == 7 and tt == NT - 1))
                P.op("dve", lambda e: e.tensor_tensor(gm.t[:].rearrange("p (t g) -> p t g", g=16), pb[0].t[:, 0:256].rearrange("p (t g) -> p t g", g=16),
                                                      rowp.t[:, 1152:1168].unsqueeze(1).to_broadcast([128, 16, 16]), ALU.add), [pb[0], rowp], [gm])
                gm5 = gm.t[:].rearrange("p (t d w h) -> p t d w h", d=2, w=2, h=4)
                v4 = lambda b_: b_.t[:].rearrange("p (t d h) -> p t d h", d=2, h=4)
                P.op("dve", lambda e: e.tensor_copy(v4(ipre), gm5[:, :, :, 0, :]), [gm], [ipre])
                P.op("dve", lambda e: e.tensor_copy(v4(fpre), gm5[:, :, :, 1, :]), [gm], [fpre])
                softplus_(lf, fpre, t1, t2, 128, sign_logsig=True)
                P.op("pe", lambda e: e.matmul(pb[1].t[:, 0:128], tincl.t[:], lf.t[:], start=True, stop=True), [tincl, lf], [pb[1]], inc=False)
                P.op("pe", lambda e: e.matmul(pb[1].t[:, 128:256], tinclT.t[:], lf.t[:], start=True, stop=True), [tinclT, lf], [pb[1]], inc=False)
                P.op("pe", lambda e: e.matmul(pb[1].t[:, 256:384], onesf.t[:], lf.t[:], start=True, stop=True), [onesf, lf], [pb[1]])
                pv = lambda a, b: pb[1].t[:, a:b].rearrange("p (t d h) -> p t d h", d=2, h=4)
                P.op("dve", lambda e: e.tensor_copy(v4(bcs)[:, :, 0, :], pv(0, 128)[:, :, 0, :]), [pb[1]], [bcs])
                P.op("dve", lambda e: e.tensor_copy(v4(bcs)[:, :, 1, :], pv(128, 256)[:, :, 1, :]), [pb[1]], [bcs])
                P.op("dve", lambda e: e.tensor_copy(totb.t[:], pb[1].t[:, 256:384]), [pb[1]], [totb])
                P.op("dve", lambda e: e.tensor_sub(biasc.t[:], ipre.t[:], bcs.t[:]), [ipre, bcs], [biasc])
                P.op("act", lambda e: e.activation(expb.t[:], bcs.t[:], AF.Exp), [bcs], [expb])
                P.op("dve", lambda e: e.tensor_add(t1.t[:, 0:128], biasc.t[:], totb.t[:]), [biasc, totb], [t1])
                P.op("act", lambda e: e.activation(wgt.t[:], t1.t[:, 0:128], AF.Exp), [t1], [wgt])
                P.op("act", lambda e: e.activation(dec.t[:], totb.t[:], AF.Exp), [totb], [dec])

                qT = sb1("qT", [128, 2, S], BF16)
                kT = sb1("kT", [128, 2, S], BF16)
                ktm = sb1("ktm", [128, NT, 256], BF16)
                vtm = sb1("vtm", [128, NT, 257], BF16)
                hm = sb1("hm", [128, NT, 256], F32)
                hmv = [Buf(hm.t, "hm%d" % i) for i in range(NT)]
                Cst = [sb1("Cst%d" % d_, [128, 2, 257], F32) for d_ in range(2)]
                Cb = [sb1("Cb%d" % d_, [128, 2, 257], BF16) for d_ in range(2)]
                DT = [sb1("DT%d" % d_, [128, 128], F32) for d_ in range(2)]
                diag = [sb1("diag%d" % d_, [128, 128], F32) for d_ in range(2)]
                scT = [sb1("scT%d" % d_, [128, 128], BF16) for d_ in range(2)]
                Asb = [sb1("Asb%d" % d_, [128, 257], F32) for d_ in range(2)]
                comb = [sb1("comb%d" % d_, [128, 257], F32) for d_ in range(2)]
                rr = [sb1("rr%d" % d_, [128, 2], F32) for d_ in range(2)]
                kw = [sb1("kw%d" % d_, [128, 256], BF16) for d_ in range(2)]
                og_l = [sb1("og%d" % i, [128, 256], F32) for i in range(2)]
                ty_l = [sb1("ty%d" % i, [128, 256], F32) for i in range(2)]
                yb_l = [sb1("yb%d" % i, [128, 256], BF16) for i in range(2)]
                ty = ty_l[0]
                ssq = sb1("ssq", [128, 2 * NT], F32)
                ymT = sb1("ymT", [128, 2, S], BF16)
                P.op("dve", lambda e: e.memset(vtm.t[:, :, 256:257], 1.0), [], [vtm])

                for h in range(4):
                    wqk, wkv, wo = nextw(), nextw(), nextw()
                    wload(wqk, 0, w_in, O_MQ + h * 256, 256)
                    wload(wqk, 256, w_in, O_MK + h * 256, 256)
                    wload(wkv, 0, w_in, O_MK + h * 256, 256)
                    wload(wkv, 256, w_in, O_MV + h * 256, 256)
                    wload(wo, 0, w_in, O_MO + h * 256, 256)
                    for ft in range(2):
                        proj_fm(wqk, ft * 128, pb[0:4], lambda tb, bank: evac_copy(qT.t[:, ft, tb * 512:(tb + 1) * 512], bank.t[:, 0:512], [bank], [qT], scale=0.0625))
                        proj_fm(wqk, 256 + ft * 128, pb[0:4], lambda tb, bank: evac_copy(kT.t[:, ft, tb * 512:(tb + 1) * 512], bank.t[:, 0:512], [bank], [kT]))
                    for tt in range(NT):
                        bank = pb[tt % 4]
                        proj_tm(wkv, 0, 512, tt, bank)
                        P.op("act", lambda e: e.copy(ktm.t[:, tt, :], bank.t[:, 0:256]), [bank], [ktm])
                        P.op("dve", lambda e: e.tensor_copy(vtm.t[:, tt, 0:256], bank.t[:, 256:512]), [bank], [vtm])
                    def gen_mrec(d_, h=h):
                        bRQ, bS, bA = (pb[0], pb[3])[d_], (pb[1], pb[4])[d_], (pb[2], pb[5])[d_]
                        mask = mlo if d_ == 0 else mup
                        for step in range(NT):
                            c = step if d_ == 0 else NT - 1 - step
                            cs = slice(c * 128, (c + 1) * 128)
                            gi = (c * 2 + d_) * 4 + h
                            col = lambda b_: b_.t[:, gi:gi + 1]
                            P.op("dve", lambda e: e.tensor_scalar(diag[d_].t[:], identf.t[:], col(bcs), None, ALU.mult), [identf, bcs], [diag[d_]])
                            if step < NT - 1:
                                P.op("act", lambda e: e.activation(kw[d_].t[:], ktm.t[:, c, :], AF.Copy, scale=col(wgt)), [ktm, wgt], [kw[d_]])
                            yield
                            P.op("pe", lambda e: e.matmul(bRQ.t[:, 0:128], onesf.t[:], diag[d_].t[:], start=True, stop=False), [onesf, diag[d_]], [bRQ], inc=False)
                            P.op("pe", lambda e: e.matmul(bRQ.t[:, 0:128], identf.t[:], mask.t[:], start=False, stop=True), [identf, mask], [bRQ], inc=False)
                            for kc in range(2):
                                P.op("pe", lambda e: e.matmul(bRQ.t[:, 128:256], kT.t[:, kc, cs], qT.t[:, kc, cs], start=(kc == 0), stop=(kc == 1)), [kT, qT], [bRQ], inc=(kc == 1))
                            if step > 0:
                                for kc in range(2):
                                    P.op("pe", lambda e: e.matmul(bA.t[:, 0:257], qT.t[:, kc, cs], Cb[d_].t[:, kc, :], start=(kc == 0), stop=(kc == 1)), [qT, Cb[d_]], [bA], inc=(kc == 1))
                            yield
                            P.op("act", lambda e: e.activation(DT[d_].t[:], bRQ.t[:, 0:128], AF.Exp, bias=col(biasc), scale=1.0), [bRQ, biasc], [DT[d_]])
                            if step > 0:
                                P.op("act", lambda e: e.activation(Asb[d_].t[:], bA.t[:, 0:257], AF.Copy, scale=col(expb)), [bA, expb], [Asb[d_]])
                            yield
                            P.op("dve", lambda e: e.tensor_tensor(scT[d_].t[:], bRQ.t[:, 128:256], DT[d_].t[:], ALU.mult), [bRQ, DT[d_]], [scT[d_]])
                            yield
                            P.op("pe", lambda e: e.matmul(bS.t[:, 0:257], scT[d_].t[:], vtm.t[:, c, :], start=True, stop=True), [scT[d_], vtm], [bS])
                            yield
                            if step > 0:
                                P.op("dve", lambda e: e.tensor_tensor(comb[d_].t[:], Asb[d_].t[:], bS.t[:, 0:257], ALU.add), [Asb[d_], bS], [comb[d_]])
                            else:
                                P.op("dve", lambda e: e.tensor_copy(comb[d_].t[:], bS.t[:, 0:257]), [bS], [comb[d_]])
                            den = comb[d_].t[:, 256:257]
                            P.op("dve", lambda e: e.scalar_tensor_tensor(rr[d_].t[:, 0:1], den, -1.0, den, ALU.mult, ALU.max), [comb[d_]], [rr[d_]])
                            P.op("dve", lambda e: e.tensor_scalar_max(rr[d_].t[:, 0:1], rr[d_].t[:, 0:1], 1.0), [rr[d_]], [rr[d_]])
                            P.op("dve", lambda e: e.reciprocal(rr[d_].t[:, 1:2], rr[d_].t[:, 0:1]), [rr[d_]], [rr[d_]])
                            if step < NT - 1:
                                P.op("pe", lambda e: e.matmul(bS.t[:, 0:257], kw[d_].t[:, 0:128], vtm.t[:, c, :], start=True, stop=True), [kw[d_], vtm], [bS])
                                P.op("pe", lambda e: e.matmul(bA.t[:, 0:257], kw[d_].t[:, 128:256], vtm.t[:, c, :], start=True, stop=True), [kw[d_], vtm], [bA])
                            yield
                            first = (d_ == 0 and c < 8) or (d_ == 1 and c >= 8)
                            if first:
                                P.op("act", lambda e: e.activation(hm.t[:, c, :], comb[d_].t[:, 0:256], AF.Copy, scale=rr[d_].t[:, 1:2]), [comb[d_], rr[d_]], [hmv[c]])
                            else:
                                P.op("dve", lambda e: e.scalar_tensor_tensor(hm.t[:, c, :], comb[d_].t[:, 0:256], rr[d_].t[:, 1:2], hm.t[:, c, :], ALU.mult, ALU.add), [comb[d_], rr[d_]], [hmv[c]])
                            if step < NT - 1:
                                for m, bC in enumerate((bS, bA)):
                                    if step == 0:
                                        P.op("dve", lambda e: e.tensor_copy(Cst[d_].t[:, m, :], bC.t[:, 0:257]), [bC], [Cst[d_]])
                                    else:
                                        P.op("dve", lambda e: e.scalar_tensor_tensor(Cst[d_].t[:, m, :], Cst[d_].t[:, m, :], col(dec), bC.t[:, 0:257], ALU.mult, ALU.add), [bC, dec], [Cst[d_]])
                                yield
                                P.op("act", lambda e: e.copy(Cb[d_].t[:], Cst[d_].t[:]), [Cst[d_]], [Cb[d_]])
                            yield

                    gens = [gen_mrec(0), gen_mrec(1)]
                    while gens:
                        for g_ in list(gens):
                            try:
                                next(g_)
                            except StopIteration:
                                gens.remove(g_)
                    for tt in range(NT):
                        P.op("act", lambda e: e.activation(ty.t[:], hm.t[:, tt, :], AF.Square, accum_out=ssq.t[:, tt:tt + 1]), [hmv[tt]], [ty, ssq])
                    rsqrt_(ssq.t[:, NT:2 * NT], ssq.t[:, 0:NT], 1.0 / 256.0, [ssq], [ssq])
                    for tt in range(NT):
                        bank = pb[tt % 4]
                        og, ty, yb = og_l[tt % 2], ty_l[tt % 2], yb_l[tt % 2]
                        proj_tm(wo, 0, 256, tt, bank)
                        P.op("act", lambda e: e.activation(og.t[:], bank.t[:, 0:256], AF.Sigmoid), [bank], [og])
                        P.op("dve", lambda e: e.scalar_tensor_tensor(ty.t[:], hm.t[:, tt, :], ssq.t[:, NT + tt:NT + tt + 1], rowp.t[:, h * 256:(h + 1) * 256], ALU.mult, ALU.mult), [hmv[tt], ssq, rowp], [ty])
                        P.op("dve", lambda e: e.tensor_tensor(yb.t[:], ty.t[:], og.t[:], ALU.mult), [ty, og], [yb])
                        for j in range(2):
                            P.op("pe", lambda e: e.transpose(pbt.t[:, j * 128:(j + 1) * 128], yb.t[:, j * 128:(j + 1) * 128], identb.t[:]), [yb, identb], [pbt], inc=(j == 1))
                        P.op("act", lambda e: e.copy(ymT.t[:, :, tt * 128:(tt + 1) * 128], pbt.t[:, 0:256].rearrange("p (j t) -> p j t", j=2)), [pbt], [ymT])
                    P.dma(ysT.t[0, h * 256:(h + 1) * 256, :].rearrange("(j p) t -> p j t", p=128), ymT.t[:], [ymT], [ysv[0][2 * h], ysv[0][2 * h + 1]])
                    if stop == "m0":
                        break
            if stop in ("m0", "mlstm"):
                stopped = True
                break

            with scope() as sc:
                sb2 = sc.sb
                A2 = lambda name: sb2(name, [128, 256], F32)
                v4 = lambda b_: b_.t[:].rearrange("p (t d h) -> p t d h", d=2, h=8)
                Gc, negG, expG, bexpG, kdw, glb, negbeta, beta = [A2(n) for n in ("Gc", "negG", "expG", "bexpG", "kdw", "glb", "negbeta", "beta")]
                with scope() as sct:
                    dg = sct.sb("dg", [128, 512], F32)
                    apre, spl, gg, t1, t2 = [sct.sb(n, [128, 256], F32) for n in ("apre", "spl", "gg", "t1d", "t2d")]
                    ea = sct.sb("ea", [128, 16], F32)
                    for tt in range(NT):
                        for kc in range(8):
                            P.op("pe", lambda e: e.matmul(pb[0].t[:, tt * 32:(tt + 1) * 32], hT.t[:, kc, tt * 128:(tt + 1) * 128], wsm.t[:, kc, 16:48],
                                                          start=(kc == 0), stop=(kc == 7)), [wsm, hTv[tt]], [pb[0]], inc=(kc == 7 and tt == NT - 1))
                    P.op("act", lambda e: e.copy(dg.t[:], pb[0].t[:, 0:512]), [pb[0]], [dg])
                    dg5 = dg.t[:].rearrange("p (t d w h) -> p t d w h", d=2, w=2, h=8)
                    P.op("act", lambda e: e.activation(v4(beta), dg5[:, :, :, 0, :], AF.Sigmoid), [dg], [beta])
                    P.op("dve", lambda e: e.tensor_tensor(v4(apre), dg5[:, :, :, 1, :],
                                                          rowp.t[:, 1184:1200].rearrange("p (d h) -> p d h", d=2).unsqueeze(1).to_broadcast([128, 16, 2, 8]), ALU.add), [dg, rowp], [apre])
                    softplus_(spl, apre, t1, t2, 256)
                    P.op("act", lambda e: e.activation(ea.t[:], rowp.t[:, 1168:1184], AF.Exp), [rowp], [ea])
                    P.op("dve", lambda e: e.scalar_tensor_tensor(v4(gg), v4(spl), -1.0,
                                                                 ea.t[:].rearrange("p (d h) -> p d h", d=2).unsqueeze(1).to_broadcast([128, 16, 2, 8]), ALU.mult, ALU.mult), [spl, ea], [gg])
                    P.op("pe", lambda e: e.matmul(pb[1].t[:, 0:256], tincl.t[:], gg.t[:], start=True, stop=True), [tincl, gg], [pb[1]], inc=False)
                    P.op("pe", lambda e: e.matmul(pb[1].t[:, 256:512], tinclT.t[:], gg.t[:], start=True, stop=True), [tinclT, gg], [pb[1]], inc=False)
                    P.op("pe", lambda e: e.matmul(pb[2].t[:, 0:256], onesf.t[:], gg.t[:], start=True, stop=True), [onesf, gg], [pb[2]])
                    pv = lambda a, b: pb[1].t[:, a:b].rearrange("p (t d h) -> p t d h", d=2, h=8)
                    P.op("dve", lambda e: e.tensor_copy(v4(Gc)[:, :, 0, :], pv(0, 256)[:, :, 0, :]), [pb[1]], [Gc])
                    P.op("dve", lambda e: e.tensor_copy(v4(Gc)[:, :, 1, :], pv(256, 512)[:, :, 1, :]), [pb[1]], [Gc])
                    P.op("dve", lambda e: e.tensor_scalar_mul(negG.t[:], Gc.t[:], -1.0), [Gc], [negG])
                    P.op("act", lambda e: e.activation(expG.t[:], Gc.t[:], AF.Exp), [Gc], [expG])
                    P.op("dve", lambda e: e.tensor_mul(bexpG.t[:], beta.t[:], expG.t[:]), [beta, expG], [bexpG])
                    P.op("dve", lambda e: e.tensor_tensor(t1.t[:], pb[2].t[:, 0:256], Gc.t[:], ALU.subtract), [pb[2], Gc], [t1])
                    P.op("act", lambda e: e.activation(kdw.t[:], t1.t[:], AF.Exp), [t1], [kdw])
                    P.op("act", lambda e: e.activation(glb.t[:], pb[2].t[:, 0:256], AF.Exp), [pb[2]], [glb])
                    P.op("dve", lambda e: e.tensor_scalar_mul(negbeta.t[:], beta.t[:], -1.0), [beta], [negbeta])
                for nm, b_ in (("Gc", Gc), ("expG", expG), ("kdw", kdw), ("glb", glb), ("beta", beta)):
                    dump(nm, b_, b_.t[:], [128, 256])

                pc = sb2("pc", [128, S + 2], F32)
                cv = sb2("cv", [128, S], F32)
                sq = sb2("sq", [128, S], BF16)
                rin = sb2("rin", [128, 512], F32)
                qnT = sb2("qnT", [128, S], BF16)
                knT = sb2("knT", [128, S], BF16)
                vT = sb2("vT", [128, S], BF16)
                ktm = sb2("ktm2", [128, NT, 128], BF16)
                vtm = sb2("vtm2", [128, NT, 128], BF16)
                WTs = sb2("WTs", [128, 32, 128], BF16)
                Us = sb2("Us", [128, 32, 128], F32)
                ATs = sb2("ATs", [128, 32, 128], BF16)
                stv = [Buf(None, "st%d" % i) for i in range(32)]
                osb = sb2("osb", [128, NT, 128], F32)
                osv = [Buf(osb.t, "os%d" % i) for i in range(NT)]
                NS = NS_CFG[0]
                STAG = STAG_CFG[0]
                wlim[0] = 3 if NS_CFG[0] > 4 else NWB
                _flat = wb[3].t[:].rearrange("p a b -> p (a b)")
                _off = [0]

                def slot_tile(i, name, shape, dt):
                    if i < 4:
                        return sb2("%s%d" % (name, i), shape, dt)
                    n = shape[1] * (2 if dt == F32 else 1)
                    ap = _flat[:, _off[0]:_off[0] + n]
                    _off[0] += n
                    if dt == F32:
                        ap = ap.bitcast(F32)
                    return Buf(ap, "%s%d" % (name, i))
                Dm = [slot_tile(i, "Dm", [128, 128], F32) for i in range(NS)]
                dgl = [slot_tile(i, "dgl", [128, 128], F32) for i in range(NS)]
                Do1 = [slot_tile(i, "Do1", [128, 128], F32) for i in range(NS)]
                Do2 = [slot_tile(i, "Do2", [128, 128], F32) for i in range(NS)]
                CH = [[slot_tile(i, "CH%d_" % j, [128, 384], BF16) for j in range(2)] for i in range(NS)]
                Ao = [slot_tile(i, "Ao", [128, 256], BF16) for i in range(NS)]
                AoT = [slot_tile(i, "AoT", [128, 256], BF16) for i in range(NS)]
                attn_t = [slot_tile(i, "attn", [128, 128], BF16) for i in range(NS)]
                X0b = [slot_tile(i, "X0b", [128, 256], BF16) for i in range(NS)]
                X1b = [slot_tile(i, "X1b", [128, 256], BF16) for i in range(NS)]
                R1b = X0b
                Tmb = Ao
                Wtm = Do1
                Wb = [slot_tile(i, "Wb", [128, 128], BF16) for i in range(NS)]
                mbd = [sb2("mbd%d" % i, [128, 128], F32) for i in range(2)]
                mo1 = [sb2("mo1%d" % i, [128, 128], F32) for i in range(2)]
                mo2 = [sb2("mo2%d" % i, [128, 128], F32) for i in range(2)]
                with scope() as scm:
                    E32 = scm.sb("E32", [4, 128], F32)
                    E64 = scm.sb("E64", [2, 128], F32)
                    b32 = scm.sb("b32", [128, 128], F32)
                    b64 = scm.sb("b64", [128, 128], F32)
                    tmk = scm.sb("tmk", [128, 128], F32)
                    for E_, w_ in ((E32, 32), (E64, 64)):
                        np_ = 128 // w_
                        P.op("pool", lambda e: e.memset(E_.t[:], 1.0), [], [E_])
                        P.op("pool", lambda e: e.affine_select(E_.t[:], E_.t[:], pattern=[[1, 128]], compare_op=ALU.is_ge, fill=0.0, base=0, channel_multiplier=-w_), [E_], [E_])
                        P.op("pool", lambda e: e.affine_select(E_.t[:], E_.t[:], pattern=[[-1, 128]], compare_op=ALU.is_ge, fill=0.0, base=w_ - 1, channel_multiplier=w_), [E_], [E_])
                    P.op("pe", lambda e: e.matmul(pb[0].t[:, 0:128], E32.t[:], E32.t[:], start=True, stop=True), [E32], [pb[0]], inc=False)
                    P.op("pe", lambda e: e.matmul(pb[0].t[:, 128:256], E64.t[:], E64.t[:], start=True, stop=True), [E64], [pb[0]])
                    P.op("dve", lambda e: e.tensor_copy(b32.t[:], pb[0].t[:, 0:128]), [pb[0]], [b32])
                    P.op("dve", lambda e: e.tensor_copy(b64.t[:], pb[0].t[:, 128:256]), [pb[0]], [b64])
                    for d_, tri in ((0, slf), (1, suf)):
                        P.op("dve", lambda e: e.tensor_tensor(mbd[d_].t[:], tri.t[:], b32.t[:], ALU.mult), [tri, b32], [mbd[d_]])
                        P.op("dve", lambda e: e.tensor_tensor(tmk.t[:], b64.t[:], b32.t[:], ALU.subtract), [b64, b32], [tmk])
                        P.op("dve", lambda e: e.tensor_tensor(mo1[d_].t[:], tri.t[:], tmk.t[:], ALU.mult), [tri, tmk], [mo1[d_]])
                        P.op("dve", lambda e: e.tensor_tensor(tmk.t[:], tri.t[:], b64.t[:], ALU.mult), [tri, b64], [tmk])
                        P.op("dve", lambda e: e.tensor_tensor(mo2[d_].t[:], tri.t[:], tmk.t[:], ALU.subtract), [tri, tmk], [mo2[d_]])
                Sst = [sb2("Sst%d" % d_, [128, 128], F32) for d_ in range(2)]
                Sbb = [sb2("Sbb%d" % d_, [128, 128], BF16) for d_ in range(2)]
                vnb = [sb2("vnb%d" % d_, [128, 128], BF16) for d_ in range(2)]
                kdt = [sb2("kdt%d" % d_, [128, 128], BF16) for d_ in range(2)]
                tmo = [sb2("tmo%d" % d_, [128, 128], F32) for d_ in range(2)]
                zg_l = [sb2("zg%d" % i, [128, 128], F32) for i in range(2)]
                ty_l = [sb2("ty2%d" % i, [128, 128], F32) for i in range(2)]
                yb_l = [sb2("yb2%d" % i, [128, 128], BF16) for i in range(2)]
                ty = ty_l[0]
                ssq = sb2("ssq2", [128, 2 * NT], F32)
                ydT = sq
                P.op("dve", lambda e: e.memset(pc.t[:, 0:1], 0.0), [], [pc])
                P.op("dve", lambda e: e.memset(pc.t[:, S + 1:S + 2], 0.0), [], [pc])
                wA = wB = None
                for h in range(8):
                    hh = h % 2
                    if hh == 0:
                        wA, wB = nextw(), nextw()
                        wload(wA, 0, w_in, O_DQ + h * 128, 256)
                        wload(wA, 256, w_in, O_DK + h * 128, 256)
                        wload(wB, 0, w_in, O_DV + h * 128, 256)
                        wload(wB, 256, w_in, O_DZ + h * 128, 256)
                    for j, (wsrc, off) in enumerate(((wA, hh * 128), (wA, 256 + hh * 128), (wB, hh * 128))):
                        proj_fm(wsrc, off, pb[0:4], lambda tb, bank: evac_copy(pc.t[:, 1 + tb * 512:1 + (tb + 1) * 512], bank.t[:, 0:512], [bank], [pc]))
                        cw = lambda k: colp.t[:, 16 + k * 24 + j * 8 + h:16 + k * 24 + j * 8 + h + 1]
                        P.op("dve", lambda e: e.tensor_scalar(cv.t[:], pc.t[:, 0:S], cw(0), None, ALU.mult), [pc, colp], [cv])
                        P.op("dve", lambda e: e.scalar_tensor_tensor(cv.t[:], pc.t[:, 1:S + 1], cw(1), cv.t[:], ALU.mult, ALU.add), [pc, colp], [cv])
                        P.op("dve", lambda e: e.scalar_tensor_tensor(cv.t[:], pc.t[:, 2:S + 2], cw(2), cv.t[:], ALU.mult, ALU.add), [pc, colp], [cv])
                        P.op("act", lambda e: e.activation(cv.t[:], cv.t[:], AF.Silu), [cv], [cv])
                        if j == 2:
                            P.op("act", lambda e: e.copy(vT.t[:], cv.t[:]), [cv], [vT])
                        else:
                            dst = qnT if j == 0 else knT
                            P.op("act", lambda e: e.activation(sq.t[:], cv.t[:], AF.Square), [cv], [sq])
                            for tb in range(4):
                                bs = slice(tb * 512, (tb + 1) * 512)
                                bank = pb[4 + tb % 2]
                                P.op("pe", lambda e: e.matmul(bank.t[:, 0:512], onesb.t[:], sq.t[:, bs], start=True, stop=True), [onesb, sq], [bank])
                                rsqrt_(rin.t[:], bank.t[:, 0:512], 1.0, [bank], [rin])
                                if j == 0:
                                    P.op("dve", lambda e: e.scalar_tensor_tensor(dst.t[:, bs], cv.t[:, bs], 128.0 ** -0.5, rin.t[:], ALU.mult, ALU.mult), [cv, rin], [dst])
                                else:
                                    P.op("dve", lambda e: e.tensor_tensor(dst.t[:, bs], cv.t[:, bs], rin.t[:], ALU.mult), [cv, rin], [dst])
                    if h == 0:
                        dump("qnT", qnT, qnT.t[:], [128, S], BF16)
                        dump("knT", knT, knT.t[:], [128, S], BF16)
                        dump("vT", vT, vT.t[:], [128, S], BF16)
                    for src, dstm in ((knT, ktm), (vT, vtm)):
                        for g4 in range(2):
                            for i in range(8):
                                tt = g4 * 8 + i
                                P.op("pe", lambda e: e.transpose(pbt.t[:, i * 128:(i + 1) * 128], src.t[:, tt * 128:(tt + 1) * 128], identb.t[:]), [src, identb], [pbt], inc=(i == 7))
                            evac_copy(dstm.t[:, g4 * 8:(g4 + 1) * 8, :], pbt.t[:].rearrange("p (i t) -> p i t", i=8), [pbt], [dstm])
                    if stop == "dn0a":
                        break
                    def gen_prep(si, c, d_, h=h):
                        cs = slice(c * 128, (c + 1) * 128)
                        gi = (c * 2 + d_) * 8 + h
                        col = lambda b_: b_.t[:, gi:gi + 1]
                        e_ = c * 2 + d_
                        ch0, ch1 = CH[si][0], CH[si][1]
                        bank = pb[1 + si]
                        mask = mup if d_ == 0 else mlo
                        P.op("dve", lambda e: e.tensor_scalar(dgl[si].t[:], identf.t[:], col(negG), None, ALU.mult), [identf, negG], [dgl[si]])
                        yield
                        while lock["pb0"] is not None:
                            yield
                        lock["pb0"] = si
                        P.op("pe", lambda e: e.matmul(pb[0].t[:, 0:128], onesf.t[:], dgl[si].t[:], start=True, stop=False), [onesf, dgl[si]], [pb[0]], inc=False)
                        P.op("pe", lambda e: e.matmul(pb[0].t[:, 0:128], identf.t[:], mask.t[:], start=False, stop=True), [identf, mask], [pb[0]], inc=False)
                        P.op("pe", lambda e: e.matmul(pb[0].t[:, 128:256], knT.t[:, cs], knT.t[:, cs], start=True, stop=True), [knT], [pb[0]], inc=False)
                        P.op("pe", lambda e: e.matmul(pb[0].t[:, 256:384], qnT.t[:, cs], knT.t[:, cs], start=True, stop=True), [qnT, knT], [pb[0]])
                        yield
                        P.op("act", lambda e: e.activation(Dm[si].t[:], pb[0].t[:, 0:128], AF.Exp, bias=col(Gc), scale=1.0), [pb[0], Gc], [Dm[si]])
                        P.op("act", lambda e: e.activation(X0b[si].t[:, 0:128], ktm.t[:, c, :], AF.Copy, scale=col(bexpG)), [ktm, bexpG], [X0b[si]])
                        P.op("act", lambda e: e.activation(X0b[si].t[:, 128:256], vtm.t[:, c, :], AF.Copy, scale=col(beta)), [vtm, beta], [X0b[si]])
                        yield
                        P.op("pool", lambda e: e.tensor_tensor(dgl[si].t[:], Dm[si].t[:], mbd[d_].t[:], ALU.mult), [Dm[si], mbd[d_]], [dgl[si]])
                        P.op("pool", lambda e: e.tensor_tensor(Do1[si].t[:], Dm[si].t[:], mo1[d_].t[:], ALU.mult), [Dm[si], mo1[d_]], [Do1[si]])
                        P.op("pool", lambda e: e.tensor_tensor(Do2[si].t[:], Dm[si].t[:], mo2[d_].t[:], ALU.mult), [Dm[si], mo2[d_]], [Do2[si]])
                        P.op("dve", lambda e: e.tensor_tensor(attn_t[si].t[:], pb[0].t[:, 256:384], Dm[si].t[:], ALU.mult), [pb[0], Dm[si]], [attn_t[si]])
                        yield
                        P.op("dve", lambda e: e.scalar_tensor_tensor(ch0.t[:, 256:384], pb[0].t[:, 128:256], col(negbeta), dgl[si].t[:], ALU.mult, ALU.mult), [pb[0], negbeta, dgl[si]], [ch0])
                        P.op("dve", lambda e: e.scalar_tensor_tensor(Ao[si].t[:, 0:128], pb[0].t[:, 128:256], col(beta), Do1[si].t[:], ALU.mult, ALU.mult), [pb[0], beta, Do1[si]], [Ao[si]])
                        P.op("dve", lambda e: e.scalar_tensor_tensor(Ao[si].t[:, 128:256], pb[0].t[:, 128:256], col(beta), Do2[si].t[:], ALU.mult, ALU.mult), [pb[0], beta, Do2[si]], [Ao[si]])
                        lock["pb0"] = None
                        yield
                        while lock["pbt"] is not None:
                            yield
                        lock["pbt"] = si
                        P.op("pe", lambda e: e.transpose(pbt.t[:, 0:128], ch0.t[:, 256:384], identb.t[:]), [ch0, identb], [pbt], inc=False)
                        P.op("pe", lambda e: e.transpose(pbt.t[:, 128:256], Ao[si].t[:, 0:128], identb.t[:]), [Ao[si], identb], [pbt], inc=False)
                        P.op("pe", lambda e: e.transpose(pbt.t[:, 256:384], Ao[si].t[:, 128:256], identb.t[:]), [Ao[si], identb], [pbt], inc=False)
                        P.op("pe", lambda e: e.transpose(pbt.t[:, 384:512], attn_t[si].t[:], identb.t[:]), [attn_t[si], identb], [pbt])
                        yield
                        P.op("act", lambda e: e.copy(ch0.t[:, 0:128], pbt.t[:, 0:128]), [pbt], [ch0])
                        P.op("act", lambda e: e.copy(AoT[si].t[:], pbt.t[:, 128:384]), [pbt], [AoT[si]])
                        P.op("dve", lambda e: e.tensor_tensor(ch1.t[:, 128:256], pbt.t[:, 0:128], identb.t[:], ALU.add), [pbt, identb], [ch1])
                        P.op("dve", lambda e: e.tensor_copy(ATs.t[:, e_, :], pbt.t[:, 384:512]), [pbt], [stv[e_]])
                        lock["pbt"] = None
                        yield
                        P.op("pe", lambda e: e.matmul(bank.t[:, 0:128], ch0.t[:, 256:384], ch0.t[:, 0:128], start=True, stop=True), [ch0], [bank], inc=False)
                        P.op("pe", lambda e: e.matmul(bank.t[:, 256:384], ch0.t[:, 0:128], ch0.t[:, 256:384], start=True, stop=True), [ch0], [bank])
                        yield
                        P.op("act", lambda e: e.copy(ch1.t[:].rearrange("p (a b) -> p a b", b=128)[:, 0::2, :], bank.t[:, 0:384].rearrange("p (a b) -> p a b", b=128)[:, 0::2, :]), [bank], [ch1])
                        yield
                        for j in range(1, 5):
                            cur = CH[si][j % 2]
                            nxt = CH[si][(j + 1) % 2]
                            if j < 4:
                                P.op("pe", lambda e: e.matmul(bank.t[:, 0:256], cur.t[:, 256:384], cur.t[:, 0:256], start=True, stop=False), [cur], [bank], inc=False)
                                P.op("pe", lambda e: e.matmul(bank.t[:, 128:256], identb.t[:], cur.t[:, 128:256], start=False, stop=True), [cur, identb], [bank], inc=False)
                                P.op("pe", lambda e: e.matmul(bank.t[:, 256:384], cur.t[:, 0:128], cur.t[:, 256:384], start=True, stop=True), [cur], [bank])
                                yield
                                evac_copy(nxt.t[:], bank.t[:, 0:384], [bank], [nxt])
                                yield
                            else:
                                P.op("pe", lambda e: e.matmul(bank.t[:, 128:256], cur.t[:, 256:384], cur.t[:, 128:256], start=True, stop=False), [cur], [bank], inc=False)
                                P.op("pe", lambda e: e.matmul(bank.t[:, 128:256], identb.t[:], cur.t[:, 128:256], start=False, stop=True), [cur, identb], [bank])
                                yield
                                evac_copy(nxt.t[:, 128:256], bank.t[:, 128:256], [bank], [nxt])
                                yield
                        fin = CH[si][1]
                        PTf = fin.t[:, 128:256]
                        A1T, A2T = AoT[si].t[:, 0:128], AoT[si].t[:, 128:256]
                        lo, hi = bank.t[:, 0:256], bank.t[:, 256:512]
                        P.op("pe", lambda e: e.matmul(lo, PTf, X0b[si].t[:], start=True, stop=True), [fin, X0b[si]], [bank])
                        yield
                        P.op("act", lambda e: e.copy(R1b[si].t[:], lo), [bank], [R1b[si]])
                        P.op("act", lambda e: e.copy(Us.t[:, e_, :], bank.t[:, 128:256]), [bank], [stv[e_]])
                        yield
                        P.op("pe", lambda e: e.matmul(hi, A1T, R1b[si].t[:], start=True, stop=True), [AoT[si], R1b[si]], [bank])
                        yield
                        P.op("act", lambda e: e.copy(Tmb[si].t[:], hi), [bank], [Tmb[si]])
                        yield
                        P.op("pe", lambda e: e.matmul(lo, PTf, Tmb[si].t[:], start=True, stop=True), [fin, Tmb[si]], [bank])
                        yield
                        P.op("dve", lambda e: e.tensor_tensor(X1b[si].t[:], R1b[si].t[:], lo, ALU.subtract), [R1b[si], bank], [X1b[si]])
                        P.op("dve", lambda e: e.tensor_tensor(Us.t[:, e_, :], Us.t[:, e_, :], bank.t[:, 128:256], ALU.subtract), [bank], [stv[e_]])
                        yield
                        P.op("pe", lambda e: e.matmul(hi, A2T, X1b[si].t[:], start=True, stop=True), [AoT[si], X1b[si]], [bank])
                        yield
                        P.op("act", lambda e: e.copy(Tmb[si].t[:], hi), [bank], [Tmb[si]])
                        yield
                        P.op("pe", lambda e: e.matmul(lo, PTf, Tmb[si].t[:], start=True, stop=True), [fin, Tmb[si]], [bank])
                        yield
                        P.op("act", lambda e: e.copy(R1b[si].t[:], lo), [bank], [R1b[si]])
                        P.op("dve", lambda e: e.tensor_tensor(Us.t[:, e_, :], Us.t[:, e_, :], bank.t[:, 128:256], ALU.subtract), [bank], [stv[e_]])
                        yield
                        P.op("pe", lambda e: e.matmul(hi, A1T, R1b[si].t[:], start=True, stop=True), [AoT[si], R1b[si]], [bank])
                        P.op("pool", lambda e: e.tensor_tensor(Wtm[si].t[:], X1b[si].t[:, 0:128], R1b[si].t[:, 0:128], ALU.subtract), [X1b[si], R1b[si]], [Wtm[si]])
                        yield
                        P.op("act", lambda e: e.copy(Tmb[si].t[:], hi), [bank], [Tmb[si]])
                        yield
                        P.op("pe", lambda e: e.matmul(lo, PTf, Tmb[si].t[:], start=True, stop=True), [fin, Tmb[si]], [bank])
                        yield
                        P.op("dve", lambda e: e.tensor_tensor(Wb[si].t[:], Wtm[si].t[:], bank.t[:, 0:128], ALU.add), [Wtm[si], bank], [Wb[si]])
                        P.op("dve", lambda e: e.tensor_tensor(Us.t[:, e_, :], Us.t[:, e_, :], bank.t[:, 128:256], ALU.add), [bank], [stv[e_]])
                        yield
                        while lock["pbt"] is not None:
                            yield
                        lock["pbt"] = si
                        P.op("pe", lambda e: e.transpose(pbt.t[:, 0:128], Wb[si].t[:], identb.t[:]), [Wb[si], identb], [pbt])
                        yield
                        P.op("act", lambda e: e.copy(WTs.t[:, e_, :], pbt.t[:, 0:128]), [pbt], [stv[e_]])
                        lock["pbt"] = None
                        yield

                    def gen_rec(h=h):
                        bA, bB = (pb[6], pb[6]) if NS_CFG[0] > 4 else (pb[5], pb[6])
                        for step in range(NT):
                            info = []
                            for d_ in range(2):
                                c = step if d_ == 0 else NT - 1 - step
                                info.append((d_, c, slice(c * 128, (c + 1) * 128), (c * 2 + d_) * 8 + h, c * 2 + d_))
                            for d_, c, cs, gi, e_ in info:
                                if step > 0:
                                    P.op("pe", lambda e: e.matmul(bA.t[:, d_ * 256:d_ * 256 + 128], WTs.t[:, e_, :], Sbb[d_].t[:], start=True, stop=True), [stv[e_], Sbb[d_]], [bA], inc=False)
                                    P.op("pe", lambda e: e.matmul(bA.t[:, d_ * 256 + 128:d_ * 256 + 256], qnT.t[:, cs], Sbb[d_].t[:], start=True, stop=True), [qnT, Sbb[d_]], [bA])
                                if step < NT - 1:
                                    P.op("act", lambda e: e.activation(kdt[d_].t[:], ktm.t[:, c, :], AF.Copy, scale=kdw.t[:, gi:gi + 1]), [ktm, kdw], [kdt[d_]])
                            yield
                            for d_, c, cs, gi, e_ in info:
                                if step > 0:
                                    P.op("dve", lambda e: e.tensor_tensor(vnb[d_].t[:], Us.t[:, e_, :], bA.t[:, d_ * 256:d_ * 256 + 128], ALU.subtract), [stv[e_], bA], [vnb[d_]])
                                    P.op("act", lambda e: e.activation(tmo[d_].t[:], bA.t[:, d_ * 256 + 128:d_ * 256 + 256], AF.Copy, scale=expG.t[:, gi:gi + 1]), [bA, expG], [tmo[d_]])
                                else:
                                    P.op("dve", lambda e: e.tensor_copy(vnb[d_].t[:], Us.t[:, e_, :]), [stv[e_]], [vnb[d_]])
                            yield
                            for d_, c, cs, gi, e_ in info:
                                P.op("pe", lambda e: e.matmul(bB.t[:, d_ * 256:d_ * 256 + 128], ATs.t[:, e_, :], vnb[d_].t[:], start=True, stop=True), [stv[e_], vnb[d_]], [bB], inc=(step == NT - 1))
                                if step < NT - 1:
                                    P.op("pe", lambda e: e.matmul(bB.t[:, d_ * 256 + 128:d_ * 256 + 256], kdt[d_].t[:], vnb[d_].t[:], start=True, stop=True), [kdt[d_], vnb[d_]], [bB])
                            yield
                            for d_, c, cs, gi, e_ in info:
                                if step < NT - 1:
                                    if step == 0:
                                        P.op("dve", lambda e: e.tensor_copy(Sst[d_].t[:], bB.t[:, d_ * 256 + 128:d_ * 256 + 256]), [bB], [Sst[d_]])
                                    else:
                                        P.op("dve", lambda e: e.scalar_tensor_tensor(Sst[d_].t[:], Sst[d_].t[:], glb.t[:, gi:gi + 1], bB.t[:, d_ * 256 + 128:d_ * 256 + 256], ALU.mult, ALU.add), [bB, glb], [Sst[d_]])
                                    P.op("act", lambda e: e.copy(Sbb[d_].t[:], Sst[d_].t[:]), [Sst[d_]], [Sbb[d_]])
                            for d_, c, cs, gi, e_ in info:
                                first = (d_ == 0 and c < 8) or (d_ == 1 and c >= 8)
                                if step > 0:
                                    P.op("dve", lambda e: e.tensor_tensor(tmo[d_].t[:], tmo[d_].t[:], bB.t[:, d_ * 256:d_ * 256 + 128], ALU.add), [bB], [tmo[d_]])
                                    src_ap, src_b = tmo[d_].t[:], tmo[d_]
                                    if first:
                                        P.op("act", lambda e: e.copy(osb.t[:, c, :], src_ap), [src_b], [osv[c]])
                                    else:
                                        P.op("dve", lambda e: e.tensor_tensor(osb.t[:, c, :], osb.t[:, c, :], src_ap, ALU.add), [src_b], [osv[c]])
                                else:
                                    if first:
                                        P.op("dve", lambda e: e.tensor_copy(osb.t[:, c, :], bB.t[:, d_ * 256:d_ * 256 + 128]), [bB], [osv[c]])
                                    else:
                                        P.op("dve", lambda e: e.tensor_tensor(osb.t[:, c, :], osb.t[:, c, :], bB.t[:, d_ * 256:d_ * 256 + 128], ALU.add), [bB], [osv[c]])
                            yield

                    lock = {"pb0": None, "pbt": None}
                    order = []
                    for i in range(NT):
                        order.append((i, 0))
                        order.append((NT - 1 - i, 1))
                    active = [None] * NS
                    nstarted = 0
                    nfinished = 0
                    finished = [False] * 32
                    rec = gen_rec()
                    rec_step = 0
                    rec_hop = 0
                    rec_done = False
                    tick = 0
                    while nfinished < 32 or not rec_done:
                        if nstarted < 32 and tick % STAG == 0:
                            for si in range(NS):
                                if active[si] is None:
                                    c, d_ = order[nstarted]
                                    active[si] = (gen_prep(si, c, d_), nstarted)
                                    nstarted += 1
                                    break
                        for si in range(NS):
                            if active[si] is not None:
                                g_, idx = active[si]
                                try:
                                    next(g_)
                                except StopIteration:
                                    finished[idx] = True
                                    nfinished += 1
                                    active[si] = None
                        if not rec_done and (stop != "dn0b"):
                            if rec_hop > 0 or (finished[2 * rec_step] and finished[2 * rec_step + 1]):
                                try:
                                    next(rec)
                                    rec_hop += 1
                                    if rec_hop == 4:
                                        rec_hop = 0
                                        rec_step += 1
                                        if rec_step == NT:
                                            rec_done = True
                                except StopIteration:
                                    rec_done = True
                        elif stop == "dn0b":
                            rec_done = True
                        tick += 1
                    if stop == "dn0c":
                        break
                    if h == 0:
                        dump("osb", osv[0], osb.t[:], [128, NT, 128])
                    for tt in range(NT):
                        P.op("act", lambda e: e.activation(ty.t[:], osb.t[:, tt, :], AF.Square, accum_out=ssq.t[:, tt:tt + 1]), [osv[tt]], [ty, ssq])
                    rsqrt_(ssq.t[:, NT:2 * NT], ssq.t[:, 0:NT], 1.0 / 128.0, [ssq], [ssq])
                    for tt in range(NT):
                        bank = pb[4 + tt % 2]
                        zg, ty, yb = zg_l[tt % 2], ty_l[tt % 2], yb_l[tt % 2]
                        proj_tm(wB, 256 + hh * 128, 128, tt, bank)
                        P.op("act", lambda e: e.activation(zg.t[:], bank.t[:, 0:128], AF.Silu), [bank], [zg])
                        P.op("dve", lambda e: e.scalar_tensor_tensor(ty.t[:], osb.t[:, tt, :], ssq.t[:, NT + tt:NT + tt + 1], rowp.t[:, 1024:1152], ALU.mult, ALU.mult), [osv[tt], ssq, rowp], [ty])
                        P.op("dve", lambda e: e.tensor_tensor(yb.t[:], ty.t[:], zg.t[:], ALU.mult), [ty, zg], [yb])
                        P.op("pe", lambda e: e.transpose(pbt.t[:, 0:128], yb.t[:], identb.t[:]), [yb, identb], [pbt])
                        P.op("act", lambda e: e.copy(ydT.t[:, tt * 128:(tt + 1) * 128], pbt.t[:, 0:128]), [pbt], [ydT])
                    P.dma(ysT.t[1, h * 128:(h + 1) * 128, :], ydT.t[:], [ydT], [ysv[1][h]])
                    if stop == "dn0":
                        break
            if stop in ("dn0", "dn", "dn0a", "dn0b", "dn0c"):
                stopped = True
                break

            wlim[0] = NWB
            with scope() as sc:
                cx = sc.sb("cx", [128, S + 2], F32)
                Bsb = sc.sb("Bsb", [128, S], F32)
                ycv = sc.sb("ycv", [128, S], F32)
                tmx_l = [sc.sb("tmx%d" % i, [128, 512], F32) for i in range(2)]
                ycT = sc.sb("ycT", [128, S], BF16)
                P.op("dve", lambda e: e.memset(cx.t[:, 0:1], 0.0), [], [cx])
                P.op("dve", lambda e: e.memset(cx.t[:, S + 1:S + 2], 0.0), [], [cx])
                wA = wB = None
                for dc in range(8):
                    dd = dc % 2
                    if dd == 0:
                        wA, wB = nextw(), nextw()
                        wload(wA, 0, w_in, O_SB + dc * 128, 256)
                        wload(wA, 256, w_in, O_SC + dc * 128, 256)
                        wload(wB, 0, w_in, O_SX + dc * 128, 256)
                    for tb in range(4):
                        bs = slice(tb * 512, (tb + 1) * 512)
                        tmx = tmx_l[tb % 2]
                        for j, (wsrc, off) in enumerate(((wA, dd * 128), (wA, 256 + dd * 128), (wB, dd * 128))):
                            bank = pb[j + 3 * (tb % 2)]
                            for kc in range(8):
                                P.op("pe", lambda e: e.matmul(bank.t[:, 0:512], wsrc.t[:, kc, off:off + 128], hT.t[:, kc, bs], start=(kc == 0), stop=(kc == 7)),
                                     [wsrc] + hTv[tb * 4:tb * 4 + 4], [bank], inc=(kc == 7))
                        o3 = 3 * (tb % 2)
                        P.op("act", lambda e: e.copy(Bsb.t[:, bs], pb[o3].t[:, 0:512]), [pb[o3]], [Bsb])
                        P.op("act", lambda e: e.copy(tmx.t[:], pb[o3 + 2].t[:, 0:512]), [pb[o3 + 2]], [tmx])
                        P.op("dve", lambda e: e.tensor_tensor(cx.t[:, 1 + tb * 512:1 + (tb + 1) * 512], pb[o3 + 1].t[:, 0:512], tmx.t[:], ALU.mult), [pb[o3 + 1], tmx], [cx])
                    cw = lambda k: colp.t[:, 88 + k * 8 + dc:88 + k * 8 + dc + 1]
                    P.op("dve", lambda e: e.tensor_scalar(ycv.t[:], cx.t[:, 0:S], cw(0), None, ALU.mult), [cx, colp], [ycv])
                    P.op("dve", lambda e: e.scalar_tensor_tensor(ycv.t[:], cx.t[:, 1:S + 1], cw(1), ycv.t[:], ALU.mult, ALU.add), [cx, colp], [ycv])
                    P.op("dve", lambda e: e.scalar_tensor_tensor(ycv.t[:], cx.t[:, 2:S + 2], cw(2), ycv.t[:], ALU.mult, ALU.add), [cx, colp], [ycv])
                    P.op("dve", lambda e: e.tensor_tensor(ycT.t[:], ycv.t[:], Bsb.t[:], ALU.mult), [ycv, Bsb], [ycT])
                    P.dma(ysT.t[2, dc * 128:(dc + 1) * 128, :], ycT.t[:], [ycT], [ysv[2][dc]])
            if stop == "sc":
                stopped = True
                break

            if l + 1 < nlayers:
                P.dma(colps[(l + 1) % 2].t[:], colp_d[l + 1], [], [colps[(l + 1) % 2]])
            last = (l == DEPTH - 1)
            for half in range(2):
                with scope() as sch:
                    xres = sch.sb("xres", [128, 8, D], F32)
                    xrv = [Buf(xres.t, "xr%d" % i) for i in range(8)]
                    with scope() as sc:
                        ys_sb = [sc.sb("ys_sb%d" % n, [128, 8, 1024], BF16) for n in range(3)]
                        sg_l = [[sc.sb("sg%d_%d" % (n, i), [128, 512], F32) for n in range(3)] for i in range(2)]
                        acc_l = [sc.sb("acc%d" % i, [128, 512], F32) for i in range(2)]
                        tmm_l = [sc.sb("tmm", [128, 512], F32)] * 2
                        mixT = sc.sb("mixT", [128, 8, 1024], BF16)
                        for n in range(3):
                            P.dma(ys_sb[n].t[:], ysT.t[n].rearrange("(kc p) t -> p kc t", p=128)[:, :, half * 1024:(half + 1) * 1024], ysv[n], [ys_sb[n]])
                        wA = wB = wC = None
                        for dc in range(8):
                            dd = dc % 2
                            if dd == 0:
                                wA, wB, wC = nextw(), nextw(), nextw()
                                wload(wA, 0, w_br_d[l, 0], dc * 128, 256)
                                wload(wA, 256, w_br_d[l, 1], dc * 128, 256)
                                wload(wB, 0, w_br_d[l, 2], dc * 128, 256)
                                wload(wB, 256, w_in, O_MRG + dc * 128, 256)
                                wload(wC, 0, w_in, O_MRG + 1024 + dc * 128, 256)
                                wload(wC, 256, w_in, O_MRG + 2048 + dc * 128, 256)
                            wbr = ((wA, dd * 128), (wA, 256 + dd * 128), (wB, dd * 128))
                            wgt_ = ((wB, 256 + dd * 128), (wC, dd * 128), (wC, 256 + dd * 128))
                            for tbh in range(2):
                                tb = half * 2 + tbh
                                bs = slice(tb * 512, (tb + 1) * 512)
                                bsh = slice(tbh * 512, (tbh + 1) * 512)
                                sg, acc, tmm = sg_l[tbh], acc_l[tbh], tmm_l[tbh]
                                for n in range(3):
                                    wsrc, off = wgt_[n]
                                    for kc in range(8):
                                        P.op("pe", lambda e: e.matmul(pb[3 + n].t[:, 0:512], wsrc.t[:, kc, off:off + 128], hT.t[:, kc, bs], start=(kc == 0), stop=(kc == 7)),
                                             [wsrc] + hTv[tb * 4:tb * 4 + 4], [pb[3 + n]], inc=(kc == 7))
                                    P.op("act", lambda e: e.activation(sg[n].t[:], pb[3 + n].t[:, 0:512], AF.Sigmoid), [pb[3 + n]], [sg[n]])
                                for n in range(3):
                                    wsrc, off = wbr[n]
                                    for kc in range(8):
                                        P.op("pe", lambda e: e.matmul(pb[n].t[:, 0:512], wsrc.t[:, kc, off:off + 128], ys_sb[n].t[:, kc, bsh], start=(kc == 0), stop=(kc == 7)),
                                             [wsrc, ys_sb[n]], [pb[n]], inc=(kc == 7))
                                P.op("dve", lambda e: e.tensor_tensor(acc.t[:], sg[0].t[:], pb[0].t[:, 0:512], ALU.mult), [sg[0], pb[0]], [acc])
                                P.op("dve", lambda e: e.tensor_tensor(tmm.t[:], sg[1].t[:], pb[1].t[:, 0:512], ALU.mult), [sg[1], pb[1]], [tmm])
                                P.op("dve", lambda e: e.tensor_tensor(sg[2].t[:], sg[2].t[:], pb[2].t[:, 0:512], ALU.mult), [pb[2]], [sg[2]])
                                P.op("dve", lambda e: e.tensor_tensor(acc.t[:], acc.t[:], tmm.t[:], ALU.add), [tmm], [acc])
                                P.op("dve", lambda e: e.tensor_tensor(mixT.t[:, dc, bsh], acc.t[:], sg[2].t[:], ALU.add), [acc, sg[2]], [mixT])
                        wo0, wo1 = nextw(), nextw()
                        wload(wo0, 0, w_out_d[l], 0, 512)
                        wload(wo1, 0, w_out_d[l], 512, 512)
                        for t8 in range(8):
                            tt = half * 8 + t8
                            P.dma(xres.t[:, t8, :], xcur.t[tt * 128:(tt + 1) * 128, :], [xcv[tt]], [xrv[t8]])
                            for nb, wsrc in enumerate((wo0, wo1)):
                                bank = pb[(t8 * 2 + nb) % 4]
                                for dc in range(8):
                                    P.op("pe", lambda e: e.matmul(bank.t[:, 0:512], mixT.t[:, dc, t8 * 128:(t8 + 1) * 128], wsrc.t[:, dc, :], start=(dc == 0), stop=(dc == 7)),
                                         [mixT, wsrc], [bank], inc=(dc == 7))
                                P.op("dve", lambda e: e.tensor_tensor(xres.t[:, t8, nb * 512:(nb + 1) * 512], xres.t[:, t8, nb * 512:(nb + 1) * 512], bank.t[:, 0:512], ALU.add), [bank], [xrv[t8]])
                            norm_tile(xres.t[:, t8, :], xrv[t8], tt, (colp, colp.t[:, 8:16]))
                            if stop == "mix" and "d_xm" in dbg_d:
                                P.dma(dbg_d["d_xm"][tt * 128:(tt + 1) * 128, :], xres.t[:, t8, :], [xrv[t8]], [])
                    if stop == "mix":
                        continue
                    with scope() as sc:
                        upT = sc.sb("upT", [128, 8, 1024], BF16)
                        relu_t = [sc.sb("relu_t%d" % i, [128, 512], F32) for i in range(2)]
                        otile = [sc.sb("otile%d" % i, [128, D], F32) for i in range(2)] if last else None
                        gfin = sc.sb("gfin_sb", [128, D], F32) if last else None
                        if last:
                            P.dma(gfin.t[:], gfin_d.partition_broadcast(128), [], [gfin])
                        for fb in range(4):
                            wu = [nextw(), nextw()]
                            wd = [nextw(), nextw()]
                            wload(wu[0], 0, w_up_d[l], fb * 1024, 512)
                            wload(wu[1], 0, w_up_d[l], fb * 1024 + 512, 512)
                            wload(wd[0], 0, w_dn_d[l, fb * 1024:(fb + 1) * 1024, :], 0, 512)
                            wload(wd[1], 0, w_dn_d[l, fb * 1024:(fb + 1) * 1024, :], 512, 512)
                            for fc in range(8):
                                def ev(tb, bank):
                                    bsh = slice((tb - half * 2) * 512, (tb - half * 2 + 1) * 512)
                                    rl = relu_t[tb % 2]
                                    P.op("act", lambda e: e.activation(rl.t[:], bank.t[:, 0:512], AF.Relu), [bank], [rl])
                                    P.op("dve", lambda e: e.tensor_tensor(upT.t[:, fc, bsh], rl.t[:], rl.t[:], ALU.mult), [rl], [upT])
                                proj_fm(wu[fc // 4], (fc % 4) * 128, pb[0:4], ev, tbs=(half * 2, half * 2 + 1))
                            for t8 in range(8):
                                tt = half * 8 + t8
                                for nb in range(2):
                                    bank = pb[4 + (t8 * 2 + nb) % 3]
                                    for fc in range(8):
                                        P.op("pe", lambda e: e.matmul(bank.t[:, 0:512], upT.t[:, fc, t8 * 128:(t8 + 1) * 128], wd[nb].t[:, fc, :], start=(fc == 0), stop=(fc == 7)),
                                             [upT, wd[nb]], [bank], inc=(fc == 7))
                                    P.op("dve", lambda e: e.tensor_tensor(xres.t[:, t8, nb * 512:(nb + 1) * 512], xres.t[:, t8, nb * 512:(nb + 1) * 512], bank.t[:, 0:512], ALU.add), [bank], [xrv[t8]])
                                if fb == 3:
                                    if stop == "mlp" and "d_xm" in dbg_d:
                                        P.dma(dbg_d["d_xm"][tt * 128:(tt + 1) * 128, :], xres.t[:, t8, :], [xrv[t8]], [])
                                    if not last:
                                        P.dma(xcur.t[tt * 128:(tt + 1) * 128, :], xres.t[:, t8, :], [xrv[t8]], [xcv[tt]])
                                        if l + 1 < nlayers:
                                            cn = colps[(l + 1) % 2]
                                            norm_tile(xres.t[:, t8, :], xrv[t8], tt, (cn, cn.t[:, 0:8]))
                                    else:
                                        ot = otile[t8 % 2]
                                        nsq, nss = nsq_l[t8 % 2], nss_l[t8 % 2]
                                        P.op("act", lambda e: e.activation(nsq.t[:], xres.t[:, t8, :], AF.Square, accum_out=nss.t[:, 0:1]), [xrv[t8]], [nsq, nss])
                                        rsqrt_(nss.t[:, 1:2], nss.t[:, 0:1], 1.0 / D, [nss], [nss])
                                        P.op("dve", lambda e: e.scalar_tensor_tensor(ot.t[:], xres.t[:, t8, :], nss.t[:, 1:2], gfin.t[:], ALU.mult, ALU.mult), [xrv[t8], nss, gfin], [ot])
                                        P.dma(out_d[tt * 128:(tt + 1) * 128, :], ot.t[:], [ot], [])
            if stop in ("mix", "mlp"):
                stopped = True
                break
        if stopped:
            dump_ys()
        P.finish()
        print("build: ops", P.nops, "waits", P.nwait, {k: P.cnt[k] for k in P.cnt})
    return nc


def make_params(inp):
    colp = np.zeros((DEPTH, 128, NCOL), np.float32)
    rowp = np.zeros((DEPTH, NROW), np.float32)
    for l in range(DEPTH):
        colp[l, :, 0:8] = inp["norm_mix_g"][l].reshape(8, 128).T
        colp[l, :, 8:16] = inp["norm_mlp_g"][l].reshape(8, 128).T
        colp[l, :, 16:88] = inp["dn_conv_w"][l].reshape(3, 24, 128).transpose(2, 0, 1).reshape(128, 72)
        colp[l, :, 88:112] = inp["sc_conv_w"][l].reshape(3, 8, 128).transpose(2, 0, 1).reshape(128, 24)
        rowp[l, 0:1024] = inp["m_norm_g"][l]
        rowp[l, 1024:1152] = inp["dn_norm_g"][l]
        rowp[l, 1152:1168] = inp["m_gate_b"][l].reshape(16)
        rowp[l, 1168:1184] = inp["dn_a_log"][l].reshape(16)
        rowp[l, 1184:1200] = inp["dn_dt_bias"][l].reshape(16)
    return colp, rowp


def make_in_maps(inp, cores):
    colp, rowp = make_params(inp)
    shared = {"w_in": np.ascontiguousarray(inp["w_in"]), "w_branch": np.ascontiguousarray(inp["w_branch"]),
              "w_out": np.ascontiguousarray(inp["w_out"]), "w_up": np.ascontiguousarray(inp["w_up"]),
              "w_down": np.ascontiguousarray(inp["w_down"]), "colp": colp, "rowp": rowp,
              "gfin": np.ascontiguousarray(inp["norm_final_g"])}
    return [dict(shared, x=np.ascontiguousarray(inp["x"][b])) for b in cores]


def kernel(**inputs):
    inp = {k: np.asarray(v, dtype=np.float32) for k, v in inputs.items()}
    nc = build()
    in_maps = make_in_maps(inp, list(range(8)))
    res = run_bass_kernel_spmd(nc, in_maps, core_ids=list(range(8)))
    return np.stack([r["out"] for r in res.results], axis=0).astype(np.float32)
```

```python
import numpy as np
import concourse.bass as bass
import concourse.mybir as mybir
from concourse.bass_utils import run_bass_kernel_spmd
from contextlib import ExitStack

F32 = mybir.dt.float32
BF16 = mybir.dt.bfloat16
ALU = mybir.AluOpType
AF = mybir.ActivationFunctionType
AX = mybir.AxisListType

S = 2048
D = 1024
NT = 16
DEPTH = 4
NPROJ = 14384
DFF = 4096
EPS = 1e-6
NCOL = 112
NROW = 1200
O_MQ, O_MK, O_MV, O_MO, O_MG = 0, 1024, 2048, 3072, 4096
O_DQ, O_DK, O_DV, O_DZ, O_DG = 4112, 5136, 6160, 7184, 8208
O_SB, O_SC, O_SX, O_MRG = 8240, 9264, 10288, 11312
NEG = -30000.0


class Tok:
    __slots__ = ("sem", "val", "clk")

    def __init__(self, sem, val, clk):
        self.sem, self.val, self.clk = sem, val, clk


class Buf:
    __slots__ = ("t", "w", "r", "name", "excl", "lastrd")

    def __init__(self, t, name, excl=False):
        self.t, self.name = t, name
        self.w = None
        self.r = []
        self.excl = excl
        self.lastrd = None


class Prog:
    NDMA = 12

    def __init__(self, nc, es, same_engine_sync=True):
        self.nc, self.es = nc, es
        self.same = same_engine_sync
        self.eng = {"pe": nc.tensor, "act": nc.scalar, "dve": nc.vector, "pool": nc.gpsimd, "sp": nc.sync}
        self.sem = {k: es.enter_context(nc.semaphore("s_" + k)) for k in self.eng}
        self.cnt = {k: 0 for k in self.eng}
        self.clk = {k: {} for k in self.eng}
        self.pend = {k: [] for k in self.eng}
        self.dsem = [es.enter_context(nc.semaphore("d%d" % i)) for i in range(2 * self.NDMA)]
        self.dcnt = [0] * (2 * self.NDMA)
        self.dnext = {"sp": 0, "pool": 0}
        self.nwait = 0
        self.nops = 0

    def sb(self, name, shape, dt):
        t = self.es.enter_context(self.nc.sbuf_tensor(name, list(shape), dt))
        return Buf(t, name)

    def ps(self, name, shape, dt):
        t = self.es.enter_context(self.nc.psum_tensor(name, list(shape), dt))
        return Buf(t, name, excl=True)

    def dram(self, name, shape, dt):
        t = self.nc.dram_tensor(name, list(shape), dt, kind="Internal").ap()
        return Buf(t, name)

    def _need(self, reads, writes):
        toks = []
        for b in reads:
            if b.w is not None:
                toks.append(b.w)
        for b in writes:
            if b.w is not None:
                toks.append(b.w)
            toks.extend(b.r)
        return toks

    def _wait(self, e, toks):
        clk = self.clk[e]
        eng = self.eng[e]
        own = self.sem[e]
        best = {}
        for t in toks:
            if t.sem is own and (e == "pe" or not self.same):
                continue
            k = id(t.sem)
            if clk.get(k, 0) >= t.val:
                continue
            if k not in best or best[k].val < t.val:
                best[k] = t
        for k, t in best.items():
            if clk.get(k, 0) >= t.val:
                continue
            eng.wait_ge(t.sem, t.val)
            self.nwait += 1
            clk[k] = t.val
            for kk, vv in t.clk.items():
                if clk.get(kk, 0) < vv:
                    clk[kk] = vv

    def op(self, e, fn, reads, writes, inc=True):
        xr = [b for b in reads if b.excl]
        if xr:
            reads = [b for b in reads if not b.excl]
        toks = self._need(reads, writes)
        for b in xr:
            if b.w is not None and not (b.lastrd == e and b.w.sem is self.sem[e]):
                toks.append(b.w)
            toks.extend(b.r)
        self._wait(e, toks)
        for b in writes:
            b.lastrd = None
        if xr:
            writes = list(writes) + xr
        ins = fn(self.eng[e])
        self.nops += 1
        if not inc:
            self.pend[e].append((reads, writes))
            return ins
        self.cnt[e] += 1
        ins.then_inc(self.sem[e], 1)
        tok = Tok(self.sem[e], self.cnt[e], dict(self.clk[e]))
        tok.clk[id(self.sem[e])] = self.cnt[e]
        for (rs, ws) in self.pend[e] + [(reads, writes)]:
            for b in rs:
                b.r.append(tok)
            for b in ws:
                b.w = tok
                b.r = []
        for b in xr:
            b.lastrd = e
        self.pend[e] = []
        return ins

    def dma(self, out, in_, reads, writes, q="sp"):
        toks = self._need(reads, writes)
        i = self.dnext[q] + (self.NDMA if q == "pool" else 0)
        self.dnext[q] = (self.dnext[q] + 1) % self.NDMA
        s = self.dsem[i]
        if self.dcnt[i] > 0:
            toks.append(Tok(s, self.dcnt[i], {}))
        self._wait(q, toks)
        ins = self.eng[q].dma_start(out=out, in_=in_)
        self.dcnt[i] += 16
        ins.then_inc(s, 16)
        tok = Tok(s, self.dcnt[i], dict(self.clk[q]))
        for b in reads:
            b.r.append(tok)
        for b in writes:
            b.w = tok
            b.r = []
        self.nops += 1
        return ins

    def barrier(self):
        toks = []
        for i in range(2 * self.NDMA):
            if self.dcnt[i] > 0:
                toks.append(Tok(self.dsem[i], self.dcnt[i], {}))
        for k in self.eng:
            if self.cnt[k] > 0:
                toks.append(Tok(self.sem[k], self.cnt[k], {}))
        for e in self.eng:
            self._wait(e, [t for t in toks if t.sem is not self.sem[e]])

    def finish(self):
        toks = []
        for i in range(2 * self.NDMA):
            if self.dcnt[i] > 0:
                toks.append(Tok(self.dsem[i], self.dcnt[i], {}))
        for k in self.eng:
            if self.cnt[k] > 0 and k != "sp":
                toks.append(Tok(self.sem[k], self.cnt[k], {}))
        self._wait("sp", toks)


STAG_CFG = [4]
NS_CFG = [4]


def build(nlayers=DEPTH, dbg=(), stop=None):
    nc = bass.Bass("TRN2", target_bir_lowering=False)

    def din(name, shape):
        return nc.dram_tensor(name, list(shape), F32, kind="ExternalInput").ap()

    x_d = din("x", [S, D])
    w_in_d = din("w_in", [DEPTH, D, NPROJ])
    w_br_d = din("w_branch", [DEPTH, 3, D, D])
    w_out_d = din("w_out", [DEPTH, D, D])
    w_up_d = din("w_up", [DEPTH, D, DFF])
    w_dn_d = din("w_down", [DEPTH, DFF, D])
    colp_d = din("colp", [DEPTH, 128, NCOL])
    rowp_d = din("rowp", [DEPTH, NROW])
    gfin_d = din("gfin", [D])
    out_d = nc.dram_tensor("out", [S, D], F32, kind="ExternalOutput").ap()
    dbg_d = {}
    for name, shape, dt in (("d_ys", [3, D, S], BF16), ("d_xm", [S, D], F32)):
        if name in dbg:
            dbg_d[name] = nc.dram_tensor(name, shape, dt, kind="ExternalOutput").ap()

    with ExitStack() as es:
        P = Prog(nc, es)
        hT = P.sb("hT", [128, 8, S], BF16)
        hTv = [Buf(hT.t, "hT%d" % i) for i in range(NT)]
        NWB = 4
        wb = [P.sb("wb%d" % i, [128, 8, 512], BF16) for i in range(NWB)]
        wbi = [0]

        wlim = [NWB]

        def nextw():
            b = wb[wbi[0] % wlim[0]]
            wbi[0] += 1
            return b

        identb = P.sb("identb", [128, 128], BF16)
        identf = P.sb("identf", [128, 128], F32)
        onesb = P.sb("onesb", [128, 128], BF16)
        onesf = P.sb("onesf", [128, 128], F32)
        tincl = P.sb("tincl", [128, 128], F32)
        tinclT = P.sb("tinclT", [128, 128], F32)
        mlo = P.sb("mlo", [128, 128], F32)
        mup = P.sb("mup", [128, 128], F32)
        slf = P.sb("slf", [128, 128], F32)
        suf = P.sb("suf", [128, 128], F32)
        colps = [P.sb("colp_sb%d" % i, [128, NCOL], F32) for i in range(2)]
        rowp = P.sb("rowp_sb", [128, NROW], F32)
        wsm = P.sb("wsm", [128, 8, 48], BF16)
        wsf = P.sb("wsf", [128, 8, 48], F32)
        nsq_l = [P.sb("nsq%d" % i, [128, D], BF16) for i in range(2)]
        nss_l = [P.sb("nss%d" % i, [128, 2], F32) for i in range(2)]
        nxn_l = [P.sb("nxn%d" % i, [128, D], BF16) for i in range(2)]
        nrm_i = [0]
        pb = [P.ps("pb%d" % i, [128, 512], F32) for i in range(7)]
        pbt = P.ps("pbt", [128, 1024], BF16)
        ysT = P.dram("ysT", [3, D, S], BF16)
        ysv = [[Buf(ysT.t, "ys%d_%d" % (n, c)) for c in range(8)] for n in range(3)]
        xcur = P.dram("xcur", [S, D], F32)
        xcv = [Buf(xcur.t, "xc%d" % i) for i in range(NT)]

        def pool(fn, r, w):
            P.op("pool", fn, r, w)

        pool(lambda e: e.memset(onesf.t[:], 1.0), [], [onesf])
        pool(lambda e: e.memset(onesb.t[:], 1.0), [], [onesb])
        pool(lambda e: e.memset(identf.t[:], 0.0), [], [identf])
        pool(lambda e: e.affine_select(identf.t[:], identf.t[:], pattern=[[-1, 128]], compare_op=ALU.not_equal, fill=1.0, base=0, channel_multiplier=1), [identf], [identf])
        pool(lambda e: e.tensor_copy(identb.t[:], identf.t[:]), [identf], [identb])
        pool(lambda e: e.affine_select(tincl.t[:], onesf.t[:], pattern=[[1, 128]], compare_op=ALU.is_ge, fill=0.0, base=0, channel_multiplier=-1), [onesf], [tincl])
        pool(lambda e: e.affine_select(tinclT.t[:], onesf.t[:], pattern=[[-1, 128]], compare_op=ALU.is_ge, fill=0.0, base=0, channel_multiplier=1), [onesf], [tinclT])
        pool(lambda e: e.memset(mlo.t[:], 0.0), [], [mlo])
        pool(lambda e: e.affine_select(mlo.t[:], mlo.t[:], pattern=[[1, 128]], compare_op=ALU.is_ge, fill=NEG, base=0, channel_multiplier=-1), [mlo], [mlo])
        pool(lambda e: e.memset(mup.t[:], 0.0), [], [mup])
        pool(lambda e: e.affine_select(mup.t[:], mup.t[:], pattern=[[-1, 128]], compare_op=ALU.is_ge, fill=NEG, base=0, channel_multiplier=1), [mup], [mup])
        pool(lambda e: e.affine_select(slf.t[:], onesf.t[:], pattern=[[-1, 128]], compare_op=ALU.is_gt, fill=0.0, base=0, channel_multiplier=1), [onesf], [slf])
        pool(lambda e: e.affine_select(suf.t[:], onesf.t[:], pattern=[[1, 128]], compare_op=ALU.is_gt, fill=0.0, base=0, channel_multiplier=-1), [onesf], [suf])

        def dump(name, buf, ap, shape, dt=F32):
            if ("D_" + name) in dbg:
                dd = nc.dram_tensor("D_" + name, list(shape), dt, kind="ExternalOutput").ap()
                P.dma(dd, ap, [buf], [])

        uid = [0]

        def scope():
            class _S:
                def __enter__(s_):
                    s_.es = ExitStack()
                    s_.es.__enter__()
                    return s_

                def sb(s_, name, shape, dt):
                    uid[0] += 1
                    return Buf(s_.es.enter_context(nc.sbuf_tensor("%s_u%d" % (name, uid[0]), list(shape), dt)), name)

                def __exit__(s_, *a):
                    P.barrier()
                    return s_.es.__exit__(*a)
            return _S()

        evi = [0]

        def evac_copy(out_ap, in_ap, reads, writes, scale=None):
            evi[0] += 1
            if evi[0] % 2 == 0:
                if scale is None:
                    P.op("act", lambda e: e.copy(out_ap, in_ap), reads, writes)
                else:
                    P.op("act", lambda e: e.mul(out_ap, in_ap, scale), reads, writes)
            else:
                if scale is None:
                    P.op("dve", lambda e: e.tensor_copy(out_ap, in_ap), reads, writes)
                else:
                    P.op("dve", lambda e: e.tensor_scalar_mul(out_ap, in_ap, scale), reads, writes)

        def wload(dst, col0, src2d, c0, n):
            P.dma(dst.t[:, :, col0:col0 + n], src2d.rearrange("(kc p) n -> p kc n", p=128)[:, :, c0:c0 + n], [], [dst], q="pool")

        def rsqrt_(dst_ap, src_ap, scale, reads, writes):
            P.op("act", lambda e: e.activation(dst_ap, src_ap, AF.Ln, bias=EPS, scale=scale), reads, writes)
            P.op("act", lambda e: e.activation(dst_ap, dst_ap, AF.Exp, scale=-0.5), writes, writes)

        def norm_tile(xt_ap, xbuf, tt, gcol):
            gbuf, gap = gcol
            nrm_i[0] += 1
            nsq, nss, nxn = nsq_l[nrm_i[0] % 2], nss_l[nrm_i[0] % 2], nxn_l[nrm_i[0] % 2]
            pbo = (nrm_i[0] % 2) * 0
            P.op("act", lambda e: e.activation(nsq.t[:], xt_ap, AF.Square, accum_out=nss.t[:, 0:1]), [xbuf], [nsq, nss])
            rsqrt_(nss.t[:, 1:2], nss.t[:, 0:1], 1.0 / D, [nss], [nss])
            P.op("dve", lambda e: e.tensor_scalar(nxn.t[:], xt_ap, nss.t[:, 1:2], None, ALU.mult), [xbuf, nss], [nxn])
            for c in range(8):
                P.op("pe", lambda e: e.transpose(pbt.t[:, c * 128:(c + 1) * 128], nxn.t[:, c * 128:(c + 1) * 128], identb.t[:]), [nxn, identb], [pbt], inc=(c == 7))
            P.op("dve", lambda e: e.tensor_tensor(hT.t[:, :, tt * 128:(tt + 1) * 128], pbt.t[:].rearrange("p (c t) -> p c t", c=8),
                                                  gap.unsqueeze(2).to_broadcast([128, 8, 128]), ALU.mult), [pbt, gbuf], [hTv[tt]])

        def softplus_(dst, src, t1, t2, n, sign_logsig=False):
            P.op("dve", lambda e: e.scalar_tensor_tensor(t1.t[:, 0:n], src.t[:, 0:n], -1.0, src.t[:, 0:n], ALU.mult, ALU.max), [src], [t1])
            P.op("act", lambda e: e.activation(t2.t[:, 0:n], t1.t[:, 0:n], AF.Exp, scale=-1.0), [t1], [t2])
            P.op("act", lambda e: e.activation(t2.t[:, 0:n], t2.t[:, 0:n], AF.Ln, bias=1.0), [t2], [t2])
            if sign_logsig:
                P.op("dve", lambda e: e.scalar_tensor_tensor(dst.t[:, 0:n], src.t[:, 0:n], 0.0, t2.t[:, 0:n], ALU.min, ALU.subtract), [src, t2], [dst])
            else:
                P.op("dve", lambda e: e.scalar_tensor_tensor(dst.t[:, 0:n], src.t[:, 0:n], 0.0, t2.t[:, 0:n], ALU.max, ALU.add), [src, t2], [dst])

        def decay(out, negc_ap, bias_ap, cbufs, mask, bank, diag):
            P.op("dve", lambda e: e.tensor_scalar(diag.t[:], identf.t[:], negc_ap, None, ALU.mult), [identf] + cbufs, [diag])
            P.op("pe", lambda e: e.matmul(bank.t[:, 0:128], onesf.t[:], diag.t[:], start=True, stop=False), [onesf, diag], [bank], inc=False)
            P.op("pe", lambda e: e.matmul(bank.t[:, 0:128], identf.t[:], mask.t[:], start=False, stop=True), [identf, mask], [bank])
            P.op("act", lambda e: e.activation(out.t[:], bank.t[:, 0:128], AF.Exp, bias=bias_ap, scale=1.0), [bank] + cbufs, [out])

        def proj_fm(w, wcol, banks, evac, tbs=range(4)):
            for tb in tbs:
                bank = banks[tb % len(banks)]
                for kc in range(8):
                    P.op("pe", lambda e: e.matmul(bank.t[:, 0:512], w.t[:, kc, wcol:wcol + 128], hT.t[:, kc, tb * 512:(tb + 1) * 512],
                                                  start=(kc == 0), stop=(kc == 7)), [w] + hTv[tb * 4:tb * 4 + 4], [bank], inc=(kc == 7))
                evac(tb, bank)

        def proj_tm(w, wcol, ncols, tt, bank):
            for kc in range(8):
                P.op("pe", lambda e: e.matmul(bank.t[:, 0:ncols], hT.t[:, kc, tt * 128:(tt + 1) * 128], w.t[:, kc, wcol:wcol + ncols],
                                              start=(kc == 0), stop=(kc == 7)), [w, hTv[tt]], [bank], inc=(kc == 7))

        def dump_ys():
            if "d_ys" in dbg_d:
                for n in range(3):
                    for c in range(8):
                        P.dma(dbg_d["d_ys"][n, c * 128:(c + 1) * 128, :], ysT.t[n, c * 128:(c + 1) * 128, :], [ysv[n][c]], [])

        stopped = False
        P.dma(colps[0].t[:], colp_d[0], [], [colps[0]])
        for l in range(nlayers):
            w_in = w_in_d[l]
            colp = colps[l % 2]
            P.dma(rowp.t[:], rowp_d[l].partition_broadcast(128), [], [rowp])
            wrr = w_in.rearrange("(kc p) n -> p kc n", p=128)
            P.dma(wsf.t[:, :, 0:16], wrr[:, :, O_MG:O_MG + 16], [], [wsf])
            P.dma(wsf.t[:, :, 16:48], wrr[:, :, O_DG:O_DG + 32], [], [wsf])
            P.op("dve", lambda e: e.tensor_copy(wsm.t[:], wsf.t[:]), [wsf], [wsm])
            if l == 0:
                with scope() as sc:
                    xin = [sc.sb("xin%d" % i, [128, D], F32) for i in range(2)]
                    for tt in range(NT):
                        xb_ = xin[tt % 2]
                        P.dma(xb_.t[:], x_d[tt * 128:(tt + 1) * 128, :], [], [xb_])
                        P.dma(xcur.t[tt * 128:(tt + 1) * 128, :], xb_.t[:], [xb_], [xcv[tt]])
                        norm_tile(xb_.t[:], xb_, tt, (colp, colp.t[:, 0:8]))

            with scope() as sc:
                sb1 = sc.sb
                gm = sb1("gm", [128, 256], F32)
                ipre = sb1("ipre", [128, 128], F32)
                fpre = sb1("fpre", [128, 128], F32)
                lf = sb1("lf", [128, 128], F32)
                t1 = sb1("t1", [128, 256], F32)
                t2 = sb1("t2", [128, 256], F32)
                bcs = sb1("bcs", [128, 128], F32)
                totb = sb1("totb", [128, 128], F32)
                biasc = sb1("biasc", [128, 128], F32)
                expb = sb1("expb", [128, 128], F32)
                wgt = sb1("wgt", [128, 128], F32)
                dec = sb1("dec", [128, 128], F32)
                for tt in range(NT):
                    for kc in range(8):
                        P.op("pe", lambda e: e.matmul(pb[0].t[:, tt * 16:(tt + 1) * 16], hT.t[:, kc, tt * 128:(tt + 1) * 128], wsm.t[:, kc, 0:16],
                                                      start=(kc == 0), stop=(kc == 7)), [wsm, hTv[tt]], [pb[0]], inc=(kc == 7 and tt == NT - 1))
                P.op("dve", lambda e: e.tensor_tensor(gm.t[:].rearrange("p (t g) -> p t g", g=16), pb[0].t[:, 0:256].rearrange("p (t g) -> p t g", g=16),
                                                      rowp.t[:, 1152:1168].unsqueeze(1).to_broadcast([128, 16, 16]), ALU.add), [pb[0], rowp], [gm])
                gm5 = gm.t[:].rearrange("p (t d w h) -> p t d w h", d=2, w=2, h=4)
                v4 = lambda b_: b_.t[:].rearrange("p (t d h) -> p t d h", d=2, h=4)
                P.op("dve", lambda e: e.tensor_copy(v4(ipre), gm5[:, :, :, 0, :]), [gm], [ipre])
                P.op("dve", lambda e: e.tensor_copy(v4(fpre), gm5[:, :, :, 1, :]), [gm], [fpre])
                softplus_(lf, fpre, t1, t2, 128, sign_logsig=True)
                P.op("pe", lambda e: e.matmul(pb[1].t[:, 0:128], tincl.t[:], lf.t[:], start=True, stop=True), [tincl, lf], [pb[1]], inc=False)
                P.op("pe", lambda e: e.matmul(pb[1].t[:, 128:256], tinclT.t[:], lf.t[:], start=True, stop=True), [tinclT, lf], [pb[1]], inc=False)
                P.op("pe", lambda e: e.matmul(pb[1].t[:, 256:384], onesf.t[:], lf.t[:], start=True, stop=True), [onesf, lf], [pb[1]])
                pv = lambda a, b: pb[1].t[:, a:b].rearrange("p (t d h) -> p t d h", d=2, h=4)
                P.op("dve", lambda e: e.tensor_copy(v4(bcs)[:, :, 0, :], pv(0, 128)[:, :, 0, :]), [pb[1]], [bcs])
                P.op("dve", lambda e: e.tensor_copy(v4(bcs)[:, :, 1, :], pv(128, 256)[:, :, 1, :]), [pb[1]], [bcs])
                P.op("dve", lambda e: e.tensor_copy(totb.t[:], pb[1].t[:, 256:384]), [pb[1]], [totb])
                P.op("dve", lambda e: e.tensor_sub(biasc.t[:], ipre.t[:], bcs.t[:]), [ipre, bcs], [biasc])
                P.op("act", lambda e: e.activation(expb.t[:], bcs.t[:], AF.Exp), [bcs], [expb])
                P.op("dve", lambda e: e.tensor_add(t1.t[:, 0:128], biasc.t[:], totb.t[:]), [biasc, totb], [t1])
                P.op("act", lambda e: e.activation(wgt.t[:], t1.t[:, 0:128], AF.Exp), [t1], [wgt])
                P.op("act", lambda e: e.activation(dec.t[:], totb.t[:], AF.Exp), [totb], [dec])

                qT = sb1("qT", [128, 2, S], BF16)
                kT = sb1("kT", [128, 2, S], BF16)
                ktm = sb1("ktm", [128, NT, 256], BF16)
                vtm = sb1("vtm", [128, NT, 257], BF16)
                hm = sb1("hm", [128, NT, 256], F32)
                hmv = [Buf(hm.t, "hm%d" % i) for i in range(NT)]
                Cst = [sb1("Cst%d" % d_, [128, 2, 257], F32) for d_ in range(2)]
                Cb = [sb1("Cb%d" % d_, [128, 2, 257], BF16) for d_ in range(2)]
                DT = [sb1("DT%d" % d_, [128, 128], F32) for d_ in range(2)]
                diag = [sb1("diag%d" % d_, [128, 128], F32) for d_ in range(2)]
                scT = [sb1("scT%d" % d_, [128, 128], BF16) for d_ in range(2)]
                Asb = [sb1("Asb%d" % d_, [128, 257], F32) for d_ in range(2)]
                comb = [sb1("comb%d" % d_, [128, 257], F32) for d_ in range(2)]
                rr = [sb1("rr%d" % d_, [128, 2], F32) for d_ in range(2)]
                kw = [sb1("kw%d" % d_, [128, 256], BF16) for d_ in range(2)]
                og_l = [sb1("og%d" % i, [128, 256], F32) for i in range(2)]
                ty_l = [sb1("ty%d" % i, [128, 256], F32) for i in range(2)]
                yb_l = [sb1("yb%d" % i, [128, 256], BF16) for i in range(2)]
                ty = ty_l[0]
                ssq = sb1("ssq", [128, 2 * NT], F32)
                ymT = sb1("ymT", [128, 2, S], BF16)
                P.op("dve", lambda e: e.memset(vtm.t[:, :, 256:257], 1.0), [], [vtm])

                for h in range(4):
                    wqk, wkv, wo = nextw(), nextw(), nextw()
                    wload(wqk, 0, w_in, O_MQ + h * 256, 256)
                    wload(wqk, 256, w_in, O_MK + h * 256, 256)
                    wload(wkv, 0, w_in, O_MK + h * 256, 256)
                    wload(wkv, 256, w_in, O_MV + h * 256, 256)
                    wload(wo, 0, w_in, O_MO + h * 256, 256)
                    for ft in range(2):
                        proj_fm(wqk, ft * 128, pb[0:4], lambda tb, bank: evac_copy(qT.t[:, ft, tb * 512:(tb + 1) * 512], bank.t[:, 0:512], [bank], [qT], scale=0.0625))
                        proj_fm(wqk, 256 + ft * 128, pb[0:4], lambda tb, bank: evac_copy(kT.t[:, ft, tb * 512:(tb + 1) * 512], bank.t[:, 0:512], [bank], [kT]))
                    for tt in range(NT):
                        bank = pb[tt % 4]
                        proj_tm(wkv, 0, 512, tt, bank)
                        P.op("act", lambda e: e.copy(ktm.t[:, tt, :], bank.t[:, 0:256]), [bank], [ktm])
                        P.op("dve", lambda e: e.tensor_copy(vtm.t[:, tt, 0:256], bank.t[:, 256:512]), [bank], [vtm])
                    def gen_mrec(d_, h=h):
                        bRQ, bS, bA = (pb[0], pb[3])[d_], (pb[1], pb[4])[d_], (pb[2], pb[5])[d_]
                        mask = mlo if d_ == 0 else mup
                        for step in range(NT):
                            c = step if d_ == 0 else NT - 1 - step
                            cs = slice(c * 128, (c + 1) * 128)
                            gi = (c * 2 + d_) * 4 + h
                            col = lambda b_: b_.t[:, gi:gi + 1]
                            P.op("dve", lambda e: e.tensor_scalar(diag[d_].t[:], identf.t[:], col(bcs), None, ALU.mult), [identf, bcs], [diag[d_]])
                            if step < NT - 1:
                                P.op("act", lambda e: e.activation(kw[d_].t[:], ktm.t[:, c, :], AF.Copy, scale=col(wgt)), [ktm, wgt], [kw[d_]])
                            yield
                            P.op("pe", lambda e: e.matmul(bRQ.t[:, 0:128], onesf.t[:], diag[d_].t[:], start=True, stop=False), [onesf, diag[d_]], [bRQ], inc=False)
                            P.op("pe", lambda e: e.matmul(bRQ.t[:, 0:128], identf.t[:], mask.t[:], start=False, stop=True), [identf, mask], [bRQ], inc=False)
                            for kc in range(2):
                                P.op("pe", lambda e: e.matmul(bRQ.t[:, 128:256], kT.t[:, kc, cs], qT.t[:, kc, cs], start=(kc == 0), stop=(kc == 1)), [kT, qT], [bRQ], inc=(kc == 1))
                            if step > 0:
                                for kc in range(2):
                                    P.op("pe", lambda e: e.matmul(bA.t[:, 0:257], qT.t[:, kc, cs], Cb[d_].t[:, kc, :], start=(kc == 0), stop=(kc == 1)), [qT, Cb[d_]], [bA], inc=(kc == 1))
                            yield
                            P.op("act", lambda e: e.activation(DT[d_].t[:], bRQ.t[:, 0:128], AF.Exp, bias=col(biasc), scale=1.0), [bRQ, biasc], [DT[d_]])
                            if step > 0:
                                P.op("act", lambda e: e.activation(Asb[d_].t[:], bA.t[:, 0:257], AF.Copy, scale=col(expb)), [bA, expb], [Asb[d_]])
                            yield
                            P.op("dve", lambda e: e.tensor_tensor(scT[d_].t[:], bRQ.t[:, 128:256], DT[d_].t[:], ALU.mult), [bRQ, DT[d_]], [scT[d_]])
                            yield
                            P.op("pe", lambda e: e.matmul(bS.t[:, 0:257], scT[d_].t[:], vtm.t[:, c, :], start=True, stop=True), [scT[d_], vtm], [bS])
                            yield
                            if step > 0:
                                P.op("dve", lambda e: e.tensor_tensor(comb[d_].t[:], Asb[d_].t[:], bS.t[:, 0:257], ALU.add), [Asb[d_], bS], [comb[d_]])
                            else:
                                P.op("dve", lambda e: e.tensor_copy(comb[d_].t[:], bS.t[:, 0:257]), [bS], [comb[d_]])
                            den = comb[d_].t[:, 256:257]
                            P.op("dve", lambda e: e.scalar_tensor_tensor(rr[d_].t[:, 0:1], den, -1.0, den, ALU.mult, ALU.max), [comb[d_]], [rr[d_]])
                            P.op("dve", lambda e: e.tensor_scalar_max(rr[d_].t[:, 0:1], rr[d_].t[:, 0:1], 1.0), [rr[d_]], [rr[d_]])
                            P.op("dve", lambda e: e.reciprocal(rr[d_].t[:, 1:2], rr[d_].t[:, 0:1]), [rr[d_]], [rr[d_]])
                            if step < NT - 1:
                                P.op("pe", lambda e: e.matmul(bS.t[:, 0:257], kw[d_].t[:, 0:128], vtm.t[:, c, :], start=True, stop=True), [kw[d_], vtm], [bS])
                                P.op("pe", lambda e: e.matmul(bA.t[:, 0:257], kw[d_].t[:, 128:256], vtm.t[:, c, :], start=True, stop=True), [kw[d_], vtm], [bA])
                            yield
                            first = (d_ == 0 and c < 8) or (d_ == 1 and c >= 8)
                            if first:
                                P.op("act", lambda e: e.activation(hm.t[:, c, :], comb[d_].t[:, 0:256], AF.Copy, scale=rr[d_].t[:, 1:2]), [comb[d_], rr[d_]], [hmv[c]])
                            else:
                                P.op("dve", lambda e: e.scalar_tensor_tensor(hm.t[:, c, :], comb[d_].t[:, 0:256], rr[d_].t[:, 1:2], hm.t[:, c, :], ALU.mult, ALU.add), [comb[d_], rr[d_]], [hmv[c]])
                            if step < NT - 1:
                                for m, bC in enumerate((bS, bA)):
                                    if step == 0:
                                        P.op("dve", lambda e: e.tensor_copy(Cst[d_].t[:, m, :], bC.t[:, 0:257]), [bC], [Cst[d_]])
                                    else:
                                        P.op("dve", lambda e: e.scalar_tensor_tensor(Cst[d_].t[:, m, :], Cst[d_].t[:, m, :], col(dec), bC.t[:, 0:257], ALU.mult, ALU.add), [bC, dec], [Cst[d_]])
                                yield
                                P.op("act", lambda e: e.copy(Cb[d_].t[:], Cst[d_].t[:]), [Cst[d_]], [Cb[d_]])
                            yield

                    gens = [gen_mrec(0), gen_mrec(1)]
                    while gens:
                        for g_ in list(gens):
                            try:
                                next(g_)
                            except StopIteration:
                                gens.remove(g_)
                    for tt in range(NT):
                        P.op("act", lambda e: e.activation(ty.t[:], hm.t[:, tt, :], AF.Square, accum_out=ssq.t[:, tt:tt + 1]), [hmv[tt]], [ty, ssq])
                    rsqrt_(ssq.t[:, NT:2 * NT], ssq.t[:, 0:NT], 1.0 / 256.0, [ssq], [ssq])
                    for tt in range(NT):
                        bank = pb[tt % 4]
                        og, ty, yb = og_l[tt % 2], ty_l[tt % 2], yb_l[tt % 2]
                        proj_tm(wo, 0, 256, tt, bank)
                        P.op("act", lambda e: e.activation(og.t[:], bank.t[:, 0:256], AF.Sigmoid), [bank], [og])
                        P.op("dve", lambda e: e.scalar_tensor_tensor(ty.t[:], hm.t[:, tt, :], ssq.t[:, NT + tt:NT + tt + 1], rowp.t[:, h * 256:(h + 1) * 256], ALU.mult, ALU.mult), [hmv[tt], ssq, rowp], [ty])
                        P.op("dve", lambda e: e.tensor_tensor(yb.t[:], ty.t[:], og.t[:], ALU.mult), [ty, og], [yb])
                        for j in range(2):
                            P.op("pe", lambda e: e.transpose(pbt.t[:, j * 128:(j + 1) * 128], yb.t[:, j * 128:(j + 1) * 128], identb.t[:]), [yb, identb], [pbt], inc=(j == 1))
                        P.op("act", lambda e: e.copy(ymT.t[:, :, tt * 128:(tt + 1) * 128], pbt.t[:, 0:256].rearrange("p (j t) -> p j t", j=2)), [pbt], [ymT])
                    P.dma(ysT.t[0, h * 256:(h + 1) * 256, :].rearrange("(j p) t -> p j t", p=128), ymT.t[:], [ymT], [ysv[0][2 * h], ysv[0][2 * h + 1]])
                    if stop == "m0":
                        break
            if stop in ("m0", "mlstm"):
                stopped = True
                break

            with scope() as sc:
                sb2 = sc.sb
                A2 = lambda name: sb2(name, [128, 256], F32)
                v4 = lambda b_: b_.t[:].rearrange("p (t d h) -> p t d h", d=2, h=8)
                Gc, negG, expG, bexpG, kdw, glb, negbeta, beta = [A2(n) for n in ("Gc", "negG", "expG", "bexpG", "kdw", "glb", "negbeta", "beta")]
                with scope() as sct:
                    dg = sct.sb("dg", [128, 512], F32)
                    apre, spl, gg, t1, t2 = [sct.sb(n, [128, 256], F32) for n in ("apre", "spl", "gg", "t1d", "t2d")]
                    ea = sct.sb("ea", [128, 16], F32)
                    for tt in range(NT):
                        for kc in range(8):
                            P.op("pe", lambda e: e.matmul(pb[0].t[:, tt * 32:(tt + 1) * 32], hT.t[:, kc, tt * 128:(tt + 1) * 128], wsm.t[:, kc, 16:48],
                                                          start=(kc == 0), stop=(kc == 7)), [wsm, hTv[tt]], [pb[0]], inc=(kc == 7 and tt == NT - 1))
                    P.op("act", lambda e: e.copy(dg.t[:], pb[0].t[:, 0:512]), [pb[0]], [dg])
                    dg5 = dg.t[:].rearrange("p (t d w h) -> p t d w h", d=2, w=2, h=8)
                    P.op("act", lambda e: e.activation(v4(beta), dg5[:, :, :, 0, :], AF.Sigmoid), [dg], [beta])
                    P.op("dve", lambda e: e.tensor_tensor(v4(apre), dg5[:, :, :, 1, :],
                                                          rowp.t[:, 1184:1200].rearrange("p (d h) -> p d h", d=2).unsqueeze(1).to_broadcast([128, 16, 2, 8]), ALU.add), [dg, rowp], [apre])
                    softplus_(spl, apre, t1, t2, 256)
                    P.op("act", lambda e: e.activation(ea.t[:], rowp.t[:, 1168:1184], AF.Exp), [rowp], [ea])
                    P.op("dve", lambda e: e.scalar_tensor_tensor(v4(gg), v4(spl), -1.0,
                                                                 ea.t[:].rearrange("p (d h) -> p d h", d=2).unsqueeze(1).to_broadcast([128, 16, 2, 8]), ALU.mult, ALU.mult), [spl, ea], [gg])
                    P.op("pe", lambda e: e.matmul(pb[1].t[:, 0:256], tincl.t[:], gg.t[:], start=True, stop=True), [tincl, gg], [pb[1]], inc=False)
                    P.op("pe", lambda e: e.matmul(pb[1].t[:, 256:512], tinclT.t[:], gg.t[:], start=True, stop=True), [tinclT, gg], [pb[1]], inc=False)
                    P.op("pe", lambda e: e.matmul(pb[2].t[:, 0:256], onesf.t[:], gg.t[:], start=True, stop=True), [onesf, gg], [pb[2]])
                    pv = lambda a, b: pb[1].t[:, a:b].rearrange("p (t d h) -> p t d h", d=2, h=8)
                    P.op("dve", lambda e: e.tensor_copy(v4(Gc)[:, :, 0, :], pv(0, 256)[:, :, 0, :]), [pb[1]], [Gc])
                    P.op("dve", lambda e: e.tensor_copy(v4(Gc)[:, :, 1, :], pv(256, 512)[:, :, 1, :]), [pb[1]], [Gc])
                    P.op("dve", lambda e: e.tensor_scalar_mul(negG.t[:], Gc.t[:], -1.0), [Gc], [negG])
                    P.op("act", lambda e: e.activation(expG.t[:], Gc.t[:], AF.Exp), [Gc], [expG])
                    P.op("dve", lambda e: e.tensor_mul(bexpG.t[:], beta.t[:], expG.t[:]), [beta, expG], [bexpG])
                    P.op("dve", lambda e: e.tensor_tensor(t1.t[:], pb[2].t[:, 0:256], Gc.t[:], ALU.subtract), [pb[2], Gc], [t1])
                    P.op("act", lambda e: e.activation(kdw.t[:], t1.t[:], AF.Exp), [t1], [kdw])
                    P.op("act", lambda e: e.activation(glb.t[:], pb[2].t[:, 0:256], AF.Exp), [pb[2]], [glb])
                    P.op("dve", lambda e: e.tensor_scalar_mul(negbeta.t[:], beta.t[:], -1.0), [beta], [negbeta])
                for nm, b_ in (("Gc", Gc), ("expG", expG), ("kdw", kdw), ("glb", glb), ("beta", beta)):
                    dump(nm, b_, b_.t[:], [128, 256])

                pc = sb2("pc", [128, S + 2], F32)
                cv = sb2("cv", [128, S], F32)
                sq = sb2("sq", [128, S], BF16)
                rin = sb2("rin", [128, 512], F32)
                qnT = sb2("qnT", [128, S], BF16)
                knT = sb2("knT", [128, S], BF16)
                vT = sb2("vT", [128, S], BF16)
                ktm = sb2("ktm2", [128, NT, 128], BF16)
                vtm = sb2("vtm2", [128, NT, 128], BF16)
                WTs = sb2("WTs", [128, 32, 128], BF16)
                Us = sb2("Us", [128, 32, 128], F32)
                ATs = sb2("ATs", [128, 32, 128], BF16)
                stv = [Buf(None, "st%d" % i) for i in range(32)]
                osb = sb2("osb", [128, NT, 128], F32)
                osv = [Buf(osb.t, "os%d" % i) for i in range(NT)]
                NS = NS_CFG[0]
                STAG = STAG_CFG[0]
                wlim[0] = 3 if NS_CFG[0] > 4 else NWB
                _flat = wb[3].t[:].rearrange("p a b -> p (a b)")
                _off = [0]

                def slot_tile(i, name, shape, dt):
                    if i < 4:
                        return sb2("%s%d" % (name, i), shape, dt)
                    n = shape[1] * (2 if dt == F32 else 1)
                    ap = _flat[:, _off[0]:_off[0] + n]
                    _off[0] += n
                    if dt == F32:
                        ap = ap.bitcast(F32)
                    return Buf(ap, "%s%d" % (name, i))
                Dm = [slot_tile(i, "Dm", [128, 128], F32) for i in range(NS)]
                dgl = [slot_tile(i, "dgl", [128, 128], F32) for i in range(NS)]
                Do1 = [slot_tile(i, "Do1", [128, 128], F32) for i in range(NS)]
                Do2 = [slot_tile(i, "Do2", [128, 128], F32) for i in range(NS)]
                CH = [[slot_tile(i, "CH%d_" % j, [128, 384], BF16) for j in range(2)] for i in range(NS)]
                Ao = [slot_tile(i, "Ao", [128, 256], BF16) for i in range(NS)]
                AoT = [slot_tile(i, "AoT", [128, 256], BF16) for i in range(NS)]
                attn_t = [slot_tile(i, "attn", [128, 128], BF16) for i in range(NS)]
                X0b = [slot_tile(i, "X0b", [128, 256], BF16) for i in range(NS)]
                X1b = [slot_tile(i, "X1b", [128, 256], BF16) for i in range(NS)]
                R1b = X0b
                Tmb = Ao
                Wtm = Do1
                Wb = [slot_tile(i, "Wb", [128, 128], BF16) for i in range(NS)]
                mbd = [sb2("mbd%d" % i, [128, 128], F32) for i in range(2)]
                mo1 = [sb2("mo1%d" % i, [128, 128], F32) for i in range(2)]
                mo2 = [sb2("mo2%d" % i, [128, 128], F32) for i in range(2)]
                with scope() as scm:
                    E32 = scm.sb("E32", [4, 128], F32)
                    E64 = scm.sb("E64", [2, 128], F32)
                    b32 = scm.sb("b32", [128, 128], F32)
                    b64 = scm.sb("b64", [128, 128], F32)
                    tmk = scm.sb("tmk", [128, 128], F32)
                    for E_, w_ in ((E32, 32), (E64, 64)):
                        np_ = 128 // w_
                        P.op("pool", lambda e: e.memset(E_.t[:], 1.0), [], [E_])
                        P.op("pool", lambda e: e.affine_select(E_.t[:], E_.t[:], pattern=[[1, 128]], compare_op=ALU.is_ge, fill=0.0, base=0, channel_multiplier=-w_), [E_], [E_])
                        P.op("pool", lambda e: e.affine_select(E_.t[:], E_.t[:], pattern=[[-1, 128]], compare_op=ALU.is_ge, fill=0.0, base=w_ - 1, channel_multiplier=w_), [E_], [E_])
                    P.op("pe", lambda e: e.matmul(pb[0].t[:, 0:128], E32.t[:], E32.t[:], start=True, stop=True), [E32], [pb[0]], inc=False)
                    P.op("pe", lambda e: e.matmul(pb[0].t[:, 128:256], E64.t[:], E64.t[:], start=True, stop=True), [E64], [pb[0]])
                    P.op("dve", lambda e: e.tensor_copy(b32.t[:], pb[0].t[:, 0:128]), [pb[0]], [b32])
                    P.op("dve", lambda e: e.tensor_copy(b64.t[:], pb[0].t[:, 128:256]), [pb[0]], [b64])
                    for d_, tri in ((0, slf), (1, suf)):
                        P.op("dve", lambda e: e.tensor_tensor(mbd[d_].t[:], tri.t[:], b32.t[:], ALU.mult), [tri, b32], [mbd[d_]])
                        P.op("dve", lambda e: e.tensor_tensor(tmk.t[:], b64.t[:], b32.t[:], ALU.subtract), [b64, b32], [tmk])
                        P.op("dve", lambda e: e.tensor_tensor(mo1[d_].t[:], tri.t[:], tmk.t[:], ALU.mult), [tri, tmk], [mo1[d_]])
                        P.op("dve", lambda e: e.tensor_tensor(tmk.t[:], tri.t[:], b64.t[:], ALU.mult), [tri, b64], [tmk])
                        P.op("dve", lambda e: e.tensor_tensor(mo2[d_].t[:], tri.t[:], tmk.t[:], ALU.subtract), [tri, tmk], [mo2[d_]])
                Sst = [sb2("Sst%d" % d_, [128, 128], F32) for d_ in range(2)]
                Sbb = [sb2("Sbb%d" % d_, [128, 128], BF16) for d_ in range(2)]
                vnb = [sb2("vnb%d" % d_, [128, 128], BF16) for d_ in range(2)]
                kdt = [sb2("kdt%d" % d_, [128, 128], BF16) for d_ in range(2)]
                tmo = [sb2("tmo%d" % d_, [128, 128], F32) for d_ in range(2)]
                zg_l = [sb2("zg%d" % i, [128, 128], F32) for i in range(2)]
                ty_l = [sb2("ty2%d" % i, [128, 128], F32) for i in range(2)]
                yb_l = [sb2("yb2%d" % i, [128, 128], BF16) for i in range(2)]
                ty = ty_l[0]
                ssq = sb2("ssq2", [128, 2 * NT], F32)
                ydT = sq
                P.op("dve", lambda e: e.memset(pc.t[:, 0:1], 0.0), [], [pc])
                P.op("dve", lambda e: e.memset(pc.t[:, S + 1:S + 2], 0.0), [], [pc])
                wA = wB = None
                for h in range(8):
                    hh = h % 2
                    if hh == 0:
                        wA, wB = nextw(), nextw()
                        wload(wA, 0, w_in, O_DQ + h * 128, 256)
                        wload(wA, 256, w_in, O_DK + h * 128, 256)
                        wload(wB, 0, w_in, O_DV + h * 128, 256)
                        wload(wB, 256, w_in, O_DZ + h * 128, 256)
                    for j, (wsrc, off) in enumerate(((wA, hh * 128), (wA, 256 + hh * 128), (wB, hh * 128))):
                        proj_fm(wsrc, off, pb[0:4], lambda tb, bank: evac_copy(pc.t[:, 1 + tb * 512:1 + (tb + 1) * 512], bank.t[:, 0:512], [bank], [pc]))
                        cw = lambda k: colp.t[:, 16 + k * 24 + j * 8 + h:16 + k * 24 + j * 8 + h + 1]
                        P.op("dve", lambda e: e.tensor_scalar(cv.t[:], pc.t[:, 0:S], cw(0), None, ALU.mult), [pc, colp], [cv])
                        P.op("dve", lambda e: e.scalar_tensor_tensor(cv.t[:], pc.t[:, 1:S + 1], cw(1), cv.t[:], ALU.mult, ALU.add), [pc, colp], [cv])
                        P.op("dve", lambda e: e.scalar_tensor_tensor(cv.t[:], pc.t[:, 2:S + 2], cw(2), cv.t[:], ALU.mult, ALU.add), [pc, colp], [cv])
                        P.op("act", lambda e: e.activation(cv.t[:], cv.t[:], AF.Silu), [cv], [cv])
                        if j == 2:
                            P.op("act", lambda e: e.copy(vT.t[:], cv.t[:]), [cv], [vT])
                        else:
                            dst = qnT if j == 0 else knT
                            P.op("act", lambda e: e.activation(sq.t[:], cv.t[:], AF.Square), [cv], [sq])
                            for tb in range(4):
                                bs = slice(tb * 512, (tb + 1) * 512)
                                bank = pb[4 + tb % 2]
                                P.op("pe", lambda e: e.matmul(bank.t[:, 0:512], onesb.t[:], sq.t[:, bs], start=True, stop=True), [onesb, sq], [bank])
                                rsqrt_(rin.t[:], bank.t[:, 0:512], 1.0, [bank], [rin])
                                if j == 0:
                                    P.op("dve", lambda e: e.scalar_tensor_tensor(dst.t[:, bs], cv.t[:, bs], 128.0 ** -0.5, rin.t[:], ALU.mult, ALU.mult), [cv, rin], [dst])
                                else:
                                    P.op("dve", lambda e: e.tensor_tensor(dst.t[:, bs], cv.t[:, bs], rin.t[:], ALU.mult), [cv, rin], [dst])
                    if h == 0:
                        dump("qnT", qnT, qnT.t[:], [128, S], BF16)
                        dump("knT", knT, knT.t[:], [128, S], BF16)
                        dump("vT", vT, vT.t[:], [128, S], BF16)
                    for src, dstm in ((knT, ktm), (vT, vtm)):
                        for g4 in range(2):
                            for i in range(8):
                                tt = g4 * 8 + i
                                P.op("pe", lambda e: e.transpose(pbt.t[:, i * 128:(i + 1) * 128], src.t[:, tt * 128:(tt + 1) * 128], identb.t[:]), [src, identb], [pbt], inc=(i == 7))
                            evac_copy(dstm.t[:, g4 * 8:(g4 + 1) * 8, :], pbt.t[:].rearrange("p (i t) -> p i t", i=8), [pbt], [dstm])
                    if stop == "dn0a":
                        break
                    def gen_prep(si, c, d_, h=h):
                        cs = slice(c * 128, (c + 1) * 128)
                        gi = (c * 2 + d_) * 8 + h
                        col = lambda b_: b_.t[:, gi:gi + 1]
                        e_ = c * 2 + d_
                        ch0, ch1 = CH[si][0], CH[si][1]
                        bank = pb[1 + si]
                        mask = mup if d_ == 0 else mlo
                        P.op("dve", lambda e: e.tensor_scalar(dgl[si].t[:], identf.t[:], col(negG), None, ALU.mult), [identf, negG], [dgl[si]])
                        yield
                        while lock["pb0"] is not None:
                            yield
                        lock["pb0"] = si
                        P.op("pe", lambda e: e.matmul(pb[0].t[:, 0:128], onesf.t[:], dgl[si].t[:], start=True, stop=False), [onesf, dgl[si]], [pb[0]], inc=False)
                        P.op("pe", lambda e: e.matmul(pb[0].t[:, 0:128], identf.t[:], mask.t[:], start=False, stop=True), [identf, mask], [pb[0]], inc=False)
                        P.op("pe", lambda e: e.matmul(pb[0].t[:, 128:256], knT.t[:, cs], knT.t[:, cs], start=True, stop=True), [knT], [pb[0]], inc=False)
                        P.op("pe", lambda e: e.matmul(pb[0].t[:, 256:384], qnT.t[:, cs], knT.t[:, cs], start=True, stop=True), [qnT, knT], [pb[0]])
                        yield
                        P.op("act", lambda e: e.activation(Dm[si].t[:], pb[0].t[:, 0:128], AF.Exp, bias=col(Gc), scale=1.0), [pb[0], Gc], [Dm[si]])
                        P.op("act", lambda e: e.activation(X0b[si].t[:, 0:128], ktm.t[:, c, :], AF.Copy, scale=col(bexpG)), [ktm, bexpG], [X0b[si]])
                        P.op("act", lambda e: e.activation(X0b[si].t[:, 128:256], vtm.t[:, c, :], AF.Copy, scale=col(beta)), [vtm, beta], [X0b[si]])
                        yield
                        P.op("pool", lambda e: e.tensor_tensor(dgl[si].t[:], Dm[si].t[:], mbd[d_].t[:], ALU.mult), [Dm[si], mbd[d_]], [dgl[si]])
                        P.op("pool", lambda e: e.tensor_tensor(Do1[si].t[:], Dm[si].t[:], mo1[d_].t[:], ALU.mult), [Dm[si], mo1[d_]], [Do1[si]])
                        P.op("pool", lambda e: e.tensor_tensor(Do2[si].t[:], Dm[si].t[:], mo2[d_].t[:], ALU.mult), [Dm[si], mo2[d_]], [Do2[si]])
                        P.op("dve", lambda e: e.tensor_tensor(attn_t[si].t[:], pb[0].t[:, 256:384], Dm[si].t[:], ALU.mult), [pb[0], Dm[si]], [attn_t[si]])
                        yield
                        P.op("dve", lambda e: e.scalar_tensor_tensor(ch0.t[:, 256:384], pb[0].t[:, 128:256], col(negbeta), dgl[si].t[:], ALU.mult, ALU.mult), [pb[0], negbeta, dgl[si]], [ch0])
                        P.op("dve", lambda e: e.scalar_tensor_tensor(Ao[si].t[:, 0:128], pb[0].t[:, 128:256], col(beta), Do1[si].t[:], ALU.mult, ALU.mult), [pb[0], beta, Do1[si]], [Ao[si]])
                        P.op("dve", lambda e: e.scalar_tensor_tensor(Ao[si].t[:, 128:256], pb[0].t[:, 128:256], col(beta), Do2[si].t[:], ALU.mult, ALU.mult), [pb[0], beta, Do2[si]], [Ao[si]])
                        lock["pb0"] = None
                        yield
                        while lock["pbt"] is not None:
                            yield
                        lock["pbt"] = si
                        P.op("pe", lambda e: e.transpose(pbt.t[:, 0:128], ch0.t[:, 256:384], identb.t[:]), [ch0, identb], [pbt], inc=False)
                        P.op("pe", lambda e: e.transpose(pbt.t[:, 128:256], Ao[si].t[:, 0:128], identb.t[:]), [Ao[si], identb], [pbt], inc=False)
                        P.op("pe", lambda e: e.transpose(pbt.t[:, 256:384], Ao[si].t[:, 128:256], identb.t[:]), [Ao[si], identb], [pbt], inc=False)
                        P.op("pe", lambda e: e.transpose(pbt.t[:, 384:512], attn_t[si].t[:], identb.t[:]), [attn_t[si], identb], [pbt])
                        yield
                        P.op("act", lambda e: e.copy(ch0.t[:, 0:128], pbt.t[:, 0:128]), [pbt], [ch0])
                        P.op("act", lambda e: e.copy(AoT[si].t[:], pbt.t[:, 128:384]), [pbt], [AoT[si]])
                        P.op("dve", lambda e: e.tensor_tensor(ch1.t[:, 128:256], pbt.t[:, 0:128], identb.t[:], ALU.add), [pbt, identb], [ch1])
                        P.op("dve", lambda e: e.tensor_copy(ATs.t[:, e_, :], pbt.t[:, 384:512]), [pbt], [stv[e_]])
                        lock["pbt"] = None
                        yield
                        P.op("pe", lambda e: e.matmul(bank.t[:, 0:128], ch0.t[:, 256:384], ch0.t[:, 0:128], start=True, stop=True), [ch0], [bank], inc=False)
                        P.op("pe", lambda e: e.matmul(bank.t[:, 256:384], ch0.t[:, 0:128], ch0.t[:, 256:384], start=True, stop=True), [ch0], [bank])
                        yield
                        P.op("act", lambda e: e.copy(ch1.t[:].rearrange("p (a b) -> p a b", b=128)[:, 0::2, :], bank.t[:, 0:384].rearrange("p (a b) -> p a b", b=128)[:, 0::2, :]), [bank], [ch1])
                        yield
                        for j in range(1, 5):
                            cur = CH[si][j % 2]
                            nxt = CH[si][(j + 1) % 2]
                            if j < 4:
                                P.op("pe", lambda e: e.matmul(bank.t[:, 0:256], cur.t[:, 256:384], cur.t[:, 0:256], start=True, stop=False), [cur], [bank], inc=False)
                                P.op("pe", lambda e: e.matmul(bank.t[:, 128:256], identb.t[:], cur.t[:, 128:256], start=False, stop=True), [cur, identb], [bank], inc=False)
                                P.op("pe", lambda e: e.matmul(bank.t[:, 256:384], cur.t[:, 0:128], cur.t[:, 256:384], start=True, stop=True), [cur], [bank])
                                yield
                                evac_copy(nxt.t[:], bank.t[:, 0:384], [bank], [nxt])
                                yield
                            else:
                                P.op("pe", lambda e: e.matmul(bank.t[:, 128:256], cur.t[:, 256:384], cur.t[:, 128:256], start=True, stop=False), [cur], [bank], inc=False)
                                P.op("pe", lambda e: e.matmul(bank.t[:, 128:256], identb.t[:], cur.t[:, 128:256], start=False, stop=True), [cur, identb], [bank])
                                yield
                                evac_copy(nxt.t[:, 128:256], bank.t[:, 128:256], [bank], [nxt])
                                yield
                        fin = CH[si][1]
                        PTf = fin.t[:, 128:256]
                        A1T, A2T = AoT[si].t[:, 0:128], AoT[si].t[:, 128:256]
                        lo, hi = bank.t[:, 0:256], bank.t[:, 256:512]
                        P.op("pe", lambda e: e.matmul(lo, PTf, X0b[si].t[:], start=True, stop=True), [fin, X0b[si]], [bank])
                        yield
                        P.op("act", lambda e: e.copy(R1b[si].t[:], lo), [bank], [R1b[si]])
                        P.op("act", lambda e: e.copy(Us.t[:, e_, :], bank.t[:, 128:256]), [bank], [stv[e_]])
                        yield
                        P.op("pe", lambda e: e.matmul(hi, A1T, R1b[si].t[:], start=True, stop=True), [AoT[si], R1b[si]], [bank])
                        yield
                        P.op("act", lambda e: e.copy(Tmb[si].t[:], hi), [bank], [Tmb[si]])
                        yield
                        P.op("pe", lambda e: e.matmul(lo, PTf, Tmb[si].t[:], start=True, stop=True), [fin, Tmb[si]], [bank])
                        yield
                        P.op("dve", lambda e: e.tensor_tensor(X1b[si].t[:], R1b[si].t[:], lo, ALU.subtract), [R1b[si], bank], [X1b[si]])
                        P.op("dve", lambda e: e.tensor_tensor(Us.t[:, e_, :], Us.t[:, e_, :], bank.t[:, 128:256], ALU.subtract), [bank], [stv[e_]])
                        yield
                        P.op("pe", lambda e: e.matmul(hi, A2T, X1b[si].t[:], start=True, stop=True), [AoT[si], X1b[si]], [bank])
                        yield
                        P.op("act", lambda e: e.copy(Tmb[si].t[:], hi), [bank], [Tmb[si]])
                        yield
                        P.op("pe", lambda e: e.matmul(lo, PTf, Tmb[si].t[:], start=True, stop=True), [fin, Tmb[si]], [bank])
                        yield
                        P.op("act", lambda e: e.copy(R1b[si].t[:], lo), [bank], [R1b[si]])
                        P.op("dve", lambda e: e.tensor_tensor(Us.t[:, e_, :], Us.t[:, e_, :], bank.t[:, 128:256], ALU.subtract), [bank], [stv[e_]])
                        yield
                        P.op("pe", lambda e: e.matmul(hi, A1T, R1b[si].t[:], start=True, stop=True), [AoT[si], R1b[si]], [bank])
                        P.op("pool", lambda e: e.tensor_tensor(Wtm[si].t[:], X1b[si].t[:, 0:128], R1b[si].t[:, 0:128], ALU.subtract), [X1b[si], R1b[si]], [Wtm[si]])
                        yield
                        P.op("act", lambda e: e.copy(Tmb[si].t[:], hi), [bank], [Tmb[si]])
                        yield
                        P.op("pe", lambda e: e.matmul(lo, PTf, Tmb[si].t[:], start=True, stop=True), [fin, Tmb[si]], [bank])
                        yield
                        P.op("dve", lambda e: e.tensor_tensor(Wb[si].t[:], Wtm[si].t[:], bank.t[:, 0:128], ALU.add), [Wtm[si], bank], [Wb[si]])
                        P.op("dve", lambda e: e.tensor_tensor(Us.t[:, e_, :], Us.t[:, e_, :], bank.t[:, 128:256], ALU.add), [bank], [stv[e_]])
                        yield
                        while lock["pbt"] is not None:
                            yield
                        lock["pbt"] = si
                        P.op("pe", lambda e: e.transpose(pbt.t[:, 0:128], Wb[si].t[:], identb.t[:]), [Wb[si], identb], [pbt])
                        yield
                        P.op("act", lambda e: e.copy(WTs.t[:, e_, :], pbt.t[:, 0:128]), [pbt], [stv[e_]])
                        lock["pbt"] = None
                        yield

                    def gen_rec(h=h):
                        bA, bB = (pb[6], pb[6]) if NS_CFG[0] > 4 else (pb[5], pb[6])
                        for step in range(NT):
                            info = []
                            for d_ in range(2):
                                c = step if d_ == 0 else NT - 1 - step
                                info.append((d_, c, slice(c * 128, (c + 1) * 128), (c * 2 + d_) * 8 + h, c * 2 + d_))
                            for d_, c, cs, gi, e_ in info:
                                if step > 0:
                                    P.op("pe", lambda e: e.matmul(bA.t[:, d_ * 256:d_ * 256 + 128], WTs.t[:, e_, :], Sbb[d_].t[:], start=True, stop=True), [stv[e_], Sbb[d_]], [bA], inc=False)
                                    P.op("pe", lambda e: e.matmul(bA.t[:, d_ * 256 + 128:d_ * 256 + 256], qnT.t[:, cs], Sbb[d_].t[:], start=True, stop=True), [qnT, Sbb[d_]], [bA])
                                if step < NT - 1:
                                    P.op("act", lambda e: e.activation(kdt[d_].t[:], ktm.t[:, c, :], AF.Copy, scale=kdw.t[:, gi:gi + 1]), [ktm, kdw], [kdt[d_]])
                            yield
                            for d_, c, cs, gi, e_ in info:
                                if step > 0:
                                    P.op("dve", lambda e: e.tensor_tensor(vnb[d_].t[:], Us.t[:, e_, :], bA.t[:, d_ * 256:d_ * 256 + 128], ALU.subtract), [stv[e_], bA], [vnb[d_]])
                                else:
                                    P.op("dve", lambda e: e.tensor_copy(vnb[d_].t[:], Us.t[:, e_, :]), [stv[e_]], [vnb[d_]])
                            for d_, c, cs, gi, e_ in info:
                                if step > 0:
                                    P.op("act", lambda e: e.activation(tmo[d_].t[:], bA.t[:, d_ * 256 + 128:d_ * 256 + 256], AF.Copy, scale=expG.t[:, gi:gi + 1]), [bA, expG], [tmo[d_]])
                            yield
                            for d_, c, cs, gi, e_ in info:
                                P.op("pe", lambda e: e.matmul(bB.t[:, d_ * 256:d_ * 256 + 128], ATs.t[:, e_, :], vnb[d_].t[:], start=True, stop=True), [stv[e_], vnb[d_]], [bB], inc=(step == NT - 1))
                                if step < NT - 1:
                                    P.op("pe", lambda e: e.matmul(bB.t[:, d_ * 256 + 128:d_ * 256 + 256], kdt[d_].t[:], vnb[d_].t[:], start=True, stop=True), [kdt[d_], vnb[d_]], [bB])
                            yield
                            for d_, c, cs, gi, e_ in info:
                                if step < NT - 1:
                                    if step == 0:
                                        P.op("dve", lambda e: e.tensor_copy(Sst[d_].t[:], bB.t[:, d_ * 256 + 128:d_ * 256 + 256]), [bB], [Sst[d_]])
                                    else:
                                        P.op("dve", lambda e: e.scalar_tensor_tensor(Sst[d_].t[:], Sst[d_].t[:], glb.t[:, gi:gi + 1], bB.t[:, d_ * 256 + 128:d_ * 256 + 256], ALU.mult, ALU.add), [bB, glb], [Sst[d_]])
                                    P.op("act", lambda e: e.copy(Sbb[d_].t[:], Sst[d_].t[:]), [Sst[d_]], [Sbb[d_]])
                            for d_, c, cs, gi, e_ in info:
                                first = (d_ == 0 and c < 8) or (d_ == 1 and c >= 8)
                                if step > 0:
                                    P.op("dve", lambda e: e.tensor_tensor(tmo[d_].t[:], tmo[d_].t[:], bB.t[:, d_ * 256:d_ * 256 + 128], ALU.add), [bB], [tmo[d_]])
                                    src_ap, src_b = tmo[d_].t[:], tmo[d_]
                                    if first:
                                        P.op("act", lambda e: e.copy(osb.t[:, c, :], src_ap), [src_b], [osv[c]])
                                    else:
                                        P.op("dve", lambda e: e.tensor_tensor(osb.t[:, c, :], osb.t[:, c, :], src_ap, ALU.add), [src_b], [osv[c]])
                                else:
                                    if first:
                                        P.op("dve", lambda e: e.tensor_copy(osb.t[:, c, :], bB.t[:, d_ * 256:d_ * 256 + 128]), [bB], [osv[c]])
                                    else:
                                        P.op("dve", lambda e: e.tensor_tensor(osb.t[:, c, :], osb.t[:, c, :], bB.t[:, d_ * 256:d_ * 256 + 128], ALU.add), [bB], [osv[c]])
                            yield

                    lock = {"pb0": None, "pbt": None}
                    order = []
                    for i in range(NT):
                        order.append((i, 0))
                        order.append((NT - 1 - i, 1))
                    active = [None] * NS
                    nstarted = 0
                    nfinished = 0
                    finished = [False] * 32
                    rec = gen_rec()
                    rec_step = 0
                    rec_hop = 0
                    rec_done = False
                    tick = 0
                    while nfinished < 32 or not rec_done:
                        if nstarted < 32 and tick % STAG == 0:
                            for si in range(NS):
                                if active[si] is None:
                                    c, d_ = order[nstarted]
                                    active[si] = (gen_prep(si, c, d_), nstarted)
                                    nstarted += 1
                                    break
                        for si in range(NS):
                            if active[si] is not None:
                                g_, idx = active[si]
                                try:
                                    next(g_)
                                except StopIteration:
                                    finished[idx] = True
                                    nfinished += 1
                                    active[si] = None
                        if not rec_done and (stop != "dn0b"):
                            if rec_hop > 0 or (finished[2 * rec_step] and finished[2 * rec_step + 1]):
                                try:
                                    next(rec)
                                    rec_hop += 1
                                    if rec_hop == 4:
                                        rec_hop = 0
                                        rec_step += 1
                                        if rec_step == NT:
                                            rec_done = True
                                except StopIteration:
                                    rec_done = True
                        elif stop == "dn0b":
                            rec_done = True
                        tick += 1
                    if stop == "dn0c":
                        break
                    if h == 0:
                        dump("osb", osv[0], osb.t[:], [128, NT, 128])
                    for tt in range(NT):
                        P.op("act", lambda e: e.activation(ty.t[:], osb.t[:, tt, :], AF.Square, accum_out=ssq.t[:, tt:tt + 1]), [osv[tt]], [ty, ssq])
                    rsqrt_(ssq.t[:, NT:2 * NT], ssq.t[:, 0:NT], 1.0 / 128.0, [ssq], [ssq])
                    for tt in range(NT):
                        bank = pb[4 + tt % 2]
                        zg, ty, yb = zg_l[tt % 2], ty_l[tt % 2], yb_l[tt % 2]
                        proj_tm(wB, 256 + hh * 128, 128, tt, bank)
                        P.op("act", lambda e: e.activation(zg.t[:], bank.t[:, 0:128], AF.Silu), [bank], [zg])
                        P.op("dve", lambda e: e.scalar_tensor_tensor(ty.t[:], osb.t[:, tt, :], ssq.t[:, NT + tt:NT + tt + 1], rowp.t[:, 1024:1152], ALU.mult, ALU.mult), [osv[tt], ssq, rowp], [ty])
                        P.op("dve", lambda e: e.tensor_tensor(yb.t[:], ty.t[:], zg.t[:], ALU.mult), [ty, zg], [yb])
                        P.op("pe", lambda e: e.transpose(pbt.t[:, 0:128], yb.t[:], identb.t[:]), [yb, identb], [pbt])
                        P.op("act", lambda e: e.copy(ydT.t[:, tt * 128:(tt + 1) * 128], pbt.t[:, 0:128]), [pbt], [ydT])
                    P.dma(ysT.t[1, h * 128:(h + 1) * 128, :], ydT.t[:], [ydT], [ysv[1][h]])
                    if stop == "dn0":
                        break
            if stop in ("dn0", "dn", "dn0a", "dn0b", "dn0c"):
                stopped = True
                break

            wlim[0] = NWB
            with scope() as sc:
                cx = sc.sb("cx", [128, S + 2], F32)
                Bsb = sc.sb("Bsb", [128, S], F32)
                ycv = sc.sb("ycv", [128, S], F32)
                tmx_l = [sc.sb("tmx%d" % i, [128, 512], F32) for i in range(2)]
                ycT = sc.sb("ycT", [128, S], BF16)
                P.op("dve", lambda e: e.memset(cx.t[:, 0:1], 0.0), [], [cx])
                P.op("dve", lambda e: e.memset(cx.t[:, S + 1:S + 2], 0.0), [], [cx])
                wA = wB = None
                for dc in range(8):
                    dd = dc % 2
                    if dd == 0:
                        wA, wB = nextw(), nextw()
                        wload(wA, 0, w_in, O_SB + dc * 128, 256)
                        wload(wA, 256, w_in, O_SC + dc * 128, 256)
                        wload(wB, 0, w_in, O_SX + dc * 128, 256)
                    for tb in range(4):
                        bs = slice(tb * 512, (tb + 1) * 512)
                        tmx = tmx_l[tb % 2]
                        for j, (wsrc, off) in enumerate(((wA, dd * 128), (wA, 256 + dd * 128), (wB, dd * 128))):
                            bank = pb[j + 3 * (tb % 2)]
                            for kc in range(8):
                                P.op("pe", lambda e: e.matmul(bank.t[:, 0:512], wsrc.t[:, kc, off:off + 128], hT.t[:, kc, bs], start=(kc == 0), stop=(kc == 7)),
                                     [wsrc] + hTv[tb * 4:tb * 4 + 4], [bank], inc=(kc == 7))
                        o3 = 3 * (tb % 2)
                        P.op("act", lambda e: e.copy(Bsb.t[:, bs], pb[o3].t[:, 0:512]), [pb[o3]], [Bsb])
                        P.op("act", lambda e: e.copy(tmx.t[:], pb[o3 + 2].t[:, 0:512]), [pb[o3 + 2]], [tmx])
                        P.op("dve", lambda e: e.tensor_tensor(cx.t[:, 1 + tb * 512:1 + (tb + 1) * 512], pb[o3 + 1].t[:, 0:512], tmx.t[:], ALU.mult), [pb[o3 + 1], tmx], [cx])
                    cw = lambda k: colp.t[:, 88 + k * 8 + dc:88 + k * 8 + dc + 1]
                    P.op("dve", lambda e: e.tensor_scalar(ycv.t[:], cx.t[:, 0:S], cw(0), None, ALU.mult), [cx, colp], [ycv])
                    P.op("dve", lambda e: e.scalar_tensor_tensor(ycv.t[:], cx.t[:, 1:S + 1], cw(1), ycv.t[:], ALU.mult, ALU.add), [cx, colp], [ycv])
                    P.op("dve", lambda e: e.scalar_tensor_tensor(ycv.t[:], cx.t[:, 2:S + 2], cw(2), ycv.t[:], ALU.mult, ALU.add), [cx, colp], [ycv])
                    P.op("dve", lambda e: e.tensor_tensor(ycT.t[:], ycv.t[:], Bsb.t[:], ALU.mult), [ycv, Bsb], [ycT])
                    P.dma(ysT.t[2, dc * 128:(dc + 1) * 128, :], ycT.t[:], [ycT], [ysv[2][dc]])
            if stop == "sc":
                stopped = True
                break

            if l + 1 < nlayers:
                P.dma(colps[(l + 1) % 2].t[:], colp_d[l + 1], [], [colps[(l + 1) % 2]])
            last = (l == DEPTH - 1)
            for half in range(2):
                with scope() as sch:
                    xres = sch.sb("xres", [128, 8, D], F32)
                    xrv = [Buf(xres.t, "xr%d" % i) for i in range(8)]
                    with scope() as sc:
                        ys_sb = [sc.sb("ys_sb%d" % n, [128, 8, 1024], BF16) for n in range(3)]
                        sg_l = [[sc.sb("sg%d_%d" % (n, i), [128, 512], F32) for n in range(3)] for i in range(2)]
                        acc_l = [sc.sb("acc%d" % i, [128, 512], F32) for i in range(2)]
                        tmm_l = [sc.sb("tmm", [128, 512], F32)] * 2
                        mixT = sc.sb("mixT", [128, 8, 1024], BF16)
                        for n in range(3):
                            P.dma(ys_sb[n].t[:], ysT.t[n].rearrange("(kc p) t -> p kc t", p=128)[:, :, half * 1024:(half + 1) * 1024], ysv[n], [ys_sb[n]])
                        wA = wB = wC = None
                        for dc in range(8):
                            dd = dc % 2
                            if dd == 0:
                                wA, wB, wC = nextw(), nextw(), nextw()
                                wload(wA, 0, w_br_d[l, 0], dc * 128, 256)
                                wload(wA, 256, w_br_d[l, 1], dc * 128, 256)
                                wload(wB, 0, w_br_d[l, 2], dc * 128, 256)
                                wload(wB, 256, w_in, O_MRG + dc * 128, 256)
                                wload(wC, 0, w_in, O_MRG + 1024 + dc * 128, 256)
                                wload(wC, 256, w_in, O_MRG + 2048 + dc * 128, 256)
                            wbr = ((wA, dd * 128), (wA, 256 + dd * 128), (wB, dd * 128))
                            wgt_ = ((wB, 256 + dd * 128), (wC, dd * 128), (wC, 256 + dd * 128))
                            for tbh in range(2):
                                tb = half * 2 + tbh
                                bs = slice(tb * 512, (tb + 1) * 512)
                                bsh = slice(tbh * 512, (tbh + 1) * 512)
                                sg, acc, tmm = sg_l[tbh], acc_l[tbh], tmm_l[tbh]
                                for n in range(3):
                                    wsrc, off = wgt_[n]
                                    for kc in range(8):
                                        P.op("pe", lambda e: e.matmul(pb[3 + n].t[:, 0:512], wsrc.t[:, kc, off:off + 128], hT.t[:, kc, bs], start=(kc == 0), stop=(kc == 7)),
                                             [wsrc] + hTv[tb * 4:tb * 4 + 4], [pb[3 + n]], inc=(kc == 7))
                                    P.op("act", lambda e: e.activation(sg[n].t[:], pb[3 + n].t[:, 0:512], AF.Sigmoid), [pb[3 + n]], [sg[n]])
                                for n in range(3):
                                    wsrc, off = wbr[n]
                                    for kc in range(8):
                                        P.op("pe", lambda e: e.matmul(pb[n].t[:, 0:512], wsrc.t[:, kc, off:off + 128], ys_sb[n].t[:, kc, bsh], start=(kc == 0), stop=(kc == 7)),
                                             [wsrc, ys_sb[n]], [pb[n]], inc=(kc == 7))
                                P.op("dve", lambda e: e.tensor_tensor(acc.t[:], sg[0].t[:], pb[0].t[:, 0:512], ALU.mult), [sg[0], pb[0]], [acc])
                                P.op("dve", lambda e: e.tensor_tensor(tmm.t[:], sg[1].t[:], pb[1].t[:, 0:512], ALU.mult), [sg[1], pb[1]], [tmm])
                                P.op("dve", lambda e: e.tensor_tensor(sg[2].t[:], sg[2].t[:], pb[2].t[:, 0:512], ALU.mult), [pb[2]], [sg[2]])
                                P.op("dve", lambda e: e.tensor_tensor(acc.t[:], acc.t[:], tmm.t[:], ALU.add), [tmm], [acc])
                                P.op("dve", lambda e: e.tensor_tensor(mixT.t[:, dc, bsh], acc.t[:], sg[2].t[:], ALU.add), [acc, sg[2]], [mixT])
                        wo0, wo1 = nextw(), nextw()
                        wload(wo0, 0, w_out_d[l], 0, 512)
                        wload(wo1, 0, w_out_d[l], 512, 512)
                        for t8 in range(8):
                            tt = half * 8 + t8
                            P.dma(xres.t[:, t8, :], xcur.t[tt * 128:(tt + 1) * 128, :], [xcv[tt]], [xrv[t8]])
                            for nb, wsrc in enumerate((wo0, wo1)):
                                bank = pb[(t8 * 2 + nb) % 4]
                                for dc in range(8):
                                    P.op("pe", lambda e: e.matmul(bank.t[:, 0:512], mixT.t[:, dc, t8 * 128:(t8 + 1) * 128], wsrc.t[:, dc, :], start=(dc == 0), stop=(dc == 7)),
                                         [mixT, wsrc], [bank], inc=(dc == 7))
                                P.op("dve", lambda e: e.tensor_tensor(xres.t[:, t8, nb * 512:(nb + 1) * 512], xres.t[:, t8, nb * 512:(nb + 1) * 512], bank.t[:, 0:512], ALU.add), [bank], [xrv[t8]])
                            norm_tile(xres.t[:, t8, :], xrv[t8], tt, (colp, colp.t[:, 8:16]))
                            if stop == "mix" and "d_xm" in dbg_d:
                                P.dma(dbg_d["d_xm"][tt * 128:(tt + 1) * 128, :], xres.t[:, t8, :], [xrv[t8]], [])
                    if stop == "mix":
                        continue
                    with scope() as sc:
                        upT = sc.sb("upT", [128, 8, 1024], BF16)
                        relu_t = [sc.sb("relu_t%d" % i, [128, 512], F32) for i in range(2)]
                        otile = [sc.sb("otile%d" % i, [128, D], F32) for i in range(2)] if last else None
                        gfin = sc.sb("gfin_sb", [128, D], F32) if last else None
                        if last:
                            P.dma(gfin.t[:], gfin_d.partition_broadcast(128), [], [gfin])
                        for fb in range(4):
                            wu = [nextw(), nextw()]
                            wd = [nextw(), nextw()]
                            wload(wu[0], 0, w_up_d[l], fb * 1024, 512)
                            wload(wu[1], 0, w_up_d[l], fb * 1024 + 512, 512)
                            wload(wd[0], 0, w_dn_d[l, fb * 1024:(fb + 1) * 1024, :], 0, 512)
                            wload(wd[1], 0, w_dn_d[l, fb * 1024:(fb + 1) * 1024, :], 512, 512)
                            for fc in range(8):
                                def ev(tb, bank):
                                    bsh = slice((tb - half * 2) * 512, (tb - half * 2 + 1) * 512)
                                    rl = relu_t[tb % 2]
                                    P.op("act", lambda e: e.activation(rl.t[:], bank.t[:, 0:512], AF.Relu), [bank], [rl])
                                    P.op("dve", lambda e: e.tensor_tensor(upT.t[:, fc, bsh], rl.t[:], rl.t[:], ALU.mult), [rl], [upT])
                                proj_fm(wu[fc // 4], (fc % 4) * 128, pb[0:4], ev, tbs=(half * 2, half * 2 + 1))
                            for t8 in range(8):
                                tt = half * 8 + t8
                                for nb in range(2):
                                    bank = pb[4 + (t8 * 2 + nb) % 3]
                                    for fc in range(8):
                                        P.op("pe", lambda e: e.matmul(bank.t[:, 0:512], upT.t[:, fc, t8 * 128:(t8 + 1) * 128], wd[nb].t[:, fc, :], start=(fc == 0), stop=(fc == 7)),
                                             [upT, wd[nb]], [bank], inc=(fc == 7))
                                    P.op("dve", lambda e: e.tensor_tensor(xres.t[:, t8, nb * 512:(nb + 1) * 512], xres.t[:, t8, nb * 512:(nb + 1) * 512], bank.t[:, 0:512], ALU.add), [bank], [xrv[t8]])
                                if fb == 3:
                                    if stop == "mlp" and "d_xm" in dbg_d:
                                        P.dma(dbg_d["d_xm"][tt * 128:(tt + 1) * 128, :], xres.t[:, t8, :], [xrv[t8]], [])
                                    if not last:
                                        P.dma(xcur.t[tt * 128:(tt + 1) * 128, :], xres.t[:, t8, :], [xrv[t8]], [xcv[tt]])
                                        if l + 1 < nlayers:
                                            cn = colps[(l + 1) % 2]
                                            norm_tile(xres.t[:, t8, :], xrv[t8], tt, (cn, cn.t[:, 0:8]))
                                    else:
                                        ot = otile[t8 % 2]
                                        nsq, nss = nsq_l[t8 % 2], nss_l[t8 % 2]
                                        P.op("act", lambda e: e.activation(nsq.t[:], xres.t[:, t8, :], AF.Square, accum_out=nss.t[:, 0:1]), [xrv[t8]], [nsq, nss])
                                        rsqrt_(nss.t[:, 1:2], nss.t[:, 0:1], 1.0 / D, [nss], [nss])
                                        P.op("dve", lambda e: e.scalar_tensor_tensor(ot.t[:], xres.t[:, t8, :], nss.t[:, 1:2], gfin.t[:], ALU.mult, ALU.mult), [xrv[t8], nss, gfin], [ot])
                                        P.dma(out_d[tt * 128:(tt + 1) * 128, :], ot.t[:], [ot], [])
            if stop in ("mix", "mlp"):
                stopped = True
                break
        if stopped:
            dump_ys()
        P.finish()
        print("build: ops", P.nops, "waits", P.nwait, {k: P.cnt[k] for k in P.cnt})
    return nc


def make_params(inp):
    colp = np.zeros((DEPTH, 128, NCOL), np.float32)
    rowp = np.zeros((DEPTH, NROW), np.float32)
    for l in range(DEPTH):
        colp[l, :, 0:8] = inp["norm_mix_g"][l].reshape(8, 128).T
        colp[l, :, 8:16] = inp["norm_mlp_g"][l].reshape(8, 128).T
        colp[l, :, 16:88] = inp["dn_conv_w"][l].reshape(3, 24, 128).transpose(2, 0, 1).reshape(128, 72)
        colp[l, :, 88:112] = inp["sc_conv_w"][l].reshape(3, 8, 128).transpose(2, 0, 1).reshape(128, 24)
        rowp[l, 0:1024] = inp["m_norm_g"][l]
        rowp[l, 1024:1152] = inp["dn_norm_g"][l]
        rowp[l, 1152:1168] = inp["m_gate_b"][l].reshape(16)
        rowp[l, 1168:1184] = inp["dn_a_log"][l].reshape(16)
        rowp[l, 1184:1200] = inp["dn_dt_bias"][l].reshape(16)
    return colp, rowp


def make_in_maps(inp, cores):
    colp, rowp = make_params(inp)
    shared = {"w_in": np.ascontiguousarray(inp["w_in"]), "w_branch": np.ascontiguousarray(inp["w_branch"]),
              "w_out": np.ascontiguousarray(inp["w_out"]), "w_up": np.ascontiguousarray(inp["w_up"]),
              "w_down": np.ascontiguousarray(inp["w_down"]), "colp": colp, "rowp": rowp,
              "gfin": np.ascontiguousarray(inp["norm_final_g"])}
    return [dict(shared, x=np.ascontiguousarray(inp["x"][b])) for b in cores]


def kernel(**inputs):
    inp = {k: np.asarray(v, dtype=np.float32) for k, v in inputs.items()}
    nc = build()
    in_maps = make_in_maps(inp, list(range(8)))
    res = run_bass_kernel_spmd(nc, in_maps, core_ids=list(range(8)))
    return np.stack([r["out"] for r in res.results], axis=0).astype(np.float32)
```

```python
import numpy as np
import concourse.bass as bass
import concourse.mybir as mybir
from concourse.bass_utils import run_bass_kernel_spmd
from contextlib import ExitStack

F32 = mybir.dt.float32
BF16 = mybir.dt.bfloat16
ALU = mybir.AluOpType
AF = mybir.ActivationFunctionType
AX = mybir.AxisListType

S = 2048
D = 1024
NT = 16
DEPTH = 4
NPROJ = 14384
DFF = 4096
EPS = 1e-6
NCOL = 112
NROW = 1200
O_MQ, O_MK, O_MV, O_MO, O_MG = 0, 1024, 2048, 3072, 4096
O_DQ, O_DK, O_DV, O_DZ, O_DG = 4112, 5136, 6160, 7184, 8208
O_SB, O_SC, O_SX, O_MRG = 8240, 9264, 10288, 11312
NEG = -30000.0


class Tok:
    __slots__ = ("sem", "val", "clk")

    def __init__(self, sem, val, clk):
        self.sem, self.val, self.clk = sem, val, clk


class Buf:
    __slots__ = ("t", "w", "r", "name", "excl", "lastrd")

    def __init__(self, t, name, excl=False):
        self.t, self.name = t, name
        self.w = None
        self.r = []
        self.excl = excl
        self.lastrd = None


class Prog:
    NDMA = 12

    def __init__(self, nc, es, same_engine_sync=True):
        self.nc, self.es = nc, es
        self.same = same_engine_sync
        self.eng = {"pe": nc.tensor, "act": nc.scalar, "dve": nc.vector, "pool": nc.gpsimd, "sp": nc.sync}
        self.sem = {k: es.enter_context(nc.semaphore("s_" + k)) for k in self.eng}
        self.cnt = {k: 0 for k in self.eng}
        self.clk = {k: {} for k in self.eng}
        self.pend = {k: [] for k in self.eng}
        self.dsem = [es.enter_context(nc.semaphore("d%d" % i)) for i in range(2 * self.NDMA)]
        self.dcnt = [0] * (2 * self.NDMA)
        self.dnext = {"sp": 0, "pool": 0}
        self.nwait = 0
        self.nops = 0

    def sb(self, name, shape, dt):
        t = self.es.enter_context(self.nc.sbuf_tensor(name, list(shape), dt))
        return Buf(t, name)

    def ps(self, name, shape, dt):
        t = self.es.enter_context(self.nc.psum_tensor(name, list(shape), dt))
        return Buf(t, name, excl=True)

    def dram(self, name, shape, dt):
        t = self.nc.dram_tensor(name, list(shape), dt, kind="Internal").ap()
        return Buf(t, name)

    def _need(self, reads, writes):
        toks = []
        for b in reads:
            if b.w is not None:
                toks.append(b.w)
        for b in writes:
            if b.w is not None:
                toks.append(b.w)
            toks.extend(b.r)
        return toks

    def _wait(self, e, toks):
        clk = self.clk[e]
        eng = self.eng[e]
        own = self.sem[e]
        best = {}
        for t in toks:
            if t.sem is own and (e == "pe" or not self.same):
                continue
            k = id(t.sem)
            if clk.get(k, 0) >= t.val:
                continue
            if k not in best or best[k].val < t.val:
                best[k] = t
        for k, t in best.items():
            if clk.get(k, 0) >= t.val:
                continue
            eng.wait_ge(t.sem, t.val)
            self.nwait += 1
            clk[k] = t.val
            for kk, vv in t.clk.items():
                if clk.get(kk, 0) < vv:
                    clk[kk] = vv

    def op(self, e, fn, reads, writes, inc=True):
        xr = [b for b in reads if b.excl]
        if xr:
            reads = [b for b in reads if not b.excl]
        toks = self._need(reads, writes)
        for b in xr:
            if b.w is not None and not (b.lastrd == e and b.w.sem is self.sem[e]):
                toks.append(b.w)
            toks.extend(b.r)
        self._wait(e, toks)
        for b in writes:
            b.lastrd = None
        if xr:
            writes = list(writes) + xr
        ins = fn(self.eng[e])
        self.nops += 1
        if not inc:
            self.pend[e].append((reads, writes))
            return ins
        self.cnt[e] += 1
        ins.then_inc(self.sem[e], 1)
        tok = Tok(self.sem[e], self.cnt[e], dict(self.clk[e]))
        tok.clk[id(self.sem[e])] = self.cnt[e]
        for (rs, ws) in self.pend[e] + [(reads, writes)]:
            for b in rs:
                b.r.append(tok)
            for b in ws:
                b.w = tok
                b.r = []
        for b in xr:
            b.lastrd = e
        self.pend[e] = []
        return ins

    def dma(self, out, in_, reads, writes, q="sp"):
        toks = self._need(reads, writes)
        i = self.dnext[q] + (self.NDMA if q == "pool" else 0)
        self.dnext[q] = (self.dnext[q] + 1) % self.NDMA
        s = self.dsem[i]
        if self.dcnt[i] > 0:
            toks.append(Tok(s, self.dcnt[i], {}))
        self._wait(q, toks)
        ins = self.eng[q].dma_start(out=out, in_=in_)
        self.dcnt[i] += 16
        ins.then_inc(s, 16)
        tok = Tok(s, self.dcnt[i], dict(self.clk[q]))
        for b in reads:
            b.r.append(tok)
        for b in writes:
            b.w = tok
            b.r = []
        self.nops += 1
        return ins

    def barrier(self):
        toks = []
        for i in range(2 * self.NDMA):
            if self.dcnt[i] > 0:
                toks.append(Tok(self.dsem[i], self.dcnt[i], {}))
        for k in self.eng:
            if self.cnt[k] > 0:
                toks.append(Tok(self.sem[k], self.cnt[k], {}))
        for e in self.eng:
            self._wait(e, [t for t in toks if t.sem is not self.sem[e]])

    def finish(self):
        toks = []
        for i in range(2 * self.NDMA):
            if self.dcnt[i] > 0:
                toks.append(Tok(self.dsem[i], self.dcnt[i], {}))
        for k in self.eng:
            if self.cnt[k] > 0 and k != "sp":
                toks.append(Tok(self.sem[k], self.cnt[k], {}))
        self._wait("sp", toks)


STAG_CFG = [4]
NS_CFG = [4]


def build(nlayers=DEPTH, dbg=(), stop=None):
    nc = bass.Bass("TRN2", target_bir_lowering=False)

    def din(name, shape):
        return nc.dram_tensor(name, list(shape), F32, kind="ExternalInput").ap()

    x_d = din("x", [S, D])
    w_in_d = din("w_in", [DEPTH, D, NPROJ])
    w_br_d = din("w_branch", [DEPTH, 3, D, D])
    w_out_d = din("w_out", [DEPTH, D, D])
    w_up_d = din("w_up", [DEPTH, D, DFF])
    w_dn_d = din("w_down", [DEPTH, DFF, D])
    colp_d = din("colp", [DEPTH, 128, NCOL])
    rowp_d = din("rowp", [DEPTH, NROW])
    gfin_d = din("gfin", [D])
    out_d = nc.dram_tensor("out", [S, D], F32, kind="ExternalOutput").ap()
    dbg_d = {}
    for name, shape, dt in (("d_ys", [3, D, S], BF16), ("d_xm", [S, D], F32)):
        if name in dbg:
            dbg_d[name] = nc.dram_tensor(name, shape, dt, kind="ExternalOutput").ap()

    with ExitStack() as es:
        P = Prog(nc, es)
        hT = P.sb("hT", [128, 8, S], BF16)
        hTv = [Buf(hT.t, "hT%d" % i) for i in range(NT)]
        NWB = 4
        wb = [P.sb("wb%d" % i, [128, 8, 512], BF16) for i in range(NWB)]
        wbi = [0]

        wlim = [NWB]

        def nextw():
            b = wb[wbi[0] % wlim[0]]
            wbi[0] += 1
            return b

        identb = P.sb("identb", [128, 128], BF16)
        identf = P.sb("identf", [128, 128], F32)
        onesb = P.sb("onesb", [128, 128], BF16)
        onesf = P.sb("onesf", [128, 128], F32)
        tincl = P.sb("tincl", [128, 128], F32)
        tinclT = P.sb("tinclT", [128, 128], F32)
        mlo = P.sb("mlo", [128, 128], F32)
        mup = P.sb("mup", [128, 128], F32)
        slf = P.sb("slf", [128, 128], F32)
        suf = P.sb("suf", [128, 128], F32)
        colps = [P.sb("colp_sb%d" % i, [128, NCOL], F32) for i in range(2)]
        rowp = P.sb("rowp_sb", [128, NROW], F32)
        wsm = P.sb("wsm", [128, 8, 48], BF16)
        wsf = P.sb("wsf", [128, 8, 48], F32)
        nsq_l = [P.sb("nsq%d" % i, [128, D], BF16) for i in range(2)]
        nss_l = [P.sb("nss%d" % i, [128, 2], F32) for i in range(2)]
        nxn_l = [P.sb("nxn%d" % i, [128, D], BF16) for i in range(2)]
        nrm_i = [0]
        pb = [P.ps("pb%d" % i, [128, 512], F32) for i in range(7)]
        pbt = P.ps("pbt", [128, 1024], BF16)
        ysT = P.dram("ysT", [3, D, S], BF16)
        ysv = [[Buf(ysT.t, "ys%d_%d" % (n, c)) for c in range(8)] for n in range(3)]
        xcur = P.dram("xcur", [S, D], F32)
        xcv = [Buf(xcur.t, "xc%d" % i) for i in range(NT)]

        def pool(fn, r, w):
            P.op("pool", fn, r, w)

        pool(lambda e: e.memset(onesf.t[:], 1.0), [], [onesf])
        pool(lambda e: e.memset(onesb.t[:], 1.0), [], [onesb])
        pool(lambda e: e.memset(identf.t[:], 0.0), [], [identf])
        pool(lambda e: e.affine_select(identf.t[:], identf.t[:], pattern=[[-1, 128]], compare_op=ALU.not_equal, fill=1.0, base=0, channel_multiplier=1), [identf], [identf])
        pool(lambda e: e.tensor_copy(identb.t[:], identf.t[:]), [identf], [identb])
        pool(lambda e: e.affine_select(tincl.t[:], onesf.t[:], pattern=[[1, 128]], compare_op=ALU.is_ge, fill=0.0, base=0, channel_multiplier=-1), [onesf], [tincl])
        pool(lambda e: e.affine_select(tinclT.t[:], onesf.t[:], pattern=[[-1, 128]], compare_op=ALU.is_ge, fill=0.0, base=0, channel_multiplier=1), [onesf], [tinclT])
        pool(lambda e: e.memset(mlo.t[:], 0.0), [], [mlo])
        pool(lambda e: e.affine_select(mlo.t[:], mlo.t[:], pattern=[[1, 128]], compare_op=ALU.is_ge, fill=NEG, base=0, channel_multiplier=-1), [mlo], [mlo])
        pool(lambda e: e.memset(mup.t[:], 0.0), [], [mup])
        pool(lambda e: e.affine_select(mup.t[:], mup.t[:], pattern=[[-1, 128]], compare_op=ALU.is_ge, fill=NEG, base=0, channel_multiplier=1), [mup], [mup])
        pool(lambda e: e.affine_select(slf.t[:], onesf.t[:], pattern=[[-1, 128]], compare_op=ALU.is_gt, fill=0.0, base=0, channel_multiplier=1), [onesf], [slf])
        pool(lambda e: e.affine_select(suf.t[:], onesf.t[:], pattern=[[1, 128]], compare_op=ALU.is_gt, fill=0.0, base=0, channel_multiplier=-1), [onesf], [suf])

        def dump(name, buf, ap, shape, dt=F32):
            if ("D_" + name) in dbg:
                dd = nc.dram_tensor("D_" + name, list(shape), dt, kind="ExternalOutput").ap()
                P.dma(dd, ap, [buf], [])

        uid = [0]

        def scope():
            class _S:
                def __enter__(s_):
                    s_.es = ExitStack()
                    s_.es.__enter__()
                    return s_

                def sb(s_, name, shape, dt):
                    uid[0] += 1
                    return Buf(s_.es.enter_context(nc.sbuf_tensor("%s_u%d" % (name, uid[0]), list(shape), dt)), name)

                def __exit__(s_, *a):
                    P.barrier()
                    return s_.es.__exit__(*a)
            return _S()

        evi = [0]

        def evac_copy(out_ap, in_ap, reads, writes, scale=None):
            evi[0] += 1
            if evi[0] % 2 == 0:
                if scale is None:
                    P.op("act", lambda e: e.copy(out_ap, in_ap), reads, writes)
                else:
                    P.op("act", lambda e: e.mul(out_ap, in_ap, scale), reads, writes)
            else:
                if scale is None:
                    P.op("dve", lambda e: e.tensor_copy(out_ap, in_ap), reads, writes)
                else:
                    P.op("dve", lambda e: e.tensor_scalar_mul(out_ap, in_ap, scale), reads, writes)

        def wload(dst, col0, src2d, c0, n):
            P.dma(dst.t[:, :, col0:col0 + n], src2d.rearrange("(kc p) n -> p kc n", p=128)[:, :, c0:c0 + n], [], [dst], q="pool")

        def rsqrt_(dst_ap, src_ap, scale, reads, writes):
            P.op("act", lambda e: e.activation(dst_ap, src_ap, AF.Ln, bias=EPS, scale=scale), reads, writes)
            P.op("act", lambda e: e.activation(dst_ap, dst_ap, AF.Exp, scale=-0.5), writes, writes)

        def norm_tile(xt_ap, xbuf, tt, gcol):
            gbuf, gap = gcol
            nrm_i[0] += 1
            nsq, nss, nxn = nsq_l[nrm_i[0] % 2], nss_l[nrm_i[0] % 2], nxn_l[nrm_i[0] % 2]
            pbo = (nrm_i[0] % 2) * 0
            P.op("act", lambda e: e.activation(nsq.t[:], xt_ap, AF.Square, accum_out=nss.t[:, 0:1]), [xbuf], [nsq, nss])
            rsqrt_(nss.t[:, 1:2], nss.t[:, 0:1], 1.0 / D, [nss], [nss])
            P.op("dve", lambda e: e.tensor_scalar(nxn.t[:], xt_ap, nss.t[:, 1:2], None, ALU.mult), [xbuf, nss], [nxn])
            for c in range(8):
                P.op("pe", lambda e: e.transpose(pbt.t[:, c * 128:(c + 1) * 128], nxn.t[:, c * 128:(c + 1) * 128], identb.t[:]), [nxn, identb], [pbt], inc=(c == 7))
            P.op("dve", lambda e: e.tensor_tensor(hT.t[:, :, tt * 128:(tt + 1) * 128], pbt.t[:].rearrange("p (c t) -> p c t", c=8),
                                                  gap.unsqueeze(2).to_broadcast([128, 8, 128]), ALU.mult), [pbt, gbuf], [hTv[tt]])

        def softplus_(dst, src, t1, t2, n, sign_logsig=False):
            P.op("dve", lambda e: e.scalar_tensor_tensor(t1.t[:, 0:n], src.t[:, 0:n], -1.0, src.t[:, 0:n], ALU.mult, ALU.max), [src], [t1])
            P.op("act", lambda e: e.activation(t2.t[:, 0:n], t1.t[:, 0:n], AF.Exp, scale=-1.0), [t1], [t2])
            P.op("act", lambda e: e.activation(t2.t[:, 0:n], t2.t[:, 0:n], AF.Ln, bias=1.0), [t2], [t2])
            if sign_logsig:
                P.op("dve", lambda e: e.scalar_tensor_tensor(dst.t[:, 0:n], src.t[:, 0:n], 0.0, t2.t[:, 0:n], ALU.min, ALU.subtract), [src, t2], [dst])
            else:
                P.op("dve", lambda e: e.scalar_tensor_tensor(dst.t[:, 0:n], src.t[:, 0:n], 0.0, t2.t[:, 0:n], ALU.max, ALU.add), [src, t2], [dst])

        def decay(out, negc_ap, bias_ap, cbufs, mask, bank, diag):
            P.op("dve", lambda e: e.tensor_scalar(diag.t[:], identf.t[:], negc_ap, None, ALU.mult), [identf] + cbufs, [diag])
            P.op("pe", lambda e: e.matmul(bank.t[:, 0:128], onesf.t[:], diag.t[:], start=True, stop=False), [onesf, diag], [bank], inc=False)
            P.op("pe", lambda e: e.matmul(bank.t[:, 0:128], identf.t[:], mask.t[:], start=False, stop=True), [identf, mask], [bank])
            P.op("act", lambda e: e.activation(out.t[:], bank.t[:, 0:128], AF.Exp, bias=bias_ap, scale=1.0), [bank] + cbufs, [out])

        def proj_fm(w, wcol, banks, evac, tbs=range(4)):
            for tb in tbs:
                bank = banks[tb % len(banks)]
                for kc in range(8):
                    P.op("pe", lambda e: e.matmul(bank.t[:, 0:512], w.t[:, kc, wcol:wcol + 128], hT.t[:, kc, tb * 512:(tb + 1) * 512],
                                                  start=(kc == 0), stop=(kc == 7)), [w] + hTv[tb * 4:tb * 4 + 4], [bank], inc=(kc == 7))
                evac(tb, bank)

        def proj_tm(w, wcol, ncols, tt, bank):
            for kc in range(8):
                P.op("pe", lambda e: e.matmul(bank.t[:, 0:ncols], hT.t[:, kc, tt * 128:(tt + 1) * 128], w.t[:, kc, wcol:wcol + ncols],
                                              start=(kc == 0), stop=(kc == 7)), [w, hTv[tt]], [bank], inc=(kc == 7))

        def dump_ys():
            if "d_ys" in dbg_d:
                for n in range(3):
                    for c in range(8):
                        P.dma(dbg_d["d_ys"][n, c * 128:(c + 1) * 128, :], ysT.t[n, c * 128:(c + 1) * 128, :], [ysv[n][c]], [])

        stopped = False
        P.dma(colps[0].t[:], colp_d[0], [], [colps[0]])
        for l in range(nlayers):
            w_in = w_in_d[l]
            colp = colps[l % 2]
            P.dma(rowp.t[:], rowp_d[l].partition_broadcast(128), [], [rowp])
            wrr = w_in.rearrange("(kc p) n -> p kc n", p=128)
            P.dma(wsf.t[:, :, 0:16], wrr[:, :, O_MG:O_MG + 16], [], [wsf])
            P.dma(wsf.t[:, :, 16:48], wrr[:, :, O_DG:O_DG + 32], [], [wsf])
            P.op("dve", lambda e: e.tensor_copy(wsm.t[:], wsf.t[:]), [wsf], [wsm])
            if l == 0:
                with scope() as sc:
                    xin = [sc.sb("xin%d" % i, [128, D], F32) for i in range(2)]
                    for tt in range(NT):
                        xb_ = xin[tt % 2]
                        P.dma(xb_.t[:], x_d[tt * 128:(tt + 1) * 128, :], [], [xb_])
                        P.dma(xcur.t[tt * 128:(tt + 1) * 128, :], xb_.t[:], [xb_], [xcv[tt]])
                        norm_tile(xb_.t[:], xb_, tt, (colp, colp.t[:, 0:8]))

            with scope() as sc:
                sb1 = sc.sb
                gm = sb1("gm", [128, 256], F32)
                ipre = sb1("ipre", [128, 128], F32)
                fpre = sb1("fpre", [128, 128], F32)
                lf = sb1("lf", [128, 128], F32)
                t1 = sb1("t1", [128, 256], F32)
                t2 = sb1("t2", [128, 256], F32)
                bcs = sb1("bcs", [128, 128], F32)
                totb = sb1("totb", [128, 128], F32)
                biasc = sb1("biasc", [128, 128], F32)
                expb = sb1("expb", [128, 128], F32)
                wgt = sb1("wgt", [128, 128], F32)
                dec = sb1("dec", [128, 128], F32)
                for tt in range(NT):
                    for kc in range(8):
                        P.op("pe", lambda e: e.matmul(pb[0].t[:, tt * 16:(tt + 1) * 16], hT.t[:, kc, tt * 128:(tt + 1) * 128], wsm.t[:, kc, 0:16],
                                                      start=(kc == 0), stop=(kc == 7)), [wsm, hTv[tt]], [pb[0]], inc=(kc == 7 and tt == NT - 1))
                P.op("dve", lambda e: e.tensor_tensor(gm.t[:].rearrange("p (t g) -> p t g", g=16), pb[0].t[:, 0:256].rearrange("p (t g) -> p t g", g=16),
                                                      rowp.t[:, 1152:1168].unsqueeze(1).to_broadcast([128, 16, 16]), ALU.add), [pb[0], rowp], [gm])
                gm5 = gm.t[:].rearrange("p (t d w h) -> p t d w h", d=2, w=2, h=4)
                v4 = lambda b_: b_.t[:].rearrange("p (t d h) -> p t d h", d=2, h=4)
                P.op("dve", lambda e: e.tensor_copy(v4(ipre), gm5[:, :, :, 0, :]), [gm], [ipre])
                P.op("dve", lambda e: e.tensor_copy(v4(fpre), gm5[:, :, :, 1, :]), [gm], [fpre])
                softplus_(lf, fpre, t1, t2, 128, sign_logsig=True)
                P.op("pe", lambda e: e.matmul(pb[1].t[:, 0:128], tincl.t[:], lf.t[:], start=True, stop=True), [tincl, lf], [pb[1]], inc=False)
                P.op("pe", lambda e: e.matmul(pb[1].t[:, 128:256], tinclT.t[:], lf.t[:], start=True, stop=True), [tinclT, lf], [pb[1]], inc=False)
                P.op("pe", lambda e: e.matmul(pb[1].t[:, 256:384], onesf.t[:], lf.t[:], start=True, stop=True), [onesf, lf], [pb[1]])
                pv = lambda a, b: pb[1].t[:, a:b].rearrange("p (t d h) -> p t d h", d=2, h=4)
                P.op("dve", lambda e: e.tensor_copy(v4(bcs)[:, :, 0, :], pv(0, 128)[:, :, 0, :]), [pb[1]], [bcs])
                P.op("dve", lambda e: e.tensor_copy(v4(bcs)[:, :, 1, :], pv(128, 256)[:, :, 1, :]), [pb[1]], [bcs])
                P.op("dve", lambda e: e.tensor_copy(totb.t[:], pb[1].t[:, 256:384]), [pb[1]], [totb])
                P.op("dve", lambda e: e.tensor_sub(biasc.t[:], ipre.t[:], bcs.t[:]), [ipre, bcs], [biasc])
                P.op("act", lambda e: e.activation(expb.t[:], bcs.t[:], AF.Exp), [bcs], [expb])
                P.op("dve", lambda e: e.tensor_add(t1.t[:, 0:128], biasc.t[:], totb.t[:]), [biasc, totb], [t1])
                P.op("act", lambda e: e.activation(wgt.t[:], t1.t[:, 0:128], AF.Exp), [t1], [wgt])
                P.op("act", lambda e: e.activation(dec.t[:], totb.t[:], AF.Exp), [totb], [dec])

                qT = sb1("qT", [128, 2, S], BF16)
                kT = sb1("kT", [128, 2, S], BF16)
                ktm = sb1("ktm", [128, NT, 256], BF16)
                vtm = sb1("vtm", [128, NT, 257], BF16)
                hm = sb1("hm", [128, NT, 256], F32)
                hmv = [Buf(hm.t, "hm%d" % i) for i in range(NT)]
                Cst = [sb1("Cst%d" % d_, [128, 2, 257], F32) for d_ in range(2)]
                Cb = [sb1("Cb%d" % d_, [128, 2, 257], BF16) for d_ in range(2)]
                DT = [sb1("DT%d" % d_, [128, 128], F32) for d_ in range(2)]
                diag = [sb1("diag%d" % d_, [128, 128], F32) for d_ in range(2)]
                scT = [sb1("scT%d" % d_, [128, 128], BF16) for d_ in range(2)]
                Asb = [sb1("Asb%d" % d_, [128, 257], F32) for d_ in range(2)]
                comb = [sb1("comb%d" % d_, [128, 257], F32) for d_ in range(2)]
                rr = [sb1("rr%d" % d_, [128, 2], F32) for d_ in range(2)]
                kw = [sb1("kw%d" % d_, [128, 256], BF16) for d_ in range(2)]
                og_l = [sb1("og%d" % i, [128, 256], F32) for i in range(2)]
                ty_l = [sb1("ty%d" % i, [128, 256], F32) for i in range(2)]
                yb_l = [sb1("yb%d" % i, [128, 256], BF16) for i in range(2)]
                ty = ty_l[0]
                ssq = sb1("ssq", [128, 2 * NT], F32)
                ymT = sb1("ymT", [128, 2, S], BF16)
                P.op("dve", lambda e: e.memset(vtm.t[:, :, 256:257], 1.0), [], [vtm])

                for h in range(4):
                    wqk, wkv, wo = nextw(), nextw(), nextw()
                    wload(wqk, 0, w_in, O_MQ + h * 256, 256)
                    wload(wqk, 256, w_in, O_MK + h * 256, 256)
                    wload(wkv, 0, w_in, O_MK + h * 256, 256)
                    wload(wkv, 256, w_in, O_MV + h * 256, 256)
                    wload(wo, 0, w_in, O_MO + h * 256, 256)
                    for ft in range(2):
                        proj_fm(wqk, ft * 128, pb[0:4], lambda tb, bank: evac_copy(qT.t[:, ft, tb * 512:(tb + 1) * 512], bank.t[:, 0:512], [bank], [qT], scale=0.0625))
                        proj_fm(wqk, 256 + ft * 128, pb[0:4], lambda tb, bank: evac_copy(kT.t[:, ft, tb * 512:(tb + 1) * 512], bank.t[:, 0:512], [bank], [kT]))
                    for tt in range(NT):
                        bank = pb[tt % 4]
                        proj_tm(wkv, 0, 512, tt, bank)
                        if tt % 2 == 0:
                            P.op("act", lambda e: e.copy(ktm.t[:, tt, :], bank.t[:, 0:256]), [bank], [ktm])
                            P.op("act", lambda e: e.copy(vtm.t[:, tt, 0:256], bank.t[:, 256:512]), [bank], [vtm])
                        else:
                            P.op("dve", lambda e: e.tensor_copy(ktm.t[:, tt, :], bank.t[:, 0:256]), [bank], [ktm])
                            P.op("dve", lambda e: e.tensor_copy(vtm.t[:, tt, 0:256], bank.t[:, 256:512]), [bank], [vtm])
                    def gen_mrec(d_, h=h):
                        bRQ, bS, bA = (pb[0], pb[3])[d_], (pb[1], pb[4])[d_], (pb[2], pb[5])[d_]
                        mask = mlo if d_ == 0 else mup
                        for step in range(NT):
                            c = step if d_ == 0 else NT - 1 - step
                            cs = slice(c * 128, (c + 1) * 128)
                            gi = (c * 2 + d_) * 4 + h
                            col = lambda b_: b_.t[:, gi:gi + 1]
                            P.op("dve", lambda e: e.tensor_scalar(diag[d_].t[:], identf.t[:], col(bcs), None, ALU.mult), [identf, bcs], [diag[d_]])
                            if step < NT - 1:
                                P.op("act", lambda e: e.activation(kw[d_].t[:], ktm.t[:, c, :], AF.Copy, scale=col(wgt)), [ktm, wgt], [kw[d_]])
                            yield
                            P.op("pe", lambda e: e.matmul(bRQ.t[:, 0:128], onesf.t[:], diag[d_].t[:], start=True, stop=False), [onesf, diag[d_]], [bRQ], inc=False)
                            P.op("pe", lambda e: e.matmul(bRQ.t[:, 0:128], identf.t[:], mask.t[:], start=False, stop=True), [identf, mask], [bRQ], inc=False)
                            for kc in range(2):
                                P.op("pe", lambda e: e.matmul(bRQ.t[:, 128:256], kT.t[:, kc, cs], qT.t[:, kc, cs], start=(kc == 0), stop=(kc == 1)), [kT, qT], [bRQ], inc=(kc == 1))
                            if step > 0:
                                for kc in range(2):
                                    P.op("pe", lambda e: e.matmul(bA.t[:, 0:257], qT.t[:, kc, cs], Cb[d_].t[:, kc, :], start=(kc == 0), stop=(kc == 1)), [qT, Cb[d_]], [bA], inc=(kc == 1))
                            yield
                            P.op("act", lambda e: e.activation(DT[d_].t[:], bRQ.t[:, 0:128], AF.Exp, bias=col(biasc), scale=1.0), [bRQ, biasc], [DT[d_]])
                            if step > 0:
                                P.op("act", lambda e: e.activation(Asb[d_].t[:], bA.t[:, 0:257], AF.Copy, scale=col(expb)), [bA, expb], [Asb[d_]])
                            yield
                            P.op("dve", lambda e: e.tensor_tensor(scT[d_].t[:], bRQ.t[:, 128:256], DT[d_].t[:], ALU.mult), [bRQ, DT[d_]], [scT[d_]])
                            yield
                            P.op("pe", lambda e: e.matmul(bS.t[:, 0:257], scT[d_].t[:], vtm.t[:, c, :], start=True, stop=True), [scT[d_], vtm], [bS])
                            yield
                            if step > 0:
                                P.op("dve", lambda e: e.tensor_tensor(comb[d_].t[:], Asb[d_].t[:], bS.t[:, 0:257], ALU.add), [Asb[d_], bS], [comb[d_]])
                            else:
                                P.op("dve", lambda e: e.tensor_copy(comb[d_].t[:], bS.t[:, 0:257]), [bS], [comb[d_]])
                            den = comb[d_].t[:, 256:257]
                            P.op("dve", lambda e: e.scalar_tensor_tensor(rr[d_].t[:, 0:1], den, -1.0, den, ALU.mult, ALU.max), [comb[d_]], [rr[d_]])
                            P.op("dve", lambda e: e.tensor_scalar_max(rr[d_].t[:, 0:1], rr[d_].t[:, 0:1], 1.0), [rr[d_]], [rr[d_]])
                            P.op("dve", lambda e: e.reciprocal(rr[d_].t[:, 1:2], rr[d_].t[:, 0:1]), [rr[d_]], [rr[d_]])
                            if step < NT - 1:
                                P.op("pe", lambda e: e.matmul(bS.t[:, 0:257], kw[d_].t[:, 0:128], vtm.t[:, c, :], start=True, stop=True), [kw[d_], vtm], [bS])
                                P.op("pe", lambda e: e.matmul(bA.t[:, 0:257], kw[d_].t[:, 128:256], vtm.t[:, c, :], start=True, stop=True), [kw[d_], vtm], [bA])
                            yield
                            first = (d_ == 0 and c < 8) or (d_ == 1 and c >= 8)
                            if first:
                                P.op("act", lambda e: e.activation(hm.t[:, c, :], comb[d_].t[:, 0:256], AF.Copy, scale=rr[d_].t[:, 1:2]), [comb[d_], rr[d_]], [hmv[c]])
                            else:
                                P.op("dve", lambda e: e.scalar_tensor_tensor(hm.t[:, c, :], comb[d_].t[:, 0:256], rr[d_].t[:, 1:2], hm.t[:, c, :], ALU.mult, ALU.add), [comb[d_], rr[d_]], [hmv[c]])
                            if step < NT - 1:
                                for m, bC in enumerate((bS, bA)):
                                    if step == 0:
                                        P.op("dve", lambda e: e.tensor_copy(Cst[d_].t[:, m, :], bC.t[:, 0:257]), [bC], [Cst[d_]])
                                    else:
                                        P.op("dve", lambda e: e.scalar_tensor_tensor(Cst[d_].t[:, m, :], Cst[d_].t[:, m, :], col(dec), bC.t[:, 0:257], ALU.mult, ALU.add), [bC, dec], [Cst[d_]])
                                yield
                                P.op("act", lambda e: e.copy(Cb[d_].t[:], Cst[d_].t[:]), [Cst[d_]], [Cb[d_]])
                            yield

                    gens = [gen_mrec(0), gen_mrec(1)]
                    while gens:
                        for g_ in list(gens):
                            try:
                                next(g_)
                            except StopIteration:
                                gens.remove(g_)
                    for tt in range(NT):
                        P.op("act", lambda e: e.activation(ty.t[:], hm.t[:, tt, :], AF.Square, accum_out=ssq.t[:, tt:tt + 1]), [hmv[tt]], [ty, ssq])
                    rsqrt_(ssq.t[:, NT:2 * NT], ssq.t[:, 0:NT], 1.0 / 256.0, [ssq], [ssq])
                    for tt in range(NT):
                        bank = pb[tt % 4]
                        og, ty, yb = og_l[tt % 2], ty_l[tt % 2], yb_l[tt % 2]
                        proj_tm(wo, 0, 256, tt, bank)
                        P.op("act", lambda e: e.activation(og.t[:], bank.t[:, 0:256], AF.Sigmoid), [bank], [og])
                        P.op("dve", lambda e: e.scalar_tensor_tensor(ty.t[:], hm.t[:, tt, :], ssq.t[:, NT + tt:NT + tt + 1], rowp.t[:, h * 256:(h + 1) * 256], ALU.mult, ALU.mult), [hmv[tt], ssq, rowp], [ty])
                        P.op("dve", lambda e: e.tensor_tensor(yb.t[:], ty.t[:], og.t[:], ALU.mult), [ty, og], [yb])
                        for j in range(2):
                            P.op("pe", lambda e: e.transpose(pbt.t[:, j * 128:(j + 1) * 128], yb.t[:, j * 128:(j + 1) * 128], identb.t[:]), [yb, identb], [pbt], inc=(j == 1))
                        P.op("act", lambda e: e.copy(ymT.t[:, :, tt * 128:(tt + 1) * 128], pbt.t[:, 0:256].rearrange("p (j t) -> p j t", j=2)), [pbt], [ymT])
                    P.dma(ysT.t[0, h * 256:(h + 1) * 256, :].rearrange("(j p) t -> p j t", p=128), ymT.t[:], [ymT], [ysv[0][2 * h], ysv[0][2 * h + 1]])
                    if stop == "m0":
                        break
            if stop in ("m0", "mlstm"):
                stopped = True
                break

            with scope() as sc:
                sb2 = sc.sb
                A2 = lambda name: sb2(name, [128, 256], F32)
                v4 = lambda b_: b_.t[:].rearrange("p (t d h) -> p t d h", d=2, h=8)
                Gc, negG, expG, bexpG, kdw, glb, negbeta, beta = [A2(n) for n in ("Gc", "negG", "expG", "bexpG", "kdw", "glb", "negbeta", "beta")]
                with scope() as sct:
                    dg = sct.sb("dg", [128, 512], F32)
                    apre, spl, gg, t1, t2 = [sct.sb(n, [128, 256], F32) for n in ("apre", "spl", "gg", "t1d", "t2d")]
                    ea = sct.sb("ea", [128, 16], F32)
                    for tt in range(NT):
                        for kc in range(8):
                            P.op("pe", lambda e: e.matmul(pb[0].t[:, tt * 32:(tt + 1) * 32], hT.t[:, kc, tt * 128:(tt + 1) * 128], wsm.t[:, kc, 16:48],
                                                          start=(kc == 0), stop=(kc == 7)), [wsm, hTv[tt]], [pb[0]], inc=(kc == 7 and tt == NT - 1))
                    P.op("act", lambda e: e.copy(dg.t[:], pb[0].t[:, 0:512]), [pb[0]], [dg])
                    dg5 = dg.t[:].rearrange("p (t d w h) -> p t d w h", d=2, w=2, h=8)
                    P.op("act", lambda e: e.activation(v4(beta), dg5[:, :, :, 0, :], AF.Sigmoid), [dg], [beta])
                    P.op("dve", lambda e: e.tensor_tensor(v4(apre), dg5[:, :, :, 1, :],
                                                          rowp.t[:, 1184:1200].rearrange("p (d h) -> p d h", d=2).unsqueeze(1).to_broadcast([128, 16, 2, 8]), ALU.add), [dg, rowp], [apre])
                    softplus_(spl, apre, t1, t2, 256)
                    P.op("act", lambda e: e.activation(ea.t[:], rowp.t[:, 1168:1184], AF.Exp), [rowp], [ea])
                    P.op("dve", lambda e: e.scalar_tensor_tensor(v4(gg), v4(spl), -1.0,
                                                                 ea.t[:].rearrange("p (d h) -> p d h", d=2).unsqueeze(1).to_broadcast([128, 16, 2, 8]), ALU.mult, ALU.mult), [spl, ea], [gg])
                    P.op("pe", lambda e: e.matmul(pb[1].t[:, 0:256], tincl.t[:], gg.t[:], start=True, stop=True), [tincl, gg], [pb[1]], inc=False)
                    P.op("pe", lambda e: e.matmul(pb[1].t[:, 256:512], tinclT.t[:], gg.t[:], start=True, stop=True), [tinclT, gg], [pb[1]], inc=False)
                    P.op("pe", lambda e: e.matmul(pb[2].t[:, 0:256], onesf.t[:], gg.t[:], start=True, stop=True), [onesf, gg], [pb[2]])
                    pv = lambda a, b: pb[1].t[:, a:b].rearrange("p (t d h) -> p t d h", d=2, h=8)
                    P.op("dve", lambda e: e.tensor_copy(v4(Gc)[:, :, 0, :], pv(0, 256)[:, :, 0, :]), [pb[1]], [Gc])
                    P.op("dve", lambda e: e.tensor_copy(v4(Gc)[:, :, 1, :], pv(256, 512)[:, :, 1, :]), [pb[1]], [Gc])
                    P.op("dve", lambda e: e.tensor_scalar_mul(negG.t[:], Gc.t[:], -1.0), [Gc], [negG])
                    P.op("act", lambda e: e.activation(expG.t[:], Gc.t[:], AF.Exp), [Gc], [expG])
                    P.op("dve", lambda e: e.tensor_mul(bexpG.t[:], beta.t[:], expG.t[:]), [beta, expG], [bexpG])
                    P.op("dve", lambda e: e.tensor_tensor(t1.t[:], pb[2].t[:, 0:256], Gc.t[:], ALU.subtract), [pb[2], Gc], [t1])
                    P.op("act", lambda e: e.activation(kdw.t[:], t1.t[:], AF.Exp), [t1], [kdw])
                    P.op("act", lambda e: e.activation(glb.t[:], pb[2].t[:, 0:256], AF.Exp), [pb[2]], [glb])
                    P.op("dve", lambda e: e.tensor_scalar_mul(negbeta.t[:], beta.t[:], -1.0), [beta], [negbeta])
                for nm, b_ in (("Gc", Gc), ("expG", expG), ("kdw", kdw), ("glb", glb), ("beta", beta)):
                    dump(nm, b_, b_.t[:], [128, 256])

                pc = sb2("pc", [128, S + 2], F32)
                cv = sb2("cv", [128, S], F32)
                sq = sb2("sq", [128, S], BF16)
                rin = sb2("rin", [128, 512], F32)
                qnT = sb2("qnT", [128, S], BF16)
                knT = sb2("knT", [128, S], BF16)
                vT = sb2("vT", [128, S], BF16)
                ktm = sb2("ktm2", [128, NT, 128], BF16)
                vtm = sb2("vtm2", [128, NT, 128], BF16)
                WTs = sb2("WTs", [128, 32, 128], BF16)
                Us = sb2("Us", [128, 32, 128], F32)
                ATs = sb2("ATs", [128, 32, 128], BF16)
                stv = [Buf(None, "st%d" % i) for i in range(32)]
                osb = sb2("osb", [128, NT, 128], F32)
                osv = [Buf(osb.t, "os%d" % i) for i in range(NT)]
                NS = NS_CFG[0]
                STAG = STAG_CFG[0]
                wlim[0] = 3 if NS_CFG[0] > 4 else NWB
                _flat = wb[3].t[:].rearrange("p a b -> p (a b)")
                _off = [0]

                def slot_tile(i, name, shape, dt):
                    if i < 4:
                        return sb2("%s%d" % (name, i), shape, dt)
                    n = shape[1] * (2 if dt == F32 else 1)
                    ap = _flat[:, _off[0]:_off[0] + n]
                    _off[0] += n
                    if dt == F32:
                        ap = ap.bitcast(F32)
                    return Buf(ap, "%s%d" % (name, i))
                Dm = [slot_tile(i, "Dm", [128, 128], F32) for i in range(NS)]
                dgl = [slot_tile(i, "dgl", [128, 128], F32) for i in range(NS)]
                Do1 = [slot_tile(i, "Do1", [128, 128], F32) for i in range(NS)]
                Do2 = [slot_tile(i, "Do2", [128, 128], F32) for i in range(NS)]
                CH = [[slot_tile(i, "CH%d_" % j, [128, 384], BF16) for j in range(2)] for i in range(NS)]
                Ao = [slot_tile(i, "Ao", [128, 256], BF16) for i in range(NS)]
                AoT = [slot_tile(i, "AoT", [128, 256], BF16) for i in range(NS)]
                attn_t = [slot_tile(i, "attn", [128, 128], BF16) for i in range(NS)]
                X0b = [slot_tile(i, "X0b", [128, 256], BF16) for i in range(NS)]
                X1b = [slot_tile(i, "X1b", [128, 256], BF16) for i in range(NS)]
                R1b = X0b
                Tmb = Ao
                Wtm = Do1
                Wb = [slot_tile(i, "Wb", [128, 128], BF16) for i in range(NS)]
                mbd = [sb2("mbd%d" % i, [128, 128], F32) for i in range(2)]
                mo1 = [sb2("mo1%d" % i, [128, 128], F32) for i in range(2)]
                mo2 = [sb2("mo2%d" % i, [128, 128], F32) for i in range(2)]
                with scope() as scm:
                    E32 = scm.sb("E32", [4, 128], F32)
                    E64 = scm.sb("E64", [2, 128], F32)
                    b32 = scm.sb("b32", [128, 128], F32)
                    b64 = scm.sb("b64", [128, 128], F32)
                    tmk = scm.sb("tmk", [128, 128], F32)
                    for E_, w_ in ((E32, 32), (E64, 64)):
                        np_ = 128 // w_
                        P.op("pool", lambda e: e.memset(E_.t[:], 1.0), [], [E_])
                        P.op("pool", lambda e: e.affine_select(E_.t[:], E_.t[:], pattern=[[1, 128]], compare_op=ALU.is_ge, fill=0.0, base=0, channel_multiplier=-w_), [E_], [E_])
                        P.op("pool", lambda e: e.affine_select(E_.t[:], E_.t[:], pattern=[[-1, 128]], compare_op=ALU.is_ge, fill=0.0, base=w_ - 1, channel_multiplier=w_), [E_], [E_])
                    P.op("pe", lambda e: e.matmul(pb[0].t[:, 0:128], E32.t[:], E32.t[:], start=True, stop=True), [E32], [pb[0]], inc=False)
                    P.op("pe", lambda e: e.matmul(pb[0].t[:, 128:256], E64.t[:], E64.t[:], start=True, stop=True), [E64], [pb[0]])
                    P.op("dve", lambda e: e.tensor_copy(b32.t[:], pb[0].t[:, 0:128]), [pb[0]], [b32])
                    P.op("dve", lambda e: e.tensor_copy(b64.t[:], pb[0].t[:, 128:256]), [pb[0]], [b64])
                    for d_, tri in ((0, slf), (1, suf)):
                        P.op("dve", lambda e: e.tensor_tensor(mbd[d_].t[:], tri.t[:], b32.t[:], ALU.mult), [tri, b32], [mbd[d_]])
                        P.op("dve", lambda e: e.tensor_tensor(tmk.t[:], b64.t[:], b32.t[:], ALU.subtract), [b64, b32], [tmk])
                        P.op("dve", lambda e: e.tensor_tensor(mo1[d_].t[:], tri.t[:], tmk.t[:], ALU.mult), [tri, tmk], [mo1[d_]])
                        P.op("dve", lambda e: e.tensor_tensor(tmk.t[:], tri.t[:], b64.t[:], ALU.mult), [tri, b64], [tmk])
                        P.op("dve", lambda e: e.tensor_tensor(mo2[d_].t[:], tri.t[:], tmk.t[:], ALU.subtract), [tri, tmk], [mo2[d_]])
                Sst = [sb2("Sst%d" % d_, [128, 128], F32) for d_ in range(2)]
                Sbb = [sb2("Sbb%d" % d_, [128, 128], BF16) for d_ in range(2)]
                vnb = [sb2("vnb%d" % d_, [128, 128], BF16) for d_ in range(2)]
                kdt = [sb2("kdt%d" % d_, [128, 128], BF16) for d_ in range(2)]
                tmo = [sb2("tmo%d" % d_, [128, 128], F32) for d_ in range(2)]
                zg_l = [sb2("zg%d" % i, [128, 128], F32) for i in range(2)]
                ty_l = [sb2("ty2%d" % i, [128, 128], F32) for i in range(2)]
                yb_l = [sb2("yb2%d" % i, [128, 128], BF16) for i in range(2)]
                ty = ty_l[0]
                ssq = sb2("ssq2", [128, 2 * NT], F32)
                ydT = sq
                P.op("dve", lambda e: e.memset(pc.t[:, 0:1], 0.0), [], [pc])
                P.op("dve", lambda e: e.memset(pc.t[:, S + 1:S + 2], 0.0), [], [pc])
                wA = wB = None
                for h in range(8):
                    hh = h % 2
                    if hh == 0:
                        wA, wB = nextw(), nextw()
                        wload(wA, 0, w_in, O_DQ + h * 128, 256)
                        wload(wA, 256, w_in, O_DK + h * 128, 256)
                        wload(wB, 0, w_in, O_DV + h * 128, 256)
                        wload(wB, 256, w_in, O_DZ + h * 128, 256)
                    for j, (wsrc, off) in enumerate(((wA, hh * 128), (wA, 256 + hh * 128), (wB, hh * 128))):
                        proj_fm(wsrc, off, pb[0:4], lambda tb, bank: evac_copy(pc.t[:, 1 + tb * 512:1 + (tb + 1) * 512], bank.t[:, 0:512], [bank], [pc]))
                        cw = lambda k: colp.t[:, 16 + k * 24 + j * 8 + h:16 + k * 24 + j * 8 + h + 1]
                        P.op("dve", lambda e: e.tensor_scalar(cv.t[:], pc.t[:, 0:S], cw(0), None, ALU.mult), [pc, colp], [cv])
                        P.op("dve", lambda e: e.scalar_tensor_tensor(cv.t[:], pc.t[:, 1:S + 1], cw(1), cv.t[:], ALU.mult, ALU.add), [pc, colp], [cv])
                        P.op("dve", lambda e: e.scalar_tensor_tensor(cv.t[:], pc.t[:, 2:S + 2], cw(2), cv.t[:], ALU.mult, ALU.add), [pc, colp], [cv])
                        P.op("act", lambda e: e.activation(cv.t[:], cv.t[:], AF.Silu), [cv], [cv])
                        if j == 2:
                            P.op("act", lambda e: e.copy(vT.t[:], cv.t[:]), [cv], [vT])
                        else:
                            dst = qnT if j == 0 else knT
                            P.op("act", lambda e: e.activation(sq.t[:], cv.t[:], AF.Square), [cv], [sq])
                            for tb in range(4):
                                bs = slice(tb * 512, (tb + 1) * 512)
                                bank = pb[4 + tb % 2]
                                P.op("pe", lambda e: e.matmul(bank.t[:, 0:512], onesb.t[:], sq.t[:, bs], start=True, stop=True), [onesb, sq], [bank])
                                rsqrt_(rin.t[:], bank.t[:, 0:512], 1.0, [bank], [rin])
                                if j == 0:
                                    P.op("dve", lambda e: e.scalar_tensor_tensor(dst.t[:, bs], cv.t[:, bs], 128.0 ** -0.5, rin.t[:], ALU.mult, ALU.mult), [cv, rin], [dst])
                                else:
                                    P.op("dve", lambda e: e.tensor_tensor(dst.t[:, bs], cv.t[:, bs], rin.t[:], ALU.mult), [cv, rin], [dst])
                    if h == 0:
                        dump("qnT", qnT, qnT.t[:], [128, S], BF16)
                        dump("knT", knT, knT.t[:], [128, S], BF16)
                        dump("vT", vT, vT.t[:], [128, S], BF16)
                    for src, dstm in ((knT, ktm), (vT, vtm)):
                        for g4 in range(2):
                            for i in range(8):
                                tt = g4 * 8 + i
                                P.op("pe", lambda e: e.transpose(pbt.t[:, i * 128:(i + 1) * 128], src.t[:, tt * 128:(tt + 1) * 128], identb.t[:]), [src, identb], [pbt], inc=(i == 7))
                            evac_copy(dstm.t[:, g4 * 8:(g4 + 1) * 8, :], pbt.t[:].rearrange("p (i t) -> p i t", i=8), [pbt], [dstm])
                    if stop == "dn0a":
                        break
                    def gen_prep(si, c, d_, h=h):
                        cs = slice(c * 128, (c + 1) * 128)
                        gi = (c * 2 + d_) * 8 + h
                        col = lambda b_: b_.t[:, gi:gi + 1]
                        e_ = c * 2 + d_
                        ch0, ch1 = CH[si][0], CH[si][1]
                        bank = pb[1 + si]
                        mask = mup if d_ == 0 else mlo
                        P.op("dve", lambda e: e.tensor_scalar(dgl[si].t[:], identf.t[:], col(negG), None, ALU.mult), [identf, negG], [dgl[si]])
                        yield
                        while lock["pb0"] is not None:
                            yield
                        lock["pb0"] = si
                        P.op("pe", lambda e: e.matmul(pb[0].t[:, 0:128], onesf.t[:], dgl[si].t[:], start=True, stop=False), [onesf, dgl[si]], [pb[0]], inc=False)
                        P.op("pe", lambda e: e.matmul(pb[0].t[:, 0:128], identf.t[:], mask.t[:], start=False, stop=True), [identf, mask], [pb[0]], inc=False)
                        P.op("pe", lambda e: e.matmul(pb[0].t[:, 128:256], knT.t[:, cs], knT.t[:, cs], start=True, stop=True), [knT], [pb[0]], inc=False)
                        P.op("pe", lambda e: e.matmul(pb[0].t[:, 256:384], qnT.t[:, cs], knT.t[:, cs], start=True, stop=True), [qnT, knT], [pb[0]])
                        yield
                        P.op("act", lambda e: e.activation(Dm[si].t[:], pb[0].t[:, 0:128], AF.Exp, bias=col(Gc), scale=1.0), [pb[0], Gc], [Dm[si]])
                        P.op("act", lambda e: e.activation(X0b[si].t[:, 0:128], ktm.t[:, c, :], AF.Copy, scale=col(bexpG)), [ktm, bexpG], [X0b[si]])
                        P.op("act", lambda e: e.activation(X0b[si].t[:, 128:256], vtm.t[:, c, :], AF.Copy, scale=col(beta)), [vtm, beta], [X0b[si]])
                        yield
                        P.op("pool", lambda e: e.tensor_tensor(dgl[si].t[:], Dm[si].t[:], mbd[d_].t[:], ALU.mult), [Dm[si], mbd[d_]], [dgl[si]])
                        P.op("pool", lambda e: e.tensor_tensor(Do1[si].t[:], Dm[si].t[:], mo1[d_].t[:], ALU.mult), [Dm[si], mo1[d_]], [Do1[si]])
                        P.op("pool", lambda e: e.tensor_tensor(Do2[si].t[:], Dm[si].t[:], mo2[d_].t[:], ALU.mult), [Dm[si], mo2[d_]], [Do2[si]])
                        P.op("dve", lambda e: e.tensor_tensor(attn_t[si].t[:], pb[0].t[:, 256:384], Dm[si].t[:], ALU.mult), [pb[0], Dm[si]], [attn_t[si]])
                        yield
                        P.op("dve", lambda e: e.scalar_tensor_tensor(ch0.t[:, 256:384], pb[0].t[:, 128:256], col(negbeta), dgl[si].t[:], ALU.mult, ALU.mult), [pb[0], negbeta, dgl[si]], [ch0])
                        P.op("dve", lambda e: e.scalar_tensor_tensor(Ao[si].t[:, 0:128], pb[0].t[:, 128:256], col(beta), Do1[si].t[:], ALU.mult, ALU.mult), [pb[0], beta, Do1[si]], [Ao[si]])
                        P.op("dve", lambda e: e.scalar_tensor_tensor(Ao[si].t[:, 128:256], pb[0].t[:, 128:256], col(beta), Do2[si].t[:], ALU.mult, ALU.mult), [pb[0], beta, Do2[si]], [Ao[si]])
                        lock["pb0"] = None
                        yield
                        while lock["pbt"] is not None:
                            yield
                        lock["pbt"] = si
                        P.op("pe", lambda e: e.transpose(pbt.t[:, 0:128], ch0.t[:, 256:384], identb.t[:]), [ch0, identb], [pbt], inc=False)
                        P.op("pe", lambda e: e.transpose(pbt.t[:, 128:256], Ao[si].t[:, 0:128], identb.t[:]), [Ao[si], identb], [pbt], inc=False)
                        P.op("pe", lambda e: e.transpose(pbt.t[:, 256:384], Ao[si].t[:, 128:256], identb.t[:]), [Ao[si], identb], [pbt], inc=False)
                        P.op("pe", lambda e: e.transpose(pbt.t[:, 384:512], attn_t[si].t[:], identb.t[:]), [attn_t[si], identb], [pbt])
                        yield
                        P.op("act", lambda e: e.copy(ch0.t[:, 0:128], pbt.t[:, 0:128]), [pbt], [ch0])
                        P.op("act", lambda e: e.copy(AoT[si].t[:], pbt.t[:, 128:384]), [pbt], [AoT[si]])
                        P.op("dve", lambda e: e.tensor_tensor(ch1.t[:, 128:256], pbt.t[:, 0:128], identb.t[:], ALU.add), [pbt, identb], [ch1])
                        P.op("dve", lambda e: e.tensor_copy(ATs.t[:, e_, :], pbt.t[:, 384:512]), [pbt], [stv[e_]])
                        lock["pbt"] = None
                        yield
                        P.op("pe", lambda e: e.matmul(bank.t[:, 0:128], ch0.t[:, 256:384], ch0.t[:, 0:128], start=True, stop=True), [ch0], [bank], inc=False)
                        P.op("pe", lambda e: e.matmul(bank.t[:, 256:384], ch0.t[:, 0:128], ch0.t[:, 256:384], start=True, stop=True), [ch0], [bank])
                        yield
                        P.op("act", lambda e: e.copy(ch1.t[:].rearrange("p (a b) -> p a b", b=128)[:, 0::2, :], bank.t[:, 0:384].rearrange("p (a b) -> p a b", b=128)[:, 0::2, :]), [bank], [ch1])
                        yield
                        for j in range(1, 5):
                            cur = CH[si][j % 2]
                            nxt = CH[si][(j + 1) % 2]
                            if j < 4:
                                P.op("pe", lambda e: e.matmul(bank.t[:, 0:256], cur.t[:, 256:384], cur.t[:, 0:256], start=True, stop=False), [cur], [bank], inc=False)
                                P.op("pe", lambda e: e.matmul(bank.t[:, 128:256], identb.t[:], cur.t[:, 128:256], start=False, stop=True), [cur, identb], [bank], inc=False)
                                P.op("pe", lambda e: e.matmul(bank.t[:, 256:384], cur.t[:, 0:128], cur.t[:, 256:384], start=True, stop=True), [cur], [bank])
                                yield
                                evac_copy(nxt.t[:], bank.t[:, 0:384], [bank], [nxt])
                                yield
                            else:
                                P.op("pe", lambda e: e.matmul(bank.t[:, 128:256], cur.t[:, 256:384], cur.t[:, 128:256], start=True, stop=False), [cur], [bank], inc=False)
                                P.op("pe", lambda e: e.matmul(bank.t[:, 128:256], identb.t[:], cur.t[:, 128:256], start=False, stop=True), [cur, identb], [bank])
                                yield
                                evac_copy(nxt.t[:, 128:256], bank.t[:, 128:256], [bank], [nxt])
                                yield
                        fin = CH[si][1]
                        PTf = fin.t[:, 128:256]
                        A1T, A2T = AoT[si].t[:, 0:128], AoT[si].t[:, 128:256]
                        lo, hi = bank.t[:, 0:256], bank.t[:, 256:512]
                        P.op("pe", lambda e: e.matmul(lo, PTf, X0b[si].t[:], start=True, stop=True), [fin, X0b[si]], [bank])
                        yield
                        P.op("act", lambda e: e.copy(R1b[si].t[:], lo), [bank], [R1b[si]])
                        P.op("act", lambda e: e.copy(Us.t[:, e_, :], bank.t[:, 128:256]), [bank], [stv[e_]])
                        yield
                        P.op("pe", lambda e: e.matmul(hi, A1T, R1b[si].t[:], start=True, stop=True), [AoT[si], R1b[si]], [bank])
                        yield
                        P.op("act", lambda e: e.copy(Tmb[si].t[:], hi), [bank], [Tmb[si]])
                        yield
                        P.op("pe", lambda e: e.matmul(lo, PTf, Tmb[si].t[:], start=True, stop=True), [fin, Tmb[si]], [bank])
                        yield
                        P.op("dve", lambda e: e.tensor_tensor(X1b[si].t[:], R1b[si].t[:], lo, ALU.subtract), [R1b[si], bank], [X1b[si]])
                        P.op("dve", lambda e: e.tensor_tensor(Us.t[:, e_, :], Us.t[:, e_, :], bank.t[:, 128:256], ALU.subtract), [bank], [stv[e_]])
                        yield
                        P.op("pe", lambda e: e.matmul(hi, A2T, X1b[si].t[:], start=True, stop=True), [AoT[si], X1b[si]], [bank])
                        yield
                        P.op("act", lambda e: e.copy(Tmb[si].t[:], hi), [bank], [Tmb[si]])
                        yield
                        P.op("pe", lambda e: e.matmul(lo, PTf, Tmb[si].t[:], start=True, stop=True), [fin, Tmb[si]], [bank])
                        yield
                        P.op("act", lambda e: e.copy(R1b[si].t[:], lo), [bank], [R1b[si]])
                        P.op("dve", lambda e: e.tensor_tensor(Us.t[:, e_, :], Us.t[:, e_, :], bank.t[:, 128:256], ALU.subtract), [bank], [stv[e_]])
                        yield
                        P.op("pe", lambda e: e.matmul(hi, A1T, R1b[si].t[:], start=True, stop=True), [AoT[si], R1b[si]], [bank])
                        P.op("pool", lambda e: e.tensor_tensor(Wtm[si].t[:], X1b[si].t[:, 0:128], R1b[si].t[:, 0:128], ALU.subtract), [X1b[si], R1b[si]], [Wtm[si]])
                        yield
                        P.op("act", lambda e: e.copy(Tmb[si].t[:], hi), [bank], [Tmb[si]])
                        yield
                        P.op("pe", lambda e: e.matmul(lo, PTf, Tmb[si].t[:], start=True, stop=True), [fin, Tmb[si]], [bank])
                        yield
                        P.op("dve", lambda e: e.tensor_tensor(Wb[si].t[:], Wtm[si].t[:], bank.t[:, 0:128], ALU.add), [Wtm[si], bank], [Wb[si]])
                        P.op("dve", lambda e: e.tensor_tensor(Us.t[:, e_, :], Us.t[:, e_, :], bank.t[:, 128:256], ALU.add), [bank], [stv[e_]])
                        yield
                        while lock["pbt"] is not None:
                            yield
                        lock["pbt"] = si
                        P.op("pe", lambda e: e.transpose(pbt.t[:, 0:128], Wb[si].t[:], identb.t[:]), [Wb[si], identb], [pbt])
                        yield
                        P.op("act", lambda e: e.copy(WTs.t[:, e_, :], pbt.t[:, 0:128]), [pbt], [stv[e_]])
                        lock["pbt"] = None
                        yield

                    def gen_rec(h=h):
                        bA, bB = (pb[6], pb[6]) if NS_CFG[0] > 4 else (pb[5], pb[6])
                        for step in range(NT):
                            info = []
                            for d_ in range(2):
                                c = step if d_ == 0 else NT - 1 - step
                                info.append((d_, c, slice(c * 128, (c + 1) * 128), (c * 2 + d_) * 8 + h, c * 2 + d_))
                            for d_, c, cs, gi, e_ in info:
                                if step > 0:
                                    P.op("pe", lambda e: e.matmul(bA.t[:, d_ * 256:d_ * 256 + 128], WTs.t[:, e_, :], Sbb[d_].t[:], start=True, stop=True), [stv[e_], Sbb[d_]], [bA], inc=False)
                                    P.op("pe", lambda e: e.matmul(bA.t[:, d_ * 256 + 128:d_ * 256 + 256], qnT.t[:, cs], Sbb[d_].t[:], start=True, stop=True), [qnT, Sbb[d_]], [bA])
                                if step < NT - 1:
                                    P.op("act", lambda e: e.activation(kdt[d_].t[:], ktm.t[:, c, :], AF.Copy, scale=kdw.t[:, gi:gi + 1]), [ktm, kdw], [kdt[d_]])
                            yield
                            for d_, c, cs, gi, e_ in info:
                                if step > 0:
                                    P.op("dve", lambda e: e.tensor_tensor(vnb[d_].t[:], Us.t[:, e_, :], bA.t[:, d_ * 256:d_ * 256 + 128], ALU.subtract), [stv[e_], bA], [vnb[d_]])
                                else:
                                    P.op("dve", lambda e: e.tensor_copy(vnb[d_].t[:], Us.t[:, e_, :]), [stv[e_]], [vnb[d_]])
                            for d_, c, cs, gi, e_ in info:
                                if step > 0:
                                    P.op("act", lambda e: e.activation(tmo[d_].t[:], bA.t[:, d_ * 256 + 128:d_ * 256 + 256], AF.Copy, scale=expG.t[:, gi:gi + 1]), [bA, expG], [tmo[d_]])
                            yield
                            for d_, c, cs, gi, e_ in info:
                                P.op("pe", lambda e: e.matmul(bB.t[:, d_ * 256:d_ * 256 + 128], ATs.t[:, e_, :], vnb[d_].t[:], start=True, stop=True), [stv[e_], vnb[d_]], [bB], inc=(step == NT - 1))
                                if step < NT - 1:
                                    P.op("pe", lambda e: e.matmul(bB.t[:, d_ * 256 + 128:d_ * 256 + 256], kdt[d_].t[:], vnb[d_].t[:], start=True, stop=True), [kdt[d_], vnb[d_]], [bB])
                            yield
                            for d_, c, cs, gi, e_ in info:
                                if step < NT - 1:
                                    if step == 0:
                                        P.op("dve", lambda e: e.tensor_copy(Sst[d_].t[:], bB.t[:, d_ * 256 + 128:d_ * 256 + 256]), [bB], [Sst[d_]])
                                    else:
                                        P.op("dve", lambda e: e.scalar_tensor_tensor(Sst[d_].t[:], Sst[d_].t[:], glb.t[:, gi:gi + 1], bB.t[:, d_ * 256 + 128:d_ * 256 + 256], ALU.mult, ALU.add), [bB, glb], [Sst[d_]])
                                    P.op("act", lambda e: e.copy(Sbb[d_].t[:], Sst[d_].t[:]), [Sst[d_]], [Sbb[d_]])
                            for d_, c, cs, gi, e_ in info:
                                first = (d_ == 0 and c < 8) or (d_ == 1 and c >= 8)
                                if step > 0:
                                    P.op("dve", lambda e: e.tensor_tensor(tmo[d_].t[:], tmo[d_].t[:], bB.t[:, d_ * 256:d_ * 256 + 128], ALU.add), [bB], [tmo[d_]])
                                    src_ap, src_b = tmo[d_].t[:], tmo[d_]
                                    if first:
                                        P.op("act", lambda e: e.copy(osb.t[:, c, :], src_ap), [src_b], [osv[c]])
                                    else:
                                        P.op("dve", lambda e: e.tensor_tensor(osb.t[:, c, :], osb.t[:, c, :], src_ap, ALU.add), [src_b], [osv[c]])
                                else:
                                    if first:
                                        P.op("dve", lambda e: e.tensor_copy(osb.t[:, c, :], bB.t[:, d_ * 256:d_ * 256 + 128]), [bB], [osv[c]])
                                    else:
                                        P.op("dve", lambda e: e.tensor_tensor(osb.t[:, c, :], osb.t[:, c, :], bB.t[:, d_ * 256:d_ * 256 + 128], ALU.add), [bB], [osv[c]])
                            yield

                    lock = {"pb0": None, "pbt": None}
                    order = []
                    for i in range(NT):
                        order.append((i, 0))
                        order.append((NT - 1 - i, 1))
                    active = [None] * NS
                    nstarted = 0
                    nfinished = 0
                    finished = [False] * 32
                    rec = gen_rec()
                    rec_step = 0
                    rec_hop = 0
                    rec_done = False
                    tick = 0
                    while nfinished < 32 or not rec_done:
                        if nstarted < 32 and tick % STAG == 0:
                            for si in range(NS):
                                if active[si] is None:
                                    c, d_ = order[nstarted]
                                    active[si] = (gen_prep(si, c, d_), nstarted)
                                    nstarted += 1
                                    break
                        for si in range(NS):
                            if active[si] is not None:
                                g_, idx = active[si]
                                try:
                                    next(g_)
                                except StopIteration:
                                    finished[idx] = True
                                    nfinished += 1
                                    active[si] = None
                        if not rec_done and (stop != "dn0b"):
                            if rec_hop > 0 or (finished[2 * rec_step] and finished[2 * rec_step + 1]):
                                try:
                                    next(rec)
                                    rec_hop += 1
                                    if rec_hop == 4:
                                        rec_hop = 0
                                        rec_step += 1
                                        if rec_step == NT:
                                            rec_done = True
                                except StopIteration:
                                    rec_done = True
                        elif stop == "dn0b":
                            rec_done = True
                        tick += 1
                    if stop == "dn0c":
                        break
                    if h == 0:
                        dump("osb", osv[0], osb.t[:], [128, NT, 128])
                    for tt in range(NT):
                        P.op("act", lambda e: e.activation(ty.t[:], osb.t[:, tt, :], AF.Square, accum_out=ssq.t[:, tt:tt + 1]), [osv[tt]], [ty, ssq])
                    rsqrt_(ssq.t[:, NT:2 * NT], ssq.t[:, 0:NT], 1.0 / 128.0, [ssq], [ssq])
                    for tt in range(NT):
                        bank = pb[4 + tt % 2]
                        zg, ty, yb = zg_l[tt % 2], ty_l[tt % 2], yb_l[tt % 2]
                        proj_tm(wB, 256 + hh * 128, 128, tt, bank)
                        P.op("act", lambda e: e.activation(zg.t[:], bank.t[:, 0:128], AF.Silu), [bank], [zg])
                        P.op("dve", lambda e: e.scalar_tensor_tensor(ty.t[:], osb.t[:, tt, :], ssq.t[:, NT + tt:NT + tt + 1], rowp.t[:, 1024:1152], ALU.mult, ALU.mult), [osv[tt], ssq, rowp], [ty])
                        P.op("dve", lambda e: e.tensor_tensor(yb.t[:], ty.t[:], zg.t[:], ALU.mult), [ty, zg], [yb])
                        P.op("pe", lambda e: e.transpose(pbt.t[:, 0:128], yb.t[:], identb.t[:]), [yb, identb], [pbt])
                        P.op("act", lambda e: e.copy(ydT.t[:, tt * 128:(tt + 1) * 128], pbt.t[:, 0:128]), [pbt], [ydT])
                    P.dma(ysT.t[1, h * 128:(h + 1) * 128, :], ydT.t[:], [ydT], [ysv[1][h]])
                    if stop == "dn0":
                        break
            if stop in ("dn0", "dn", "dn0a", "dn0b", "dn0c"):
                stopped = True
                break

            wlim[0] = NWB
            with scope() as sc:
                cx = sc.sb("cx", [128, S + 2], F32)
                Bsb = sc.sb("Bsb", [128, S], F32)
                ycv = sc.sb("ycv", [128, S], F32)
                tmx_l = [sc.sb("tmx%d" % i, [128, 512], F32) for i in range(2)]
                ycT = sc.sb("ycT", [128, S], BF16)
                P.op("dve", lambda e: e.memset(cx.t[:, 0:1], 0.0), [], [cx])
                P.op("dve", lambda e: e.memset(cx.t[:, S + 1:S + 2], 0.0), [], [cx])
                wA = wB = None
                for dc in range(8):
                    dd = dc % 2
                    if dd == 0:
                        wA, wB = nextw(), nextw()
                        wload(wA, 0, w_in, O_SB + dc * 128, 256)
                        wload(wA, 256, w_in, O_SC + dc * 128, 256)
                        wload(wB, 0, w_in, O_SX + dc * 128, 256)
                    for tb in range(4):
                        bs = slice(tb * 512, (tb + 1) * 512)
                        tmx = tmx_l[tb % 2]
                        for j, (wsrc, off) in enumerate(((wA, dd * 128), (wA, 256 + dd * 128), (wB, dd * 128))):
                            bank = pb[j + 3 * (tb % 2)]
                            for kc in range(8):
                                P.op("pe", lambda e: e.matmul(bank.t[:, 0:512], wsrc.t[:, kc, off:off + 128], hT.t[:, kc, bs], start=(kc == 0), stop=(kc == 7)),
                                     [wsrc] + hTv[tb * 4:tb * 4 + 4], [bank], inc=(kc == 7))
                        o3 = 3 * (tb % 2)
                        P.op("act", lambda e: e.copy(Bsb.t[:, bs], pb[o3].t[:, 0:512]), [pb[o3]], [Bsb])
                        P.op("act", lambda e: e.copy(tmx.t[:], pb[o3 + 2].t[:, 0:512]), [pb[o3 + 2]], [tmx])
                        P.op("dve", lambda e: e.tensor_tensor(cx.t[:, 1 + tb * 512:1 + (tb + 1) * 512], pb[o3 + 1].t[:, 0:512], tmx.t[:], ALU.mult), [pb[o3 + 1], tmx], [cx])
                    cw = lambda k: colp.t[:, 88 + k * 8 + dc:88 + k * 8 + dc + 1]
                    P.op("dve", lambda e: e.tensor_scalar(ycv.t[:], cx.t[:, 0:S], cw(0), None, ALU.mult), [cx, colp], [ycv])
                    P.op("dve", lambda e: e.scalar_tensor_tensor(ycv.t[:], cx.t[:, 1:S + 1], cw(1), ycv.t[:], ALU.mult, ALU.add), [cx, colp], [ycv])
                    P.op("dve", lambda e: e.scalar_tensor_tensor(ycv.t[:], cx.t[:, 2:S + 2], cw(2), ycv.t[:], ALU.mult, ALU.add), [cx, colp], [ycv])
                    P.op("dve", lambda e: e.tensor_tensor(ycT.t[:], ycv.t[:], Bsb.t[:], ALU.mult), [ycv, Bsb], [ycT])
                    P.dma(ysT.t[2, dc * 128:(dc + 1) * 128, :], ycT.t[:], [ycT], [ysv[2][dc]])
            if stop == "sc":
                stopped = True
                break

            if l + 1 < nlayers:
                P.dma(colps[(l + 1) % 2].t[:], colp_d[l + 1], [], [colps[(l + 1) % 2]])
            last = (l == DEPTH - 1)
            for half in range(2):
                with scope() as sch:
                    xres = sch.sb("xres", [128, 8, D], F32)
                    xrv = [Buf(xres.t, "xr%d" % i) for i in range(8)]
                    with scope() as sc:
                        ys_sb = [sc.sb("ys_sb%d" % n, [128, 8, 1024], BF16) for n in range(3)]
                        sg_l = [[sc.sb("sg%d_%d" % (n, i), [128, 512], F32) for n in range(3)] for i in range(2)]
                        acc_l = [sc.sb("acc%d" % i, [128, 512], F32) for i in range(2)]
                        tmm_l = [sc.sb("tmm", [128, 512], F32)] * 2
                        mixT = sc.sb("mixT", [128, 8, 1024], BF16)
                        for n in range(3):
                            P.dma(ys_sb[n].t[:], ysT.t[n].rearrange("(kc p) t -> p kc t", p=128)[:, :, half * 1024:(half + 1) * 1024], ysv[n], [ys_sb[n]])
                        wA = wB = wC = None
                        for dc in range(8):
                            dd = dc % 2
                            if dd == 0:
                                wA, wB, wC = nextw(), nextw(), nextw()
                                wload(wA, 0, w_br_d[l, 0], dc * 128, 256)
                                wload(wA, 256, w_br_d[l, 1], dc * 128, 256)
                                wload(wB, 0, w_br_d[l, 2], dc * 128, 256)
                                wload(wB, 256, w_in, O_MRG + dc * 128, 256)
                                wload(wC, 0, w_in, O_MRG + 1024 + dc * 128, 256)
                                wload(wC, 256, w_in, O_MRG + 2048 + dc * 128, 256)
                            wbr = ((wA, dd * 128), (wA, 256 + dd * 128), (wB, dd * 128))
                            wgt_ = ((wB, 256 + dd * 128), (wC, dd * 128), (wC, 256 + dd * 128))
                            for tbh in range(2):
                                tb = half * 2 + tbh
                                bs = slice(tb * 512, (tb + 1) * 512)
                                bsh = slice(tbh * 512, (tbh + 1) * 512)
                                sg, acc, tmm = sg_l[tbh], acc_l[tbh], tmm_l[tbh]
                                for n in range(3):
                                    wsrc, off = wgt_[n]
                                    for kc in range(8):
                                        P.op("pe", lambda e: e.matmul(pb[3 + n].t[:, 0:512], wsrc.t[:, kc, off:off + 128], hT.t[:, kc, bs], start=(kc == 0), stop=(kc == 7)),
                                             [wsrc] + hTv[tb * 4:tb * 4 + 4], [pb[3 + n]], inc=(kc == 7))
                                    P.op("act", lambda e: e.activation(sg[n].t[:], pb[3 + n].t[:, 0:512], AF.Sigmoid), [pb[3 + n]], [sg[n]])
                                for n in range(3):
                                    wsrc, off = wbr[n]
                                    for kc in range(8):
                                        P.op("pe", lambda e: e.matmul(pb[n].t[:, 0:512], wsrc.t[:, kc, off:off + 128], ys_sb[n].t[:, kc, bsh], start=(kc == 0), stop=(kc == 7)),
                                             [wsrc, ys_sb[n]], [pb[n]], inc=(kc == 7))
                                P.op("dve", lambda e: e.tensor_tensor(acc.t[:], sg[0].t[:], pb[0].t[:, 0:512], ALU.mult), [sg[0], pb[0]], [acc])
                                P.op("dve", lambda e: e.tensor_tensor(tmm.t[:], sg[1].t[:], pb[1].t[:, 0:512], ALU.mult), [sg[1], pb[1]], [tmm])
                                P.op("dve", lambda e: e.tensor_tensor(sg[2].t[:], sg[2].t[:], pb[2].t[:, 0:512], ALU.mult), [pb[2]], [sg[2]])
                                P.op("dve", lambda e: e.tensor_tensor(acc.t[:], acc.t[:], tmm.t[:], ALU.add), [tmm], [acc])
                                P.op("dve", lambda e: e.tensor_tensor(mixT.t[:, dc, bsh], acc.t[:], sg[2].t[:], ALU.add), [acc, sg[2]], [mixT])
                        wo0, wo1 = nextw(), nextw()
                        wload(wo0, 0, w_out_d[l], 0, 512)
                        wload(wo1, 0, w_out_d[l], 512, 512)
                        for t8 in range(8):
                            tt = half * 8 + t8
                            P.dma(xres.t[:, t8, :], xcur.t[tt * 128:(tt + 1) * 128, :], [xcv[tt]], [xrv[t8]])
                            for nb, wsrc in enumerate((wo0, wo1)):
                                bank = pb[(t8 * 2 + nb) % 4]
                                for dc in range(8):
                                    P.op("pe", lambda e: e.matmul(bank.t[:, 0:512], mixT.t[:, dc, t8 * 128:(t8 + 1) * 128], wsrc.t[:, dc, :], start=(dc == 0), stop=(dc == 7)),
                                         [mixT, wsrc], [bank], inc=(dc == 7))
                                P.op("dve", lambda e: e.tensor_tensor(xres.t[:, t8, nb * 512:(nb + 1) * 512], xres.t[:, t8, nb * 512:(nb + 1) * 512], bank.t[:, 0:512], ALU.add), [bank], [xrv[t8]])
                            norm_tile(xres.t[:, t8, :], xrv[t8], tt, (colp, colp.t[:, 8:16]))
                            if stop == "mix" and "d_xm" in dbg_d:
                                P.dma(dbg_d["d_xm"][tt * 128:(tt + 1) * 128, :], xres.t[:, t8, :], [xrv[t8]], [])
                    if stop == "mix":
                        continue
                    with scope() as sc:
                        upT = sc.sb("upT", [128, 8, 1024], BF16)
                        relu_t = [sc.sb("relu_t%d" % i, [128, 512], F32) for i in range(2)]
                        otile = [sc.sb("otile%d" % i, [128, D], F32) for i in range(2)] if last else None
                        gfin = sc.sb("gfin_sb", [128, D], F32) if last else None
                        if last:
                            P.dma(gfin.t[:], gfin_d.partition_broadcast(128), [], [gfin])
                        for fb in range(4):
                            wu = [nextw(), nextw()]
                            wd = [nextw(), nextw()]
                            wload(wu[0], 0, w_up_d[l], fb * 1024, 512)
                            wload(wu[1], 0, w_up_d[l], fb * 1024 + 512, 512)
                            wload(wd[0], 0, w_dn_d[l, fb * 1024:(fb + 1) * 1024, :], 0, 512)
                            wload(wd[1], 0, w_dn_d[l, fb * 1024:(fb + 1) * 1024, :], 512, 512)
                            for fc in range(8):
                                def ev(tb, bank):
                                    bsh = slice((tb - half * 2) * 512, (tb - half * 2 + 1) * 512)
                                    rl = relu_t[tb % 2]
                                    P.op("act", lambda e: e.activation(rl.t[:], bank.t[:, 0:512], AF.Relu), [bank], [rl])
                                    P.op("dve", lambda e: e.tensor_tensor(upT.t[:, fc, bsh], rl.t[:], rl.t[:], ALU.mult), [rl], [upT])
                                proj_fm(wu[fc // 4], (fc % 4) * 128, pb[0:4], ev, tbs=(half * 2, half * 2 + 1))
                            for t8 in range(8):
                                tt = half * 8 + t8
                                for nb in range(2):
                                    bank = pb[4 + (t8 * 2 + nb) % 3]
                                    for fc in range(8):
                                        P.op("pe", lambda e: e.matmul(bank.t[:, 0:512], upT.t[:, fc, t8 * 128:(t8 + 1) * 128], wd[nb].t[:, fc, :], start=(fc == 0), stop=(fc == 7)),
                                             [upT, wd[nb]], [bank], inc=(fc == 7))
                                    P.op("dve", lambda e: e.tensor_tensor(xres.t[:, t8, nb * 512:(nb + 1) * 512], xres.t[:, t8, nb * 512:(nb + 1) * 512], bank.t[:, 0:512], ALU.add), [bank], [xrv[t8]])
                                if fb == 3:
                                    if stop == "mlp" and "d_xm" in dbg_d:
                                        P.dma(dbg_d["d_xm"][tt * 128:(tt + 1) * 128, :], xres.t[:, t8, :], [xrv[t8]], [])
                                    if not last:
                                        P.dma(xcur.t[tt * 128:(tt + 1) * 128, :], xres.t[:, t8, :], [xrv[t8]], [xcv[tt]])
                                        if l + 1 < nlayers:
                                            cn = colps[(l + 1) % 2]
                                            norm_tile(xres.t[:, t8, :], xrv[t8], tt, (cn, cn.t[:, 0:8]))
                                    else:
                                        ot = otile[t8 % 2]
                                        nsq, nss = nsq_l[t8 % 2], nss_l[t8 % 2]
                                        P.op("act", lambda e: e.activation(nsq.t[:], xres.t[:, t8, :], AF.Square, accum_out=nss.t[:, 0:1]), [xrv[t8]], [nsq, nss])
                                        rsqrt_(nss.t[:, 1:2], nss.t[:, 0:1], 1.0 / D, [nss], [nss])
                                        P.op("dve", lambda e: e.scalar_tensor_tensor(ot.t[:], xres.t[:, t8, :], nss.t[:, 1:2], gfin.t[:], ALU.mult, ALU.mult), [xrv[t8], nss, gfin], [ot])
                                        P.dma(out_d[tt * 128:(tt + 1) * 128, :], ot.t[:], [ot], [])
            if stop in ("mix", "mlp"):
                stopped = True
                break
        if stopped:
            dump_ys()
        P.finish()
        print("build: ops", P.nops, "waits", P.nwait, {k: P.cnt[k] for k in P.cnt})
    return nc


def make_params(inp):
    colp = np.zeros((DEPTH, 128, NCOL), np.float32)
    rowp = np.zeros((DEPTH, NROW), np.float32)
    for l in range(DEPTH):
        colp[l, :, 0:8] = inp["norm_mix_g"][l].reshape(8, 128).T
        colp[l, :, 8:16] = inp["norm_mlp_g"][l].reshape(8, 128).T
        colp[l, :, 16:88] = inp["dn_conv_w"][l].reshape(3, 24, 128).transpose(2, 0, 1).reshape(128, 72)
        colp[l, :, 88:112] = inp["sc_conv_w"][l].reshape(3, 8, 128).transpose(2, 0, 1).reshape(128, 24)
        rowp[l, 0:1024] = inp["m_norm_g"][l]
        rowp[l, 1024:1152] = inp["dn_norm_g"][l]
        rowp[l, 1152:1168] = inp["m_gate_b"][l].reshape(16)
        rowp[l, 1168:1184] = inp["dn_a_log"][l].reshape(16)
        rowp[l, 1184:1200] = inp["dn_dt_bias"][l].reshape(16)
    return colp, rowp


def make_in_maps(inp, cores):
    colp, rowp = make_params(inp)
    shared = {"w_in": np.ascontiguousarray(inp["w_in"]), "w_branch": np.ascontiguousarray(inp["w_branch"]),
              "w_out": np.ascontiguousarray(inp["w_out"]), "w_up": np.ascontiguousarray(inp["w_up"]),
              "w_down": np.ascontiguousarray(inp["w_down"]), "colp": colp, "rowp": rowp,
              "gfin": np.ascontiguousarray(inp["norm_final_g"])}
    return [dict(shared, x=np.ascontiguousarray(inp["x"][b])) for b in cores]


def kernel(**inputs):
    inp = {k: np.asarray(v, dtype=np.float32) for k, v in inputs.items()}
    nc = build()
    in_maps = make_in_maps(inp, list(range(8)))
    res = run_bass_kernel_spmd(nc, in_maps, core_ids=list(range(8)))
    return np.stack([r["out"] for r in res.results], axis=0).astype(np.float32)
```
